# Optimizing a Trainium2 kernel written in Bass

```python
import jax, jax.numpy as jnp
from jax import lax
import numpy as np

D_MODEL = 1024
BATCH = 8
SEQ = 2048
DEPTH = 2

N_MIXERS = 4
GROUP_W = D_MODEL // N_MIXERS
HEAD_DIM = 64
N_HEADS = GROUP_W // HEAD_DIM
D_FF = 2816
FFN_RESID = 0.5
SC_WIDTH = 3
CHUNK = 128
CM_WIDTH = 31
RANK_W = 32
RANK_A = 32
RANK_G = 64
A_COLS = 3 * GROUP_W
B_COLS = 2 * GROUP_W
C_COLS = 3 * GROUP_W + RANK_W + RANK_A + RANK_G
D_COLS = 2 * GROUP_W
IN_COLS = A_COLS + B_COLS + C_COLS + D_COLS
RMS_EPS = 1e-6
LN_EPS = 1e-5
GN_EPS = 1e-5 * HEAD_DIM

kernel_name = "hybrid_headgroup_conv_gmlp_rwkv7_conformer"


def rms_norm(x, g):
    xf = x.astype(jnp.float32)
    y = xf * lax.rsqrt(jnp.mean(xf * xf, axis=-1, keepdims=True) + RMS_EPS)
    return (y * g.astype(jnp.float32)).astype(x.dtype)


def layer_norm(x, g, b, eps=LN_EPS):
    xf = x.astype(jnp.float32)
    mu = jnp.mean(xf, axis=-1, keepdims=True)
    var = jnp.mean(jnp.square(xf - mu), axis=-1, keepdims=True)
    y = (xf - mu) * lax.rsqrt(var + eps) * g.astype(jnp.float32) + b.astype(jnp.float32)
    return y.astype(x.dtype)


def swiglu(h, w_gate, w_up, w_down):
    return (jax.nn.silu(h @ w_gate) * (h @ w_up)) @ w_down


def token_shift(t):
    return jnp.pad(t, ((0, 0), (1, 0), (0, 0)))[:, :-1, :]


def causal_depthwise_conv(x, w):
    K, C = w.shape
    return lax.conv_general_dilated(
        x, w[:, None, :].astype(x.dtype), window_strides=(1,), padding=[(K - 1, 0)],
        dimension_numbers=("NWC", "WIO", "NWC"), feature_group_count=C)


def spatial_gating(u, v, w_s, b_s, ln_w, ln_b):
    Bsz, S, G = v.shape
    v = layer_norm(v, ln_w, ln_b)
    v = v.reshape(Bsz, S // CHUNK, CHUNK, N_HEADS, G // N_HEADS)
    w = w_s * jnp.tril(jnp.ones((CHUNK, CHUNK), w_s.dtype))
    s = jnp.einsum("hts,bnshd->bnthd", w, v) + b_s.T[None, None, :, :, None]
    return u * s.reshape(Bsz, S, G)


def rwkv7_recurrence(r, w, k, v, a, b):
    Bsz, S, H, N = r.shape
    seq = tuple(jnp.swapaxes(t.astype(jnp.float32), 0, 1) for t in (r, w, k, v, a, b))

    def step(state, inp):
        r_t, w_t, k_t, v_t, a_t, b_t = inp
        sa = jnp.einsum("bhij,bhj->bhi", state, a_t)
        state = (state * w_t[:, :, None, :] + sa[..., None] * b_t[:, :, None, :]
                 + v_t[..., None] * k_t[:, :, None, :])
        y = jnp.einsum("bhij,bhj->bhi", state, r_t)
        return state, y

    s0 = jnp.zeros((Bsz, H, N, N), jnp.float32)
    _, ys = lax.scan(step, s0, seq)
    return jnp.swapaxes(ys, 0, 1)


def rwkv7_time_mix(pc, mu, w0, w_up, a0, a_up, g_up, k_k, k_a, r_k, ln_w, ln_b):
    Bsz, S, _ = pc.shape
    pc = pc + (token_shift(pc) - pc) * mu
    G = GROUP_W
    r, k, v, wd, ad, gd = jnp.split(
        pc, [G, 2 * G, 3 * G, 3 * G + RANK_W, 3 * G + RANK_W + RANK_A], axis=-1)
    w_log = -jax.nn.softplus(-(w0 + jnp.tanh(wd) @ w_up)) - 0.5
    decay = jnp.exp(-jnp.exp(w_log.astype(jnp.float32)))
    a = jax.nn.sigmoid(a0 + ad @ a_up)
    g = jax.nn.sigmoid(gd) @ g_up
    heads = lambda t: t.reshape(Bsz, S, N_HEADS, HEAD_DIM)
    kk = heads(k * k_k).astype(jnp.float32)
    kk = kk / jnp.maximum(jnp.sqrt(jnp.sum(kk * kk, axis=-1, keepdims=True)), 1e-12)
    k = k * (1.0 + (a - 1.0) * k_a)
    rh, kh, vh, ah = heads(r), heads(k), heads(v), heads(a)
    o = rwkv7_recurrence(rh, heads(decay), kh, vh, -kk, kk * ah).astype(pc.dtype)
    o = layer_norm(o, ln_w.reshape(N_HEADS, HEAD_DIM), ln_b.reshape(N_HEADS, HEAD_DIM), eps=GN_EPS)
    o = o + jnp.sum(rh * kh * r_k, axis=-1, keepdims=True) * vh
    return o.reshape(Bsz, S, G) * g


def hybrid_mix(h, w_in, sc_conv_w, sg_ln_w, sg_ln_b, sg_w, sg_b,
               rk_mu, rk_w0, rk_w_up, rk_a0, rk_a_up, rk_g_up, rk_k_k, rk_k_a, rk_r_k,
               rk_ln_w, rk_ln_b, cm_conv_w, cm_conv_b, cm_ln_w, cm_ln_b, w_out):
    p = h @ w_in
    pa, pb, pc, pd = jnp.split(p, [A_COLS, A_COLS + B_COLS, A_COLS + B_COLS + C_COLS], axis=-1)
    gate_b, gate_c, xa = jnp.split(pa, 3, axis=-1)
    y_a = gate_b * causal_depthwise_conv(gate_c * xa, sc_conv_w)
    u, v = jnp.split(pb, 2, axis=-1)
    y_b = spatial_gating(u, v, sg_w, sg_b, sg_ln_w, sg_ln_b)
    y_c = rwkv7_time_mix(pc, rk_mu, rk_w0, rk_w_up, rk_a0, rk_a_up, rk_g_up,
                         rk_k_k, rk_k_a, rk_r_k, rk_ln_w, rk_ln_b)
    z1, z2 = jnp.split(pd, 2, axis=-1)
    zd = causal_depthwise_conv(z1 * jax.nn.sigmoid(z2), cm_conv_w) + cm_conv_b
    y_d = jax.nn.silu(layer_norm(zd, cm_ln_w, cm_ln_b))
    return jnp.concatenate([y_a, y_b, y_c, y_d], axis=-1) @ w_out


def setup_inputs(seed: int = 0) -> dict:
    key = jax.random.key(seed)
    ks = iter(jax.random.split(key, 48))
    nrm = lambda shape, scale: jax.random.normal(next(ks), shape, jnp.float32) * scale
    gain = lambda shape: 1.0 + 0.02 * jax.random.normal(next(ks), shape, jnp.float32)
    L, D, F, G = DEPTH, D_MODEL, D_FF, GROUP_W
    return {
        "x": nrm((BATCH, SEQ, D), 1.0),
        "ffn1_pre_g": gain((L, D)),
        "ffn1_w_gate": nrm((L, D, F), D ** -0.5),
        "ffn1_w_up": nrm((L, D, F), D ** -0.5),
        "ffn1_w_down": nrm((L, F, D), F ** -0.5),
        "ffn1_post_g": gain((L, D)),
        "mix_pre_g": gain((L, D)),
        "w_in": nrm((L, D, IN_COLS), D ** -0.5),
        "sc_conv_w": nrm((L, SC_WIDTH, G), SC_WIDTH ** -0.5),
        "sg_ln_w": gain((L, G)),
        "sg_ln_b": nrm((L, G), 0.02),
        "sg_w": nrm((L, N_HEADS, CHUNK, CHUNK), 0.5 * CHUNK ** -0.5),
        "sg_b": gain((L, N_HEADS, CHUNK)) + nrm((L, N_HEADS, CHUNK), 0.1),
        "rk_mu": jax.random.uniform(next(ks), (L, C_COLS), jnp.float32, 0.0, 1.0),
        "rk_w0": jax.random.uniform(next(ks), (L, G), jnp.float32, -6.0, 1.0),
        "rk_w_up": nrm((L, RANK_W, G), 0.1 * RANK_W ** -0.5),
        "rk_a0": nrm((L, G), 0.1),
        "rk_a_up": nrm((L, RANK_A, G), 0.1 * RANK_A ** -0.5),
        "rk_g_up": nrm((L, RANK_G, G), RANK_G ** -0.5),
        "rk_k_k": 0.85 + nrm((L, G), 0.02),
        "rk_k_a": gain((L, G)),
        "rk_r_k": nrm((L, N_HEADS, HEAD_DIM), 0.1),
        "rk_ln_w": gain((L, G)),
        "rk_ln_b": nrm((L, G), 0.02),
        "cm_conv_w": nrm((L, CM_WIDTH, G), CM_WIDTH ** -0.5),
        "cm_conv_b": nrm((L, G), 0.02),
        "cm_ln_w": gain((L, G)),
        "cm_ln_b": nrm((L, G), 0.02),
        "w_out": nrm((L, D, D), D ** -0.5),
        "mix_post_g": gain((L, D)),
        "ffn2_pre_g": gain((L, D)),
        "ffn2_w_gate": nrm((L, D, F), D ** -0.5),
        "ffn2_w_up": nrm((L, D, F), D ** -0.5),
        "ffn2_w_down": nrm((L, F, D), F ** -0.5),
        "ffn2_post_g": gain((L, D)),
    }


def reference(x, ffn1_pre_g, ffn1_w_gate, ffn1_w_up, ffn1_w_down, ffn1_post_g,
              mix_pre_g, w_in, sc_conv_w, sg_ln_w, sg_ln_b, sg_w, sg_b,
              rk_mu, rk_w0, rk_w_up, rk_a0, rk_a_up, rk_g_up, rk_k_k, rk_k_a, rk_r_k,
              rk_ln_w, rk_ln_b, cm_conv_w, cm_conv_b, cm_ln_w, cm_ln_b, w_out, mix_post_g,
              ffn2_pre_g, ffn2_w_gate, ffn2_w_up, ffn2_w_down, ffn2_post_g):
    for l in range(DEPTH):
        h = rms_norm(x, ffn1_pre_g[l])
        x = x + FFN_RESID * rms_norm(
            swiglu(h, ffn1_w_gate[l], ffn1_w_up[l], ffn1_w_down[l]), ffn1_post_g[l])
        h = rms_norm(x, mix_pre_g[l])
        m = hybrid_mix(h, w_in[l], sc_conv_w[l], sg_ln_w[l], sg_ln_b[l], sg_w[l], sg_b[l],
                       rk_mu[l], rk_w0[l], rk_w_up[l], rk_a0[l], rk_a_up[l], rk_g_up[l],
                       rk_k_k[l], rk_k_a[l], rk_r_k[l], rk_ln_w[l], rk_ln_b[l],
                       cm_conv_w[l], cm_conv_b[l], cm_ln_w[l], cm_ln_b[l], w_out[l])
        x = x + rms_norm(m, mix_post_g[l])
        h = rms_norm(x, ffn2_pre_g[l])
        x = x + FFN_RESID * rms_norm(
            swiglu(h, ffn2_w_gate[l], ffn2_w_up[l], ffn2_w_down[l]), ffn2_post_g[l])
    return x
```

```python
import contextlib
import numpy as np
import concourse.bass as bass
import concourse.mybir as mybir
from concourse.bass_utils import run_bass_kernel_spmd

F32 = mybir.dt.float32
BF16 = mybir.dt.bfloat16
AF = mybir.ActivationFunctionType
ALU = mybir.AluOpType

D = 1024; T = 2048; DFF = 2816; NFC = 22; G = 256; INC = 2688
NL = 2
SAME_ENGINE_SYNC = True
import os
DBG = int(os.environ.get('KDBG', '99'))
DBG2 = int(os.environ.get('KDBG2', '0'))
C_DECAY = float(np.exp(-0.5))

V_G = 0
V_SC = 48
V_CM = 54
V_CMB = 116; V_CMLW = 118; V_CMLB = 120
V_A0 = 122; V_KK = 124; V_KA = 126; V_RK = 128; V_RLW = 130; V_RLB = 132
V_MU = 134
NV = 141


class _Rec:
    def __getattr__(self, name):
        def f(*a, **kw):
            self.call = (name, a, kw)
            return self
        return f


def _call(fn):
    r = _Rec()
    fn(r)
    return r.call


class Prog:
    ENG = ('sync', 'scalar', 'vector', 'gpsimd', 'tensor')

    def __init__(s, nc, stack):
        s.nc = nc; s.stack = stack
        s.streams = {e: [] for e in s.ENG}
        s.sem = {}; s.cnt = {}
        s.lastw = {}; s.readers = {}
        s.known = {e: {} for e in s.ENG}
        for e in s.ENG:
            s._mksem(e)

    def _mksem(s, key):
        s.sem[key] = s.stack.enter_context(s.nc.semaphore("s_" + str(key)))
        s.cnt[key] = 0

    def _deps(s, eng, reads, writes):
        need = {}

        def add(tok):
            if tok is None:
                return
            k, v = tok
            if k == 'tensor' and eng == 'tensor':
                return
            if k == eng and not SAME_ENGINE_SYNC:
                return
            if k not in s.ENG:
                v = s.cnt[k]
            if need.get(k, 0) < v:
                need[k] = v
        for r in reads:
            add(s.lastw.get(r))
            if isinstance(r, tuple) and r[0] == 'ps':
                for k, v in s.readers.get(r, {}).items():
                    if k != eng:
                        add((k, v))
        for w in writes:
            add(s.lastw.get(w))
            for k, v in s.readers.get(w, {}).items():
                add((k, v))
        waits = []
        for k, v in need.items():
            if s.known[eng].get(k, 0) < v:
                s.known[eng][k] = v
                waits.append((k, v))
        return waits

    def _commit(s, tok, reads, writes):
        for r in reads:
            d = s.readers.setdefault(r, {})
            if d.get(tok[0], 0) < tok[1]:
                d[tok[0]] = tok[1]
        for w in writes:
            s.lastw[w] = tok
            s.readers[w] = {}

    def op(s, eng, fn, reads=(), writes=()):
        reads = list(reads); writes = list(writes)
        waits = s._deps(eng, reads, writes)
        s.cnt[eng] += 1
        tok = (eng, s.cnt[eng])
        s.streams[eng].append((waits, _call(fn), eng, 1))
        s._commit(tok, reads, writes)

    def dma(s, eng, semkey, fn, reads=(), writes=()):
        reads = list(reads); writes = list(writes)
        if semkey not in s.sem:
            s._mksem(semkey)
        waits = s._deps(eng, reads, writes)
        s.cnt[semkey] += 16
        tok = (semkey, s.cnt[semkey])
        s.streams[eng].append((waits, _call(fn), semkey, 16))
        s._commit(tok, reads, writes)

    def barrier(s):
        for eng in s.ENG:
            waits = []
            for k, v in s.cnt.items():
                if v > 0 and s.known[eng].get(k, 0) < v and not (k == eng and k == 'tensor'):
                    s.known[eng][k] = v
                    waits.append((k, v))
            if waits:
                s.streams[eng].append((waits, None, None, 0))

    def wait_all(s, eng, keys):
        need = {}
        for k in keys:
            tok = s.lastw.get(k)
            if tok is not None and need.get(tok[0], 0) < tok[1]:
                need[tok[0]] = tok[1]
        s.streams[eng].append((list(need.items()), None, None, 0))

    def emit(s, block):
        for eng in s.ENG:
            items = s.streams[eng]

            def body(e, items=items):
                for waits, fn, semkey, amt in items:
                    for k, v in waits:
                        e.wait_ge(s.sem[k], v)
                    if fn is not None:
                        name, a, kw = fn
                        getattr(e, name)(*a, **kw).then_inc(s.sem[semkey], amt)
            getattr(block, eng)(body)


class Alloc:
    def __init__(s, nc):
        s.nc = nc
        s.base = (nc.sbuf_base + 63) // 64 * 64
        s.top = nc.sbuf_top
        s.off = s.base
        s.n = 0

    def __call__(s, shape, dtype):
        sz = int(np.prod(shape[1:])) * (4 if dtype == F32 else 2)
        sz = (sz + 63) // 64 * 64
        assert s.off + sz <= s.top, ("SBUF overflow", s.off, sz, s.top)
        s.n += 1
        t = s.nc.alloc_sbuf_tensor_at("sb%d" % s.n, list(shape), dtype, offset=s.off)
        s.off += sz
        return t

    def mark(s):
        return s.off

    def release(s, m):
        s.off = m


def pk(b, lo=0, hi=512):
    return [('ps', b)]


def build(n_layers=NL, stop_after=None):
    nc = bass.Bass("TRN2", target_bir_lowering=False)
    dt = lambda name, shape, kind="ExternalInput": nc.dram_tensor(name, list(shape), F32, kind=kind).ap()
    xin = dt("xT", [128, 8, T])
    consts_d = dt("consts", [128, 5 * 128])
    vecs_d = dt("vecs", [128, NL * NV])
    wgu_d = dt("wgu", [NL, 2, NFC, 128, 2, 8, 128])
    wd_d = dt("wd", [NL, 2, 8, 128, NFC, 128])
    win_d = dt("win", [NL, 128, 8, INC])
    wout_d = dt("wout", [NL, 128, 8, D])
    bc_d = dt("bc", [NL, 128, 768])
    lora_d = dt("lora", [NL, 128, 768])
    wmt_d = dt("wmt", [NL, 128, 4, 128])
    sgb_d = dt("sgbT", [NL, 128, 2, 128])
    out_d = dt("outT", [128, 8, T], kind="ExternalOutput")

    stack = contextlib.ExitStack()
    P = Prog(nc, stack)
    A = Alloc(nc)
    op = P.op

    XT = A([128, 8, T], F32)
    CF = A([128, 5 * 128], F32)
    IDF = CF[:, 0:128]; TRI_IF = CF[:, 128:256]; TRI_SF = CF[:, 256:384]
    CB = A([128, 5 * 128], BF16)
    IDB = CB[:, 0:128]; MASK_SI = CB[:, 128:384]
    TRILB = CB[:, 384:512]; BONES = CB[:, 512:640]
    ONESB = A([128, 128], BF16)
    ONESF = A([128, 128], F32)
    VEC = A([128, NL * NV], F32)
    HALFG = A([128, NL * 16], F32)
    EPS = A([128, 4], F32)
    OMMV = A([128, NL * 7], F32)
    ps = [nc.alloc_psum_tensor("psb%d" % i, [128, 512], F32) for i in range(7)]
    psT = nc.alloc_psum_tensor("psT", [128, 1024], BF16)

    for c in range(8):
        P.dma('sync', 'ld_x', lambda e, c=c: e.dma_start(out=XT[:, c, :], in_=xin[:, c, :]),
              writes=[('XT', c, tb) for tb in range(4)])
    P.dma('sync', 'ld_c', lambda e: e.dma_start(out=CF[:], in_=consts_d[:, :]), writes=['CF'])
    P.dma('sync', 'ld_c', lambda e: e.dma_start(out=VEC[:], in_=vecs_d[:, :]), writes=['VEC'])
    op('vector', lambda e: e.tensor_copy(out=CB[:], in_=CF[:]), reads=['CF'], writes=['CB'])
    op('vector', lambda e: e.memset(ONESB[:], 1.0), writes=['ONES'])
    op('vector', lambda e: e.memset(ONESF[:], 1.0), writes=['ONES'])
    for i, v in enumerate([1e-6, 1e-5, 64e-5, 1e-24]):
        op('vector', lambda e, i=i, v=v: e.memset(EPS[:, i:i + 1], v), writes=['EPS'])
    for l in range(NL):
        for j, gi in enumerate([1, 5]):
            op('vector', lambda e, l=l, j=j, gi=gi: e.tensor_scalar(
                out=HALFG[:, l * 16 + j * 8: l * 16 + j * 8 + 8],
                in0=VEC[:, l * NV + V_G + gi * 8: l * NV + V_G + gi * 8 + 8],
                scalar1=0.5, scalar2=None, op0=ALU.mult), reads=['VEC'], writes=['HALFG'])
    for l in range(NL):
        op('vector', lambda e, l=l: e.tensor_scalar(out=OMMV[:, l * 7:l * 7 + 7], in0=VEC[:, l * NV + V_MU:l * NV + V_MU + 7],
                                                    scalar1=-1.0, scalar2=1.0, op0=ALU.mult, op1=ALU.add),
           reads=['VEC'], writes=['OMMV'])

    def vcol(l, off):
        return VEC[:, l * NV + off: l * NV + off + 1]

    def rstd_from(psb, n, scale, epsi, LNV, RS, rk, wk, lk):
        op('scalar', lambda e: e.activation(out=LNV[:, 0:n], in_=psb[:, 0:n], func=AF.Ln,
                                            bias=EPS[:, epsi:epsi + 1], scale=scale),
           reads=rk + ['EPS'], writes=[lk])
        op('scalar', lambda e: e.activation(out=RS[:, 0:n], in_=LNV[:, 0:n], func=AF.Exp, scale=-0.5),
           reads=[lk], writes=[wk])

    work_mark = A.mark()

    def ffn(l, which):
        A.release(work_mark); P.barrier()
        gpre = V_G + (0 if which == 0 else 4) * 8
        HY = A([128, 8, 1024], BF16)
        ACTB = A([128, NFC, 1024], BF16)
        WGU = [A([128, 2, 8, 128], BF16) for _ in range(2)]
        WD = [A([128, NFC, 128], BF16) for _ in range(2)]
        SQ = [A([128, 512], BF16) for _ in range(2)]
        LNV = A([128, 512], F32)
        RS = [A([128, 512], F32) for _ in range(2)]
        SG = [A([128, 512], F32) for _ in range(2)]
        TMP = [A([128, 512], F32) for _ in range(2)]
        tag = 'f%d%d' % (l, which)
        K = lambda name, *idx: (tag, name) + idx
        it = 0
        if DBG <= 0:
            return
        for half in range(2):
            T0 = half * 1024
            for tb in range(2):
                t0 = T0 + tb * 512; gtb = half * 2 + tb
                for c in range(8):
                    b = c % 2
                    op('scalar', lambda e, c=c, b=b, t0=t0: e.activation(out=SQ[b][:], in_=XT[:, c, t0:t0 + 512], func=AF.Square),
                       reads=[('XT', c, gtb)], writes=[K('SQ', b)])
                    op('tensor', lambda e, c=c, b=b: e.matmul(ps[6][:], lhsT=ONESB[:], rhs=SQ[b][:], start=(c == 0), stop=(c == 7)),
                       reads=[K('SQ', b), 'ONES'], writes=pk(6))
                rstd_from(ps[6], 512, 1.0 / D, 0, LNV, RS[tb], pk(6), K('RS', tb), K('LNV'))
                for c in range(8):
                    op('vector', lambda e, c=c, t0=t0, tb=tb: e.scalar_tensor_tensor(
                        out=HY[:, c, tb * 512:(tb + 1) * 512], in0=XT[:, c, t0:t0 + 512], scalar=vcol(l, gpre + c),
                        in1=RS[tb][:], op0=ALU.mult, op1=ALU.mult),
                       reads=[('XT', c, gtb), K('RS', tb), 'VEC'], writes=[K('HY', c, tb)])
            if DBG <= 1:
                return
            for fc in range(NFC):
                wb = fc % 2
                P.dma('gpsimd', K('ldgu', wb), lambda e, fc=fc, wb=wb: e.dma_start(out=WGU[wb][:], in_=wgu_d[l, which, fc]),
                      writes=[K('WGU', wb)])
                for tb in range(2):
                    pg = (it % 2) * 2; pu = pg + 1; sb = it % 2; it += 1
                    for gi, pb in ((0, pg), (1, pu)):
                        for k in range(8):
                            op('tensor', lambda e, gi=gi, pb=pb, k=k, wb=wb, tb=tb: e.matmul(
                                ps[pb][:], lhsT=WGU[wb][:, gi, k, :], rhs=HY[:, k, tb * 512:(tb + 1) * 512],
                                start=(k == 0), stop=(k == 7)),
                               reads=[K('WGU', wb), K('HY', k, tb)], writes=pk(pb))
                    op('scalar', lambda e, pg=pg, sb=sb: e.activation(out=SG[sb][:], in_=ps[pg][:], func=AF.Silu),
                       reads=pk(pg), writes=[K('SG', sb)])
                    op('vector', lambda e, pu=pu, sb=sb, fc=fc, tb=tb: e.tensor_tensor(
                        out=ACTB[:, fc, tb * 512:(tb + 1) * 512], in0=SG[sb][:], in1=ps[pu][:], op=ALU.mult),
                       reads=[K('SG', sb)] + pk(pu), writes=[K('ACT', fc, tb)])
            if DBG <= 2:
                return
            if DBG2 == 10:
                continue
            for dc in range(8):
                wb = dc % 2
                P.dma('gpsimd' if DBG2 != 12 else 'scalar', K('ldd', wb), lambda e, dc=dc, wb=wb: e.dma_start(out=WD[wb][:] if DBG2 != 12 else WD[wb][:, 0:11, :].bitcast(F32), in_=wd_d[l, which, dc] if DBG2 != 12 else wd_d[l, which, dc, :, 0:11, 0:64]),
                      writes=[K('WD', wb)])
                for tb in range(2):
                    if DBG2 == 2 or (DBG2 == 3 and dc >= 1):
                        continue
                    po = (it % 2); sb = it % 2; it += 1
                    if DBG2 == 9:
                        po += 2
                    for fc in range(NFC if DBG2 != 8 else 8):
                        lhs_ = WD[wb][:, fc, :] if DBG2 not in (6, 15) else WGU[wb][:, 0, fc % 8, :]
                        rhs_ = ACTB[:, fc, tb * 512:(tb + 1) * 512] if DBG2 not in (5, 15) else HY[:, fc % 8, tb * 512:(tb + 1) * 512]
                        op('tensor', lambda e, po=po, fc=fc, wb=wb, tb=tb: e.matmul(
                            ps[po][:], lhsT=lhs_, rhs=rhs_,
                            start=(fc == 0), stop=(fc == (NFC if DBG2 != 8 else 8) - 1)),
                           reads=([K('WD', wb)] if DBG2 != 13 else []) + ([K('ACT', fc, tb)] if DBG2 != 14 else []), writes=pk(po))
                    if DBG2 == 4:
                        continue
                    op('scalar', lambda e, po=po, sb=sb: e.activation(out=SQ[sb][:], in_=ps[po][:], func=AF.Square),
                       reads=pk(po), writes=[K('SQ', sb)])
                    op('vector', lambda e, po=po, dc=dc, tb=tb: e.tensor_copy(out=HY[:, dc, tb * 512:(tb + 1) * 512], in_=ps[po][:]),
                       reads=pk(po) + [K('SQ', sb)], writes=[K('HY', dc, tb)])
                    if DBG2 != 1:
                        op('tensor', lambda e, sb=sb, tb=tb, dc=dc: e.matmul(ps[4 + tb][:], lhsT=ONESB[:], rhs=SQ[sb][:],
                                                                            start=(dc == 0), stop=(dc == 7)),
                           reads=[K('SQ', sb), 'ONES'], writes=pk(4 + tb))
            if DBG <= 3:
                return
            hg = l * 16 + which * 8
            for tb in range(2):
                t0 = T0 + tb * 512; gtb = half * 2 + tb
                rstd_from(ps[4 + tb], 512, 1.0 / D, 0, LNV, RS[tb], pk(4 + tb), K('RS', tb), K('LNV'))
                for c in range(8):
                    b = c % 2
                    op('vector', lambda e, c=c, b=b, tb=tb: e.tensor_tensor(
                        out=TMP[b][:], in0=HY[:, c, tb * 512:(tb + 1) * 512], in1=RS[tb][:], op=ALU.mult),
                       reads=[K('HY', c, tb), K('RS', tb)], writes=[K('TMP', b)])
                    op('vector', lambda e, c=c, b=b, t0=t0: e.scalar_tensor_tensor(
                        out=XT[:, c, t0:t0 + 512], in0=TMP[b][:], scalar=HALFG[:, hg + c: hg + c + 1],
                        in1=XT[:, c, t0:t0 + 512], op0=ALU.mult, op1=ALU.add),
                       reads=[K('TMP', b), ('XT', c, gtb), 'HALFG'], writes=[('XT', c, gtb)])

    def mixer(l):
        A.release(work_mark); P.barrier()
        tag = 'm%d' % l
        K = lambda name, *idx: (tag, name) + idx
        HM = A([128, 8, T + 1], BF16)
        YT = A([128, 8, T], BF16)
        grp_mark = A.mark()
        LNV = A([128, 512], F32)
        RS = A([128, 512], F32)
        SQ = [A([128, 512], BF16) for _ in range(2)]
        XTall = lambda c: [('XT', c, tb) for tb in range(4)]
        for c in range(8):
            op('vector', lambda e, c=c: e.memset(HM[:, c, 0:1], 0.0), writes=[K('HM0', c)])
        for tb in range(4):
            t0 = tb * 512
            for c in range(8):
                b = c % 2
                op('scalar', lambda e, c=c, b=b, t0=t0: e.activation(out=SQ[b][:], in_=XT[:, c, t0:t0 + 512], func=AF.Square),
                   reads=[('XT', c, tb)], writes=[K('SQ', b)])
                op('tensor', lambda e, c=c, b=b: e.matmul(ps[6][:], lhsT=ONESB[:], rhs=SQ[b][:], start=(c == 0), stop=(c == 7)),
                   reads=[K('SQ', b), 'ONES'], writes=pk(6))
            rstd_from(ps[6], 512, 1.0 / D, 0, LNV, RS, pk(6), K('RS'), K('LNV'))
            for c in range(8):
                op('vector', lambda e, c=c, t0=t0: e.scalar_tensor_tensor(
                    out=HM[:, c, 1 + t0:1 + t0 + 512], in0=XT[:, c, t0:t0 + 512], scalar=vcol(l, V_G + 16 + c),
                    in1=RS[:], op0=ALU.mult, op1=ALU.mult),
                   reads=[('XT', c, tb), K('RS'), 'VEC'], writes=[K('HM', c, tb)])
        HMr = lambda k, tb: [K('HM', k, tb)]
        HMrs = lambda k, tb: [K('HM', k, tb), K('HM0', k)] + ([K('HM', k, tb - 1)] if tb > 0 else [])

        def proj_fm(pb, W, col0, tb, wkey, ncol=128, W2=None, prow=None):
            t0 = tb * 512
            outap = ps[pb][:] if prow is None else ps[pb][prow[0]:prow[1], :]
            n = 8 if W2 is None else 16
            for k in range(8):
                op('tensor', lambda e, k=k: e.matmul(outap, lhsT=W[:, k, col0:col0 + ncol], rhs=HM[:, k, 1 + t0:1 + t0 + 512],
                                                    start=(k == 0), stop=(k == n - 1)),
                   reads=[wkey] + HMr(k, tb), writes=pk(pb))
            if W2 is not None:
                for k in range(8):
                    op('tensor', lambda e, k=k: e.matmul(outap, lhsT=W2[:, k, col0:col0 + ncol], rhs=HM[:, k, t0:t0 + 512],
                                                        start=False, stop=(k == 7)),
                       reads=[wkey] + HMrs(k, tb), writes=pk(pb))

        A.release(grp_mark); P.barrier()
        WA = A([128, 8, 768], BF16)
        Z = A([128, 2, T + 2], F32)
        TA = [A([128, 512], F32) for _ in range(2)]
        ACC = [A([128, 512], F32) for _ in range(2)]
        P.dma('gpsimd', K('ldwa'), lambda e: e.dma_start(out=WA[:], in_=win_d[l, :, :, 0:768]), writes=[K('WA')])
        for c in range(2):
            op('vector', lambda e, c=c: e.memset(Z[:, c, 0:2], 0.0), writes=[K('Z0', c)])
        it = 0
        for tb in range(4):
            t0 = tb * 512
            for c in range(2):
                b = it % 2; it += 1
                pc_, px_, pb_ = 0 + 3 * b, 1 + 3 * b, 2 + 3 * b
                proj_fm(pc_, WA, 256 + c * 128, tb, K('WA'))
                proj_fm(px_, WA, 512 + c * 128, tb, K('WA'))
                proj_fm(pb_, WA, 0 + c * 128, tb, K('WA'))
                op('scalar', lambda e, b=b, pc_=pc_: e.activation(out=TA[b][:], in_=ps[pc_][:], func=AF.Copy),
                   reads=pk(pc_), writes=[K('TA', b)])
                op('vector', lambda e, b=b, px_=px_, c=c, t0=t0: e.tensor_tensor(
                    out=Z[:, c, 2 + t0:2 + t0 + 512], in0=TA[b][:], in1=ps[px_][:], op=ALU.mult),
                   reads=[K('TA', b)] + pk(px_), writes=[K('Z', c, tb)])
                zr = [K('Z', c, tb), K('Z0', c)] + ([K('Z', c, tb - 1)] if tb > 0 else [])
                op('vector', lambda e, b=b, c=c, t0=t0: e.tensor_scalar(
                    out=ACC[b][:], in0=Z[:, c, t0:t0 + 512], scalar1=vcol(l, V_SC + c * 3 + 0), scalar2=None, op0=ALU.mult),
                   reads=zr + ['VEC'], writes=[K('ACC', b)])
                for j in (1, 2):
                    op('vector', lambda e, b=b, c=c, t0=t0, j=j: e.scalar_tensor_tensor(
                        out=ACC[b][:], in0=Z[:, c, t0 + j:t0 + j + 512], scalar=vcol(l, V_SC + c * 3 + j),
                        in1=ACC[b][:], op0=ALU.mult, op1=ALU.add),
                       reads=zr + ['VEC', K('ACC', b)], writes=[K('ACC', b)])
                op('vector', lambda e, b=b, c=c, t0=t0, pb_=pb_: e.tensor_tensor(
                    out=YT[:, 0 + c, t0:t0 + 512], in0=ACC[b][:], in1=ps[pb_][:], op=ALU.mult),
                   reads=[K('ACC', b)] + pk(pb_), writes=[K('YT', 0 + c, tb)])

        A.release(grp_mark); P.barrier()
        WDm = A([128, 8, 512], BF16)
        ZG = A([128, 2, T + 30], BF16)
        DIAG = A([128, 62, 128], BF16)
        SGT = [A([128, 512], F32) for _ in range(2)]
        LNV = A([128, 512], F32)
        RS = A([128, 512], F32)
        ZD = [A([128, 512], F32) for _ in range(2)]
        ZD2 = [A([128, 512], F32) for _ in range(2)]
        MEAN = A([128, 512], F32)
        MSQ = A([128, 512], F32)
        VAR = A([128, 512], F32)
        DD = [A([128, 512], F32) for _ in range(2)]
        P.dma('gpsimd', K('ldwd'), lambda e: e.dma_start(out=WDm[:], in_=win_d[l, :, :, 2176:2688]), writes=[K('WDm')])
        for c in range(2):
            op('vector', lambda e, c=c: e.memset(ZG[:, c, 0:30], 0.0), writes=[K('ZG0', c)])
            for j in range(31):
                op('vector', lambda e, c=c, j=j: e.tensor_scalar(
                    out=DIAG[:, c * 31 + j, :], in0=IDF, scalar1=vcol(l, V_CM + c * 31 + j), scalar2=None, op0=ALU.mult),
                   reads=['CF', 'VEC'], writes=[K('DIAG', c)])
        it = 0
        for tb in range(4):
            t0 = tb * 512
            for c in range(2):
                b = it % 2; it += 1
                p1, p2 = 0 + 2 * b, 1 + 2 * b
                proj_fm(p1, WDm, 0 + c * 128, tb, K('WDm'))
                proj_fm(p2, WDm, 256 + c * 128, tb, K('WDm'))
                op('scalar', lambda e, b=b, p2=p2: e.activation(out=SGT[b][:], in_=ps[p2][:], func=AF.Sigmoid),
                   reads=pk(p2), writes=[K('SGT', b)])
                op('vector', lambda e, b=b, p1=p1, c=c, t0=t0: e.tensor_tensor(
                    out=ZG[:, c, 30 + t0:30 + t0 + 512], in0=SGT[b][:], in1=ps[p1][:], op=ALU.mult),
                   reads=[K('SGT', b)] + pk(p1), writes=[K('ZG', c, tb)])
            for c in range(2):
                zr = [K('ZG', c, tb), K('ZG0', c)] + ([K('ZG', c, tb - 1)] if tb > 0 else [])
                pcv = 4 + c
                for j in range(31):
                    op('tensor', lambda e, c=c, j=j, t0=t0, pcv=pcv: e.matmul(
                        ps[pcv][:], lhsT=DIAG[:, c * 31 + j, :], rhs=ZG[:, c, t0 + j:t0 + j + 512],
                        start=(j == 0), stop=(j == 30)),
                       reads=zr + [K('DIAG', c)], writes=pk(pcv))
                op('scalar', lambda e, c=c, pcv=pcv: e.activation(out=ZD[c][:], in_=ps[pcv][:], func=AF.Identity,
                                                                  bias=vcol(l, V_CMB + c), scale=1.0),
                   reads=pk(pcv) + ['VEC'], writes=[K('ZD', c)])
                op('vector', lambda e, c=c: e.tensor_tensor(out=ZD2[c][:], in0=ZD[c][:], in1=ZD[c][:], op=ALU.mult),
                   reads=[K('ZD', c)], writes=[K('ZD2', c)])
            for c in range(2):
                op('tensor', lambda e, c=c: e.matmul(ps[6][:], lhsT=ONESF[:], rhs=ZD[c][:], start=(c == 0), stop=(c == 1)),
                   reads=[K('ZD', c), 'ONES'], writes=pk(6))
            for c in range(2):
                op('tensor', lambda e, c=c: e.matmul(ps[0][:], lhsT=ONESF[:], rhs=ZD2[c][:], start=(c == 0), stop=(c == 1)),
                   reads=[K('ZD2', c), 'ONES'], writes=pk(0))
            op('scalar', lambda e: e.activation(out=MEAN[:], in_=ps[6][:], func=AF.Copy, scale=1.0 / G),
               reads=pk(6), writes=[K('MEAN')])
            op('vector', lambda e: e.tensor_tensor(out=MSQ[:], in0=MEAN[:], in1=MEAN[:], op=ALU.mult),
               reads=[K('MEAN')], writes=[K('MSQ')])
            op('vector', lambda e: e.scalar_tensor_tensor(out=VAR[:], in0=ps[0][:], scalar=1.0 / G, in1=MSQ[:],
                                                          op0=ALU.mult, op1=ALU.subtract),
               reads=pk(0) + [K('MSQ')], writes=[K('VAR')])
            op('scalar', lambda e: e.activation(out=LNV[:], in_=VAR[:], func=AF.Ln, bias=EPS[:, 1:2], scale=1.0),
               reads=[K('VAR'), 'EPS'], writes=[K('LNV')])
            op('scalar', lambda e: e.activation(out=RS[:], in_=LNV[:], func=AF.Exp, scale=-0.5),
               reads=[K('LNV')], writes=[K('RS')])
            for c in range(2):
                op('vector', lambda e, c=c: e.tensor_tensor(out=DD[c][:], in0=ZD[c][:], in1=MEAN[:], op=ALU.subtract),
                   reads=[K('ZD', c), K('MEAN')], writes=[K('DD', c)])
                op('vector', lambda e, c=c: e.tensor_tensor(out=DD[c][:], in0=DD[c][:], in1=RS[:], op=ALU.mult),
                   reads=[K('DD', c), K('RS')], writes=[K('DD', c)])
                op('scalar', lambda e, c=c, t0=t0: e.activation(out=YT[:, 6 + c, t0:t0 + 512], in_=DD[c][:], func=AF.Silu,
                                                                bias=vcol(l, V_CMLB + c), scale=vcol(l, V_CMLW + c)),
                   reads=[K('DD', c), 'VEC'], writes=[K('YT', 6 + c, tb)])

        A.release(grp_mark); P.barrier()
        WB = A([128, 8, 512], BF16)
        BCB = A([128, 512], F32)
        WMTF = A([128, 4, 128], F32)
        WMT = A([128, 4, 128], BF16)
        SGB = A([128, 2, 128], F32)
        ST6 = A([128, 6], F32)
        MV = A([128, 2], F32)
        RV = A([128, 2], F32)
        VN = [A([128, 256], F32) for _ in range(2)]
        VNB = [A([128, 256], BF16) for _ in range(2)]
        SB_ = [A([128, 128], F32) for _ in range(2)]
        P.dma('gpsimd', K('ldwb'), lambda e: e.dma_start(out=WB[:], in_=win_d[l, :, :, 768:1280]), writes=[K('WB')])
        P.dma('sync', K('ldb'), lambda e: e.dma_start(out=BCB[:], in_=bc_d[l, :, 256:768]), writes=[K('BCB')])
        P.dma('sync', K('ldb'), lambda e: e.dma_start(out=WMTF[:], in_=wmt_d[l]), writes=[K('WMTF')])
        P.dma('sync', K('ldb'), lambda e: e.dma_start(out=SGB[:], in_=sgb_d[l]), writes=[K('SGB')])
        for h in range(4):
            op('vector', lambda e, h=h: e.tensor_tensor(out=WMT[:, h, :], in0=WMTF[:, h, :], in1=TRI_IF, op=ALU.mult),
               reads=[K('WMTF'), 'CF'], writes=[K('WMT')])
        it = 0
        for tb in range(4):
            for c in range(2):
                proj_fm(4 + c, WB, c * 128, tb, K('WB'))
            for ti in range(4):
                tile_i = tb * 4 + ti; tt0 = tile_i * 128
                b = it % 2; it += 1
                pv = 0 + b; pss = 2 + b
                for k in range(8):
                    op('tensor', lambda e, k=k, pv=pv, tt0=tt0: e.matmul(
                        ps[pv][:, 0:256], lhsT=HM[:, k, 1 + tt0:1 + tt0 + 128], rhs=WB[:, k, 256:512],
                        start=(k == 0), stop=(k == 7)),
                       reads=[K('WB')] + HMr(k, tb), writes=pk(pv, 0, 256))
                op('vector', lambda e, pv=pv: e.bn_stats(out=ST6[:], in_=ps[pv][:, 0:256]),
                   reads=pk(pv, 0, 256), writes=[K('ST6')])
                op('vector', lambda e: e.bn_aggr(out=MV[:], in_=ST6[:]), reads=[K('ST6')], writes=[K('MV')])
                op('scalar', lambda e: e.activation(out=RV[:, 0:1], in_=MV[:, 1:2], func=AF.Ln, bias=EPS[:, 1:2], scale=1.0),
                   reads=[K('MV'), 'EPS'], writes=[K('RV0')])
                op('scalar', lambda e: e.activation(out=RV[:, 1:2], in_=RV[:, 0:1], func=AF.Exp, scale=-0.5),
                   reads=[K('RV0')], writes=[K('RV')])
                op('vector', lambda e, pv=pv, b=b: e.tensor_scalar(
                    out=VN[b][:], in0=ps[pv][:, 0:256], scalar1=MV[:, 0:1], scalar2=RV[:, 1:2],
                    op0=ALU.subtract, op1=ALU.mult),
                   reads=pk(pv, 0, 256) + [K('MV'), K('RV')], writes=[K('VN', b)])
                op('vector', lambda e, b=b: e.tensor_tensor(out=VN[b][:], in0=VN[b][:], in1=BCB[:, 0:256], op=ALU.mult),
                   reads=[K('VN', b), K('BCB')], writes=[K('VN', b)])
                op('vector', lambda e, b=b: e.tensor_tensor(out=VNB[b][:], in0=VN[b][:], in1=BCB[:, 256:512], op=ALU.add),
                   reads=[K('VN', b), K('BCB')], writes=[K('VNB', b)])
                for c in range(2):
                    for hl in range(2):
                        h = 2 * c + hl
                        op('tensor', lambda e, c=c, hl=hl, h=h, b=b, pss=pss: e.matmul(
                            ps[pss][hl * 64:(hl + 1) * 64, c * 128:(c + 1) * 128], lhsT=VNB[b][:, h * 64:(h + 1) * 64],
                            rhs=WMT[:, h, :], start=True, stop=True),
                           reads=[K('VNB', b), K('WMT')], writes=pk(pss, c * 128, c * 128 + 128))
                    op('vector', lambda e, c=c, b=b, pss=pss: e.tensor_tensor(
                        out=SB_[b][:], in0=ps[pss][:, c * 128:(c + 1) * 128], in1=SGB[:, c, :], op=ALU.add),
                       reads=pk(pss, c * 128, c * 128 + 128) + [K('SGB')], writes=[K('SB', b)])
                    op('vector', lambda e, c=c, b=b, ti=ti, tt0=tt0: e.tensor_tensor(
                        out=YT[:, 2 + c, tt0:tt0 + 128], in0=SB_[b][:], in1=ps[4 + c][:, ti * 128:(ti + 1) * 128], op=ALU.mult),
                       reads=[K('SB', b)] + pk(4 + c), writes=[K('YT', 2 + c, tb)])

        A.release(grp_mark); P.barrier()
        rwkv(l, K, HM, YT, HMr, HMrs, A)

        A.release(grp_mark); P.barrier()
        WO = A([128, 8, D], BF16)
        MY = A([128, 8, 512], BF16)
        TMP = [A([128, 512], F32) for _ in range(2)]
        LNV = A([128, 512], F32)
        RS = A([128, 512], F32)
        SQ = [A([128, 512], BF16) for _ in range(2)]
        P.dma('gpsimd', K('ldwo'), lambda e: e.dma_start(out=WO[:], in_=wout_d[l]), writes=[K('WO')])
        it = 0
        for tb in range(4):
            t0 = tb * 512
            for dc in range(8):
                po = it % 2; sb = it % 2; it += 1
                for k in range(8):
                    op('tensor', lambda e, po=po, k=k, dc=dc, t0=t0: e.matmul(
                        ps[po][:], lhsT=WO[:, k, dc * 128:(dc + 1) * 128], rhs=YT[:, k, t0:t0 + 512],
                        start=(k == 0), stop=(k == 7)),
                       reads=[K('WO'), K('YT', k, tb)], writes=pk(po))
                op('scalar', lambda e, po=po, sb=sb: e.activation(out=SQ[sb][:], in_=ps[po][:], func=AF.Square),
                   reads=pk(po), writes=[K('SQ', sb)])
                op('vector', lambda e, po=po, dc=dc: e.tensor_copy(out=MY[:, dc, :], in_=ps[po][:]),
                   reads=pk(po), writes=[K('MY', dc)])
                op('tensor', lambda e, sb=sb, dc=dc: e.matmul(ps[6][:], lhsT=ONESB[:], rhs=SQ[sb][:], start=(dc == 0), stop=(dc == 7)),
                   reads=[K('SQ', sb), 'ONES'], writes=pk(6))
            rstd_from(ps[6], 512, 1.0 / D, 0, LNV, RS, pk(6), K('RS'), K('LNV'))
            for c in range(8):
                b = c % 2
                op('vector', lambda e, c=c, b=b: e.tensor_tensor(out=TMP[b][:], in0=MY[:, c, :], in1=RS[:], op=ALU.mult),
                   reads=[K('MY', c), K('RS')], writes=[K('TMP', b)])
                op('vector', lambda e, c=c, b=b, t0=t0: e.scalar_tensor_tensor(
                    out=XT[:, c, t0:t0 + 512], in0=TMP[b][:], scalar=vcol(l, V_G + 24 + c),
                    in1=XT[:, c, t0:t0 + 512], op0=ALU.mult, op1=ALU.add),
                   reads=[K('TMP', b), ('XT', c, tb), 'VEC'], writes=[('XT', c, tb)])

    def rwkv(l, K, HM, YT, HMr, HMrs, A):
        NB = 256
        WC = A([128, 8, 896], BF16)
        LORAB = A([128, 768], BF16)
        W0B = A([128, 256], F32)
        P.dma('gpsimd', K('ldwc'), lambda e: e.dma_start(out=WC[:], in_=win_d[l, :, :, 1280:2176]), writes=[K('WC')])
        P.dma('gpsimd', K('ldl'), lambda e: e.dma_start(out=LORAB[:], in_=lora_d[l]), writes=[K('LORA')])
        P.dma('sync', K('ldc'), lambda e: e.dma_start(out=W0B[:], in_=bc_d[l, :, 0:256]), writes=[K('W0B')])
        PC = A([128, 6 * NB], F32)
        MISC = A([128, NB], BF16)
        KKb = A([128, 2 * NB], F32); KPb = A([128, 2 * NB], F32); Bb = A([128, 2 * NB], F32)
        Gb = A([128, 2 * NB], BF16); BON = A([128, 2 * NB], BF16); VTB = A([128, 2 * NB], BF16)
        TQ = [A([128, NB], F32) for _ in range(2)]
        TS = [A([128, NB], F32) for _ in range(2)]
        SQB = [A([128, NB], BF16) for _ in range(2)]
        XW = A([128, 256], F32); SIGTM = A([128, 256], F32)
        EP = A([128, 256], F32); EM = A([128, 256], F32); EE = A([128, 256], F32)
        FT_ = A([128, 8 * 128], BF16)
        FH_ = A([128, 4 * 128], BF16)
        TM_ = A([128, 8 * 128], BF16)
        NM_ = A([128, 4 * 512], BF16)
        LP = [A([128, 512], BF16) for _ in range(2)]
        NP_ = [A([128, 512], BF16) for _ in range(2)]
        XF = [A([128, 512], F32) for _ in range(2)]
        XB = A([128, 512], BF16)
        AHT = A([128, 256], BF16)
        UB = A([128, 256], BF16)
        S32 = A([128, 256], F32); SBF = A([128, 256], BF16)
        OT = A([128, 256], F32); OSQ = A([128, 256], F32)
        OM = A([128, 128], F32); OV = A([128, 128], F32); OL = A([128, 128], F32); ORS = A([128, 128], F32)
        BONF = CF[:, 512:640]
        vc = lambda off: vcol(l, off)
        cs = lambda c, a=0, b=NB: slice(c * NB + a, c * NB + b)
        hs = lambda h, a=0, b=128: slice(h * 128 + a, h * 128 + b)
        FT = lambda c, i, p0=0, p1=128: FT_[p0:p1, (c * 4 + i) * 128:(c * 4 + i + 1) * 128]
        FH = lambda c, i: FH_[:, (c * 2 + i) * 128:(c * 2 + i + 1) * 128]
        TM = lambda c, i, a=0, b=128: TM_[:, (c * 4 + i) * 128 + a:(c * 4 + i) * 128 + b]
        NM = lambda h, i, j: NM_[:, h * 512 + i * 256 + j * 128: h * 512 + i * 256 + (j + 1) * 128]
        op('vector', lambda e: e.memset(S32[:], 0.0), writes=[K('S32')])
        op('vector', lambda e: e.memset(SBF[:], 0.0), writes=[K('SBF')])
        allh = lambda nm, i: [K(nm, i, h) for h in range(4)]
        for sb in range(T // NB):
            t0 = sb * NB; tb = t0 // 512
            for cc in range(7):
                pb = cc % 2
                for k in range(8):
                    op('tensor', lambda e: e.matmul(ps[pb][:, 0:NB + 1], lhsT=WC[:, k, cc * 128:(cc + 1) * 128],
                                                    rhs=HM[:, k, t0:t0 + NB + 1], start=(k == 0), stop=(k == 7)),
                       reads=[K('WC')] + HMrs(k, tb), writes=pk(pb, 0, NB + 1))
                op('vector', lambda e: e.tensor_scalar(out=TS[pb][:], in0=ps[pb][:, 0:NB], scalar1=vc(V_MU + cc), scalar2=None,
                                                       op0=ALU.mult),
                   reads=pk(pb, 0, NB + 1) + ['VEC'], writes=[K('TS', pb)])
                dst = PC[:, cs(cc)] if cc < 6 else TQ[0][:]
                dk = K('PC', cc) if cc < 6 else K('TQ', 0)
                op('vector', lambda e: e.scalar_tensor_tensor(out=dst, in0=ps[pb][:, 1:NB + 1], scalar=OMMV[:, l * 7 + cc:l * 7 + cc + 1],
                                                              in1=TS[pb][:], op0=ALU.mult, op1=ALU.add),
                   reads=pk(pb, 0, NB + 1) + ['OMMV', K('TS', pb)], writes=[dk])
            op('scalar', lambda e: e.activation(out=MISC[0:32, :], in_=TQ[0][0:32, :], func=AF.Tanh), reads=[K('TQ', 0)], writes=[K('MISC', 0)])
            op('vector', lambda e: e.tensor_copy(out=MISC[32:64, :], in_=TQ[0][32:64, :]), reads=[K('TQ', 0)], writes=[K('MISC', 1)])
            op('scalar', lambda e: e.activation(out=MISC[64:128, :], in_=TQ[0][64:128, :], func=AF.Sigmoid),
               reads=[K('TQ', 0)], writes=[K('MISC', 2)])
            for c in range(2):
                op('tensor', lambda e: e.matmul(ps[2][:, 0:NB], lhsT=LORAB[32:64, 256 + c * 128:256 + (c + 1) * 128],
                                                rhs=MISC[32:64, :], start=True, stop=True),
                   reads=[K('LORA'), K('MISC', 1)], writes=pk(2, 0, NB))
                op('scalar', lambda e: e.activation(out=Bb[:, cs(c)], in_=ps[2][:, 0:NB], func=AF.Sigmoid, bias=vc(V_A0 + c), scale=1.0),
                   reads=pk(2, 0, NB) + ['VEC'], writes=[K('B', c)])
                op('tensor', lambda e: e.matmul(ps[3][:, 0:NB], lhsT=LORAB[64:128, 512 + c * 128:512 + (c + 1) * 128],
                                                rhs=MISC[64:128, :], start=True, stop=True),
                   reads=[K('LORA'), K('MISC', 2)], writes=pk(3, 0, NB))
                op('vector', lambda e: e.tensor_copy(out=Gb[:, cs(c)], in_=ps[3][:, 0:NB]), reads=pk(3, 0, NB), writes=[K('G', c)])
                op('vector', lambda e: e.tensor_scalar(out=KKb[:, cs(c)], in0=PC[:, cs(2 + c)], scalar1=vc(V_KK + c), scalar2=None, op0=ALU.mult),
                   reads=[K('PC', 2 + c), 'VEC'], writes=[K('KK', c)])
                op('scalar', lambda e: e.activation(out=SQB[c][:], in_=KKb[:, cs(c)], func=AF.Square), reads=[K('KK', c)], writes=[K('SQB', c)])
                op('tensor', lambda e: e.matmul(ps[4][:, 0:NB], lhsT=BONES, rhs=SQB[c][:], start=True, stop=True),
                   reads=[K('SQB', c), 'CB'], writes=pk(4, 0, NB))
                op('scalar', lambda e: e.activation(out=TQ[0][:], in_=ps[4][:, 0:NB], func=AF.Ln, bias=EPS[:, 3:4], scale=1.0),
                   reads=pk(4, 0, NB) + ['EPS'], writes=[K('TQ', 0)])
                op('scalar', lambda e: e.activation(out=TQ[1][:], in_=TQ[0][:], func=AF.Exp, scale=-0.5), reads=[K('TQ', 0)], writes=[K('TQ', 1)])
                op('vector', lambda e: e.tensor_tensor(out=KKb[:, cs(c)], in0=KKb[:, cs(c)], in1=TQ[1][:], op=ALU.mult),
                   reads=[K('KK', c), K('TQ', 1)], writes=[K('KK', c)])
                op('vector', lambda e: e.tensor_scalar(out=TQ[0][:], in0=Bb[:, cs(c)], scalar1=-1.0, scalar2=vc(V_KA + c), op0=ALU.add, op1=ALU.mult),
                   reads=[K('B', c), 'VEC'], writes=[K('TQ', 0)])
                op('vector', lambda e: e.scalar_tensor_tensor(out=KPb[:, cs(c)], in0=TQ[0][:], scalar=1.0, in1=PC[:, cs(2 + c)],
                                                              op0=ALU.add, op1=ALU.mult),
                   reads=[K('TQ', 0), K('PC', 2 + c)], writes=[K('KP', c)])
                op('vector', lambda e: e.tensor_tensor(out=Bb[:, cs(c)], in0=Bb[:, cs(c)], in1=KKb[:, cs(c)], op=ALU.mult),
                   reads=[K('B', c), K('KK', c)], writes=[K('B', c)])
                op('vector', lambda e: e.scalar_tensor_tensor(out=TQ[1][:], in0=PC[:, cs(c)], scalar=vc(V_RK + c), in1=KPb[:, cs(c)],
                                                              op0=ALU.mult, op1=ALU.mult),
                   reads=[K('PC', c), K('KP', c), 'VEC'], writes=[K('TQ', 1)])
                op('vector', lambda e: e.tensor_copy(out=SQB[c][:], in_=TQ[1][:]), reads=[K('TQ', 1)], writes=[K('SQB', c)])
                op('tensor', lambda e: e.matmul(ps[5][:, 0:NB], lhsT=BONES, rhs=SQB[c][:], start=True, stop=True),
                   reads=[K('SQB', c), 'CB'], writes=pk(5, 0, NB))
                op('scalar', lambda e: e.activation(out=BON[:, cs(c)], in_=ps[5][:, 0:NB], func=AF.Copy), reads=pk(5, 0, NB), writes=[K('BON', c)])
                op('vector', lambda e: e.tensor_copy(out=VTB[:, cs(c)], in_=PC[:, cs(4 + c)]), reads=[K('PC', 4 + c)], writes=[K('VTB', c)])
            for ti in range(NB // 128):
                q0 = ti * 128; q1 = q0 + 128; tt0 = t0 + q0
                op('tensor', lambda e: e.matmul(ps[2][:, 0:256], lhsT=MISC[0:32, q0:q1], rhs=LORAB[0:32, 0:256], start=True, stop=True),
                   reads=[K('MISC', 0), K('LORA')], writes=pk(2, 0, 256))
                op('vector', lambda e: e.tensor_tensor(out=XW[:], in0=ps[2][:, 0:256], in1=W0B[:], op=ALU.add),
                   reads=pk(2, 0, 256) + [K('W0B')], writes=[K('XW')])
                op('scalar', lambda e: e.activation(out=SIGTM[:], in_=XW[:], func=AF.Sigmoid), reads=[K('XW')], writes=[K('SIGTM')])
                for c in range(2):
                    op('tensor', lambda e: e.matmul(ps[3][:, c * 128:(c + 1) * 128], lhsT=SIGTM[:, c * 128:(c + 1) * 128], rhs=TRI_IF,
                                                    start=True, stop=True),
                       reads=[K('SIGTM'), 'CF'], writes=pk(3, c * 128, c * 128 + 128))
                    op('tensor', lambda e: e.matmul(ps[3][:, 256 + c * 128:256 + (c + 1) * 128], lhsT=SIGTM[:, c * 128:(c + 1) * 128],
                                                    rhs=TRI_SF, start=True, stop=True),
                       reads=[K('SIGTM'), 'CF'], writes=pk(3, 256 + c * 128, 256 + c * 128 + 128))
                op('scalar', lambda e: e.activation(out=EP[:], in_=ps[3][:, 0:256], func=AF.Exp, scale=-C_DECAY), reads=pk(3, 0, 256), writes=[K('EP')])
                op('scalar', lambda e: e.activation(out=EM[:], in_=ps[3][:, 0:256], func=AF.Exp, scale=C_DECAY), reads=pk(3, 0, 256), writes=[K('EM')])
                op('scalar', lambda e: e.activation(out=EE[:], in_=ps[3][:, 256:512], func=AF.Exp, scale=-C_DECAY), reads=pk(3, 256, 512), writes=[K('EE')])
                for c in range(2):
                    E_ = lambda X_: X_[:, c * 128:(c + 1) * 128]
                    gC = EP[:, c * 128 + 127:c * 128 + 128]
                    tq = slice(c * NB + q0, c * NB + q1)
                    op('vector', lambda e: e.scalar_tensor_tensor(out=FT(c, 0), in0=KKb[:, tq], scalar=-1.0, in1=E_(EE), op0=ALU.mult, op1=ALU.mult),
                       reads=[K('KK', c), K('EE')], writes=[K('FT', c)])
                    op('vector', lambda e: e.tensor_tensor(out=FT(c, 1), in0=Bb[:, tq], in1=E_(EM), op=ALU.mult),
                       reads=[K('B', c), K('EM')], writes=[K('FT', c)])
                    op('vector', lambda e: e.tensor_tensor(out=FT(c, 2), in0=KPb[:, tq], in1=E_(EM), op=ALU.mult),
                       reads=[K('KP', c), K('EM')], writes=[K('FT', c)])
                    op('vector', lambda e: e.tensor_tensor(out=FT(c, 3), in0=PC[:, tq], in1=E_(EP), op=ALU.mult),
                       reads=[K('PC', c), K('EP')], writes=[K('FT', c)])
                    op('vector', lambda e: e.scalar_tensor_tensor(out=FH(c, 0), in0=Bb[:, tq], scalar=gC, in1=E_(EM), op0=ALU.mult, op1=ALU.mult),
                       reads=[K('B', c), K('EM'), K('EP')], writes=[K('FH', c)])
                    op('vector', lambda e: e.scalar_tensor_tensor(out=FH(c, 1), in0=KPb[:, tq], scalar=gC, in1=E_(EM), op0=ALU.mult, op1=ALU.mult),
                       reads=[K('KP', c), K('EM'), K('EP')], writes=[K('FH', c)])
                    srcs = [(FT(c, 0), K('FT', c)), (VTB[:, tq], K('VTB', c)), (FH(c, 0), K('FH', c)), (FH(c, 1), K('FH', c))]
                    for i, (src, sk) in enumerate(srcs):
                        op('tensor', lambda e: e.transpose(out=psT[:, (c * 4 + i) * 128:(c * 4 + i + 1) * 128], in_=src, identity=IDB),
                           reads=[sk, 'CB'], writes=[('ps', 7)])
                    if c == 0:
                        op('scalar', lambda e: e.activation(out=TM_[:, 0:512], in_=psT[:, 0:512], func=AF.Copy), reads=[('ps', 7)], writes=[K('TM', 0)])
                    else:
                        op('vector', lambda e: e.tensor_copy(out=TM_[:, 512:1024], in_=psT[:, 512:1024]), reads=[('ps', 7)], writes=[K('TM', 1)])
                for h in range(4):
                    c = h // 2; hl = h % 2; r0 = hl * 64; r1 = r0 + 64
                    aT = FT(c, 0, r0, r1); bT = FT(c, 1, r0, r1); kT = FT(c, 2, r0, r1); rT = FT(c, 3, r0, r1)
                    pn = h % 2
                    for i, lt in enumerate((bT, kT)):
                        for j2, rt in enumerate((aT, rT)):
                            qq = i * 2 + j2
                            op('tensor', lambda e: e.matmul(ps[pn][:, qq * 128:(qq + 1) * 128], lhsT=lt, rhs=rt, start=True, stop=True),
                               reads=[K('FT', c)], writes=pk(pn, qq * 128, qq * 128 + 128))
                    for i in range(2):
                        op('vector', lambda e: e.tensor_tensor(out=NM(h, i, 0), in0=ps[pn][:, (i * 2) * 128:(i * 2 + 1) * 128],
                                                               in1=CB[:, 256:384], op=ALU.mult),
                           reads=pk(pn, i * 256, i * 256 + 128) + ['CB'], writes=[K('NM', h)])
                        op('vector', lambda e: e.tensor_tensor(out=NM(h, i, 1), in0=ps[pn][:, (i * 2 + 1) * 128:(i * 2 + 2) * 128],
                                                               in1=CB[:, 128:256], op=ALU.mult),
                           reads=pk(pn, i * 256 + 128, i * 256 + 256) + ['CB'], writes=[K('NM', h)])
                    op('tensor', lambda e: e.matmul(ps[2][:, hs(h)], lhsT=aT, rhs=bT, start=True, stop=True),
                       reads=[K('FT', c)], writes=pk(2, h * 128, h * 128 + 128))
                    op('vector', lambda e: e.tensor_tensor(out=LP[0][:, hs(h)], in0=ps[2][:, hs(h)], in1=TRILB, op=ALU.mult),
                       reads=pk(2, h * 128, h * 128 + 128) + ['CB'], writes=[K('LP', 0, h)])
                    op('gpsimd', lambda e: e.tensor_copy(out=NP_[0][:, hs(h)], in_=NM(h, 0, 0)), reads=[K('NM', h)], writes=[K('NP', 0, h)])
                    op('tensor', lambda e: e.matmul(ps[3][:, hs(h, 64, 128)], lhsT=NM(h, 1, 0), rhs=TM(c, 1, r0, r1), start=True, stop=True),
                       reads=[K('NM', h), K('TM', c)], writes=pk(3, h * 128, h * 128 + 128))
                    op('vector', lambda e: e.tensor_copy(out=XF[0][:, hs(h, 64, 128)], in_=ps[3][:, hs(h, 64, 128)]),
                       reads=pk(3, h * 128, h * 128 + 128), writes=[K('XF', 0, h)])
                    op('gpsimd', lambda e: e.tensor_copy(out=XF[0][:, hs(h, 0, 64)], in_=TM(c, 0, r0, r1)),
                       reads=[K('TM', c), K('XF', 0, h)], writes=[K('XF', 0, h)])
                cur = 0
                for lvl in range(7):
                    nxt = 1 - cur
                    op('vector', lambda e: e.tensor_copy(out=XB[:], in_=XF[cur][:]), reads=allh('XF', cur), writes=[K('XB')])
                    for h in range(4):
                        op('tensor', lambda e: e.matmul(ps[4][:, hs(h)], lhsT=NP_[cur][:, hs(h)], rhs=XB[:, hs(h)], start=True, stop=True),
                           reads=[K('NP', cur, h), K('XB')], writes=pk(4, h * 128, h * 128 + 128))
                    op('vector', lambda e: e.tensor_tensor(out=XF[nxt][:], in0=XF[cur][:], in1=ps[4][:], op=ALU.add),
                       reads=allh('XF', cur) + pk(4), writes=allh('XF', nxt))
                    if lvl < 6:
                        for h in range(4):
                            op('tensor', lambda e: e.matmul(ps[5][:, hs(h)], lhsT=LP[cur][:, hs(h)], rhs=NP_[cur][:, hs(h)], start=True, stop=True),
                               reads=[K('LP', cur, h), K('NP', cur, h)], writes=pk(5, h * 128, h * 128 + 128))
                        op('scalar', lambda e: e.activation(out=NP_[nxt][:], in_=ps[5][:], func=AF.Copy), reads=pk(5), writes=allh('NP', nxt))
                        if lvl < 5:
                            for h in range(4):
                                op('tensor', lambda e: e.matmul(ps[6][:, hs(h)], lhsT=NP_[cur][:, hs(h)], rhs=LP[cur][:, hs(h)],
                                                                start=True, stop=True),
                                   reads=[K('LP', cur, h), K('NP', cur, h)], writes=pk(6, h * 128, h * 128 + 128))
                            op('scalar', lambda e: e.activation(out=LP[nxt][:], in_=ps[6][:], func=AF.Copy), reads=pk(6), writes=allh('LP', nxt))
                    cur = nxt
                XFf = XF[cur]
                op('vector', lambda e: e.tensor_copy(out=XB[:], in_=XFf[:]), reads=allh('XF', cur), writes=[K('XB')])
                for c in range(2):
                    for hl in range(2):
                        h = 2 * c + hl
                        op('tensor', lambda e: e.transpose(out=psT[hl * 64:(hl + 1) * 64, c * 128:(c + 1) * 128], in_=XB[:, hs(h, 0, 64)], identity=IDB),
                           reads=[K('XB'), 'CB'], writes=[('ps', 7)])
                op('vector', lambda e: e.tensor_copy(out=AHT[:], in_=psT[:, 0:256]), reads=[('ps', 7)], writes=[K('AHT')])
                for c in range(2):
                    cb = slice(c * 128, (c + 1) * 128)
                    op('tensor', lambda e: e.matmul(ps[0][:, cb], lhsT=AHT[:, cb], rhs=SBF[:, cb], start=True, stop=True),
                       reads=[K('AHT'), K('SBF')], writes=pk(0, c * 128, c * 128 + 128))
                    for hl in range(2):
                        h = 2 * c + hl
                        op('vector', lambda e: e.tensor_tensor(out=UB[:, h * 64:(h + 1) * 64], in0=ps[0][:, c * 128 + hl * 64:c * 128 + hl * 64 + 64],
                                                               in1=XFf[:, hs(h, 64, 128)], op=ALU.add),
                           reads=pk(0, c * 128, c * 128 + 128) + [K('XF', cur, h)], writes=[K('UB', h)])
                    op('tensor', lambda e: e.matmul(ps[1][:, cb], lhsT=SBF[:, cb], rhs=FT(c, 3), start=True, stop=False, skip_group_check=True),
                       reads=[K('SBF'), K('FT', c)], writes=pk(1, c * 128, c * 128 + 128))
                    for hl in range(2):
                        h = 2 * c + hl; r0 = hl * 64; r1 = r0 + 64
                        op('tensor', lambda e: e.matmul(ps[1][r0:r1, cb], lhsT=UB[:, h * 64:(h + 1) * 64], rhs=NM(h, 0, 1),
                                                        start=False, stop=False, skip_group_check=True),
                           reads=[K('UB', h), K('NM', h)], writes=pk(1, c * 128, c * 128 + 128))
                        op('tensor', lambda e: e.matmul(ps[1][r0:r1, cb], lhsT=TM(c, 1, r0, r1), rhs=NM(h, 1, 1),
                                                        start=False, stop=True, skip_group_check=True),
                           reads=[K('TM', c), K('NM', h)], writes=pk(1, c * 128, c * 128 + 128))
                    for hl in range(2):
                        h = 2 * c + hl; r0 = hl * 64; r1 = r0 + 64
                        so = slice(256 + c * 128 + r0, 256 + c * 128 + r1)
                        sd = slice(c * 128 + r0, c * 128 + r1)
                        op('tensor', lambda e: e.matmul(ps[0][r0:r1, so], lhsT=TM(c, 2, r0, r1), rhs=UB[:, h * 64:(h + 1) * 64],
                                                        start=True, stop=False, skip_group_check=True),
                           reads=[K('TM', c), K('UB', h)], writes=pk(0, 256 + c * 128, 256 + c * 128 + 128))
                        op('tensor', lambda e: e.matmul(ps[0][r0:r1, so], lhsT=TM(c, 3, r0, r1), rhs=TM(c, 1, r0, r1),
                                                        start=False, stop=True, skip_group_check=True),
                           reads=[K('TM', c)], writes=pk(0, 256 + c * 128, 256 + c * 128 + 128))
                        op('vector', lambda e: e.scalar_tensor_tensor(out=S32[r0:r1, sd], in0=S32[r0:r1, sd], scalar=EP[r0:r1, c * 128 + 127:c * 128 + 128],
                                                                      in1=ps[0][r0:r1, so], op0=ALU.mult, op1=ALU.add),
                           reads=[K('S32'), K('EP')] + pk(0, 256 + c * 128, 256 + c * 128 + 128), writes=[K('S32')])
                        op('vector', lambda e: e.tensor_copy(out=SBF[r0:r1, sd], in_=S32[r0:r1, sd]), reads=[K('S32')], writes=[K('SBF')])
                op('scalar', lambda e: e.activation(out=OT[:], in_=ps[1][:, 0:256], func=AF.Copy), reads=pk(1, 0, 256), writes=[K('OT')])
                op('vector', lambda e: e.tensor_tensor(out=OSQ[:], in0=OT[:], in1=OT[:], op=ALU.mult), reads=[K('OT')], writes=[K('OSQ')])
                for c in range(2):
                    cb = slice(c * 128, (c + 1) * 128)
                    op('tensor', lambda e: e.matmul(ps[2][:, cb], lhsT=BONF, rhs=OT[:, cb], start=True, stop=True),
                       reads=[K('OT'), 'CF'], writes=pk(2, c * 128, c * 128 + 128))
                    op('tensor', lambda e: e.matmul(ps[2][:, 256 + c * 128:256 + (c + 1) * 128], lhsT=BONF, rhs=OSQ[:, cb], start=True, stop=True),
                       reads=[K('OSQ'), 'CF'], writes=pk(2, 256 + c * 128, 256 + c * 128 + 128))
                for c in range(2):
                    cb = slice(c * 128, (c + 1) * 128)
                    tq = slice(c * NB + q0, c * NB + q1)
                    op('scalar', lambda e: e.activation(out=OM[:], in_=ps[2][:, cb], func=AF.Copy, scale=1.0 / 64),
                       reads=pk(2, c * 128, c * 128 + 128), writes=[K('OM')])
                    op('vector', lambda e: e.tensor_tensor(out=OV[:], in0=OM[:], in1=OM[:], op=ALU.mult), reads=[K('OM')], writes=[K('OV')])
                    op('vector', lambda e: e.scalar_tensor_tensor(out=OV[:], in0=ps[2][:, 256 + c * 128:256 + (c + 1) * 128], scalar=1.0 / 64,
                                                                  in1=OV[:], op0=ALU.mult, op1=ALU.subtract),
                       reads=pk(2, 256 + c * 128, 256 + c * 128 + 128) + [K('OV')], writes=[K('OV')])
                    op('scalar', lambda e: e.activation(out=OL[:], in_=OV[:], func=AF.Ln, bias=EPS[:, 2:3], scale=1.0), reads=[K('OV'), 'EPS'], writes=[K('OL')])
                    op('scalar', lambda e: e.activation(out=ORS[:], in_=OL[:], func=AF.Exp, scale=-0.5), reads=[K('OL')], writes=[K('ORS')])
                    op('vector', lambda e: e.tensor_tensor(out=OM[:], in0=OT[:, cb], in1=OM[:], op=ALU.subtract), reads=[K('OT'), K('OM')], writes=[K('OM')])
                    op('vector', lambda e: e.tensor_tensor(out=OM[:], in0=OM[:], in1=ORS[:], op=ALU.mult), reads=[K('OM'), K('ORS')], writes=[K('OM')])
                    op('vector', lambda e: e.tensor_scalar(out=OM[:], in0=OM[:], scalar1=vc(V_RLW + c), scalar2=vc(V_RLB + c), op0=ALU.mult, op1=ALU.add),
                       reads=[K('OM'), 'VEC'], writes=[K('OM')])
                    op('vector', lambda e: e.tensor_tensor(out=OV[:], in0=BON[:, tq], in1=VTB[:, tq], op=ALU.mult),
                       reads=[K('BON', c), K('VTB', c)], writes=[K('OV')])
                    op('vector', lambda e: e.tensor_tensor(out=OM[:], in0=OM[:], in1=OV[:], op=ALU.add), reads=[K('OM'), K('OV')], writes=[K('OM')])
                    op('vector', lambda e: e.tensor_tensor(out=YT[:, 4 + c, tt0:tt0 + 128], in0=OM[:], in1=Gb[:, tq], op=ALU.mult),
                       reads=[K('OM'), K('G', c)], writes=[K('YT', 4 + c, tb)])

    seq = []
    for l in range(n_layers):
        seq += [('ffn', l, 0), ('mix', l), ('ffn', l, 1)]
    for s_ in seq:
        if s_[0] == 'ffn':
            ffn(s_[1], s_[2])
        else:
            mixer(s_[1])
        if stop_after is not None and tuple(stop_after) == tuple(s_):
            break
    for c in range(8):
        P.dma('sync', 'st_o', lambda e, c=c: e.dma_start(out=out_d[:, c, :], in_=XT[:, c, :]),
              reads=[('XT', c, tb) for tb in range(4)], writes=[('OUT', c)])
    P.wait_all('sync', [('OUT', c) for c in range(8)])
    with nc.Block() as block:
        P.emit(block)
    stack.close()
    return nc


def host_prep(inp):
    f = lambda a: np.ascontiguousarray(a, dtype=np.float32)
    tri_incl = np.triu(np.ones((128, 128), np.float32))
    tri_strict = np.triu(np.ones((128, 128), np.float32), 1)
    bo = np.zeros((128, 128), np.float32); bo[:64, :64] = 1; bo[64:, 64:] = 1
    consts = np.concatenate([np.eye(128, dtype=np.float32), tri_incl, tri_strict, tri_strict.T.copy(), bo], axis=1)
    fm = lambda v: np.asarray(v).reshape(-1, 128).T
    vecs = np.zeros((128, NL * NV), np.float32)
    for l in range(NL):
        o = l * NV
        for i, nm in enumerate(['ffn1_pre_g', 'ffn1_post_g', 'mix_pre_g', 'mix_post_g', 'ffn2_pre_g', 'ffn2_post_g']):
            vecs[:, o + V_G + i * 8: o + V_G + i * 8 + 8] = fm(inp[nm][l])
        for c in range(2):
            for j in range(3):
                vecs[:, o + V_SC + c * 3 + j] = inp['sc_conv_w'][l, j, c * 128:(c + 1) * 128]
            for j in range(31):
                vecs[:, o + V_CM + c * 31 + j] = inp['cm_conv_w'][l, j, c * 128:(c + 1) * 128]
        for off, nm in [(V_CMB, 'cm_conv_b'), (V_CMLW, 'cm_ln_w'), (V_CMLB, 'cm_ln_b'), (V_A0, 'rk_a0'), (V_KK, 'rk_k_k'),
                        (V_KA, 'rk_k_a'), (V_RLW, 'rk_ln_w'), (V_RLB, 'rk_ln_b')]:
            vecs[:, o + off: o + off + 2] = fm(inp[nm][l])
        vecs[:, o + V_RK: o + V_RK + 2] = fm(inp['rk_r_k'][l].reshape(-1))
        vecs[:, o + V_MU: o + V_MU + 7] = fm(inp['rk_mu'][l])
    wgu = np.empty((NL, 2, NFC, 128, 2, 8, 128), np.float32)
    wd = np.empty((NL, 2, 8, 128, NFC, 128), np.float32)
    for wi, pre in enumerate(['ffn1', 'ffn2']):
        for gi, nm in enumerate(['w_gate', 'w_up']):
            w = np.asarray(inp[pre + '_' + nm])
            wgu[:, wi, :, :, gi, :, :] = w.reshape(NL, 8, 128, NFC, 128).transpose(0, 3, 2, 1, 4)
        w = np.asarray(inp[pre + '_w_down'])
        wd[:, wi] = w.reshape(NL, NFC, 128, 8, 128).transpose(0, 3, 2, 1, 4)
    win = f(np.asarray(inp['w_in']).reshape(NL, 8, 128, INC).transpose(0, 2, 1, 3))
    wout = f(np.asarray(inp['w_out']).reshape(NL, 8, 128, D).transpose(0, 2, 1, 3))
    bc = np.empty((NL, 128, 768), np.float32)
    lora = np.zeros((NL, 128, 768), np.float32)
    for l in range(NL):
        row = np.concatenate([inp['rk_w0'][l], inp['sg_ln_w'][l], inp['sg_ln_b'][l]])
        bc[l] = np.broadcast_to(row[None, :], (128, row.shape[0]))
        lora[l, 0:32, 0:256] = inp['rk_w_up'][l]
        lora[l, 32:64, 256:512] = inp['rk_a_up'][l]
        lora[l, 64:128, 512:768] = inp['rk_g_up'][l]
    wmt = f(np.asarray(inp['sg_w']).transpose(0, 3, 1, 2))
    sgb = np.asarray(inp['sg_b'])
    sgbT = f(np.repeat(sgb.reshape(NL, 2, 2, 1, 128), 64, axis=3).reshape(NL, 2, 128, 128).transpose(0, 2, 1, 3))
    shared = dict(consts=f(consts), vecs=f(vecs), wgu=wgu, wd=wd, win=win, wout=wout, bc=bc, lora=lora, wmt=wmt, sgbT=sgbT)
    x = np.asarray(inp['x'])
    maps = []
    for b in range(8):
        xt = f(x[b].T.reshape(8, 128, T).transpose(1, 0, 2))
        m = dict(shared); m['xT'] = xt
        maps.append(m)
    return maps


_NC = None


def kernel(**inputs):
    global _NC
    inp = {k: np.asarray(v) for k, v in inputs.items()}
    maps = host_prep(inp)
    if _NC is None:
        _NC = build()
    res = run_bass_kernel_spmd(_NC, maps, core_ids=list(range(8)))
    out = np.empty((8, T, D), np.float32)
    for b in range(8):
        o = np.asarray(res.results[b]["outT"])
        out[b] = o.transpose(1, 0, 2).reshape(D, T).T
    return out
```

```python
import contextlib
import numpy as np
import concourse.bass as bass
import concourse.mybir as mybir
from concourse.bass_utils import run_bass_kernel_spmd

F32 = mybir.dt.float32
BF16 = mybir.dt.bfloat16
AF = mybir.ActivationFunctionType
ALU = mybir.AluOpType

D = 1024; T = 2048; DFF = 2816; NFC = 22; G = 256; INC = 2688
NL = 2
SAME_ENGINE_SYNC = True
import os
DBG = int(os.environ.get('KDBG', '99'))
DBG2 = int(os.environ.get('KDBG2', '0'))
C_DECAY = float(np.exp(-0.5))

V_G = 0
V_SC = 48
V_CM = 54
V_CMB = 116; V_CMLW = 118; V_CMLB = 120
V_A0 = 122; V_KK = 124; V_KA = 126; V_RK = 128; V_RLW = 130; V_RLB = 132
V_MU = 134
NV = 141


class _Rec:
    def __getattr__(self, name):
        def f(*a, **kw):
            self.call = (name, a, kw)
            return self
        return f


def _call(fn):
    r = _Rec()
    fn(r)
    return r.call


class Prog:
    ENG = ('sync', 'scalar', 'vector', 'gpsimd', 'tensor')

    def __init__(s, nc, stack):
        s.nc = nc; s.stack = stack
        s.streams = {e: [] for e in s.ENG}
        s.sem = {}; s.cnt = {}
        s.lastw = {}; s.readers = {}
        s.known = {e: {} for e in s.ENG}
        for e in s.ENG:
            s._mksem(e)

    def _mksem(s, key):
        s.sem[key] = s.stack.enter_context(s.nc.semaphore("s_" + str(key)))
        s.cnt[key] = 0

    def _deps(s, eng, reads, writes):
        need = {}

        def add(tok):
            if tok is None:
                return
            k, v = tok
            if k == 'tensor' and eng == 'tensor':
                return
            if k == eng and not SAME_ENGINE_SYNC:
                return
            if k not in s.ENG:
                v = s.cnt[k]
            if need.get(k, 0) < v:
                need[k] = v
        for r in reads:
            add(s.lastw.get(r))
            if isinstance(r, tuple) and r[0] == 'ps':
                for k, v in s.readers.get(r, {}).items():
                    if k != eng:
                        add((k, v))
        for w in writes:
            add(s.lastw.get(w))
            for k, v in s.readers.get(w, {}).items():
                add((k, v))
        waits = []
        for k, v in need.items():
            if s.known[eng].get(k, 0) < v:
                s.known[eng][k] = v
                waits.append((k, v))
        return waits

    def _commit(s, tok, reads, writes):
        for r in reads:
            d = s.readers.setdefault(r, {})
            if d.get(tok[0], 0) < tok[1]:
                d[tok[0]] = tok[1]
        for w in writes:
            s.lastw[w] = tok
            s.readers[w] = {}

    def op(s, eng, fn, reads=(), writes=()):
        reads = list(reads); writes = list(writes)
        waits = s._deps(eng, reads, writes)
        s.cnt[eng] += 1
        tok = (eng, s.cnt[eng])
        s.streams[eng].append((waits, _call(fn), eng, 1))
        s._commit(tok, reads, writes)

    def dma(s, eng, semkey, fn, reads=(), writes=()):
        reads = list(reads); writes = list(writes)
        if semkey not in s.sem:
            s._mksem(semkey)
        waits = s._deps(eng, reads, writes)
        s.cnt[semkey] += 16
        tok = (semkey, s.cnt[semkey])
        s.streams[eng].append((waits, _call(fn), semkey, 16))
        s._commit(tok, reads, writes)

    def barrier(s):
        for eng in s.ENG:
            waits = []
            for k, v in s.cnt.items():
                if v > 0 and s.known[eng].get(k, 0) < v and not (k == eng and k == 'tensor'):
                    s.known[eng][k] = v
                    waits.append((k, v))
            if waits:
                s.streams[eng].append((waits, None, None, 0))

    def wait_all(s, eng, keys):
        need = {}
        for k in keys:
            tok = s.lastw.get(k)
            if tok is not None and need.get(tok[0], 0) < tok[1]:
                need[tok[0]] = tok[1]
        s.streams[eng].append((list(need.items()), None, None, 0))

    def emit(s, block):
        for eng in s.ENG:
            items = s.streams[eng]

            def body(e, items=items):
                for waits, fn, semkey, amt in items:
                    for k, v in waits:
                        e.wait_ge(s.sem[k], v)
                    if fn is not None:
                        name, a, kw = fn
                        getattr(e, name)(*a, **kw).then_inc(s.sem[semkey], amt)
            getattr(block, eng)(body)


class Alloc:
    def __init__(s, nc):
        s.nc = nc
        s.base = (nc.sbuf_base + 63) // 64 * 64
        s.top = nc.sbuf_top
        s.off = s.base
        s.n = 0

    def __call__(s, shape, dtype):
        sz = int(np.prod(shape[1:])) * (4 if dtype == F32 else 2)
        sz = (sz + 63) // 64 * 64
        assert s.off + sz <= s.top, ("SBUF overflow", s.off, sz, s.top)
        s.n += 1
        t = s.nc.alloc_sbuf_tensor_at("sb%d" % s.n, list(shape), dtype, offset=s.off)
        s.off += sz
        return t

    def mark(s):
        return s.off

    def release(s, m):
        s.off = m


def pk(b, lo=0, hi=512):
    return [('ps', b)]


def build(n_layers=NL, stop_after=None):
    nc = bass.Bass("TRN2", target_bir_lowering=False)
    dt = lambda name, shape, kind="ExternalInput": nc.dram_tensor(name, list(shape), F32, kind=kind).ap()
    xin = dt("xT", [128, 8, T])
    consts_d = dt("consts", [128, 5 * 128])
    vecs_d = dt("vecs", [128, NL * NV])
    wgu_d = dt("wgu", [NL, 2, NFC, 128, 2, 8, 128])
    wd_d = dt("wd", [NL, 2, 8, 128, NFC, 128])
    win_d = dt("win", [NL, 128, 8, INC])
    wout_d = dt("wout", [NL, 128, 8, D])
    bc_d = dt("bc", [NL, 128, 768])
    lora_d = dt("lora", [NL, 128, 768])
    wmt_d = dt("wmt", [NL, 128, 4, 128])
    sgb_d = dt("sgbT", [NL, 128, 2, 128])
    out_d = dt("outT", [128, 8, T], kind="ExternalOutput")

    stack = contextlib.ExitStack()
    P = Prog(nc, stack)
    A = Alloc(nc)
    op = P.op

    XT = A([128, 8, T], F32)
    CF = A([128, 5 * 128], F32)
    IDF = CF[:, 0:128]; TRI_IF = CF[:, 128:256]; TRI_SF = CF[:, 256:384]
    CB = A([128, 5 * 128], BF16)
    IDB = CB[:, 0:128]; MASK_SI = CB[:, 128:384]
    TRILB = CB[:, 384:512]; BONES = CB[:, 512:640]
    ONESB = A([128, 128], BF16)
    CBX = A([128, 1024], BF16)
    ONESF = A([128, 128], F32)
    VEC = A([128, NL * NV], F32)
    HALFG = A([128, NL * 16], F32)
    EPS = A([128, 4], F32)
    OMMV = A([128, NL * 7], F32)
    ps = [nc.alloc_psum_tensor("psb%d" % i, [128, 512], F32) for i in range(7)]
    psT = nc.alloc_psum_tensor("psT", [128, 1024], BF16)

    for c in range(8):
        P.dma('sync', 'ld_x', lambda e, c=c: e.dma_start(out=XT[:, c, :], in_=xin[:, c, :]),
              writes=[('XT', c, tb) for tb in range(4)])
    P.dma('sync', 'ld_c', lambda e: e.dma_start(out=CF[:], in_=consts_d[:, :]), writes=['CF'])
    P.dma('sync', 'ld_c', lambda e: e.dma_start(out=VEC[:], in_=vecs_d[:, :]), writes=['VEC'])
    op('vector', lambda e: e.tensor_copy(out=CB[:], in_=CF[:]), reads=['CF'], writes=['CB'])
    op('vector', lambda e: e.memset(ONESB[:], 1.0), writes=['ONES'])
    for q in range(4):
        src = CB[:, 256:384] if q % 2 == 0 else CB[:, 128:256]
        op('vector', lambda e: e.tensor_copy(out=CBX[:, q * 128:(q + 1) * 128], in_=src), reads=['CB'], writes=['CBX'])
        op('vector', lambda e: e.tensor_copy(out=CBX[:, 512 + q * 128:512 + (q + 1) * 128], in_=CB[:, 384:512]), reads=['CB'], writes=['CBX'])
    op('vector', lambda e: e.memset(ONESF[:], 1.0), writes=['ONES'])
    for i, v in enumerate([1e-6, 1e-5, 64e-5, 1e-24]):
        op('vector', lambda e, i=i, v=v: e.memset(EPS[:, i:i + 1], v), writes=['EPS'])
    for l in range(NL):
        for j, gi in enumerate([1, 5]):
            op('vector', lambda e, l=l, j=j, gi=gi: e.tensor_scalar(
                out=HALFG[:, l * 16 + j * 8: l * 16 + j * 8 + 8],
                in0=VEC[:, l * NV + V_G + gi * 8: l * NV + V_G + gi * 8 + 8],
                scalar1=0.5, scalar2=None, op0=ALU.mult), reads=['VEC'], writes=['HALFG'])
    for l in range(NL):
        op('vector', lambda e, l=l: e.tensor_scalar(out=OMMV[:, l * 7:l * 7 + 7], in0=VEC[:, l * NV + V_MU:l * NV + V_MU + 7],
                                                    scalar1=-1.0, scalar2=1.0, op0=ALU.mult, op1=ALU.add),
           reads=['VEC'], writes=['OMMV'])

    def vcol(l, off):
        return VEC[:, l * NV + off: l * NV + off + 1]

    def rstd_from(psb, n, scale, epsi, LNV, RS, rk, wk, lk):
        op('scalar', lambda e: e.activation(out=LNV[:, 0:n], in_=psb[:, 0:n], func=AF.Ln,
                                            bias=EPS[:, epsi:epsi + 1], scale=scale),
           reads=rk + ['EPS'], writes=[lk])
        op('scalar', lambda e: e.activation(out=RS[:, 0:n], in_=LNV[:, 0:n], func=AF.Exp, scale=-0.5),
           reads=[lk], writes=[wk])

    work_mark = A.mark()

    def ffn(l, which):
        A.release(work_mark); P.barrier()
        gpre = V_G + (0 if which == 0 else 4) * 8
        HY = A([128, 8, T], BF16)
        ACTB = A([128, NFC, 1024], BF16)
        WGU = [A([128, 2, 8, 128], BF16) for _ in range(2)]
        WD = [A([128, NFC, 128], BF16) for _ in range(2)]
        SQ = [A([128, 512], BF16) for _ in range(2)]
        LNV = A([128, 512], F32)
        RS = [A([128, 512], F32) for _ in range(2)]
        SG = [A([128, 512], F32) for _ in range(2)]
        TMP = [A([128, 512], F32) for _ in range(2)]
        tag = 'f%d%d' % (l, which)
        K = lambda name, *idx: (tag, name) + idx
        it = 0
        if DBG <= 0:
            return
        for gtb in range(4):
            t0 = gtb * 512; rb = gtb % 2
            for c in range(8):
                b = c % 2
                op('scalar', lambda e, c=c, b=b, t0=t0: e.activation(out=SQ[b][:], in_=XT[:, c, t0:t0 + 512], func=AF.Square),
                   reads=[('XT', c, gtb)], writes=[K('SQ', b)])
                op('tensor', lambda e, c=c, b=b: e.matmul(ps[6][:], lhsT=ONESB[:], rhs=SQ[b][:], start=(c == 0), stop=(c == 7)),
                   reads=[K('SQ', b), 'ONES'], writes=pk(6))
            rstd_from(ps[6], 512, 1.0 / D, 0, LNV, RS[rb], pk(6), K('RS', rb), K('LNV'))
            for c in range(8):
                op('vector', lambda e, c=c, t0=t0, rb=rb: e.scalar_tensor_tensor(
                    out=HY[:, c, t0:t0 + 512], in0=XT[:, c, t0:t0 + 512], scalar=vcol(l, gpre + c),
                    in1=RS[rb][:], op0=ALU.mult, op1=ALU.mult),
                   reads=[('XT', c, gtb), K('RS', rb), 'VEC'], writes=[K('HY', c, gtb)])
        for half in range(2):
            T0 = half * 1024
            if DBG <= 1:
                return
            for fc in range(NFC):
                wb = fc % 2
                P.dma('gpsimd', K('ldgu', wb), lambda e, fc=fc, wb=wb: (e.dma_start(out=WGU[wb][:], in_=wgu_d[l, which, fc]) if not os.environ.get('KHALFDMA') else e.dma_start(out=WGU[wb][:, 0:1], in_=wgu_d[l, which, fc, :, 0:1])),
                      writes=[K('WGU', wb)])
                for tb in range(2):
                    pg = (it % 2) * 2; pu = pg + 1; sb = it % 2; it += 1
                    for gi, pb in ((0, pg), (1, pu)):
                        for k in range(8):
                            op('tensor', lambda e, gi=gi, pb=pb, k=k, wb=wb, tb=tb: e.matmul(
                                ps[pb][:], lhsT=WGU[wb][:, gi, k, :], rhs=HY[:, k, T0 + tb * 512:T0 + (tb + 1) * 512],
                                start=(k == 0), stop=(k == 7)),
                               reads=[K('WGU', wb), K('HY', k, half * 2 + tb)], writes=pk(pb))
                    op('scalar', lambda e, pg=pg, sb=sb: e.activation(out=SG[sb][:], in_=ps[pg][:], func=AF.Silu),
                       reads=pk(pg), writes=[K('SG', sb)])
                    op('vector', lambda e, pu=pu, sb=sb, fc=fc, tb=tb: e.tensor_tensor(
                        out=ACTB[:, fc, tb * 512:(tb + 1) * 512], in0=SG[sb][:], in1=ps[pu][:], op=ALU.mult),
                       reads=[K('SG', sb)] + pk(pu), writes=[K('ACT', fc, tb)])
            if DBG <= 2:
                return
            if DBG2 == 10:
                continue
            for dc in range(8):
                wb = dc % 2
                P.dma('gpsimd' if DBG2 != 12 else 'scalar', K('ldd', wb), lambda e, dc=dc, wb=wb: e.dma_start(out=WD[wb][:] if DBG2 != 12 else WD[wb][:, 0:11, :].bitcast(F32), in_=wd_d[l, which, dc] if DBG2 != 12 else wd_d[l, which, dc, :, 0:11, 0:64]),
                      writes=[K('WD', wb)])
                for tb in range(2):
                    if DBG2 == 2 or (DBG2 == 3 and dc >= 1):
                        continue
                    po = (it % 2); sb = it % 2; it += 1
                    if DBG2 == 9:
                        po += 2
                    for fc in range(NFC if DBG2 != 8 else 8):
                        lhs_ = WD[wb][:, fc, :] if DBG2 not in (6, 15) else WGU[wb][:, 0, fc % 8, :]
                        rhs_ = ACTB[:, fc, tb * 512:(tb + 1) * 512] if DBG2 not in (5, 15) else HY[:, fc % 8, T0 + tb * 512:T0 + (tb + 1) * 512]
                        op('tensor', lambda e, po=po, fc=fc, wb=wb, tb=tb: e.matmul(
                            ps[po][:], lhsT=lhs_, rhs=rhs_,
                            start=(fc == 0), stop=(fc == (NFC if DBG2 != 8 else 8) - 1)),
                           reads=([K('WD', wb)] if DBG2 != 13 else []) + ([K('ACT', fc, tb)] if DBG2 != 14 else []), writes=pk(po))
                    if DBG2 == 4:
                        continue
                    op('scalar', lambda e, po=po, sb=sb: e.activation(out=SQ[sb][:], in_=ps[po][:], func=AF.Square),
                       reads=pk(po), writes=[K('SQ', sb)])
                    op('vector', lambda e, po=po, dc=dc, tb=tb: e.tensor_copy(out=HY[:, dc, T0 + tb * 512:T0 + (tb + 1) * 512], in_=ps[po][:]),
                       reads=pk(po) + [K('SQ', sb)], writes=[K('HY', dc, half * 2 + tb)])
                    if DBG2 != 1:
                        op('tensor', lambda e, sb=sb, tb=tb, dc=dc: e.matmul(ps[4 + tb][:], lhsT=ONESB[:], rhs=SQ[sb][:],
                                                                            start=(dc == 0), stop=(dc == 7)),
                           reads=[K('SQ', sb), 'ONES'], writes=pk(4 + tb))
            if DBG <= 3:
                return
            hg = l * 16 + which * 8
            for tb in range(2):
                t0 = T0 + tb * 512; gtb = half * 2 + tb
                rstd_from(ps[4 + tb], 512, 1.0 / D, 0, LNV, RS[tb], pk(4 + tb), K('RS', tb), K('LNV'))
                for c in range(8):
                    b = c % 2
                    op('vector', lambda e, c=c, b=b, tb=tb: e.tensor_tensor(
                        out=TMP[b][:], in0=HY[:, c, T0 + tb * 512:T0 + (tb + 1) * 512], in1=RS[tb][:], op=ALU.mult),
                       reads=[K('HY', c, half * 2 + tb), K('RS', tb)], writes=[K('TMP', b)])
                    op('vector', lambda e, c=c, b=b, t0=t0: e.scalar_tensor_tensor(
                        out=XT[:, c, t0:t0 + 512], in0=TMP[b][:], scalar=HALFG[:, hg + c: hg + c + 1],
                        in1=XT[:, c, t0:t0 + 512], op0=ALU.mult, op1=ALU.add),
                       reads=[K('TMP', b), ('XT', c, gtb), 'HALFG'], writes=[('XT', c, gtb)])

    def mixer(l):
        A.release(work_mark); P.barrier()
        tag = 'm%d' % l
        K = lambda name, *idx: (tag, name) + idx
        HM = A([128, 8, T + 1], BF16)
        YT = A([128, 8, T], BF16)
        grp_mark = A.mark()
        LNV = A([128, 512], F32)
        RS = A([128, 512], F32)
        SQ = [A([128, 512], BF16) for _ in range(2)]
        XTall = lambda c: [('XT', c, tb) for tb in range(4)]
        for c in range(8):
            op('vector', lambda e, c=c: e.memset(HM[:, c, 0:1], 0.0), writes=[K('HM0', c)])
        for tb in range(4):
            t0 = tb * 512
            for c in range(8):
                b = c % 2
                op('scalar', lambda e, c=c, b=b, t0=t0: e.activation(out=SQ[b][:], in_=XT[:, c, t0:t0 + 512], func=AF.Square),
                   reads=[('XT', c, tb)], writes=[K('SQ', b)])
                op('tensor', lambda e, c=c, b=b: e.matmul(ps[6][:], lhsT=ONESB[:], rhs=SQ[b][:], start=(c == 0), stop=(c == 7)),
                   reads=[K('SQ', b), 'ONES'], writes=pk(6))
            rstd_from(ps[6], 512, 1.0 / D, 0, LNV, RS, pk(6), K('RS'), K('LNV'))
            for c in range(8):
                op('vector', lambda e, c=c, t0=t0: e.scalar_tensor_tensor(
                    out=HM[:, c, 1 + t0:1 + t0 + 512], in0=XT[:, c, t0:t0 + 512], scalar=vcol(l, V_G + 16 + c),
                    in1=RS[:], op0=ALU.mult, op1=ALU.mult),
                   reads=[('XT', c, tb), K('RS'), 'VEC'], writes=[K('HM', c, tb)])
        HMr = lambda k, tb: [K('HM', k, tb)]
        HMrs = lambda k, tb: [K('HM', k, tb), K('HM0', k)] + ([K('HM', k, tb - 1)] if tb > 0 else [])

        def proj_fm(pb, W, col0, tb, wkey, ncol=128, W2=None, prow=None):
            t0 = tb * 512
            outap = ps[pb][:] if prow is None else ps[pb][prow[0]:prow[1], :]
            n = 8 if W2 is None else 16
            for k in range(8):
                op('tensor', lambda e, k=k: e.matmul(outap, lhsT=W[:, k, col0:col0 + ncol], rhs=HM[:, k, 1 + t0:1 + t0 + 512],
                                                    start=(k == 0), stop=(k == n - 1)),
                   reads=[wkey] + HMr(k, tb), writes=pk(pb))
            if W2 is not None:
                for k in range(8):
                    op('tensor', lambda e, k=k: e.matmul(outap, lhsT=W2[:, k, col0:col0 + ncol], rhs=HM[:, k, t0:t0 + 512],
                                                        start=False, stop=(k == 7)),
                       reads=[wkey] + HMrs(k, tb), writes=pk(pb))

        A.release(grp_mark); P.barrier()
        WA = A([128, 8, 768], BF16)
        Z = A([128, 2, T + 2], F32)
        TA = [A([128, 512], F32) for _ in range(2)]
        ACC = [A([128, 512], F32) for _ in range(2)]
        P.dma('gpsimd', K('ldwa'), lambda e: e.dma_start(out=WA[:], in_=win_d[l, :, :, 0:768]), writes=[K('WA')])
        for c in range(2):
            op('vector', lambda e, c=c: e.memset(Z[:, c, 0:2], 0.0), writes=[K('Z0', c)])
        it = 0
        for tb in range(4):
            t0 = tb * 512
            for c in range(2):
                b = it % 2; it += 1
                pc_, px_, pb_ = 0 + 3 * b, 1 + 3 * b, 2 + 3 * b
                proj_fm(pc_, WA, 256 + c * 128, tb, K('WA'))
                proj_fm(px_, WA, 512 + c * 128, tb, K('WA'))
                proj_fm(pb_, WA, 0 + c * 128, tb, K('WA'))
                op('scalar', lambda e, b=b, pc_=pc_: e.activation(out=TA[b][:], in_=ps[pc_][:], func=AF.Copy),
                   reads=pk(pc_), writes=[K('TA', b)])
                op('vector', lambda e, b=b, px_=px_, c=c, t0=t0: e.tensor_tensor(
                    out=Z[:, c, 2 + t0:2 + t0 + 512], in0=TA[b][:], in1=ps[px_][:], op=ALU.mult),
                   reads=[K('TA', b)] + pk(px_), writes=[K('Z', c, tb)])
                zr = [K('Z', c, tb), K('Z0', c)] + ([K('Z', c, tb - 1)] if tb > 0 else [])
                op('vector', lambda e, b=b, c=c, t0=t0: e.tensor_scalar(
                    out=ACC[b][:], in0=Z[:, c, t0:t0 + 512], scalar1=vcol(l, V_SC + c * 3 + 0), scalar2=None, op0=ALU.mult),
                   reads=zr + ['VEC'], writes=[K('ACC', b)])
                for j in (1, 2):
                    op('vector', lambda e, b=b, c=c, t0=t0, j=j: e.scalar_tensor_tensor(
                        out=ACC[b][:], in0=Z[:, c, t0 + j:t0 + j + 512], scalar=vcol(l, V_SC + c * 3 + j),
                        in1=ACC[b][:], op0=ALU.mult, op1=ALU.add),
                       reads=zr + ['VEC', K('ACC', b)], writes=[K('ACC', b)])
                op('vector', lambda e, b=b, c=c, t0=t0, pb_=pb_: e.tensor_tensor(
                    out=YT[:, 0 + c, t0:t0 + 512], in0=ACC[b][:], in1=ps[pb_][:], op=ALU.mult),
                   reads=[K('ACC', b)] + pk(pb_), writes=[K('YT', 0 + c, tb)])

        A.release(grp_mark); P.barrier()
        WDm = A([128, 8, 512], BF16)
        ZG = A([128, 2, T + 30], BF16)
        DIAG = A([128, 62, 128], BF16)
        SGT = [A([128, 512], F32) for _ in range(2)]
        LNV = A([128, 512], F32)
        RS = A([128, 512], F32)
        ZD = [A([128, 512], F32) for _ in range(2)]
        ZD2 = [A([128, 512], F32) for _ in range(2)]
        MEAN = A([128, 512], F32)
        MSQ = A([128, 512], F32)
        VAR = A([128, 512], F32)
        DD = [A([128, 512], F32) for _ in range(2)]
        P.dma('gpsimd', K('ldwd'), lambda e: e.dma_start(out=WDm[:], in_=win_d[l, :, :, 2176:2688]), writes=[K('WDm')])
        for c in range(2):
            op('vector', lambda e, c=c: e.memset(ZG[:, c, 0:30], 0.0), writes=[K('ZG0', c)])
            for j in range(31):
                op('vector', lambda e, c=c, j=j: e.tensor_scalar(
                    out=DIAG[:, c * 31 + j, :], in0=IDF, scalar1=vcol(l, V_CM + c * 31 + j), scalar2=None, op0=ALU.mult),
                   reads=['CF', 'VEC'], writes=[K('DIAG', c)])
        it = 0
        for tb in range(4):
            t0 = tb * 512
            for c in range(2):
                b = it % 2; it += 1
                p1, p2 = 0 + 2 * b, 1 + 2 * b
                proj_fm(p1, WDm, 0 + c * 128, tb, K('WDm'))
                proj_fm(p2, WDm, 256 + c * 128, tb, K('WDm'))
                op('scalar', lambda e, b=b, p2=p2: e.activation(out=SGT[b][:], in_=ps[p2][:], func=AF.Sigmoid),
                   reads=pk(p2), writes=[K('SGT', b)])
                op('vector', lambda e, b=b, p1=p1, c=c, t0=t0: e.tensor_tensor(
                    out=ZG[:, c, 30 + t0:30 + t0 + 512], in0=SGT[b][:], in1=ps[p1][:], op=ALU.mult),
                   reads=[K('SGT', b)] + pk(p1), writes=[K('ZG', c, tb)])
            for c in range(2):
                zr = [K('ZG', c, tb), K('ZG0', c)] + ([K('ZG', c, tb - 1)] if tb > 0 else [])
                pcv = 4 + c
                for j in range(31):
                    op('tensor', lambda e, c=c, j=j, t0=t0, pcv=pcv: e.matmul(
                        ps[pcv][:], lhsT=DIAG[:, c * 31 + j, :], rhs=ZG[:, c, t0 + j:t0 + j + 512],
                        start=(j == 0), stop=(j == 30)),
                       reads=zr + [K('DIAG', c)], writes=pk(pcv))
                op('scalar', lambda e, c=c, pcv=pcv: e.activation(out=ZD[c][:], in_=ps[pcv][:], func=AF.Identity,
                                                                  bias=vcol(l, V_CMB + c), scale=1.0),
                   reads=pk(pcv) + ['VEC'], writes=[K('ZD', c)])
                op('vector', lambda e, c=c: e.tensor_tensor(out=ZD2[c][:], in0=ZD[c][:], in1=ZD[c][:], op=ALU.mult),
                   reads=[K('ZD', c)], writes=[K('ZD2', c)])
            for c in range(2):
                op('tensor', lambda e, c=c: e.matmul(ps[6][:], lhsT=ONESF[:], rhs=ZD[c][:], start=(c == 0), stop=(c == 1)),
                   reads=[K('ZD', c), 'ONES'], writes=pk(6))
            for c in range(2):
                op('tensor', lambda e, c=c: e.matmul(ps[0][:], lhsT=ONESF[:], rhs=ZD2[c][:], start=(c == 0), stop=(c == 1)),
                   reads=[K('ZD2', c), 'ONES'], writes=pk(0))
            op('scalar', lambda e: e.activation(out=MEAN[:], in_=ps[6][:], func=AF.Copy, scale=1.0 / G),
               reads=pk(6), writes=[K('MEAN')])
            op('vector', lambda e: e.tensor_tensor(out=MSQ[:], in0=MEAN[:], in1=MEAN[:], op=ALU.mult),
               reads=[K('MEAN')], writes=[K('MSQ')])
            op('vector', lambda e: e.scalar_tensor_tensor(out=VAR[:], in0=ps[0][:], scalar=1.0 / G, in1=MSQ[:],
                                                          op0=ALU.mult, op1=ALU.subtract),
               reads=pk(0) + [K('MSQ')], writes=[K('VAR')])
            op('scalar', lambda e: e.activation(out=LNV[:], in_=VAR[:], func=AF.Ln, bias=EPS[:, 1:2], scale=1.0),
               reads=[K('VAR'), 'EPS'], writes=[K('LNV')])
            op('scalar', lambda e: e.activation(out=RS[:], in_=LNV[:], func=AF.Exp, scale=-0.5),
               reads=[K('LNV')], writes=[K('RS')])
            for c in range(2):
                op('vector', lambda e, c=c: e.tensor_tensor(out=DD[c][:], in0=ZD[c][:], in1=MEAN[:], op=ALU.subtract),
                   reads=[K('ZD', c), K('MEAN')], writes=[K('DD', c)])
                op('vector', lambda e, c=c: e.tensor_tensor(out=DD[c][:], in0=DD[c][:], in1=RS[:], op=ALU.mult),
                   reads=[K('DD', c), K('RS')], writes=[K('DD', c)])
                op('scalar', lambda e, c=c, t0=t0: e.activation(out=YT[:, 6 + c, t0:t0 + 512], in_=DD[c][:], func=AF.Silu,
                                                                bias=vcol(l, V_CMLB + c), scale=vcol(l, V_CMLW + c)),
                   reads=[K('DD', c), 'VEC'], writes=[K('YT', 6 + c, tb)])

        A.release(grp_mark); P.barrier()
        WB = A([128, 8, 512], BF16)
        BCB = A([128, 512], F32)
        WMTF = A([128, 4, 128], F32)
        WMT = A([128, 4, 128], BF16)
        SGB = A([128, 2, 128], F32)
        ST6 = A([128, 6], F32)
        MV = A([128, 2], F32)
        RV = A([128, 2], F32)
        VN = [A([128, 256], F32) for _ in range(2)]
        VNB = [A([128, 256], BF16) for _ in range(2)]
        SB_ = [A([128, 128], F32) for _ in range(2)]
        P.dma('gpsimd', K('ldwb'), lambda e: e.dma_start(out=WB[:], in_=win_d[l, :, :, 768:1280]), writes=[K('WB')])
        P.dma('sync', K('ldb'), lambda e: e.dma_start(out=BCB[:], in_=bc_d[l, :, 256:768]), writes=[K('BCB')])
        P.dma('sync', K('ldb'), lambda e: e.dma_start(out=WMTF[:], in_=wmt_d[l]), writes=[K('WMTF')])
        P.dma('sync', K('ldb'), lambda e: e.dma_start(out=SGB[:], in_=sgb_d[l]), writes=[K('SGB')])
        for h in range(4):
            op('vector', lambda e, h=h: e.tensor_tensor(out=WMT[:, h, :], in0=WMTF[:, h, :], in1=TRI_IF, op=ALU.mult),
               reads=[K('WMTF'), 'CF'], writes=[K('WMT')])
        it = 0
        for tb in range(4):
            for c in range(2):
                proj_fm(4 + c, WB, c * 128, tb, K('WB'))
            for ti in range(4):
                tile_i = tb * 4 + ti; tt0 = tile_i * 128
                b = it % 2; it += 1
                pv = 0 + b; pss = 2 + b
                for k in range(8):
                    op('tensor', lambda e, k=k, pv=pv, tt0=tt0: e.matmul(
                        ps[pv][:, 0:256], lhsT=HM[:, k, 1 + tt0:1 + tt0 + 128], rhs=WB[:, k, 256:512],
                        start=(k == 0), stop=(k == 7)),
                       reads=[K('WB')] + HMr(k, tb), writes=pk(pv, 0, 256))
                op('vector', lambda e, pv=pv: e.bn_stats(out=ST6[:], in_=ps[pv][:, 0:256]),
                   reads=pk(pv, 0, 256), writes=[K('ST6')])
                op('vector', lambda e: e.bn_aggr(out=MV[:], in_=ST6[:]), reads=[K('ST6')], writes=[K('MV')])
                op('scalar', lambda e: e.activation(out=RV[:, 0:1], in_=MV[:, 1:2], func=AF.Ln, bias=EPS[:, 1:2], scale=1.0),
                   reads=[K('MV'), 'EPS'], writes=[K('RV0')])
                op('scalar', lambda e: e.activation(out=RV[:, 1:2], in_=RV[:, 0:1], func=AF.Exp, scale=-0.5),
                   reads=[K('RV0')], writes=[K('RV')])
                op('vector', lambda e, pv=pv, b=b: e.tensor_scalar(
                    out=VN[b][:], in0=ps[pv][:, 0:256], scalar1=MV[:, 0:1], scalar2=RV[:, 1:2],
                    op0=ALU.subtract, op1=ALU.mult),
                   reads=pk(pv, 0, 256) + [K('MV'), K('RV')], writes=[K('VN', b)])
                op('vector', lambda e, b=b: e.tensor_tensor(out=VN[b][:], in0=VN[b][:], in1=BCB[:, 0:256], op=ALU.mult),
                   reads=[K('VN', b), K('BCB')], writes=[K('VN', b)])
                op('vector', lambda e, b=b: e.tensor_tensor(out=VNB[b][:], in0=VN[b][:], in1=BCB[:, 256:512], op=ALU.add),
                   reads=[K('VN', b), K('BCB')], writes=[K('VNB', b)])
                for c in range(2):
                    for hl in range(2):
                        h = 2 * c + hl
                        op('tensor', lambda e, c=c, hl=hl, h=h, b=b, pss=pss: e.matmul(
                            ps[pss][hl * 64:(hl + 1) * 64, c * 128:(c + 1) * 128], lhsT=VNB[b][:, h * 64:(h + 1) * 64],
                            rhs=WMT[:, h, :], start=True, stop=True),
                           reads=[K('VNB', b), K('WMT')], writes=pk(pss, c * 128, c * 128 + 128))
                    op('vector', lambda e, c=c, b=b, pss=pss: e.tensor_tensor(
                        out=SB_[b][:], in0=ps[pss][:, c * 128:(c + 1) * 128], in1=SGB[:, c, :], op=ALU.add),
                       reads=pk(pss, c * 128, c * 128 + 128) + [K('SGB')], writes=[K('SB', b)])
                    op('vector', lambda e, c=c, b=b, ti=ti, tt0=tt0: e.tensor_tensor(
                        out=YT[:, 2 + c, tt0:tt0 + 128], in0=SB_[b][:], in1=ps[4 + c][:, ti * 128:(ti + 1) * 128], op=ALU.mult),
                       reads=[K('SB', b)] + pk(4 + c), writes=[K('YT', 2 + c, tb)])

        A.release(grp_mark); P.barrier()
        if 'C' not in os.environ.get('KSKIP', ''):
            rwkv(l, K, HM, YT, HMr, HMrs, A)

        A.release(grp_mark); P.barrier()
        WO = A([128, 8, D], BF16)
        MY = A([128, 8, 512], BF16)
        TMP = [A([128, 512], F32) for _ in range(2)]
        LNV = A([128, 512], F32)
        RS = A([128, 512], F32)
        SQ = [A([128, 512], BF16) for _ in range(2)]
        P.dma('gpsimd', K('ldwo'), lambda e: e.dma_start(out=WO[:], in_=wout_d[l]), writes=[K('WO')])
        it = 0
        for tb in range(4):
            t0 = tb * 512
            for dc in range(8):
                po = it % 2; sb = it % 2; it += 1
                for k in range(8):
                    op('tensor', lambda e, po=po, k=k, dc=dc, t0=t0: e.matmul(
                        ps[po][:], lhsT=WO[:, k, dc * 128:(dc + 1) * 128], rhs=YT[:, k, t0:t0 + 512],
                        start=(k == 0), stop=(k == 7)),
                       reads=[K('WO'), K('YT', k, tb)], writes=pk(po))
                op('scalar', lambda e, po=po, sb=sb: e.activation(out=SQ[sb][:], in_=ps[po][:], func=AF.Square),
                   reads=pk(po), writes=[K('SQ', sb)])
                op('vector', lambda e, po=po, dc=dc: e.tensor_copy(out=MY[:, dc, :], in_=ps[po][:]),
                   reads=pk(po), writes=[K('MY', dc)])
                op('tensor', lambda e, sb=sb, dc=dc: e.matmul(ps[6][:], lhsT=ONESB[:], rhs=SQ[sb][:], start=(dc == 0), stop=(dc == 7)),
                   reads=[K('SQ', sb), 'ONES'], writes=pk(6))
            rstd_from(ps[6], 512, 1.0 / D, 0, LNV, RS, pk(6), K('RS'), K('LNV'))
            for c in range(8):
                b = c % 2
                op('vector', lambda e, c=c, b=b: e.tensor_tensor(out=TMP[b][:], in0=MY[:, c, :], in1=RS[:], op=ALU.mult),
                   reads=[K('MY', c), K('RS')], writes=[K('TMP', b)])
                op('vector', lambda e, c=c, b=b, t0=t0: e.scalar_tensor_tensor(
                    out=XT[:, c, t0:t0 + 512], in0=TMP[b][:], scalar=vcol(l, V_G + 24 + c),
                    in1=XT[:, c, t0:t0 + 512], op0=ALU.mult, op1=ALU.add),
                   reads=[K('TMP', b), ('XT', c, tb), 'VEC'], writes=[('XT', c, tb)])

    def rwkv(l, K, HM, YT, HMr, HMrs, A):
        NB = 256
        WC = A([128, 8, 896], BF16)
        LORAB = A([128, 768], BF16)
        W0B = A([128, 256], F32)
        P.dma('gpsimd', K('ldwc'), lambda e: e.dma_start(out=WC[:], in_=win_d[l, :, :, 1280:2176]), writes=[K('WC')])
        P.dma('gpsimd', K('ldl'), lambda e: e.dma_start(out=LORAB[:], in_=lora_d[l]), writes=[K('LORA')])
        P.dma('sync', K('ldc'), lambda e: e.dma_start(out=W0B[:], in_=bc_d[l, :, 0:256]), writes=[K('W0B')])
        PC = A([128, 6 * NB], F32)
        MISC = A([128, NB], BF16)
        KKb = A([128, 2 * NB], F32); KPb = A([128, 2 * NB], F32); Bb = A([128, 2 * NB], F32)
        Gb = A([128, 2 * NB], BF16); BON = A([128, 2 * NB], BF16); VTB = A([128, 2 * NB], BF16)
        TQ = [A([128, NB], F32) for _ in range(2)]
        TS = [A([128, NB], F32) for _ in range(2)]
        SQB = [A([128, NB], BF16) for _ in range(2)]
        XW = A([128, 256], F32); SIGTM = A([128, 256], F32)
        EP = A([128, 256], F32); EM = A([128, 256], F32); EE = A([128, 256], F32)
        FT_ = A([128, 8 * 128], BF16)
        FH_ = A([128, 4 * 128], BF16)
        TM_ = A([128, 8 * 128], BF16)
        NM_ = A([128, 4 * 512], BF16)
        LP = [A([128, 512], BF16) for _ in range(2)]
        NP_ = [A([128, 512], BF16) for _ in range(2)]
        XF = [A([128, 512], F32) for _ in range(2)]
        XB = A([128, 512], BF16)
        AHT = A([128, 256], BF16)
        UB = A([128, 256], BF16)
        S32 = A([128, 256], F32); SBF = A([128, 256], BF16)
        OT = A([128, 256], F32); OSQ = A([128, 256], F32)
        OM = A([128, 256], F32); OV = A([128, 256], F32); OL = A([128, 256], F32); ORS = A([128, 256], F32)
        BONF = CF[:, 512:640]
        vc = lambda off: vcol(l, off)
        cs = lambda c, a=0, b=NB: slice(c * NB + a, c * NB + b)
        hs = lambda h, a=0, b=128: slice(h * 128 + a, h * 128 + b)
        FT = lambda c, i, p0=0, p1=128: FT_[p0:p1, (c * 4 + i) * 128:(c * 4 + i + 1) * 128]
        FH = lambda c, i: FH_[:, (c * 2 + i) * 128:(c * 2 + i + 1) * 128]
        TM = lambda c, i, a=0, b=128: TM_[:, (c * 4 + i) * 128 + a:(c * 4 + i) * 128 + b]
        NM = lambda h, i, j: NM_[:, h * 512 + i * 256 + j * 128: h * 512 + i * 256 + (j + 1) * 128]
        op('vector', lambda e: e.memset(S32[:], 0.0), writes=[K('S32')])
        op('vector', lambda e: e.memset(SBF[:], 0.0), writes=[K('SBF')])
        allh = lambda nm, i: [K(nm, i, h) for h in range(4)]
        for sb in range(T // NB):
            t0 = sb * NB; tb = t0 // 512
            for cc in range(7):
                pb = cc % 2
                for k in range(8):
                    op('tensor', lambda e: e.matmul(ps[pb][:, 0:NB + 1], lhsT=WC[:, k, cc * 128:(cc + 1) * 128],
                                                    rhs=HM[:, k, t0:t0 + NB + 1], start=(k == 0), stop=(k == 7)),
                       reads=[K('WC')] + HMrs(k, tb), writes=pk(pb, 0, NB + 1))
                op('vector', lambda e: e.tensor_scalar(out=TS[pb][:], in0=ps[pb][:, 0:NB], scalar1=vc(V_MU + cc), scalar2=None,
                                                       op0=ALU.mult),
                   reads=pk(pb, 0, NB + 1) + ['VEC'], writes=[K('TS', pb)])
                dst = PC[:, cs(cc)] if cc < 6 else TQ[0][:]
                dk = K('PC', cc) if cc < 6 else K('TQ', 0)
                op('vector', lambda e: e.scalar_tensor_tensor(out=dst, in0=ps[pb][:, 1:NB + 1], scalar=OMMV[:, l * 7 + cc:l * 7 + cc + 1],
                                                              in1=TS[pb][:], op0=ALU.mult, op1=ALU.add),
                   reads=pk(pb, 0, NB + 1) + ['OMMV', K('TS', pb)], writes=[dk])
            op('scalar', lambda e: e.activation(out=MISC[0:32, :], in_=TQ[0][0:32, :], func=AF.Tanh), reads=[K('TQ', 0)], writes=[K('MISC', 0)])
            op('vector', lambda e: e.tensor_copy(out=MISC[32:64, :], in_=TQ[0][32:64, :]), reads=[K('TQ', 0)], writes=[K('MISC', 1)])
            op('scalar', lambda e: e.activation(out=MISC[64:128, :], in_=TQ[0][64:128, :], func=AF.Sigmoid),
               reads=[K('TQ', 0)], writes=[K('MISC', 2)])
            for c in range(2):
                op('tensor', lambda e: e.matmul(ps[2][:, 0:NB], lhsT=LORAB[32:64, 256 + c * 128:256 + (c + 1) * 128],
                                                rhs=MISC[32:64, :], start=True, stop=True),
                   reads=[K('LORA'), K('MISC', 1)], writes=pk(2, 0, NB))
                op('scalar', lambda e: e.activation(out=Bb[:, cs(c)], in_=ps[2][:, 0:NB], func=AF.Sigmoid, bias=vc(V_A0 + c), scale=1.0),
                   reads=pk(2, 0, NB) + ['VEC'], writes=[K('B', c)])
                op('tensor', lambda e: e.matmul(ps[3][:, 0:NB], lhsT=LORAB[64:128, 512 + c * 128:512 + (c + 1) * 128],
                                                rhs=MISC[64:128, :], start=True, stop=True),
                   reads=[K('LORA'), K('MISC', 2)], writes=pk(3, 0, NB))
                op('vector', lambda e: e.tensor_copy(out=Gb[:, cs(c)], in_=ps[3][:, 0:NB]), reads=pk(3, 0, NB), writes=[K('G', c)])
                op('vector', lambda e: e.tensor_scalar(out=KKb[:, cs(c)], in0=PC[:, cs(2 + c)], scalar1=vc(V_KK + c), scalar2=None, op0=ALU.mult),
                   reads=[K('PC', 2 + c), 'VEC'], writes=[K('KK', c)])
                op('scalar', lambda e: e.activation(out=SQB[c][:], in_=KKb[:, cs(c)], func=AF.Square), reads=[K('KK', c)], writes=[K('SQB', c)])
                op('tensor', lambda e: e.matmul(ps[4][:, 0:NB], lhsT=BONES, rhs=SQB[c][:], start=True, stop=True),
                   reads=[K('SQB', c), 'CB'], writes=pk(4, 0, NB))
                op('scalar', lambda e: e.activation(out=TQ[0][:], in_=ps[4][:, 0:NB], func=AF.Ln, bias=EPS[:, 3:4], scale=1.0),
                   reads=pk(4, 0, NB) + ['EPS'], writes=[K('TQ', 0)])
                op('scalar', lambda e: e.activation(out=TQ[1][:], in_=TQ[0][:], func=AF.Exp, scale=-0.5), reads=[K('TQ', 0)], writes=[K('TQ', 1)])
                op('vector', lambda e: e.tensor_tensor(out=KKb[:, cs(c)], in0=KKb[:, cs(c)], in1=TQ[1][:], op=ALU.mult),
                   reads=[K('KK', c), K('TQ', 1)], writes=[K('KK', c)])
                op('vector', lambda e: e.tensor_scalar(out=TQ[0][:], in0=Bb[:, cs(c)], scalar1=-1.0, scalar2=vc(V_KA + c), op0=ALU.add, op1=ALU.mult),
                   reads=[K('B', c), 'VEC'], writes=[K('TQ', 0)])
                op('vector', lambda e: e.scalar_tensor_tensor(out=KPb[:, cs(c)], in0=TQ[0][:], scalar=1.0, in1=PC[:, cs(2 + c)],
                                                              op0=ALU.add, op1=ALU.mult),
                   reads=[K('TQ', 0), K('PC', 2 + c)], writes=[K('KP', c)])
                op('vector', lambda e: e.tensor_tensor(out=Bb[:, cs(c)], in0=Bb[:, cs(c)], in1=KKb[:, cs(c)], op=ALU.mult),
                   reads=[K('B', c), K('KK', c)], writes=[K('B', c)])
                op('vector', lambda e: e.scalar_tensor_tensor(out=TQ[1][:], in0=PC[:, cs(c)], scalar=vc(V_RK + c), in1=KPb[:, cs(c)],
                                                              op0=ALU.mult, op1=ALU.mult),
                   reads=[K('PC', c), K('KP', c), 'VEC'], writes=[K('TQ', 1)])
                op('vector', lambda e: e.tensor_copy(out=SQB[c][:], in_=TQ[1][:]), reads=[K('TQ', 1)], writes=[K('SQB', c)])
                op('tensor', lambda e: e.matmul(ps[5][:, 0:NB], lhsT=BONES, rhs=SQB[c][:], start=True, stop=True),
                   reads=[K('SQB', c), 'CB'], writes=pk(5, 0, NB))
                op('scalar', lambda e: e.activation(out=BON[:, cs(c)], in_=ps[5][:, 0:NB], func=AF.Copy), reads=pk(5, 0, NB), writes=[K('BON', c)])
                op('vector', lambda e: e.tensor_copy(out=VTB[:, cs(c)], in_=PC[:, cs(4 + c)]), reads=[K('PC', 4 + c)], writes=[K('VTB', c)])
            for ti in range(NB // 128):
                q0 = ti * 128; q1 = q0 + 128; tt0 = t0 + q0
                op('tensor', lambda e: e.matmul(ps[2][:, 0:256], lhsT=MISC[0:32, q0:q1], rhs=LORAB[0:32, 0:256], start=True, stop=True),
                   reads=[K('MISC', 0), K('LORA')], writes=pk(2, 0, 256))
                op('vector', lambda e: e.tensor_tensor(out=XW[:], in0=ps[2][:, 0:256], in1=W0B[:], op=ALU.add),
                   reads=pk(2, 0, 256) + [K('W0B')], writes=[K('XW')])
                op('scalar', lambda e: e.activation(out=SIGTM[:], in_=XW[:], func=AF.Sigmoid), reads=[K('XW')], writes=[K('SIGTM')])
                for c in range(2):
                    op('tensor', lambda e: e.matmul(ps[3][:, c * 128:(c + 1) * 128], lhsT=SIGTM[:, c * 128:(c + 1) * 128], rhs=TRI_IF,
                                                    start=True, stop=True),
                       reads=[K('SIGTM'), 'CF'], writes=pk(3, c * 128, c * 128 + 128))
                    op('tensor', lambda e: e.matmul(ps[3][:, 256 + c * 128:256 + (c + 1) * 128], lhsT=SIGTM[:, c * 128:(c + 1) * 128],
                                                    rhs=TRI_SF, start=True, stop=True),
                       reads=[K('SIGTM'), 'CF'], writes=pk(3, 256 + c * 128, 256 + c * 128 + 128))
                op('scalar', lambda e: e.activation(out=EP[:], in_=ps[3][:, 0:256], func=AF.Exp, scale=-C_DECAY), reads=pk(3, 0, 256), writes=[K('EP')])
                op('scalar', lambda e: e.activation(out=EM[:], in_=ps[3][:, 0:256], func=AF.Exp, scale=C_DECAY), reads=pk(3, 0, 256), writes=[K('EM')])
                op('scalar', lambda e: e.activation(out=EE[:], in_=ps[3][:, 256:512], func=AF.Exp, scale=-C_DECAY), reads=pk(3, 256, 512), writes=[K('EE')])
                for c in range(2):
                    E_ = lambda X_: X_[:, c * 128:(c + 1) * 128]
                    gC = EP[:, c * 128 + 127:c * 128 + 128]
                    tq = slice(c * NB + q0, c * NB + q1)
                    op('vector', lambda e: e.scalar_tensor_tensor(out=FT(c, 0), in0=KKb[:, tq], scalar=-1.0, in1=E_(EE), op0=ALU.mult, op1=ALU.mult),
                       reads=[K('KK', c), K('EE')], writes=[K('FT', c)])
                    op('vector', lambda e: e.tensor_tensor(out=FT(c, 1), in0=Bb[:, tq], in1=E_(EM), op=ALU.mult),
                       reads=[K('B', c), K('EM')], writes=[K('FT', c)])
                    op('vector', lambda e: e.tensor_tensor(out=FT(c, 2), in0=KPb[:, tq], in1=E_(EM), op=ALU.mult),
                       reads=[K('KP', c), K('EM')], writes=[K('FT', c)])
                    op('vector', lambda e: e.tensor_tensor(out=FT(c, 3), in0=PC[:, tq], in1=E_(EP), op=ALU.mult),
                       reads=[K('PC', c), K('EP')], writes=[K('FT', c)])
                    op('vector', lambda e: e.scalar_tensor_tensor(out=FH(c, 0), in0=Bb[:, tq], scalar=gC, in1=E_(EM), op0=ALU.mult, op1=ALU.mult),
                       reads=[K('B', c), K('EM'), K('EP')], writes=[K('FH', c)])
                    op('vector', lambda e: e.scalar_tensor_tensor(out=FH(c, 1), in0=KPb[:, tq], scalar=gC, in1=E_(EM), op0=ALU.mult, op1=ALU.mult),
                       reads=[K('KP', c), K('EM'), K('EP')], writes=[K('FH', c)])
                    srcs = [(FT(c, 0), K('FT', c)), (VTB[:, tq], K('VTB', c)), (FH(c, 0), K('FH', c)), (FH(c, 1), K('FH', c))]
                    for i, (src, sk) in enumerate(srcs):
                        op('tensor', lambda e: e.transpose(out=psT[:, (c * 4 + i) * 128:(c * 4 + i + 1) * 128], in_=src, identity=IDB),
                           reads=[sk, 'CB'], writes=[('ps', 7)])
                    if c == 0:
                        op('scalar', lambda e: e.activation(out=TM_[:, 0:512], in_=psT[:, 0:512], func=AF.Copy), reads=[('ps', 7)], writes=[K('TM', 0)])
                    else:
                        op('vector', lambda e: e.tensor_copy(out=TM_[:, 512:1024], in_=psT[:, 512:1024]), reads=[('ps', 7)], writes=[K('TM', 1)])
                for h in range(4):
                    c = h // 2; hl = h % 2; r0 = hl * 64; r1 = r0 + 64
                    aT = FT(c, 0, r0, r1); bT = FT(c, 1, r0, r1); kT = FT(c, 2, r0, r1); rT = FT(c, 3, r0, r1)
                    pn = h % 2
                    for i, lt in enumerate((bT, kT)):
                        for j2, rt in enumerate((aT, rT)):
                            qq = i * 2 + j2
                            op('tensor', lambda e: e.matmul(ps[pn][:, qq * 128:(qq + 1) * 128], lhsT=lt, rhs=rt, start=True, stop=True),
                               reads=[K('FT', c)], writes=pk(pn, qq * 128, qq * 128 + 128))
                    for i in range(2):
                        op('vector', lambda e: e.tensor_tensor(out=NM(h, i, 0), in0=ps[pn][:, (i * 2) * 128:(i * 2 + 1) * 128],
                                                               in1=CB[:, 256:384], op=ALU.mult),
                           reads=pk(pn, i * 256, i * 256 + 128) + ['CB'], writes=[K('NM', h)])
                        op('vector', lambda e: e.tensor_tensor(out=NM(h, i, 1), in0=ps[pn][:, (i * 2 + 1) * 128:(i * 2 + 2) * 128],
                                                               in1=CB[:, 128:256], op=ALU.mult),
                           reads=pk(pn, i * 256 + 128, i * 256 + 256) + ['CB'], writes=[K('NM', h)])
                    op('tensor', lambda e: e.matmul(ps[2][:, hs(h)], lhsT=aT, rhs=bT, start=True, stop=True),
                       reads=[K('FT', c)], writes=pk(2, h * 128, h * 128 + 128))
                    op('vector', lambda e: e.tensor_tensor(out=LP[0][:, hs(h)], in0=ps[2][:, hs(h)], in1=TRILB, op=ALU.mult),
                       reads=pk(2, h * 128, h * 128 + 128) + ['CB'], writes=[K('LP', 0, h)])
                    op('gpsimd', lambda e: e.tensor_copy(out=NP_[0][:, hs(h)], in_=NM(h, 0, 0)), reads=[K('NM', h)], writes=[K('NP', 0, h)])
                    op('tensor', lambda e: e.matmul(ps[3][:, hs(h, 64, 128)], lhsT=NM(h, 1, 0), rhs=TM(c, 1, r0, r1), start=True, stop=True),
                       reads=[K('NM', h), K('TM', c)], writes=pk(3, h * 128, h * 128 + 128))
                    op('vector', lambda e: e.tensor_copy(out=XF[0][:, hs(h, 64, 128)], in_=ps[3][:, hs(h, 64, 128)]),
                       reads=pk(3, h * 128, h * 128 + 128), writes=[K('XF', 0, h)])
                    op('gpsimd', lambda e: e.tensor_copy(out=XF[0][:, hs(h, 0, 64)], in_=TM(c, 0, r0, r1)),
                       reads=[K('TM', c), K('XF', 0, h)], writes=[K('XF', 0, h)])
                cur = 0
                for lvl in range(7 if 'L' not in os.environ.get('KSKIP', '') else 0):
                    nxt = 1 - cur
                    op('vector', lambda e: e.tensor_copy(out=XB[:], in_=XF[cur][:]), reads=allh('XF', cur), writes=[K('XB')])
                    for h in range(4):
                        op('tensor', lambda e: e.matmul(ps[4][:, hs(h)], lhsT=NP_[cur][:, hs(h)], rhs=XB[:, hs(h)], start=True, stop=True),
                           reads=[K('NP', cur, h), K('XB')], writes=pk(4, h * 128, h * 128 + 128))
                    op('vector', lambda e: e.tensor_tensor(out=XF[nxt][:], in0=XF[cur][:], in1=ps[4][:], op=ALU.add),
                       reads=allh('XF', cur) + pk(4), writes=allh('XF', nxt))
                    if lvl < 6:
                        for h in range(4):
                            op('tensor', lambda e: e.matmul(ps[5][:, hs(h)], lhsT=LP[cur][:, hs(h)], rhs=NP_[cur][:, hs(h)], start=True, stop=True),
                               reads=[K('LP', cur, h), K('NP', cur, h)], writes=pk(5, h * 128, h * 128 + 128))
                        op('scalar', lambda e: e.activation(out=NP_[nxt][:], in_=ps[5][:], func=AF.Copy), reads=pk(5), writes=allh('NP', nxt))
                        if lvl < 5:
                            for h in range(4):
                                op('tensor', lambda e: e.matmul(ps[6][:, hs(h)], lhsT=NP_[cur][:, hs(h)], rhs=LP[cur][:, hs(h)],
                                                                start=True, stop=True),
                                   reads=[K('LP', cur, h), K('NP', cur, h)], writes=pk(6, h * 128, h * 128 + 128))
                            op('scalar', lambda e: e.activation(out=LP[nxt][:], in_=ps[6][:], func=AF.Copy), reads=pk(6), writes=allh('LP', nxt))
                    cur = nxt
                XFf = XF[cur]
                op('vector', lambda e: e.tensor_copy(out=XB[:], in_=XFf[:]), reads=allh('XF', cur), writes=[K('XB')])
                for c in range(2):
                    for hl in range(2):
                        h = 2 * c + hl
                        op('tensor', lambda e: e.transpose(out=psT[hl * 64:(hl + 1) * 64, c * 128:(c + 1) * 128], in_=XB[:, hs(h, 0, 64)], identity=IDB),
                           reads=[K('XB'), 'CB'], writes=[('ps', 7)])
                op('vector', lambda e: e.tensor_copy(out=AHT[:], in_=psT[:, 0:256]), reads=[('ps', 7)], writes=[K('AHT')])
                for c in range(2):
                    cb = slice(c * 128, (c + 1) * 128)
                    op('tensor', lambda e: e.matmul(ps[0][:, cb], lhsT=AHT[:, cb], rhs=SBF[:, cb], start=True, stop=True),
                       reads=[K('AHT'), K('SBF')], writes=pk(0, c * 128, c * 128 + 128))
                    for hl in range(2):
                        h = 2 * c + hl
                        op('vector', lambda e: e.tensor_tensor(out=UB[:, h * 64:(h + 1) * 64], in0=ps[0][:, c * 128 + hl * 64:c * 128 + hl * 64 + 64],
                                                               in1=XFf[:, hs(h, 64, 128)], op=ALU.add),
                           reads=pk(0, c * 128, c * 128 + 128) + [K('XF', cur, h)], writes=[K('UB', h)])
                    op('tensor', lambda e: e.matmul(ps[1][:, cb], lhsT=SBF[:, cb], rhs=FT(c, 3), start=True, stop=False, skip_group_check=True),
                       reads=[K('SBF'), K('FT', c)], writes=pk(1, c * 128, c * 128 + 128))
                    for hl in range(2):
                        h = 2 * c + hl; r0 = hl * 64; r1 = r0 + 64
                        op('tensor', lambda e: e.matmul(ps[1][r0:r1, cb], lhsT=UB[:, h * 64:(h + 1) * 64], rhs=NM(h, 0, 1),
                                                        start=False, stop=False, skip_group_check=True),
                           reads=[K('UB', h), K('NM', h)], writes=pk(1, c * 128, c * 128 + 128))
                        op('tensor', lambda e: e.matmul(ps[1][r0:r1, cb], lhsT=TM(c, 1, r0, r1), rhs=NM(h, 1, 1),
                                                        start=False, stop=True, skip_group_check=True),
                           reads=[K('TM', c), K('NM', h)], writes=pk(1, c * 128, c * 128 + 128))
                    for hl in range(2):
                        h = 2 * c + hl; r0 = hl * 64; r1 = r0 + 64
                        so = slice(256 + c * 128 + r0, 256 + c * 128 + r1)
                        sd = slice(c * 128 + r0, c * 128 + r1)
                        op('tensor', lambda e: e.matmul(ps[0][r0:r1, so], lhsT=TM(c, 2, r0, r1), rhs=UB[:, h * 64:(h + 1) * 64],
                                                        start=True, stop=False, skip_group_check=True),
                           reads=[K('TM', c), K('UB', h)], writes=pk(0, 256 + c * 128, 256 + c * 128 + 128))
                        op('tensor', lambda e: e.matmul(ps[0][r0:r1, so], lhsT=TM(c, 3, r0, r1), rhs=TM(c, 1, r0, r1),
                                                        start=False, stop=True, skip_group_check=True),
                           reads=[K('TM', c)], writes=pk(0, 256 + c * 128, 256 + c * 128 + 128))
                        op('vector', lambda e: e.scalar_tensor_tensor(out=S32[r0:r1, sd], in0=S32[r0:r1, sd], scalar=EP[r0:r1, c * 128 + 127:c * 128 + 128],
                                                                      in1=ps[0][r0:r1, so], op0=ALU.mult, op1=ALU.add),
                           reads=[K('S32'), K('EP')] + pk(0, 256 + c * 128, 256 + c * 128 + 128), writes=[K('S32')])
                        op('vector', lambda e: e.tensor_copy(out=SBF[r0:r1, sd], in_=S32[r0:r1, sd]), reads=[K('S32')], writes=[K('SBF')])
                if 'E' in os.environ.get('KSKIP', ''):
                    continue
                op('scalar', lambda e: e.activation(out=OT[:], in_=ps[1][:, 0:256], func=AF.Copy), reads=pk(1, 0, 256), writes=[K('OT')])
                op('vector', lambda e: e.tensor_tensor(out=OSQ[:], in0=OT[:], in1=OT[:], op=ALU.mult), reads=[K('OT')], writes=[K('OSQ')])
                for c in range(2):
                    cb = slice(c * 128, (c + 1) * 128)
                    op('tensor', lambda e: e.matmul(ps[2][:, cb], lhsT=BONF, rhs=OT[:, cb], start=True, stop=True),
                       reads=[K('OT'), 'CF'], writes=pk(2, c * 128, c * 128 + 128))
                    op('tensor', lambda e: e.matmul(ps[2][:, 256 + c * 128:256 + (c + 1) * 128], lhsT=BONF, rhs=OSQ[:, cb], start=True, stop=True),
                       reads=[K('OSQ'), 'CF'], writes=pk(2, 256 + c * 128, 256 + c * 128 + 128))
                v2 = lambda X_: X_.rearrange("p (c x) -> p c x", c=2)
                BV = lambda X_: v2(X_[:])[:, :, q0:q1]
                op('scalar', lambda e: e.activation(out=OM[:], in_=ps[2][:, 0:256], func=AF.Copy, scale=1.0 / 64), reads=pk(2), writes=[K('OM')])
                op('vector', lambda e: e.tensor_tensor(out=OV[:], in0=OM[:], in1=OM[:], op=ALU.mult), reads=[K('OM')], writes=[K('OV')])
                op('vector', lambda e: e.scalar_tensor_tensor(out=OV[:], in0=ps[2][:, 256:512], scalar=1.0 / 64, in1=OV[:],
                                                              op0=ALU.mult, op1=ALU.subtract),
                   reads=pk(2) + [K('OV')], writes=[K('OV')])
                op('scalar', lambda e: e.activation(out=OL[:], in_=OV[:], func=AF.Ln, bias=EPS[:, 2:3], scale=1.0), reads=[K('OV'), 'EPS'], writes=[K('OL')])
                op('scalar', lambda e: e.activation(out=ORS[:], in_=OL[:], func=AF.Exp, scale=-0.5), reads=[K('OL')], writes=[K('ORS')])
                op('vector', lambda e: e.tensor_tensor(out=OM[:], in0=OT[:], in1=OM[:], op=ALU.subtract), reads=[K('OT'), K('OM')], writes=[K('OM')])
                op('vector', lambda e: e.tensor_tensor(out=OM[:], in0=OM[:], in1=ORS[:], op=ALU.mult), reads=[K('OM'), K('ORS')], writes=[K('OM')])
                for c in range(2):
                    cb = slice(c * 128, (c + 1) * 128)
                    op('vector', lambda e: e.tensor_scalar(out=OM[:, cb], in0=OM[:, cb], scalar1=vc(V_RLW + c), scalar2=vc(V_RLB + c),
                                                           op0=ALU.mult, op1=ALU.add),
                       reads=[K('OM'), 'VEC'], writes=[K('OM')])
                op('vector', lambda e: e.tensor_tensor(out=v2(OV[:]), in0=BV(BON), in1=BV(VTB), op=ALU.mult),
                   reads=[K('BON', 0), K('BON', 1), K('VTB', 0), K('VTB', 1)], writes=[K('OV')])
                op('vector', lambda e: e.tensor_tensor(out=OM[:], in0=OM[:], in1=OV[:], op=ALU.add), reads=[K('OM'), K('OV')], writes=[K('OM')])
                op('vector', lambda e: e.tensor_tensor(out=YT[:, 4:6, tt0:tt0 + 128], in0=v2(OM[:]), in1=BV(Gb), op=ALU.mult),
                   reads=[K('OM'), K('G', 0), K('G', 1)], writes=[K('YT', 4, tb), K('YT', 5, tb)])

    seq = []
    for l in range(n_layers):
        seq += [('ffn', l, 0), ('mix', l), ('ffn', l, 1)]
    for s_ in seq:
        if s_[0] == 'ffn':
            ffn(s_[1], s_[2])
        else:
            mixer(s_[1])
        if stop_after is not None and tuple(stop_after) == tuple(s_):
            break
    for c in range(8):
        P.dma('sync', 'st_o', lambda e, c=c: e.dma_start(out=out_d[:, c, :], in_=XT[:, c, :]),
              reads=[('XT', c, tb) for tb in range(4)], writes=[('OUT', c)])
    P.wait_all('sync', [('OUT', c) for c in range(8)])
    with nc.Block() as block:
        P.emit(block)
    stack.close()
    return nc


def host_prep(inp):
    f = lambda a: np.ascontiguousarray(a, dtype=np.float32)
    tri_incl = np.triu(np.ones((128, 128), np.float32))
    tri_strict = np.triu(np.ones((128, 128), np.float32), 1)
    bo = np.zeros((128, 128), np.float32); bo[:64, :64] = 1; bo[64:, 64:] = 1
    consts = np.concatenate([np.eye(128, dtype=np.float32), tri_incl, tri_strict, tri_strict.T.copy(), bo], axis=1)
    fm = lambda v: np.asarray(v).reshape(-1, 128).T
    vecs = np.zeros((128, NL * NV), np.float32)
    for l in range(NL):
        o = l * NV
        for i, nm in enumerate(['ffn1_pre_g', 'ffn1_post_g', 'mix_pre_g', 'mix_post_g', 'ffn2_pre_g', 'ffn2_post_g']):
            vecs[:, o + V_G + i * 8: o + V_G + i * 8 + 8] = fm(inp[nm][l])
        for c in range(2):
            for j in range(3):
                vecs[:, o + V_SC + c * 3 + j] = inp['sc_conv_w'][l, j, c * 128:(c + 1) * 128]
            for j in range(31):
                vecs[:, o + V_CM + c * 31 + j] = inp['cm_conv_w'][l, j, c * 128:(c + 1) * 128]
        for off, nm in [(V_CMB, 'cm_conv_b'), (V_CMLW, 'cm_ln_w'), (V_CMLB, 'cm_ln_b'), (V_A0, 'rk_a0'), (V_KK, 'rk_k_k'),
                        (V_KA, 'rk_k_a'), (V_RLW, 'rk_ln_w'), (V_RLB, 'rk_ln_b')]:
            vecs[:, o + off: o + off + 2] = fm(inp[nm][l])
        vecs[:, o + V_RK: o + V_RK + 2] = fm(inp['rk_r_k'][l].reshape(-1))
        vecs[:, o + V_MU: o + V_MU + 7] = fm(inp['rk_mu'][l])
    wgu = np.empty((NL, 2, NFC, 128, 2, 8, 128), np.float32)
    wd = np.empty((NL, 2, 8, 128, NFC, 128), np.float32)
    wsrc = {('ffn1', 'w_gate'): inp['ffn1_w_gate'], ('ffn1', 'w_up'): inp['ffn1_w_up'], ('ffn1', 'w_down'): inp['ffn1_w_down'],
            ('ffn2', 'w_gate'): inp['ffn2_w_gate'], ('ffn2', 'w_up'): inp['ffn2_w_up'], ('ffn2', 'w_down'): inp['ffn2_w_down']}
    for wi, pre in enumerate(['ffn1', 'ffn2']):
        for gi, nm in enumerate(['w_gate', 'w_up']):
            w = np.asarray(wsrc[(pre, nm)])
            wgu[:, wi, :, :, gi, :, :] = w.reshape(NL, 8, 128, NFC, 128).transpose(0, 3, 2, 1, 4)
        w = np.asarray(wsrc[(pre, 'w_down')])
        wd[:, wi] = w.reshape(NL, NFC, 128, 8, 128).transpose(0, 3, 2, 1, 4)
    win = f(np.asarray(inp['w_in']).reshape(NL, 8, 128, INC).transpose(0, 2, 1, 3))
    wout = f(np.asarray(inp['w_out']).reshape(NL, 8, 128, D).transpose(0, 2, 1, 3))
    bc = np.empty((NL, 128, 768), np.float32)
    lora = np.zeros((NL, 128, 768), np.float32)
    for l in range(NL):
        row = np.concatenate([inp['rk_w0'][l], inp['sg_ln_w'][l], inp['sg_ln_b'][l]])
        bc[l] = np.broadcast_to(row[None, :], (128, row.shape[0]))
        lora[l, 0:32, 0:256] = inp['rk_w_up'][l]
        lora[l, 32:64, 256:512] = inp['rk_a_up'][l]
        lora[l, 64:128, 512:768] = inp['rk_g_up'][l]
    wmt = f(np.asarray(inp['sg_w']).transpose(0, 3, 1, 2))
    sgb = np.asarray(inp['sg_b'])
    sgbT = f(np.repeat(sgb.reshape(NL, 2, 2, 1, 128), 64, axis=3).reshape(NL, 2, 128, 128).transpose(0, 2, 1, 3))
    shared = dict(consts=f(consts), vecs=f(vecs), wgu=wgu, wd=wd, win=win, wout=wout, bc=bc, lora=lora, wmt=wmt, sgbT=sgbT)
    x = np.asarray(inp['x'])
    maps = []
    for b in range(8):
        xt = f(x[b].T.reshape(8, 128, T).transpose(1, 0, 2))
        m = dict(shared); m['xT'] = xt
        maps.append(m)
    return maps


_NC = None


def kernel(**inputs):
    global _NC
    inp = {k: np.asarray(v) for k, v in inputs.items()}
    maps = host_prep(inp)
    if _NC is None:
        _NC = build()
    res = run_bass_kernel_spmd(_NC, maps, core_ids=list(range(8)))
    out = np.empty((8, T, D), np.float32)
    for b in range(8):
        o = np.asarray(res.results[b]["outT"])
        out[b] = o.transpose(1, 0, 2).reshape(D, T).T
    return out
```

```python
import contextlib
import numpy as np
import concourse.bass as bass
import concourse.mybir as mybir
from concourse.bass_utils import run_bass_kernel_spmd

F32 = mybir.dt.float32
BF16 = mybir.dt.bfloat16
AF = mybir.ActivationFunctionType
ALU = mybir.AluOpType

D = 1024; T = 2048; DFF = 2816; NFC = 22; G = 256; INC = 2688
NL = 2
SAME_ENGINE_SYNC = True
RELAX_SAME_ENGINE = True
import os
DBG = int(os.environ.get('KDBG', '99'))
DBG2 = int(os.environ.get('KDBG2', '0'))
C_DECAY = float(np.exp(-0.5))

V_G = 0
V_SC = 48
V_CM = 54
V_CMB = 116; V_CMLW = 118; V_CMLB = 120
V_A0 = 122; V_KK = 124; V_KA = 126; V_RK = 128; V_RLW = 130; V_RLB = 132
V_MU = 134
NV = 141


class _Rec:
    def __getattr__(self, name):
        def f(*a, **kw):
            self.call = (name, a, kw)
            return self
        return f


def _call(fn):
    r = _Rec()
    fn(r)
    return r.call


class Prog:
    ENG = ('sync', 'scalar', 'vector', 'gpsimd', 'tensor')

    def __init__(s, nc, stack):
        s.nc = nc; s.stack = stack
        s.streams = {e: [] for e in s.ENG}
        s.sem = {}; s.cnt = {}
        s.lastw = {}; s.readers = {}
        s.known = {e: {} for e in s.ENG}
        for e in s.ENG:
            s._mksem(e)

    def _mksem(s, key):
        s.sem[key] = s.stack.enter_context(s.nc.semaphore("s_" + str(key)))
        s.cnt[key] = 0

    def _deps(s, eng, reads, writes):
        need = {}

        def add(tok):
            if tok is None:
                return
            k, v = tok
            if k == 'tensor' and eng == 'tensor':
                return
            if k == eng and not SAME_ENGINE_SYNC:
                return
            if k not in s.ENG:
                v = s.cnt[k]
            if need.get(k, 0) < v:
                need[k] = v
        for r in reads:
            add(s.lastw.get(r))
            if isinstance(r, tuple) and r[0] == 'ps':
                for k, v in s.readers.get(r, {}).items():
                    if k != eng:
                        add((k, v))
        for w in writes:
            tok = s.lastw.get(w)
            if tok is not None and not (RELAX_SAME_ENGINE and tok[0] == eng):
                add(tok)
            for k, v in s.readers.get(w, {}).items():
                if not (RELAX_SAME_ENGINE and k == eng):
                    add((k, v))
        waits = []
        for k, v in need.items():
            if s.known[eng].get(k, 0) < v:
                s.known[eng][k] = v
                waits.append((k, v))
        return waits

    def _commit(s, tok, reads, writes):
        for r in reads:
            d = s.readers.setdefault(r, {})
            if d.get(tok[0], 0) < tok[1]:
                d[tok[0]] = tok[1]
        for w in writes:
            s.lastw[w] = tok
            s.readers[w] = {}

    def op(s, eng, fn, reads=(), writes=()):
        reads = list(reads); writes = list(writes)
        waits = s._deps(eng, reads, writes)
        s.cnt[eng] += 1
        tok = (eng, s.cnt[eng])
        s.streams[eng].append((waits, _call(fn), eng, 1))
        s._commit(tok, reads, writes)

    def dma(s, eng, semkey, fn, reads=(), writes=()):
        reads = list(reads); writes = list(writes)
        if semkey not in s.sem:
            s._mksem(semkey)
        waits = s._deps(eng, reads, writes)
        s.cnt[semkey] += 16
        tok = (semkey, s.cnt[semkey])
        s.streams[eng].append((waits, _call(fn), semkey, 16))
        s._commit(tok, reads, writes)

    def barrier(s):
        for eng in s.ENG:
            waits = []
            for k, v in s.cnt.items():
                if v > 0 and s.known[eng].get(k, 0) < v and not (k == eng and k == 'tensor'):
                    s.known[eng][k] = v
                    waits.append((k, v))
            if waits:
                s.streams[eng].append((waits, None, None, 0))

    def wait_all(s, eng, keys):
        need = {}
        for k in keys:
            tok = s.lastw.get(k)
            if tok is not None and need.get(tok[0], 0) < tok[1]:
                need[tok[0]] = tok[1]
        s.streams[eng].append((list(need.items()), None, None, 0))

    def emit(s, block):
        waited = {e: set() for e in s.ENG}
        for eng in s.ENG:
            for waits, fn, semkey, amt in s.streams[eng]:
                for k, v in waits:
                    if k in waited:
                        waited[k].add(v)
        rank = {}
        for e in s.ENG:
            rank[e] = {v: i + 1 for i, v in enumerate(sorted(waited[e]))}
        for eng in s.ENG:
            items = s.streams[eng]

            def body(e, items=items, eng=eng):
                idx = 0
                for waits, fn, semkey, amt in items:
                    for k, v in waits:
                        e.wait_ge(s.sem[k], rank[k][v] if k in rank else v)
                    if fn is not None:
                        name, a, kw = fn
                        ins = getattr(e, name)(*a, **kw)
                        if semkey == eng:
                            idx += 1
                            if idx in rank[eng]:
                                ins.then_inc(s.sem[semkey], 1)
                        else:
                            ins.then_inc(s.sem[semkey], amt)
            getattr(block, eng)(body)


class Alloc:
    def __init__(s, nc):
        s.nc = nc
        s.base = (nc.sbuf_base + 63) // 64 * 64
        s.top = nc.sbuf_top
        s.off = s.base
        s.n = 0

    def __call__(s, shape, dtype):
        sz = int(np.prod(shape[1:])) * (4 if dtype == F32 else 2)
        sz = (sz + 63) // 64 * 64
        assert s.off + sz <= s.top, ("SBUF overflow", s.off, sz, s.top)
        s.n += 1
        t = s.nc.alloc_sbuf_tensor_at("sb%d" % s.n, list(shape), dtype, offset=s.off)
        s.off += sz
        return t

    def mark(s):
        return s.off

    def release(s, m):
        s.off = m


def pk(b, lo=0, hi=512):
    return [('ps', b)]


def build(n_layers=NL, stop_after=None):
    nc = bass.Bass("TRN2", target_bir_lowering=False)
    dt = lambda name, shape, kind="ExternalInput": nc.dram_tensor(name, list(shape), F32, kind=kind).ap()
    xin = dt("xT", [128, 8, T])
    consts_d = dt("consts", [128, 5 * 128])
    vecs_d = dt("vecs", [128, NL * NV])
    wgu_d = dt("wgu", [NL, 2, NFC, 128, 2, 8, 128])
    wd_d = dt("wd", [NL, 2, 8, 128, NFC, 128])
    win_d = dt("win", [NL, 128, 8, INC])
    wout_d = dt("wout", [NL, 128, 8, D])
    bc_d = dt("bc", [NL, 128, 768])
    lora_d = dt("lora", [NL, 128, 768])
    wmt_d = dt("wmt", [NL, 128, 4, 128])
    sgb_d = dt("sgbT", [NL, 128, 2, 128])
    out_d = dt("outT", [128, 8, T], kind="ExternalOutput")

    stack = contextlib.ExitStack()
    P = Prog(nc, stack)
    A = Alloc(nc)
    op = P.op

    XT = A([128, 8, T], F32)
    CF = A([128, 5 * 128], F32)
    IDF = CF[:, 0:128]; TRI_IF = CF[:, 128:256]; TRI_SF = CF[:, 256:384]
    CB = A([128, 5 * 128], BF16)
    IDB = CB[:, 0:128]; MASK_SI = CB[:, 128:384]
    TRILB = CB[:, 384:512]; BONES = CB[:, 512:640]
    ONESB = A([128, 128], BF16)
    CBX = A([128, 1024], BF16)
    ONESF = A([128, 128], F32)
    VEC = A([128, NL * NV], F32)
    HALFG = A([128, NL * 16], F32)
    EPS = A([128, 4], F32)
    OMMV = A([128, NL * 7], F32)
    ps = [nc.alloc_psum_tensor("psb%d" % i, [128, 512], F32) for i in range(7)]
    psT = nc.alloc_psum_tensor("psT", [128, 1024], BF16)

    for c in range(8):
        P.dma('sync', 'ld_x', lambda e, c=c: e.dma_start(out=XT[:, c, :], in_=xin[:, c, :]),
              writes=[('XT', c, tb) for tb in range(4)])
    P.dma('sync', 'ld_c', lambda e: e.dma_start(out=CF[:], in_=consts_d[:, :]), writes=['CF'])
    P.dma('sync', 'ld_c', lambda e: e.dma_start(out=VEC[:], in_=vecs_d[:, :]), writes=['VEC'])
    op('vector', lambda e: e.tensor_copy(out=CB[:], in_=CF[:]), reads=['CF'], writes=['CB'])
    op('vector', lambda e: e.memset(ONESB[:], 1.0), writes=['ONES'])
    for q in range(4):
        src = CB[:, 256:384] if q % 2 == 0 else CB[:, 128:256]
        op('vector', lambda e: e.tensor_copy(out=CBX[:, q * 128:(q + 1) * 128], in_=src), reads=['CB'], writes=['CBX'])
        op('vector', lambda e: e.tensor_copy(out=CBX[:, 512 + q * 128:512 + (q + 1) * 128], in_=CB[:, 384:512]), reads=['CB'], writes=['CBX'])
    op('vector', lambda e: e.memset(ONESF[:], 1.0), writes=['ONES'])
    for i, v in enumerate([1e-6, 1e-5, 64e-5, 1e-24]):
        op('vector', lambda e, i=i, v=v: e.memset(EPS[:, i:i + 1], v), writes=['EPS'])
    for l in range(NL):
        for j, gi in enumerate([1, 5]):
            op('vector', lambda e, l=l, j=j, gi=gi: e.tensor_scalar(
                out=HALFG[:, l * 16 + j * 8: l * 16 + j * 8 + 8],
                in0=VEC[:, l * NV + V_G + gi * 8: l * NV + V_G + gi * 8 + 8],
                scalar1=0.5, scalar2=None, op0=ALU.mult), reads=['VEC'], writes=['HALFG'])
    for l in range(NL):
        op('vector', lambda e, l=l: e.tensor_scalar(out=OMMV[:, l * 7:l * 7 + 7], in0=VEC[:, l * NV + V_MU:l * NV + V_MU + 7],
                                                    scalar1=-1.0, scalar2=1.0, op0=ALU.mult, op1=ALU.add),
           reads=['VEC'], writes=['OMMV'])

    def vcol(l, off):
        return VEC[:, l * NV + off: l * NV + off + 1]

    def rstd_from(psb, n, scale, epsi, LNV, RS, rk, wk, lk):
        op('scalar', lambda e: e.activation(out=LNV[:, 0:n], in_=psb[:, 0:n], func=AF.Ln,
                                            bias=EPS[:, epsi:epsi + 1], scale=scale),
           reads=rk + ['EPS'], writes=[lk])
        op('scalar', lambda e: e.activation(out=RS[:, 0:n], in_=LNV[:, 0:n], func=AF.Exp, scale=-0.5),
           reads=[lk], writes=[wk])

    work_mark = A.mark()

    def ffn(l, which):
        A.release(work_mark); P.barrier()
        gpre = V_G + (0 if which == 0 else 4) * 8
        HY = A([128, 8, T], BF16)
        ACTB = A([128, NFC, 1024], BF16)
        WGU = [A([128, 2, 8, 128], BF16) for _ in range(2)]
        WD = [A([128, NFC, 128], BF16) for _ in range(2)]
        SQ = [A([128, 512], BF16) for _ in range(2)]
        LNV = A([128, 512], F32)
        RS = [A([128, 512], F32) for _ in range(2)]
        SG = [A([128, 512], F32) for _ in range(2)]
        TMP = [A([128, 512], F32) for _ in range(2)]
        tag = 'f%d%d' % (l, which)
        K = lambda name, *idx: (tag, name) + idx
        it = 0
        if DBG <= 0:
            return
        for gtb in range(4):
            t0 = gtb * 512; rb = gtb % 2
            for c in range(8):
                b = c % 2
                op('scalar', lambda e, c=c, b=b, t0=t0: e.activation(out=SQ[b][:], in_=XT[:, c, t0:t0 + 512], func=AF.Square),
                   reads=[('XT', c, gtb)], writes=[K('SQ', b)])
                op('tensor', lambda e, c=c, b=b: e.matmul(ps[6][:], lhsT=ONESB[:], rhs=SQ[b][:], start=(c == 0), stop=(c == 7)),
                   reads=[K('SQ', b), 'ONES'], writes=pk(6))
            rstd_from(ps[6], 512, 1.0 / D, 0, LNV, RS[rb], pk(6), K('RS', rb), K('LNV'))
            for c in range(8):
                op('vector', lambda e, c=c, t0=t0, rb=rb: e.scalar_tensor_tensor(
                    out=HY[:, c, t0:t0 + 512], in0=XT[:, c, t0:t0 + 512], scalar=vcol(l, gpre + c),
                    in1=RS[rb][:], op0=ALU.mult, op1=ALU.mult),
                   reads=[('XT', c, gtb), K('RS', rb), 'VEC'], writes=[K('HY', c, gtb)])
        for half in range(2):
            T0 = half * 1024
            if DBG <= 1:
                return
            for fc in range(NFC):
                wb = fc % 2
                P.dma('gpsimd', K('ldgu', wb), lambda e, fc=fc, wb=wb: (e.dma_start(out=WGU[wb][:], in_=wgu_d[l, which, fc]) if not os.environ.get('KHALFDMA') else e.dma_start(out=WGU[wb][:, 0:1], in_=wgu_d[l, which, fc, :, 0:1])),
                      writes=[K('WGU', wb)])
                for tb in range(2):
                    pg = (it % 2) * 2; pu = pg + 1; sb = it % 2; it += 1
                    for gi, pb in ((0, pg), (1, pu)):
                        for k in range(8):
                            op('tensor', lambda e, gi=gi, pb=pb, k=k, wb=wb, tb=tb: e.matmul(
                                ps[pb][:], lhsT=WGU[wb][:, gi, k, :], rhs=HY[:, k, T0 + tb * 512:T0 + (tb + 1) * 512],
                                start=(k == 0), stop=(k == 7)),
                               reads=[K('WGU', wb), K('HY', k, half * 2 + tb)], writes=pk(pb))
                    op('scalar', lambda e, pg=pg, sb=sb: e.activation(out=SG[sb][:], in_=ps[pg][:], func=AF.Silu),
                       reads=pk(pg), writes=[K('SG', sb)])
                    op('vector', lambda e, pu=pu, sb=sb, fc=fc, tb=tb: e.tensor_tensor(
                        out=ACTB[:, fc, tb * 512:(tb + 1) * 512], in0=SG[sb][:], in1=ps[pu][:], op=ALU.mult),
                       reads=[K('SG', sb)] + pk(pu), writes=[K('ACT', fc, tb)])
            if DBG <= 2:
                return
            if DBG2 == 10:
                continue
            for dc in range(8):
                wb = dc % 2
                P.dma('gpsimd' if DBG2 != 12 else 'scalar', K('ldd', wb), lambda e, dc=dc, wb=wb: e.dma_start(out=WD[wb][:] if DBG2 != 12 else WD[wb][:, 0:11, :].bitcast(F32), in_=wd_d[l, which, dc] if DBG2 != 12 else wd_d[l, which, dc, :, 0:11, 0:64]),
                      writes=[K('WD', wb)])
                for tb in range(2):
                    if DBG2 == 2 or (DBG2 == 3 and dc >= 1):
                        continue
                    po = (it % 2); sb = it % 2; it += 1
                    if DBG2 == 9:
                        po += 2
                    for fc in range(NFC if DBG2 != 8 else 8):
                        lhs_ = WD[wb][:, fc, :] if DBG2 not in (6, 15) else WGU[wb][:, 0, fc % 8, :]
                        rhs_ = ACTB[:, fc, tb * 512:(tb + 1) * 512] if DBG2 not in (5, 15) else HY[:, fc % 8, T0 + tb * 512:T0 + (tb + 1) * 512]
                        op('tensor', lambda e, po=po, fc=fc, wb=wb, tb=tb: e.matmul(
                            ps[po][:], lhsT=lhs_, rhs=rhs_,
                            start=(fc == 0), stop=(fc == (NFC if DBG2 != 8 else 8) - 1)),
                           reads=([K('WD', wb)] if DBG2 != 13 else []) + ([K('ACT', fc, tb)] if DBG2 != 14 else []), writes=pk(po))
                    if DBG2 == 4:
                        continue
                    op('scalar', lambda e, po=po, sb=sb: e.activation(out=SQ[sb][:], in_=ps[po][:], func=AF.Square),
                       reads=pk(po), writes=[K('SQ', sb)])
                    op('vector', lambda e, po=po, dc=dc, tb=tb: e.tensor_copy(out=HY[:, dc, T0 + tb * 512:T0 + (tb + 1) * 512], in_=ps[po][:]),
                       reads=pk(po) + [K('SQ', sb)], writes=[K('HY', dc, half * 2 + tb)])
                    if DBG2 != 1:
                        op('tensor', lambda e, sb=sb, tb=tb, dc=dc: e.matmul(ps[4 + tb][:], lhsT=ONESB[:], rhs=SQ[sb][:],
                                                                            start=(dc == 0), stop=(dc == 7)),
                           reads=[K('SQ', sb), 'ONES'], writes=pk(4 + tb))
            if DBG <= 3:
                return
            hg = l * 16 + which * 8
            for tb in range(2):
                t0 = T0 + tb * 512; gtb = half * 2 + tb
                rstd_from(ps[4 + tb], 512, 1.0 / D, 0, LNV, RS[tb], pk(4 + tb), K('RS', tb), K('LNV'))
                for c in range(8):
                    b = c % 2
                    op('vector', lambda e, c=c, b=b, tb=tb: e.tensor_tensor(
                        out=TMP[b][:], in0=HY[:, c, T0 + tb * 512:T0 + (tb + 1) * 512], in1=RS[tb][:], op=ALU.mult),
                       reads=[K('HY', c, half * 2 + tb), K('RS', tb)], writes=[K('TMP', b)])
                    op('vector', lambda e, c=c, b=b, t0=t0: e.scalar_tensor_tensor(
                        out=XT[:, c, t0:t0 + 512], in0=TMP[b][:], scalar=HALFG[:, hg + c: hg + c + 1],
                        in1=XT[:, c, t0:t0 + 512], op0=ALU.mult, op1=ALU.add),
                       reads=[K('TMP', b), ('XT', c, gtb), 'HALFG'], writes=[('XT', c, gtb)])

    def mixer(l):
        A.release(work_mark); P.barrier()
        tag = 'm%d' % l
        K = lambda name, *idx: (tag, name) + idx
        HM = A([128, 8, T + 1], BF16)
        YT = A([128, 8, T], BF16)
        grp_mark = A.mark()
        LNV = A([128, 512], F32)
        RS = A([128, 512], F32)
        SQ = [A([128, 512], BF16) for _ in range(2)]
        XTall = lambda c: [('XT', c, tb) for tb in range(4)]
        for c in range(8):
            op('vector', lambda e, c=c: e.memset(HM[:, c, 0:1], 0.0), writes=[K('HM0', c)])
        for tb in range(4):
            t0 = tb * 512
            for c in range(8):
                b = c % 2
                op('scalar', lambda e, c=c, b=b, t0=t0: e.activation(out=SQ[b][:], in_=XT[:, c, t0:t0 + 512], func=AF.Square),
                   reads=[('XT', c, tb)], writes=[K('SQ', b)])
                op('tensor', lambda e, c=c, b=b: e.matmul(ps[6][:], lhsT=ONESB[:], rhs=SQ[b][:], start=(c == 0), stop=(c == 7)),
                   reads=[K('SQ', b), 'ONES'], writes=pk(6))
            rstd_from(ps[6], 512, 1.0 / D, 0, LNV, RS, pk(6), K('RS'), K('LNV'))
            for c in range(8):
                op('vector', lambda e, c=c, t0=t0: e.scalar_tensor_tensor(
                    out=HM[:, c, 1 + t0:1 + t0 + 512], in0=XT[:, c, t0:t0 + 512], scalar=vcol(l, V_G + 16 + c),
                    in1=RS[:], op0=ALU.mult, op1=ALU.mult),
                   reads=[('XT', c, tb), K('RS'), 'VEC'], writes=[K('HM', c, tb)])
        HMr = lambda k, tb: [K('HM', k, tb)]
        HMrs = lambda k, tb: [K('HM', k, tb), K('HM0', k)] + ([K('HM', k, tb - 1)] if tb > 0 else [])

        def proj_fm(pb, W, col0, tb, wkey, ncol=128, W2=None, prow=None):
            t0 = tb * 512
            outap = ps[pb][:] if prow is None else ps[pb][prow[0]:prow[1], :]
            n = 8 if W2 is None else 16
            for k in range(8):
                op('tensor', lambda e, k=k: e.matmul(outap, lhsT=W[:, k, col0:col0 + ncol], rhs=HM[:, k, 1 + t0:1 + t0 + 512],
                                                    start=(k == 0), stop=(k == n - 1)),
                   reads=[wkey] + HMr(k, tb), writes=pk(pb))
            if W2 is not None:
                for k in range(8):
                    op('tensor', lambda e, k=k: e.matmul(outap, lhsT=W2[:, k, col0:col0 + ncol], rhs=HM[:, k, t0:t0 + 512],
                                                        start=False, stop=(k == 7)),
                       reads=[wkey] + HMrs(k, tb), writes=pk(pb))

        A.release(grp_mark); P.barrier()
        WA = A([128, 8, 768], BF16)
        Z = A([128, 2, T + 2], F32)
        TA = [A([128, 512], F32) for _ in range(2)]
        ACC = [A([128, 512], F32) for _ in range(2)]
        P.dma('gpsimd', K('ldwa'), lambda e: e.dma_start(out=WA[:], in_=win_d[l, :, :, 0:768]), writes=[K('WA')])
        for c in range(2):
            op('vector', lambda e, c=c: e.memset(Z[:, c, 0:2], 0.0), writes=[K('Z0', c)])
        it = 0
        for tb in range(4):
            t0 = tb * 512
            for c in range(2):
                b = it % 2; it += 1
                pc_, px_, pb_ = 0 + 3 * b, 1 + 3 * b, 2 + 3 * b
                proj_fm(pc_, WA, 256 + c * 128, tb, K('WA'))
                proj_fm(px_, WA, 512 + c * 128, tb, K('WA'))
                proj_fm(pb_, WA, 0 + c * 128, tb, K('WA'))
                op('scalar', lambda e, b=b, pc_=pc_: e.activation(out=TA[b][:], in_=ps[pc_][:], func=AF.Copy),
                   reads=pk(pc_), writes=[K('TA', b)])
                op('vector', lambda e, b=b, px_=px_, c=c, t0=t0: e.tensor_tensor(
                    out=Z[:, c, 2 + t0:2 + t0 + 512], in0=TA[b][:], in1=ps[px_][:], op=ALU.mult),
                   reads=[K('TA', b)] + pk(px_), writes=[K('Z', c, tb)])
                zr = [K('Z', c, tb), K('Z0', c)] + ([K('Z', c, tb - 1)] if tb > 0 else [])
                op('vector', lambda e, b=b, c=c, t0=t0: e.tensor_scalar(
                    out=ACC[b][:], in0=Z[:, c, t0:t0 + 512], scalar1=vcol(l, V_SC + c * 3 + 0), scalar2=None, op0=ALU.mult),
                   reads=zr + ['VEC'], writes=[K('ACC', b)])
                for j in (1, 2):
                    op('vector', lambda e, b=b, c=c, t0=t0, j=j: e.scalar_tensor_tensor(
                        out=ACC[b][:], in0=Z[:, c, t0 + j:t0 + j + 512], scalar=vcol(l, V_SC + c * 3 + j),
                        in1=ACC[b][:], op0=ALU.mult, op1=ALU.add),
                       reads=zr + ['VEC', K('ACC', b)], writes=[K('ACC', b)])
                op('vector', lambda e, b=b, c=c, t0=t0, pb_=pb_: e.tensor_tensor(
                    out=YT[:, 0 + c, t0:t0 + 512], in0=ACC[b][:], in1=ps[pb_][:], op=ALU.mult),
                   reads=[K('ACC', b)] + pk(pb_), writes=[K('YT', 0 + c, tb)])

        A.release(grp_mark); P.barrier()
        WDm = A([128, 8, 512], BF16)
        ZG = A([128, 2, T + 30], BF16)
        DIAG = A([128, 62, 128], BF16)
        SGT = [A([128, 512], F32) for _ in range(2)]
        LNV = A([128, 512], F32)
        RS = A([128, 512], F32)
        ZD = [A([128, 512], F32) for _ in range(2)]
        ZD2 = [A([128, 512], F32) for _ in range(2)]
        MEAN = A([128, 512], F32)
        MSQ = A([128, 512], F32)
        VAR = A([128, 512], F32)
        DD = [A([128, 512], F32) for _ in range(2)]
        P.dma('gpsimd', K('ldwd'), lambda e: e.dma_start(out=WDm[:], in_=win_d[l, :, :, 2176:2688]), writes=[K('WDm')])
        for c in range(2):
            op('vector', lambda e, c=c: e.memset(ZG[:, c, 0:30], 0.0), writes=[K('ZG0', c)])
            for j in range(31):
                op('vector', lambda e, c=c, j=j: e.tensor_scalar(
                    out=DIAG[:, c * 31 + j, :], in0=IDF, scalar1=vcol(l, V_CM + c * 31 + j), scalar2=None, op0=ALU.mult),
                   reads=['CF', 'VEC'], writes=[K('DIAG', c)])
        it = 0
        for tb in range(4):
            t0 = tb * 512
            for c in range(2):
                b = it % 2; it += 1
                p1, p2 = 0 + 2 * b, 1 + 2 * b
                proj_fm(p1, WDm, 0 + c * 128, tb, K('WDm'))
                proj_fm(p2, WDm, 256 + c * 128, tb, K('WDm'))
                op('scalar', lambda e, b=b, p2=p2: e.activation(out=SGT[b][:], in_=ps[p2][:], func=AF.Sigmoid),
                   reads=pk(p2), writes=[K('SGT', b)])
                op('vector', lambda e, b=b, p1=p1, c=c, t0=t0: e.tensor_tensor(
                    out=ZG[:, c, 30 + t0:30 + t0 + 512], in0=SGT[b][:], in1=ps[p1][:], op=ALU.mult),
                   reads=[K('SGT', b)] + pk(p1), writes=[K('ZG', c, tb)])
            for c in range(2):
                zr = [K('ZG', c, tb), K('ZG0', c)] + ([K('ZG', c, tb - 1)] if tb > 0 else [])
                pcv = 4 + c
                for j in range(31):
                    op('tensor', lambda e, c=c, j=j, t0=t0, pcv=pcv: e.matmul(
                        ps[pcv][:], lhsT=DIAG[:, c * 31 + j, :], rhs=ZG[:, c, t0 + j:t0 + j + 512],
                        start=(j == 0), stop=(j == 30)),
                       reads=zr + [K('DIAG', c)], writes=pk(pcv))
                op('scalar', lambda e, c=c, pcv=pcv: e.activation(out=ZD[c][:], in_=ps[pcv][:], func=AF.Identity,
                                                                  bias=vcol(l, V_CMB + c), scale=1.0),
                   reads=pk(pcv) + ['VEC'], writes=[K('ZD', c)])
                op('vector', lambda e, c=c: e.tensor_tensor(out=ZD2[c][:], in0=ZD[c][:], in1=ZD[c][:], op=ALU.mult),
                   reads=[K('ZD', c)], writes=[K('ZD2', c)])
            for c in range(2):
                op('tensor', lambda e, c=c: e.matmul(ps[6][:], lhsT=ONESF[:], rhs=ZD[c][:], start=(c == 0), stop=(c == 1)),
                   reads=[K('ZD', c), 'ONES'], writes=pk(6))
            for c in range(2):
                op('tensor', lambda e, c=c: e.matmul(ps[0][:], lhsT=ONESF[:], rhs=ZD2[c][:], start=(c == 0), stop=(c == 1)),
                   reads=[K('ZD2', c), 'ONES'], writes=pk(0))
            op('scalar', lambda e: e.activation(out=MEAN[:], in_=ps[6][:], func=AF.Copy, scale=1.0 / G),
               reads=pk(6), writes=[K('MEAN')])
            op('vector', lambda e: e.tensor_tensor(out=MSQ[:], in0=MEAN[:], in1=MEAN[:], op=ALU.mult),
               reads=[K('MEAN')], writes=[K('MSQ')])
            op('vector', lambda e: e.scalar_tensor_tensor(out=VAR[:], in0=ps[0][:], scalar=1.0 / G, in1=MSQ[:],
                                                          op0=ALU.mult, op1=ALU.subtract),
               reads=pk(0) + [K('MSQ')], writes=[K('VAR')])
            op('scalar', lambda e: e.activation(out=LNV[:], in_=VAR[:], func=AF.Ln, bias=EPS[:, 1:2], scale=1.0),
               reads=[K('VAR'), 'EPS'], writes=[K('LNV')])
            op('scalar', lambda e: e.activation(out=RS[:], in_=LNV[:], func=AF.Exp, scale=-0.5),
               reads=[K('LNV')], writes=[K('RS')])
            for c in range(2):
                op('vector', lambda e, c=c: e.tensor_tensor(out=DD[c][:], in0=ZD[c][:], in1=MEAN[:], op=ALU.subtract),
                   reads=[K('ZD', c), K('MEAN')], writes=[K('DD', c)])
                op('vector', lambda e, c=c: e.tensor_tensor(out=DD[c][:], in0=DD[c][:], in1=RS[:], op=ALU.mult),
                   reads=[K('DD', c), K('RS')], writes=[K('DD', c)])
                op('scalar', lambda e, c=c, t0=t0: e.activation(out=YT[:, 6 + c, t0:t0 + 512], in_=DD[c][:], func=AF.Silu,
                                                                bias=vcol(l, V_CMLB + c), scale=vcol(l, V_CMLW + c)),
                   reads=[K('DD', c), 'VEC'], writes=[K('YT', 6 + c, tb)])

        A.release(grp_mark); P.barrier()
        WB = A([128, 8, 512], BF16)
        BCB = A([128, 512], F32)
        WMTF = A([128, 4, 128], F32)
        WMT = A([128, 4, 128], BF16)
        SGB = A([128, 2, 128], F32)
        ST6 = A([128, 6], F32)
        MV = A([128, 2], F32)
        RV = A([128, 2], F32)
        VN = [A([128, 256], F32) for _ in range(2)]
        VNB = [A([128, 256], BF16) for _ in range(2)]
        SB_ = [A([128, 128], F32) for _ in range(2)]
        P.dma('gpsimd', K('ldwb'), lambda e: e.dma_start(out=WB[:], in_=win_d[l, :, :, 768:1280]), writes=[K('WB')])
        P.dma('sync', K('ldb'), lambda e: e.dma_start(out=BCB[:], in_=bc_d[l, :, 256:768]), writes=[K('BCB')])
        P.dma('sync', K('ldb'), lambda e: e.dma_start(out=WMTF[:], in_=wmt_d[l]), writes=[K('WMTF')])
        P.dma('sync', K('ldb'), lambda e: e.dma_start(out=SGB[:], in_=sgb_d[l]), writes=[K('SGB')])
        for h in range(4):
            op('vector', lambda e, h=h: e.tensor_tensor(out=WMT[:, h, :], in0=WMTF[:, h, :], in1=TRI_IF, op=ALU.mult),
               reads=[K('WMTF'), 'CF'], writes=[K('WMT')])
        it = 0
        for tb in range(4):
            for c in range(2):
                proj_fm(4 + c, WB, c * 128, tb, K('WB'))
            for ti in range(4):
                tile_i = tb * 4 + ti; tt0 = tile_i * 128
                b = it % 2; it += 1
                pv = 0 + b; pss = 2 + b
                for k in range(8):
                    op('tensor', lambda e, k=k, pv=pv, tt0=tt0: e.matmul(
                        ps[pv][:, 0:256], lhsT=HM[:, k, 1 + tt0:1 + tt0 + 128], rhs=WB[:, k, 256:512],
                        start=(k == 0), stop=(k == 7)),
                       reads=[K('WB')] + HMr(k, tb), writes=pk(pv, 0, 256))
                op('vector', lambda e, pv=pv: e.bn_stats(out=ST6[:], in_=ps[pv][:, 0:256]),
                   reads=pk(pv, 0, 256), writes=[K('ST6')])
                op('vector', lambda e: e.bn_aggr(out=MV[:], in_=ST6[:]), reads=[K('ST6')], writes=[K('MV')])
                op('scalar', lambda e: e.activation(out=RV[:, 0:1], in_=MV[:, 1:2], func=AF.Ln, bias=EPS[:, 1:2], scale=1.0),
                   reads=[K('MV'), 'EPS'], writes=[K('RV0')])
                op('scalar', lambda e: e.activation(out=RV[:, 1:2], in_=RV[:, 0:1], func=AF.Exp, scale=-0.5),
                   reads=[K('RV0')], writes=[K('RV')])
                op('vector', lambda e, pv=pv, b=b: e.tensor_scalar(
                    out=VN[b][:], in0=ps[pv][:, 0:256], scalar1=MV[:, 0:1], scalar2=RV[:, 1:2],
                    op0=ALU.subtract, op1=ALU.mult),
                   reads=pk(pv, 0, 256) + [K('MV'), K('RV')], writes=[K('VN', b)])
                op('vector', lambda e, b=b: e.tensor_tensor(out=VN[b][:], in0=VN[b][:], in1=BCB[:, 0:256], op=ALU.mult),
                   reads=[K('VN', b), K('BCB')], writes=[K('VN', b)])
                op('vector', lambda e, b=b: e.tensor_tensor(out=VNB[b][:], in0=VN[b][:], in1=BCB[:, 256:512], op=ALU.add),
                   reads=[K('VN', b), K('BCB')], writes=[K('VNB', b)])
                for c in range(2):
                    for hl in range(2):
                        h = 2 * c + hl
                        op('tensor', lambda e, c=c, hl=hl, h=h, b=b, pss=pss: e.matmul(
                            ps[pss][hl * 64:(hl + 1) * 64, c * 128:(c + 1) * 128], lhsT=VNB[b][:, h * 64:(h + 1) * 64],
                            rhs=WMT[:, h, :], start=True, stop=True),
                           reads=[K('VNB', b), K('WMT')], writes=pk(pss, c * 128, c * 128 + 128))
                    op('vector', lambda e, c=c, b=b, pss=pss: e.tensor_tensor(
                        out=SB_[b][:], in0=ps[pss][:, c * 128:(c + 1) * 128], in1=SGB[:, c, :], op=ALU.add),
                       reads=pk(pss, c * 128, c * 128 + 128) + [K('SGB')], writes=[K('SB', b)])
                    op('vector', lambda e, c=c, b=b, ti=ti, tt0=tt0: e.tensor_tensor(
                        out=YT[:, 2 + c, tt0:tt0 + 128], in0=SB_[b][:], in1=ps[4 + c][:, ti * 128:(ti + 1) * 128], op=ALU.mult),
                       reads=[K('SB', b)] + pk(4 + c), writes=[K('YT', 2 + c, tb)])

        A.release(grp_mark); P.barrier()
        if 'C' not in os.environ.get('KSKIP', ''):
            rwkv(l, K, HM, YT, HMr, HMrs, A)

        A.release(grp_mark); P.barrier()
        WO = A([128, 8, D], BF16)
        MY = A([128, 8, 512], BF16)
        TMP = [A([128, 512], F32) for _ in range(2)]
        LNV = A([128, 512], F32)
        RS = A([128, 512], F32)
        SQ = [A([128, 512], BF16) for _ in range(2)]
        P.dma('gpsimd', K('ldwo'), lambda e: e.dma_start(out=WO[:], in_=wout_d[l]), writes=[K('WO')])
        it = 0
        for tb in range(4):
            t0 = tb * 512
            for dc in range(8):
                po = it % 2; sb = it % 2; it += 1
                for k in range(8):
                    op('tensor', lambda e, po=po, k=k, dc=dc, t0=t0: e.matmul(
                        ps[po][:], lhsT=WO[:, k, dc * 128:(dc + 1) * 128], rhs=YT[:, k, t0:t0 + 512],
                        start=(k == 0), stop=(k == 7)),
                       reads=[K('WO'), K('YT', k, tb)], writes=pk(po))
                op('scalar', lambda e, po=po, sb=sb: e.activation(out=SQ[sb][:], in_=ps[po][:], func=AF.Square),
                   reads=pk(po), writes=[K('SQ', sb)])
                op('vector', lambda e, po=po, dc=dc: e.tensor_copy(out=MY[:, dc, :], in_=ps[po][:]),
                   reads=pk(po), writes=[K('MY', dc)])
                op('tensor', lambda e, sb=sb, dc=dc: e.matmul(ps[6][:], lhsT=ONESB[:], rhs=SQ[sb][:], start=(dc == 0), stop=(dc == 7)),
                   reads=[K('SQ', sb), 'ONES'], writes=pk(6))
            rstd_from(ps[6], 512, 1.0 / D, 0, LNV, RS, pk(6), K('RS'), K('LNV'))
            for c in range(8):
                b = c % 2
                op('vector', lambda e, c=c, b=b: e.tensor_tensor(out=TMP[b][:], in0=MY[:, c, :], in1=RS[:], op=ALU.mult),
                   reads=[K('MY', c), K('RS')], writes=[K('TMP', b)])
                op('vector', lambda e, c=c, b=b, t0=t0: e.scalar_tensor_tensor(
                    out=XT[:, c, t0:t0 + 512], in0=TMP[b][:], scalar=vcol(l, V_G + 24 + c),
                    in1=XT[:, c, t0:t0 + 512], op0=ALU.mult, op1=ALU.add),
                   reads=[K('TMP', b), ('XT', c, tb), 'VEC'], writes=[('XT', c, tb)])

    def rwkv(l, K, HM, YT, HMr, HMrs, A):
        NB = 256
        WC = A([128, 8, 896], BF16)
        LORAB = A([128, 768], BF16)
        W0B = A([128, 256], F32)
        P.dma('gpsimd', K('ldwc'), lambda e: e.dma_start(out=WC[:], in_=win_d[l, :, :, 1280:2176]), writes=[K('WC')])
        P.dma('gpsimd', K('ldl'), lambda e: e.dma_start(out=LORAB[:], in_=lora_d[l]), writes=[K('LORA')])
        P.dma('sync', K('ldc'), lambda e: e.dma_start(out=W0B[:], in_=bc_d[l, :, 0:256]), writes=[K('W0B')])
        PC = A([128, 6 * NB], F32)
        MISC = A([128, NB], BF16)
        KKb = A([128, 2 * NB], F32); KPb = A([128, 2 * NB], F32); Bb = A([128, 2 * NB], F32)
        Gb = A([128, 2 * NB], BF16); BON = A([128, 2 * NB], BF16); VTB = A([128, 2 * NB], BF16)
        TQ = [A([128, NB], F32) for _ in range(2)]
        TS = [A([128, NB], F32) for _ in range(2)]
        SQB = [A([128, NB], BF16) for _ in range(2)]
        XW = A([128, 256], F32); SIGTM = A([128, 256], F32)
        EP = A([128, 256], F32); EM = A([128, 256], F32); EE = A([128, 256], F32)
        FT_ = A([128, 8 * 128], BF16)
        FH_ = A([128, 4 * 128], BF16)
        TM_ = A([128, 8 * 128], BF16)
        NM_ = A([128, 4 * 512], BF16)
        LP = [A([128, 512], BF16) for _ in range(2)]
        NP_ = [A([128, 512], BF16) for _ in range(2)]
        XF = [A([128, 512], F32) for _ in range(2)]
        XB = A([128, 512], BF16)
        AHT = A([128, 256], BF16)
        UB = A([128, 256], BF16)
        S32 = A([128, 256], F32); SBF = A([128, 256], BF16)
        OT = A([128, 256], F32); OSQ = A([128, 256], F32)
        OM = A([128, 256], F32); OV = A([128, 256], F32); OL = A([128, 256], F32); ORS = A([128, 256], F32)
        BONF = CF[:, 512:640]
        vc = lambda off: vcol(l, off)
        cs = lambda c, a=0, b=NB: slice(c * NB + a, c * NB + b)
        hs = lambda h, a=0, b=128: slice(h * 128 + a, h * 128 + b)
        FT = lambda c, i, p0=0, p1=128: FT_[p0:p1, (c * 4 + i) * 128:(c * 4 + i + 1) * 128]
        FH = lambda c, i: FH_[:, (c * 2 + i) * 128:(c * 2 + i + 1) * 128]
        TM = lambda c, i, a=0, b=128: TM_[:, (c * 4 + i) * 128 + a:(c * 4 + i) * 128 + b]
        NM = lambda h, i, j: NM_[:, h * 512 + i * 256 + j * 128: h * 512 + i * 256 + (j + 1) * 128]
        op('vector', lambda e: e.memset(S32[:], 0.0), writes=[K('S32')])
        op('vector', lambda e: e.memset(SBF[:], 0.0), writes=[K('SBF')])
        allh = lambda nm, i: [K(nm, i, h) for h in range(4)]
        for sb in range(T // NB):
            t0 = sb * NB; tb = t0 // 512
            for cc in range(7):
                pb = cc % 2
                for k in range(8):
                    op('tensor', lambda e: e.matmul(ps[pb][:, 0:NB + 1], lhsT=WC[:, k, cc * 128:(cc + 1) * 128],
                                                    rhs=HM[:, k, t0:t0 + NB + 1], start=(k == 0), stop=(k == 7)),
                       reads=[K('WC')] + HMrs(k, tb), writes=pk(pb, 0, NB + 1))
                op('vector', lambda e: e.tensor_scalar(out=TS[pb][:], in0=ps[pb][:, 0:NB], scalar1=vc(V_MU + cc), scalar2=None,
                                                       op0=ALU.mult),
                   reads=pk(pb, 0, NB + 1) + ['VEC'], writes=[K('TS', pb)])
                dst = PC[:, cs(cc)] if cc < 6 else TQ[0][:]
                dk = K('PC', cc) if cc < 6 else K('TQ', 0)
                op('vector', lambda e: e.scalar_tensor_tensor(out=dst, in0=ps[pb][:, 1:NB + 1], scalar=OMMV[:, l * 7 + cc:l * 7 + cc + 1],
                                                              in1=TS[pb][:], op0=ALU.mult, op1=ALU.add),
                   reads=pk(pb, 0, NB + 1) + ['OMMV', K('TS', pb)], writes=[dk])
            op('scalar', lambda e: e.activation(out=MISC[0:32, :], in_=TQ[0][0:32, :], func=AF.Tanh), reads=[K('TQ', 0)], writes=[K('MISC', 0)])
            op('vector', lambda e: e.tensor_copy(out=MISC[32:64, :], in_=TQ[0][32:64, :]), reads=[K('TQ', 0)], writes=[K('MISC', 1)])
            op('scalar', lambda e: e.activation(out=MISC[64:128, :], in_=TQ[0][64:128, :], func=AF.Sigmoid),
               reads=[K('TQ', 0)], writes=[K('MISC', 2)])
            for c in range(2):
                op('tensor', lambda e: e.matmul(ps[2][:, 0:NB], lhsT=LORAB[32:64, 256 + c * 128:256 + (c + 1) * 128],
                                                rhs=MISC[32:64, :], start=True, stop=True),
                   reads=[K('LORA'), K('MISC', 1)], writes=pk(2, 0, NB))
                op('scalar', lambda e: e.activation(out=Bb[:, cs(c)], in_=ps[2][:, 0:NB], func=AF.Sigmoid, bias=vc(V_A0 + c), scale=1.0),
                   reads=pk(2, 0, NB) + ['VEC'], writes=[K('B', c)])
                op('tensor', lambda e: e.matmul(ps[3][:, 0:NB], lhsT=LORAB[64:128, 512 + c * 128:512 + (c + 1) * 128],
                                                rhs=MISC[64:128, :], start=True, stop=True),
                   reads=[K('LORA'), K('MISC', 2)], writes=pk(3, 0, NB))
                op('vector', lambda e: e.tensor_copy(out=Gb[:, cs(c)], in_=ps[3][:, 0:NB]), reads=pk(3, 0, NB), writes=[K('G', c)])
                op('vector', lambda e: e.tensor_scalar(out=KKb[:, cs(c)], in0=PC[:, cs(2 + c)], scalar1=vc(V_KK + c), scalar2=None, op0=ALU.mult),
                   reads=[K('PC', 2 + c), 'VEC'], writes=[K('KK', c)])
                op('scalar', lambda e: e.activation(out=SQB[c][:], in_=KKb[:, cs(c)], func=AF.Square), reads=[K('KK', c)], writes=[K('SQB', c)])
                op('tensor', lambda e: e.matmul(ps[4][:, 0:NB], lhsT=BONES, rhs=SQB[c][:], start=True, stop=True),
                   reads=[K('SQB', c), 'CB'], writes=pk(4, 0, NB))
                op('scalar', lambda e: e.activation(out=TQ[0][:], in_=ps[4][:, 0:NB], func=AF.Ln, bias=EPS[:, 3:4], scale=1.0),
                   reads=pk(4, 0, NB) + ['EPS'], writes=[K('TQ', 0)])
                op('scalar', lambda e: e.activation(out=TQ[1][:], in_=TQ[0][:], func=AF.Exp, scale=-0.5), reads=[K('TQ', 0)], writes=[K('TQ', 1)])
                op('vector', lambda e: e.tensor_tensor(out=KKb[:, cs(c)], in0=KKb[:, cs(c)], in1=TQ[1][:], op=ALU.mult),
                   reads=[K('KK', c), K('TQ', 1)], writes=[K('KK', c)])
                op('vector', lambda e: e.tensor_scalar(out=TQ[0][:], in0=Bb[:, cs(c)], scalar1=-1.0, scalar2=vc(V_KA + c), op0=ALU.add, op1=ALU.mult),
                   reads=[K('B', c), 'VEC'], writes=[K('TQ', 0)])
                op('vector', lambda e: e.scalar_tensor_tensor(out=KPb[:, cs(c)], in0=TQ[0][:], scalar=1.0, in1=PC[:, cs(2 + c)],
                                                              op0=ALU.add, op1=ALU.mult),
                   reads=[K('TQ', 0), K('PC', 2 + c)], writes=[K('KP', c)])
                op('vector', lambda e: e.tensor_tensor(out=Bb[:, cs(c)], in0=Bb[:, cs(c)], in1=KKb[:, cs(c)], op=ALU.mult),
                   reads=[K('B', c), K('KK', c)], writes=[K('B', c)])
                op('vector', lambda e: e.scalar_tensor_tensor(out=TQ[1][:], in0=PC[:, cs(c)], scalar=vc(V_RK + c), in1=KPb[:, cs(c)],
                                                              op0=ALU.mult, op1=ALU.mult),
                   reads=[K('PC', c), K('KP', c), 'VEC'], writes=[K('TQ', 1)])
                op('vector', lambda e: e.tensor_copy(out=SQB[c][:], in_=TQ[1][:]), reads=[K('TQ', 1)], writes=[K('SQB', c)])
                op('tensor', lambda e: e.matmul(ps[5][:, 0:NB], lhsT=BONES, rhs=SQB[c][:], start=True, stop=True),
                   reads=[K('SQB', c), 'CB'], writes=pk(5, 0, NB))
                op('scalar', lambda e: e.activation(out=BON[:, cs(c)], in_=ps[5][:, 0:NB], func=AF.Copy), reads=pk(5, 0, NB), writes=[K('BON', c)])
                op('vector', lambda e: e.tensor_copy(out=VTB[:, cs(c)], in_=PC[:, cs(4 + c)]), reads=[K('PC', 4 + c)], writes=[K('VTB', c)])
            for ti in range(NB // 128):
                q0 = ti * 128; q1 = q0 + 128; tt0 = t0 + q0
                op('tensor', lambda e: e.matmul(ps[2][:, 0:256], lhsT=MISC[0:32, q0:q1], rhs=LORAB[0:32, 0:256], start=True, stop=True),
                   reads=[K('MISC', 0), K('LORA')], writes=pk(2, 0, 256))
                op('vector', lambda e: e.tensor_tensor(out=XW[:], in0=ps[2][:, 0:256], in1=W0B[:], op=ALU.add),
                   reads=pk(2, 0, 256) + [K('W0B')], writes=[K('XW')])
                op('scalar', lambda e: e.activation(out=SIGTM[:], in_=XW[:], func=AF.Sigmoid), reads=[K('XW')], writes=[K('SIGTM')])
                for c in range(2):
                    op('tensor', lambda e: e.matmul(ps[3][:, c * 128:(c + 1) * 128], lhsT=SIGTM[:, c * 128:(c + 1) * 128], rhs=TRI_IF,
                                                    start=True, stop=True),
                       reads=[K('SIGTM'), 'CF'], writes=pk(3, c * 128, c * 128 + 128))
                    op('tensor', lambda e: e.matmul(ps[3][:, 256 + c * 128:256 + (c + 1) * 128], lhsT=SIGTM[:, c * 128:(c + 1) * 128],
                                                    rhs=TRI_SF, start=True, stop=True),
                       reads=[K('SIGTM'), 'CF'], writes=pk(3, 256 + c * 128, 256 + c * 128 + 128))
                op('scalar', lambda e: e.activation(out=EP[:], in_=ps[3][:, 0:256], func=AF.Exp, scale=-C_DECAY), reads=pk(3, 0, 256), writes=[K('EP')])
                op('scalar', lambda e: e.activation(out=EM[:], in_=ps[3][:, 0:256], func=AF.Exp, scale=C_DECAY), reads=pk(3, 0, 256), writes=[K('EM')])
                op('scalar', lambda e: e.activation(out=EE[:], in_=ps[3][:, 256:512], func=AF.Exp, scale=-C_DECAY), reads=pk(3, 256, 512), writes=[K('EE')])
                for c in range(2):
                    E_ = lambda X_: X_[:, c * 128:(c + 1) * 128]
                    gC = EP[:, c * 128 + 127:c * 128 + 128]
                    tq = slice(c * NB + q0, c * NB + q1)
                    op('vector', lambda e: e.scalar_tensor_tensor(out=FT(c, 0), in0=KKb[:, tq], scalar=-1.0, in1=E_(EE), op0=ALU.mult, op1=ALU.mult),
                       reads=[K('KK', c), K('EE')], writes=[K('FT', c)])
                    op('vector', lambda e: e.tensor_tensor(out=FT(c, 1), in0=Bb[:, tq], in1=E_(EM), op=ALU.mult),
                       reads=[K('B', c), K('EM')], writes=[K('FT', c)])
                    op('vector', lambda e: e.tensor_tensor(out=FT(c, 2), in0=KPb[:, tq], in1=E_(EM), op=ALU.mult),
                       reads=[K('KP', c), K('EM')], writes=[K('FT', c)])
                    op('vector', lambda e: e.tensor_tensor(out=FT(c, 3), in0=PC[:, tq], in1=E_(EP), op=ALU.mult),
                       reads=[K('PC', c), K('EP')], writes=[K('FT', c)])
                    op('vector', lambda e: e.scalar_tensor_tensor(out=FH(c, 0), in0=Bb[:, tq], scalar=gC, in1=E_(EM), op0=ALU.mult, op1=ALU.mult),
                       reads=[K('B', c), K('EM'), K('EP')], writes=[K('FH', c)])
                    op('vector', lambda e: e.scalar_tensor_tensor(out=FH(c, 1), in0=KPb[:, tq], scalar=gC, in1=E_(EM), op0=ALU.mult, op1=ALU.mult),
                       reads=[K('KP', c), K('EM'), K('EP')], writes=[K('FH', c)])
                    srcs = [(FT(c, 0), K('FT', c)), (VTB[:, tq], K('VTB', c)), (FH(c, 0), K('FH', c)), (FH(c, 1), K('FH', c))]
                    for i, (src, sk) in enumerate(srcs):
                        op('tensor', lambda e: e.transpose(out=psT[:, (c * 4 + i) * 128:(c * 4 + i + 1) * 128], in_=src, identity=IDB),
                           reads=[sk, 'CB'], writes=[('ps', 7)])
                    if c == 0:
                        op('scalar', lambda e: e.activation(out=TM_[:, 0:512], in_=psT[:, 0:512], func=AF.Copy), reads=[('ps', 7)], writes=[K('TM', 0)])
                    else:
                        op('vector', lambda e: e.tensor_copy(out=TM_[:, 512:1024], in_=psT[:, 512:1024]), reads=[('ps', 7)], writes=[K('TM', 1)])
                for h in range(4):
                    c = h // 2; hl = h % 2; r0 = hl * 64; r1 = r0 + 64
                    aT = FT(c, 0, r0, r1); bT = FT(c, 1, r0, r1); kT = FT(c, 2, r0, r1); rT = FT(c, 3, r0, r1)
                    pn = h % 2
                    for i, lt in enumerate((bT, kT)):
                        for j2, rt in enumerate((aT, rT)):
                            qq = i * 2 + j2
                            op('tensor', lambda e: e.matmul(ps[pn][:, qq * 128:(qq + 1) * 128], lhsT=lt, rhs=rt, start=True, stop=True),
                               reads=[K('FT', c)], writes=pk(pn, qq * 128, qq * 128 + 128))
                    for i in range(2):
                        op('vector', lambda e: e.tensor_tensor(out=NM(h, i, 0), in0=ps[pn][:, (i * 2) * 128:(i * 2 + 1) * 128],
                                                               in1=CB[:, 256:384], op=ALU.mult),
                           reads=pk(pn, i * 256, i * 256 + 128) + ['CB'], writes=[K('NM', h)])
                        op('vector', lambda e: e.tensor_tensor(out=NM(h, i, 1), in0=ps[pn][:, (i * 2 + 1) * 128:(i * 2 + 2) * 128],
                                                               in1=CB[:, 128:256], op=ALU.mult),
                           reads=pk(pn, i * 256 + 128, i * 256 + 256) + ['CB'], writes=[K('NM', h)])
                    op('tensor', lambda e: e.matmul(ps[2][:, hs(h)], lhsT=aT, rhs=bT, start=True, stop=True),
                       reads=[K('FT', c)], writes=pk(2, h * 128, h * 128 + 128))
                    op('vector', lambda e: e.tensor_tensor(out=LP[0][:, hs(h)], in0=ps[2][:, hs(h)], in1=TRILB, op=ALU.mult),
                       reads=pk(2, h * 128, h * 128 + 128) + ['CB'], writes=[K('LP', 0, h)])
                    op('gpsimd', lambda e: e.tensor_copy(out=NP_[0][:, hs(h)], in_=NM(h, 0, 0)), reads=[K('NM', h)], writes=[K('NP', 0, h)])
                    op('tensor', lambda e: e.matmul(ps[3][:, hs(h, 64, 128)], lhsT=NM(h, 1, 0), rhs=TM(c, 1, r0, r1), start=True, stop=True),
                       reads=[K('NM', h), K('TM', c)], writes=pk(3, h * 128, h * 128 + 128))
                    op('vector', lambda e: e.tensor_copy(out=XF[0][:, hs(h, 64, 128)], in_=ps[3][:, hs(h, 64, 128)]),
                       reads=pk(3, h * 128, h * 128 + 128), writes=[K('XF', 0, h)])
                    op('gpsimd', lambda e: e.tensor_copy(out=XF[0][:, hs(h, 0, 64)], in_=TM(c, 0, r0, r1)),
                       reads=[K('TM', c), K('XF', 0, h)], writes=[K('XF', 0, h)])
                cur = 0
                for lvl in range(7 if 'L' not in os.environ.get('KSKIP', '') else 0):
                    nxt = 1 - cur
                    op('vector', lambda e: e.tensor_copy(out=XB[:], in_=XF[cur][:]), reads=allh('XF', cur), writes=[K('XB')])
                    for h in range(4):
                        op('tensor', lambda e: e.matmul(ps[4][:, hs(h)], lhsT=NP_[cur][:, hs(h)], rhs=XB[:, hs(h)], start=True, stop=True),
                           reads=[K('NP', cur, h), K('XB')], writes=pk(4, h * 128, h * 128 + 128))
                    op('vector', lambda e: e.tensor_tensor(out=XF[nxt][:], in0=XF[cur][:], in1=ps[4][:], op=ALU.add),
                       reads=allh('XF', cur) + pk(4), writes=allh('XF', nxt))
                    if lvl < 6:
                        for h in range(4):
                            op('tensor', lambda e: e.matmul(ps[5][:, hs(h)], lhsT=LP[cur][:, hs(h)], rhs=NP_[cur][:, hs(h)], start=True, stop=True),
                               reads=[K('LP', cur, h), K('NP', cur, h)], writes=pk(5, h * 128, h * 128 + 128))
                        op('scalar', lambda e: e.activation(out=NP_[nxt][:], in_=ps[5][:], func=AF.Copy), reads=pk(5), writes=allh('NP', nxt))
                        if lvl < 5:
                            for h in range(4):
                                op('tensor', lambda e: e.matmul(ps[6][:, hs(h)], lhsT=NP_[cur][:, hs(h)], rhs=LP[cur][:, hs(h)],
                                                                start=True, stop=True),
                                   reads=[K('LP', cur, h), K('NP', cur, h)], writes=pk(6, h * 128, h * 128 + 128))
                            op('scalar', lambda e: e.activation(out=LP[nxt][:], in_=ps[6][:], func=AF.Copy), reads=pk(6), writes=allh('LP', nxt))
                    cur = nxt
                XFf = XF[cur]
                op('vector', lambda e: e.tensor_copy(out=XB[:], in_=XFf[:]), reads=allh('XF', cur), writes=[K('XB')])
                for c in range(2):
                    for hl in range(2):
                        h = 2 * c + hl
                        op('tensor', lambda e: e.transpose(out=psT[hl * 64:(hl + 1) * 64, c * 128:(c + 1) * 128], in_=XB[:, hs(h, 0, 64)], identity=IDB),
                           reads=[K('XB'), 'CB'], writes=[('ps', 7)])
                op('vector', lambda e: e.tensor_copy(out=AHT[:], in_=psT[:, 0:256]), reads=[('ps', 7)], writes=[K('AHT')])
                for c in range(2):
                    cb = slice(c * 128, (c + 1) * 128)
                    op('tensor', lambda e: e.matmul(ps[0][:, cb], lhsT=AHT[:, cb], rhs=SBF[:, cb], start=True, stop=True),
                       reads=[K('AHT'), K('SBF')], writes=pk(0, c * 128, c * 128 + 128))
                    for hl in range(2):
                        h = 2 * c + hl
                        op('vector', lambda e: e.tensor_tensor(out=UB[:, h * 64:(h + 1) * 64], in0=ps[0][:, c * 128 + hl * 64:c * 128 + hl * 64 + 64],
                                                               in1=XFf[:, hs(h, 64, 128)], op=ALU.add),
                           reads=pk(0, c * 128, c * 128 + 128) + [K('XF', cur, h)], writes=[K('UB', h)])
                    op('tensor', lambda e: e.matmul(ps[1][:, cb], lhsT=SBF[:, cb], rhs=FT(c, 3), start=True, stop=False, skip_group_check=True),
                       reads=[K('SBF'), K('FT', c)], writes=pk(1, c * 128, c * 128 + 128))
                    for hl in range(2):
                        h = 2 * c + hl; r0 = hl * 64; r1 = r0 + 64
                        op('tensor', lambda e: e.matmul(ps[1][r0:r1, cb], lhsT=UB[:, h * 64:(h + 1) * 64], rhs=NM(h, 0, 1),
                                                        start=False, stop=False, skip_group_check=True),
                           reads=[K('UB', h), K('NM', h)], writes=pk(1, c * 128, c * 128 + 128))
                        op('tensor', lambda e: e.matmul(ps[1][r0:r1, cb], lhsT=TM(c, 1, r0, r1), rhs=NM(h, 1, 1),
                                                        start=False, stop=True, skip_group_check=True),
                           reads=[K('TM', c), K('NM', h)], writes=pk(1, c * 128, c * 128 + 128))
                    for hl in range(2):
                        h = 2 * c + hl; r0 = hl * 64; r1 = r0 + 64
                        so = slice(256 + c * 128 + r0, 256 + c * 128 + r1)
                        sd = slice(c * 128 + r0, c * 128 + r1)
                        op('tensor', lambda e: e.matmul(ps[0][r0:r1, so], lhsT=TM(c, 2, r0, r1), rhs=UB[:, h * 64:(h + 1) * 64],
                                                        start=True, stop=False, skip_group_check=True),
                           reads=[K('TM', c), K('UB', h)], writes=pk(0, 256 + c * 128, 256 + c * 128 + 128))
                        op('tensor', lambda e: e.matmul(ps[0][r0:r1, so], lhsT=TM(c, 3, r0, r1), rhs=TM(c, 1, r0, r1),
                                                        start=False, stop=True, skip_group_check=True),
                           reads=[K('TM', c)], writes=pk(0, 256 + c * 128, 256 + c * 128 + 128))
                        op('vector', lambda e: e.scalar_tensor_tensor(out=S32[r0:r1, sd], in0=S32[r0:r1, sd], scalar=EP[r0:r1, c * 128 + 127:c * 128 + 128],
                                                                      in1=ps[0][r0:r1, so], op0=ALU.mult, op1=ALU.add),
                           reads=[K('S32'), K('EP')] + pk(0, 256 + c * 128, 256 + c * 128 + 128), writes=[K('S32')])
                        op('vector', lambda e: e.tensor_copy(out=SBF[r0:r1, sd], in_=S32[r0:r1, sd]), reads=[K('S32')], writes=[K('SBF')])
                if 'E' in os.environ.get('KSKIP', ''):
                    continue
                op('scalar', lambda e: e.activation(out=OT[:], in_=ps[1][:, 0:256], func=AF.Copy), reads=pk(1, 0, 256), writes=[K('OT')])
                op('vector', lambda e: e.tensor_tensor(out=OSQ[:], in0=OT[:], in1=OT[:], op=ALU.mult), reads=[K('OT')], writes=[K('OSQ')])
                for c in range(2):
                    cb = slice(c * 128, (c + 1) * 128)
                    op('tensor', lambda e: e.matmul(ps[2][:, cb], lhsT=BONF, rhs=OT[:, cb], start=True, stop=True),
                       reads=[K('OT'), 'CF'], writes=pk(2, c * 128, c * 128 + 128))
                    op('tensor', lambda e: e.matmul(ps[2][:, 256 + c * 128:256 + (c + 1) * 128], lhsT=BONF, rhs=OSQ[:, cb], start=True, stop=True),
                       reads=[K('OSQ'), 'CF'], writes=pk(2, 256 + c * 128, 256 + c * 128 + 128))
                v2 = lambda X_: X_.rearrange("p (c x) -> p c x", c=2)
                BV = lambda X_: v2(X_[:])[:, :, q0:q1]
                op('scalar', lambda e: e.activation(out=OM[:], in_=ps[2][:, 0:256], func=AF.Copy, scale=1.0 / 64), reads=pk(2), writes=[K('OM')])
                op('vector', lambda e: e.tensor_tensor(out=OV[:], in0=OM[:], in1=OM[:], op=ALU.mult), reads=[K('OM')], writes=[K('OV')])
                op('vector', lambda e: e.scalar_tensor_tensor(out=OV[:], in0=ps[2][:, 256:512], scalar=1.0 / 64, in1=OV[:],
                                                              op0=ALU.mult, op1=ALU.subtract),
                   reads=pk(2) + [K('OV')], writes=[K('OV')])
                op('scalar', lambda e: e.activation(out=OL[:], in_=OV[:], func=AF.Ln, bias=EPS[:, 2:3], scale=1.0), reads=[K('OV'), 'EPS'], writes=[K('OL')])
                op('scalar', lambda e: e.activation(out=ORS[:], in_=OL[:], func=AF.Exp, scale=-0.5), reads=[K('OL')], writes=[K('ORS')])
                op('vector', lambda e: e.tensor_tensor(out=OM[:], in0=OT[:], in1=OM[:], op=ALU.subtract), reads=[K('OT'), K('OM')], writes=[K('OM')])
                op('vector', lambda e: e.tensor_tensor(out=OM[:], in0=OM[:], in1=ORS[:], op=ALU.mult), reads=[K('OM'), K('ORS')], writes=[K('OM')])
                for c in range(2):
                    cb = slice(c * 128, (c + 1) * 128)
                    op('vector', lambda e: e.tensor_scalar(out=OM[:, cb], in0=OM[:, cb], scalar1=vc(V_RLW + c), scalar2=vc(V_RLB + c),
                                                           op0=ALU.mult, op1=ALU.add),
                       reads=[K('OM'), 'VEC'], writes=[K('OM')])
                op('vector', lambda e: e.tensor_tensor(out=v2(OV[:]), in0=BV(BON), in1=BV(VTB), op=ALU.mult),
                   reads=[K('BON', 0), K('BON', 1), K('VTB', 0), K('VTB', 1)], writes=[K('OV')])
                op('vector', lambda e: e.tensor_tensor(out=OM[:], in0=OM[:], in1=OV[:], op=ALU.add), reads=[K('OM'), K('OV')], writes=[K('OM')])
                op('vector', lambda e: e.tensor_tensor(out=YT[:, 4:6, tt0:tt0 + 128], in0=v2(OM[:]), in1=BV(Gb), op=ALU.mult),
                   reads=[K('OM'), K('G', 0), K('G', 1)], writes=[K('YT', 4, tb), K('YT', 5, tb)])

    seq = []
    for l in range(n_layers):
        seq += [('ffn', l, 0), ('mix', l), ('ffn', l, 1)]
    for s_ in seq:
        if s_[0] == 'ffn':
            ffn(s_[1], s_[2])
        else:
            mixer(s_[1])
        if stop_after is not None and tuple(stop_after) == tuple(s_):
            break
    for c in range(8):
        P.dma('sync', 'st_o', lambda e, c=c: e.dma_start(out=out_d[:, c, :], in_=XT[:, c, :]),
              reads=[('XT', c, tb) for tb in range(4)], writes=[('OUT', c)])
    P.wait_all('sync', [('OUT', c) for c in range(8)])
    with nc.Block() as block:
        P.emit(block)
    stack.close()
    return nc


def host_prep(inp):
    f = lambda a: np.ascontiguousarray(a, dtype=np.float32)
    tri_incl = np.triu(np.ones((128, 128), np.float32))
    tri_strict = np.triu(np.ones((128, 128), np.float32), 1)
    bo = np.zeros((128, 128), np.float32); bo[:64, :64] = 1; bo[64:, 64:] = 1
    consts = np.concatenate([np.eye(128, dtype=np.float32), tri_incl, tri_strict, tri_strict.T.copy(), bo], axis=1)
    fm = lambda v: np.asarray(v).reshape(-1, 128).T
    vecs = np.zeros((128, NL * NV), np.float32)
    for l in range(NL):
        o = l * NV
        for i, nm in enumerate(['ffn1_pre_g', 'ffn1_post_g', 'mix_pre_g', 'mix_post_g', 'ffn2_pre_g', 'ffn2_post_g']):
            vecs[:, o + V_G + i * 8: o + V_G + i * 8 + 8] = fm(inp[nm][l])
        for c in range(2):
            for j in range(3):
                vecs[:, o + V_SC + c * 3 + j] = inp['sc_conv_w'][l, j, c * 128:(c + 1) * 128]
            for j in range(31):
                vecs[:, o + V_CM + c * 31 + j] = inp['cm_conv_w'][l, j, c * 128:(c + 1) * 128]
        for off, nm in [(V_CMB, 'cm_conv_b'), (V_CMLW, 'cm_ln_w'), (V_CMLB, 'cm_ln_b'), (V_A0, 'rk_a0'), (V_KK, 'rk_k_k'),
                        (V_KA, 'rk_k_a'), (V_RLW, 'rk_ln_w'), (V_RLB, 'rk_ln_b')]:
            vecs[:, o + off: o + off + 2] = fm(inp[nm][l])
        vecs[:, o + V_RK: o + V_RK + 2] = fm(inp['rk_r_k'][l].reshape(-1))
        vecs[:, o + V_MU: o + V_MU + 7] = fm(inp['rk_mu'][l])
    wgu = np.empty((NL, 2, NFC, 128, 2, 8, 128), np.float32)
    wd = np.empty((NL, 2, 8, 128, NFC, 128), np.float32)
    wsrc = {('ffn1', 'w_gate'): inp['ffn1_w_gate'], ('ffn1', 'w_up'): inp['ffn1_w_up'], ('ffn1', 'w_down'): inp['ffn1_w_down'],
            ('ffn2', 'w_gate'): inp['ffn2_w_gate'], ('ffn2', 'w_up'): inp['ffn2_w_up'], ('ffn2', 'w_down'): inp['ffn2_w_down']}
    for wi, pre in enumerate(['ffn1', 'ffn2']):
        for gi, nm in enumerate(['w_gate', 'w_up']):
            w = np.asarray(wsrc[(pre, nm)])
            wgu[:, wi, :, :, gi, :, :] = w.reshape(NL, 8, 128, NFC, 128).transpose(0, 3, 2, 1, 4)
        w = np.asarray(wsrc[(pre, 'w_down')])
        wd[:, wi] = w.reshape(NL, NFC, 128, 8, 128).transpose(0, 3, 2, 1, 4)
    win = f(np.asarray(inp['w_in']).reshape(NL, 8, 128, INC).transpose(0, 2, 1, 3))
    wout = f(np.asarray(inp['w_out']).reshape(NL, 8, 128, D).transpose(0, 2, 1, 3))
    bc = np.empty((NL, 128, 768), np.float32)
    lora = np.zeros((NL, 128, 768), np.float32)
    for l in range(NL):
        row = np.concatenate([inp['rk_w0'][l], inp['sg_ln_w'][l], inp['sg_ln_b'][l]])
        bc[l] = np.broadcast_to(row[None, :], (128, row.shape[0]))
        lora[l, 0:32, 0:256] = inp['rk_w_up'][l]
        lora[l, 32:64, 256:512] = inp['rk_a_up'][l]
        lora[l, 64:128, 512:768] = inp['rk_g_up'][l]
    wmt = f(np.asarray(inp['sg_w']).transpose(0, 3, 1, 2))
    sgb = np.asarray(inp['sg_b'])
    sgbT = f(np.repeat(sgb.reshape(NL, 2, 2, 1, 128), 64, axis=3).reshape(NL, 2, 128, 128).transpose(0, 2, 1, 3))
    shared = dict(consts=f(consts), vecs=f(vecs), wgu=wgu, wd=wd, win=win, wout=wout, bc=bc, lora=lora, wmt=wmt, sgbT=sgbT)
    x = np.asarray(inp['x'])
    maps = []
    for b in range(8):
        xt = f(x[b].T.reshape(8, 128, T).transpose(1, 0, 2))
        m = dict(shared); m['xT'] = xt
        maps.append(m)
    return maps


_NC = None


def kernel(**inputs):
    global _NC
    inp = {k: np.asarray(v) for k, v in inputs.items()}
    maps = host_prep(inp)
    if _NC is None:
        _NC = build()
    res = run_bass_kernel_spmd(_NC, maps, core_ids=list(range(8)))
    out = np.empty((8, T, D), np.float32)
    for b in range(8):
        o = np.asarray(res.results[b]["outT"])
        out[b] = o.transpose(1, 0, 2).reshape(D, T).T
    return out
```

```python
import contextlib
import numpy as np
import concourse.bass as bass
import concourse.mybir as mybir
from concourse.bass_utils import run_bass_kernel_spmd

F32 = mybir.dt.float32
BF16 = mybir.dt.bfloat16
AF = mybir.ActivationFunctionType
ALU = mybir.AluOpType

D = 1024; T = 2048; DFF = 2816; NFC = 22; G = 256; INC = 2688
NL = 2
SAME_ENGINE_SYNC = True
RELAX_SAME_ENGINE = True
import os
DBG = int(os.environ.get('KDBG', '99'))
DBG2 = int(os.environ.get('KDBG2', '0'))
C_DECAY = float(np.exp(-0.5))

V_G = 0
V_SC = 48
V_CM = 54
V_CMB = 116; V_CMLW = 118; V_CMLB = 120
V_A0 = 122; V_KK = 124; V_KA = 126; V_RK = 128; V_RLW = 130; V_RLB = 132
V_MU = 134
NV = 141


class _Rec:
    def __getattr__(self, name):
        def f(*a, **kw):
            self.call = (name, a, kw)
            return self
        return f


def _call(fn):
    r = _Rec()
    fn(r)
    return r.call


class Prog:
    ENG = ('sync', 'scalar', 'vector', 'gpsimd', 'tensor')

    def __init__(s, nc, stack):
        s.nc = nc; s.stack = stack
        s.streams = {e: [] for e in s.ENG}
        s.sem = {}; s.cnt = {}
        s.lastw = {}; s.readers = {}
        s.known = {e: {} for e in s.ENG}
        for e in s.ENG:
            s._mksem(e)

    def _mksem(s, key):
        s.sem[key] = s.stack.enter_context(s.nc.semaphore("s_" + str(key)))
        s.cnt[key] = 0

    def _deps(s, eng, reads, writes):
        need = {}

        def add(tok):
            if tok is None:
                return
            k, v = tok
            if k == 'tensor' and eng == 'tensor':
                return
            if k == eng and not SAME_ENGINE_SYNC:
                return
            if k not in s.ENG:
                v = s.cnt[k]
            if need.get(k, 0) < v:
                need[k] = v
        for r in reads:
            add(s.lastw.get(r))
            if isinstance(r, tuple) and r[0] == 'ps':
                for k, v in s.readers.get(r, {}).items():
                    if k != eng:
                        add((k, v))
        for w in writes:
            tok = s.lastw.get(w)
            if tok is not None and not (RELAX_SAME_ENGINE and tok[0] == eng):
                add(tok)
            for k, v in s.readers.get(w, {}).items():
                if not (RELAX_SAME_ENGINE and k == eng):
                    add((k, v))
        waits = []
        for k, v in need.items():
            if s.known[eng].get(k, 0) < v:
                s.known[eng][k] = v
                waits.append((k, v))
        return waits

    def _commit(s, tok, reads, writes):
        for r in reads:
            d = s.readers.setdefault(r, {})
            if d.get(tok[0], 0) < tok[1]:
                d[tok[0]] = tok[1]
        for w in writes:
            s.lastw[w] = tok
            s.readers[w] = {}

    def op(s, eng, fn, reads=(), writes=()):
        reads = list(reads); writes = list(writes)
        waits = s._deps(eng, reads, writes)
        s.cnt[eng] += 1
        tok = (eng, s.cnt[eng])
        s.streams[eng].append((waits, _call(fn), eng, 1))
        s._commit(tok, reads, writes)

    def dma(s, eng, semkey, fn, reads=(), writes=()):
        reads = list(reads); writes = list(writes)
        if semkey not in s.sem:
            s._mksem(semkey)
        waits = s._deps(eng, reads, writes)
        s.cnt[semkey] += 16
        tok = (semkey, s.cnt[semkey])
        s.streams[eng].append((waits, _call(fn), semkey, 16))
        s._commit(tok, reads, writes)

    def barrier(s):
        for eng in s.ENG:
            waits = []
            for k, v in s.cnt.items():
                if v > 0 and s.known[eng].get(k, 0) < v and not (k == eng and k == 'tensor'):
                    s.known[eng][k] = v
                    waits.append((k, v))
            if waits:
                s.streams[eng].append((waits, None, None, 0))

    def wait_all(s, eng, keys):
        need = {}
        for k in keys:
            tok = s.lastw.get(k)
            if tok is not None and need.get(tok[0], 0) < tok[1]:
                need[tok[0]] = tok[1]
        s.streams[eng].append((list(need.items()), None, None, 0))

    def emit(s, block):
        waited = {e: set() for e in s.ENG}
        for eng in s.ENG:
            for waits, fn, semkey, amt in s.streams[eng]:
                for k, v in waits:
                    if k in waited:
                        waited[k].add(v)
        rank = {}
        for e in s.ENG:
            rank[e] = {v: i + 1 for i, v in enumerate(sorted(waited[e]))}
        for eng in s.ENG:
            items = s.streams[eng]

            def body(e, items=items, eng=eng):
                idx = 0
                for waits, fn, semkey, amt in items:
                    for k, v in waits:
                        e.wait_ge(s.sem[k], rank[k][v] if k in rank else v)
                    if fn is not None:
                        name, a, kw = fn
                        ins = getattr(e, name)(*a, **kw)
                        if semkey == eng:
                            idx += 1
                            if idx in rank[eng]:
                                ins.then_inc(s.sem[semkey], 1)
                        else:
                            ins.then_inc(s.sem[semkey], amt)
            getattr(block, eng)(body)


class Alloc:
    def __init__(s, nc):
        s.nc = nc
        s.base = (nc.sbuf_base + 63) // 64 * 64
        s.top = nc.sbuf_top
        s.off = s.base
        s.n = 0

    def __call__(s, shape, dtype):
        sz = int(np.prod(shape[1:])) * (4 if dtype == F32 else 2)
        sz = (sz + 63) // 64 * 64
        assert s.off + sz <= s.top, ("SBUF overflow", s.off, sz, s.top)
        s.n += 1
        t = s.nc.alloc_sbuf_tensor_at("sb%d" % s.n, list(shape), dtype, offset=s.off)
        s.off += sz
        return t

    def mark(s):
        return s.off

    def release(s, m):
        s.off = m


def pk(b, lo=0, hi=512):
    return [('ps', b)]


def build(n_layers=NL, stop_after=None):
    nc = bass.Bass("TRN2", target_bir_lowering=False)
    dt = lambda name, shape, kind="ExternalInput": nc.dram_tensor(name, list(shape), F32, kind=kind).ap()
    xin = dt("xT", [128, 8, T])
    consts_d = dt("consts", [128, 5 * 128])
    vecs_d = dt("vecs", [128, NL * NV])
    wgu_d = dt("wgu", [NL, 2, NFC, 128, 2, 8, 128])
    wd_d = dt("wd", [NL, 2, 8, 128, NFC, 128])
    win_d = dt("win", [NL, 128, 8, INC])
    wout_d = dt("wout", [NL, 128, 8, D])
    bc_d = dt("bc", [NL, 128, 768])
    lora_d = dt("lora", [NL, 128, 768])
    wmt_d = dt("wmt", [NL, 128, 4, 128])
    sgb_d = dt("sgbT", [NL, 128, 2, 128])
    out_d = dt("outT", [128, 8, T], kind="ExternalOutput")

    stack = contextlib.ExitStack()
    P = Prog(nc, stack)
    A = Alloc(nc)
    op = P.op

    XT = A([128, 8, T], F32)
    CF = A([128, 5 * 128], F32)
    IDF = CF[:, 0:128]; TRI_IF = CF[:, 128:256]; TRI_SF = CF[:, 256:384]
    CB = A([128, 5 * 128], BF16)
    IDB = CB[:, 0:128]; MASK_SI = CB[:, 128:384]
    TRILB = CB[:, 384:512]; BONES = CB[:, 512:640]
    ONESB = A([128, 128], BF16)
    CBX = A([128, 1024], BF16)
    ONESF = A([128, 128], F32)
    VEC = A([128, NL * NV], F32)
    HALFG = A([128, NL * 16], F32)
    EPS = A([128, 4], F32)
    OMMV = A([128, NL * 7], F32)
    ps = [nc.alloc_psum_tensor("psb%d" % i, [128, 512], F32) for i in range(7)]
    psT = nc.alloc_psum_tensor("psT", [128, 1024], BF16)

    for c in range(8):
        P.dma('sync', 'ld_x', lambda e, c=c: e.dma_start(out=XT[:, c, :], in_=xin[:, c, :]),
              writes=[('XT', c, tb) for tb in range(4)])
    P.dma('sync', 'ld_c', lambda e: e.dma_start(out=CF[:], in_=consts_d[:, :]), writes=['CF'])
    P.dma('sync', 'ld_c', lambda e: e.dma_start(out=VEC[:], in_=vecs_d[:, :]), writes=['VEC'])
    op('vector', lambda e: e.tensor_copy(out=CB[:], in_=CF[:]), reads=['CF'], writes=['CB'])
    op('vector', lambda e: e.memset(ONESB[:], 1.0), writes=['ONES'])
    for q in range(4):
        src = CB[:, 256:384] if q % 2 == 0 else CB[:, 128:256]
        op('vector', lambda e: e.tensor_copy(out=CBX[:, q * 128:(q + 1) * 128], in_=src), reads=['CB'], writes=['CBX'])
        op('vector', lambda e: e.tensor_copy(out=CBX[:, 512 + q * 128:512 + (q + 1) * 128], in_=CB[:, 384:512]), reads=['CB'], writes=['CBX'])
    op('vector', lambda e: e.memset(ONESF[:], 1.0), writes=['ONES'])
    for i, v in enumerate([1e-6, 1e-5, 64e-5, 1e-24]):
        op('vector', lambda e, i=i, v=v: e.memset(EPS[:, i:i + 1], v), writes=['EPS'])
    for l in range(NL):
        for j, gi in enumerate([1, 5]):
            op('vector', lambda e, l=l, j=j, gi=gi: e.tensor_scalar(
                out=HALFG[:, l * 16 + j * 8: l * 16 + j * 8 + 8],
                in0=VEC[:, l * NV + V_G + gi * 8: l * NV + V_G + gi * 8 + 8],
                scalar1=0.5, scalar2=None, op0=ALU.mult), reads=['VEC'], writes=['HALFG'])
    for l in range(NL):
        op('vector', lambda e, l=l: e.tensor_scalar(out=OMMV[:, l * 7:l * 7 + 7], in0=VEC[:, l * NV + V_MU:l * NV + V_MU + 7],
                                                    scalar1=-1.0, scalar2=1.0, op0=ALU.mult, op1=ALU.add),
           reads=['VEC'], writes=['OMMV'])

    def vcol(l, off):
        return VEC[:, l * NV + off: l * NV + off + 1]

    def rstd_from(psb, n, scale, epsi, LNV, RS, rk, wk, lk):
        op('scalar', lambda e: e.activation(out=LNV[:, 0:n], in_=psb[:, 0:n], func=AF.Ln,
                                            bias=EPS[:, epsi:epsi + 1], scale=scale),
           reads=rk + ['EPS'], writes=[lk])
        op('scalar', lambda e: e.activation(out=RS[:, 0:n], in_=LNV[:, 0:n], func=AF.Exp, scale=-0.5),
           reads=[lk], writes=[wk])

    work_mark = A.mark()

    def ffn(l, which):
        A.release(work_mark); P.barrier()
        gpre = V_G + (0 if which == 0 else 4) * 8
        HY = A([128, 8, T], BF16)
        ACTB = A([128, NFC, 1024], BF16)
        WGU = [A([128, 2, 8, 128], BF16) for _ in range(2)]
        WD = [A([128, NFC, 128], BF16) for _ in range(2)]
        SQ = [A([128, 512], BF16) for _ in range(2)]
        LNV = A([128, 512], F32)
        RS = [A([128, 512], F32) for _ in range(2)]
        SG = [A([128, 512], F32) for _ in range(2)]
        TMP = [A([128, 512], F32) for _ in range(2)]
        tag = 'f%d%d' % (l, which)
        K = lambda name, *idx: (tag, name) + idx
        it = 0
        if DBG <= 0:
            return
        for gtb in range(4):
            t0 = gtb * 512; rb = gtb % 2
            for c in range(8):
                b = c % 2
                op('scalar', lambda e, c=c, b=b, t0=t0: e.activation(out=SQ[b][:], in_=XT[:, c, t0:t0 + 512], func=AF.Square),
                   reads=[('XT', c, gtb)], writes=[K('SQ', b)])
                op('tensor', lambda e, c=c, b=b: e.matmul(ps[6][:], lhsT=ONESB[:], rhs=SQ[b][:], start=(c == 0), stop=(c == 7)),
                   reads=[K('SQ', b), 'ONES'], writes=pk(6))
            rstd_from(ps[6], 512, 1.0 / D, 0, LNV, RS[rb], pk(6), K('RS', rb), K('LNV'))
            for c in range(8):
                op('vector', lambda e, c=c, t0=t0, rb=rb: e.scalar_tensor_tensor(
                    out=HY[:, c, t0:t0 + 512], in0=XT[:, c, t0:t0 + 512], scalar=vcol(l, gpre + c),
                    in1=RS[rb][:], op0=ALU.mult, op1=ALU.mult),
                   reads=[('XT', c, gtb), K('RS', rb), 'VEC'], writes=[K('HY', c, gtb)])
        for half in range(2):
            T0 = half * 1024
            if DBG <= 1:
                return
            for fc in range(NFC):
                wb = fc % 2
                P.dma('gpsimd', K('ldgu', wb), lambda e, fc=fc, wb=wb: (e.dma_start(out=WGU[wb][:], in_=wgu_d[l, which, fc]) if not os.environ.get('KHALFDMA') else e.dma_start(out=WGU[wb][:, 0:1], in_=wgu_d[l, which, fc, :, 0:1])),
                      writes=[K('WGU', wb)])
                for tb in range(2):
                    pg = (it % 2) * 2; pu = pg + 1; sb = it % 2; it += 1
                    for gi, pb in ((0, pg), (1, pu)):
                        for k in range(8):
                            op('tensor', lambda e, gi=gi, pb=pb, k=k, wb=wb, tb=tb: e.matmul(
                                ps[pb][:], lhsT=WGU[wb][:, gi, k, :], rhs=HY[:, k, T0 + tb * 512:T0 + (tb + 1) * 512],
                                start=(k == 0), stop=(k == 7)),
                               reads=[K('WGU', wb), K('HY', k, half * 2 + tb)], writes=pk(pb))
                    op('scalar', lambda e, pg=pg, sb=sb: e.activation(out=SG[sb][:], in_=ps[pg][:], func=AF.Silu),
                       reads=pk(pg), writes=[K('SG', sb)])
                    op('vector', lambda e, pu=pu, sb=sb, fc=fc, tb=tb: e.tensor_tensor(
                        out=ACTB[:, fc, tb * 512:(tb + 1) * 512], in0=SG[sb][:], in1=ps[pu][:], op=ALU.mult),
                       reads=[K('SG', sb)] + pk(pu), writes=[K('ACT', fc, tb)])
            if DBG <= 2:
                return
            if DBG2 == 10:
                continue
            for dc in range(8):
                wb = dc % 2
                P.dma('gpsimd' if DBG2 != 12 else 'scalar', K('ldd', wb), lambda e, dc=dc, wb=wb: e.dma_start(out=WD[wb][:] if DBG2 != 12 else WD[wb][:, 0:11, :].bitcast(F32), in_=wd_d[l, which, dc] if DBG2 != 12 else wd_d[l, which, dc, :, 0:11, 0:64]),
                      writes=[K('WD', wb)])
                for tb in range(2):
                    if DBG2 == 2 or (DBG2 == 3 and dc >= 1):
                        continue
                    po = (it % 2); sb = it % 2; it += 1
                    if DBG2 == 9:
                        po += 2
                    for fc in range(NFC if DBG2 != 8 else 8):
                        lhs_ = WD[wb][:, fc, :] if DBG2 not in (6, 15) else WGU[wb][:, 0, fc % 8, :]
                        rhs_ = ACTB[:, fc, tb * 512:(tb + 1) * 512] if DBG2 not in (5, 15) else HY[:, fc % 8, T0 + tb * 512:T0 + (tb + 1) * 512]
                        op('tensor', lambda e, po=po, fc=fc, wb=wb, tb=tb: e.matmul(
                            ps[po][:], lhsT=lhs_, rhs=rhs_,
                            start=(fc == 0), stop=(fc == (NFC if DBG2 != 8 else 8) - 1)),
                           reads=([K('WD', wb)] if DBG2 != 13 else []) + ([K('ACT', fc, tb)] if DBG2 != 14 else []), writes=pk(po))
                    if DBG2 == 4:
                        continue
                    op('scalar', lambda e, po=po, sb=sb: e.activation(out=SQ[sb][:], in_=ps[po][:], func=AF.Square),
                       reads=pk(po), writes=[K('SQ', sb)])
                    op('vector', lambda e, po=po, dc=dc, tb=tb: e.tensor_copy(out=HY[:, dc, T0 + tb * 512:T0 + (tb + 1) * 512], in_=ps[po][:]),
                       reads=pk(po) + [K('SQ', sb)], writes=[K('HY', dc, half * 2 + tb)])
                    if DBG2 != 1:
                        op('tensor', lambda e, sb=sb, tb=tb, dc=dc: e.matmul(ps[4 + tb][:], lhsT=ONESB[:], rhs=SQ[sb][:],
                                                                            start=(dc == 0), stop=(dc == 7)),
                           reads=[K('SQ', sb), 'ONES'], writes=pk(4 + tb))
            if DBG <= 3:
                return
            hg = l * 16 + which * 8
            for tb in range(2):
                t0 = T0 + tb * 512; gtb = half * 2 + tb
                rstd_from(ps[4 + tb], 512, 1.0 / D, 0, LNV, RS[tb], pk(4 + tb), K('RS', tb), K('LNV'))
                for c in range(8):
                    b = c % 2
                    op('vector', lambda e, c=c, b=b, tb=tb: e.tensor_tensor(
                        out=TMP[b][:], in0=HY[:, c, T0 + tb * 512:T0 + (tb + 1) * 512], in1=RS[tb][:], op=ALU.mult),
                       reads=[K('HY', c, half * 2 + tb), K('RS', tb)], writes=[K('TMP', b)])
                    op('vector', lambda e, c=c, b=b, t0=t0: e.scalar_tensor_tensor(
                        out=XT[:, c, t0:t0 + 512], in0=TMP[b][:], scalar=HALFG[:, hg + c: hg + c + 1],
                        in1=XT[:, c, t0:t0 + 512], op0=ALU.mult, op1=ALU.add),
                       reads=[K('TMP', b), ('XT', c, gtb), 'HALFG'], writes=[('XT', c, gtb)])

    def mixer(l):
        A.release(work_mark); P.barrier()
        tag = 'm%d' % l
        K = lambda name, *idx: (tag, name) + idx
        HM = A([128, 8, T + 1], BF16)
        YT = A([128, 8, T], BF16)
        grp_mark = A.mark()
        LNV = A([128, 512], F32)
        RS = A([128, 512], F32)
        SQ = [A([128, 512], BF16) for _ in range(2)]
        XTall = lambda c: [('XT', c, tb) for tb in range(4)]
        for c in range(8):
            op('vector', lambda e, c=c: e.memset(HM[:, c, 0:1], 0.0), writes=[K('HM0', c)])
        for tb in range(4):
            t0 = tb * 512
            for c in range(8):
                b = c % 2
                op('scalar', lambda e, c=c, b=b, t0=t0: e.activation(out=SQ[b][:], in_=XT[:, c, t0:t0 + 512], func=AF.Square),
                   reads=[('XT', c, tb)], writes=[K('SQ', b)])
                op('tensor', lambda e, c=c, b=b: e.matmul(ps[6][:], lhsT=ONESB[:], rhs=SQ[b][:], start=(c == 0), stop=(c == 7)),
                   reads=[K('SQ', b), 'ONES'], writes=pk(6))
            rstd_from(ps[6], 512, 1.0 / D, 0, LNV, RS, pk(6), K('RS'), K('LNV'))
            for c in range(8):
                op('vector', lambda e, c=c, t0=t0: e.scalar_tensor_tensor(
                    out=HM[:, c, 1 + t0:1 + t0 + 512], in0=XT[:, c, t0:t0 + 512], scalar=vcol(l, V_G + 16 + c),
                    in1=RS[:], op0=ALU.mult, op1=ALU.mult),
                   reads=[('XT', c, tb), K('RS'), 'VEC'], writes=[K('HM', c, tb)])
        HMr = lambda k, tb: [K('HM', k, tb)]
        HMrs = lambda k, tb: [K('HM', k, tb), K('HM0', k)] + ([K('HM', k, tb - 1)] if tb > 0 else [])

        def proj_fm(pb, W, col0, tb, wkey, ncol=128, W2=None, prow=None):
            t0 = tb * 512
            outap = ps[pb][:] if prow is None else ps[pb][prow[0]:prow[1], :]
            n = 8 if W2 is None else 16
            for k in range(8):
                op('tensor', lambda e, k=k: e.matmul(outap, lhsT=W[:, k, col0:col0 + ncol], rhs=HM[:, k, 1 + t0:1 + t0 + 512],
                                                    start=(k == 0), stop=(k == n - 1)),
                   reads=[wkey] + HMr(k, tb), writes=pk(pb))
            if W2 is not None:
                for k in range(8):
                    op('tensor', lambda e, k=k: e.matmul(outap, lhsT=W2[:, k, col0:col0 + ncol], rhs=HM[:, k, t0:t0 + 512],
                                                        start=False, stop=(k == 7)),
                       reads=[wkey] + HMrs(k, tb), writes=pk(pb))

        A.release(grp_mark); P.barrier()
        WA = A([128, 8, 768], BF16)
        Z = A([128, 2, T + 2], F32)
        TA = [A([128, 512], F32) for _ in range(2)]
        ACC = [A([128, 512], F32) for _ in range(2)]
        P.dma('gpsimd', K('ldwa'), lambda e: e.dma_start(out=WA[:], in_=win_d[l, :, :, 0:768]), writes=[K('WA')])
        for c in range(2):
            op('vector', lambda e, c=c: e.memset(Z[:, c, 0:2], 0.0), writes=[K('Z0', c)])
        it = 0
        for tb in range(4):
            t0 = tb * 512
            for c in range(2):
                b = it % 2; it += 1
                pc_, px_, pb_ = 0 + 3 * b, 1 + 3 * b, 2 + 3 * b
                proj_fm(pc_, WA, 256 + c * 128, tb, K('WA'))
                proj_fm(px_, WA, 512 + c * 128, tb, K('WA'))
                proj_fm(pb_, WA, 0 + c * 128, tb, K('WA'))
                op('scalar', lambda e, b=b, pc_=pc_: e.activation(out=TA[b][:], in_=ps[pc_][:], func=AF.Copy),
                   reads=pk(pc_), writes=[K('TA', b)])
                op('vector', lambda e, b=b, px_=px_, c=c, t0=t0: e.tensor_tensor(
                    out=Z[:, c, 2 + t0:2 + t0 + 512], in0=TA[b][:], in1=ps[px_][:], op=ALU.mult),
                   reads=[K('TA', b)] + pk(px_), writes=[K('Z', c, tb)])
                zr = [K('Z', c, tb), K('Z0', c)] + ([K('Z', c, tb - 1)] if tb > 0 else [])
                op('vector', lambda e, b=b, c=c, t0=t0: e.tensor_scalar(
                    out=ACC[b][:], in0=Z[:, c, t0:t0 + 512], scalar1=vcol(l, V_SC + c * 3 + 0), scalar2=None, op0=ALU.mult),
                   reads=zr + ['VEC'], writes=[K('ACC', b)])
                for j in (1, 2):
                    op('vector', lambda e, b=b, c=c, t0=t0, j=j: e.scalar_tensor_tensor(
                        out=ACC[b][:], in0=Z[:, c, t0 + j:t0 + j + 512], scalar=vcol(l, V_SC + c * 3 + j),
                        in1=ACC[b][:], op0=ALU.mult, op1=ALU.add),
                       reads=zr + ['VEC', K('ACC', b)], writes=[K('ACC', b)])
                op('vector', lambda e, b=b, c=c, t0=t0, pb_=pb_: e.tensor_tensor(
                    out=YT[:, 0 + c, t0:t0 + 512], in0=ACC[b][:], in1=ps[pb_][:], op=ALU.mult),
                   reads=[K('ACC', b)] + pk(pb_), writes=[K('YT', 0 + c, tb)])

        A.release(grp_mark); P.barrier()
        WDm = A([128, 8, 512], BF16)
        ZG = A([128, 2, T + 30], BF16)
        DIAG = A([128, 62, 128], BF16)
        SGT = [A([128, 512], F32) for _ in range(2)]
        LNV = A([128, 512], F32)
        RS = A([128, 512], F32)
        ZD = [A([128, 512], F32) for _ in range(2)]
        ZD2 = [A([128, 512], F32) for _ in range(2)]
        MEAN = A([128, 512], F32)
        MSQ = A([128, 512], F32)
        VAR = A([128, 512], F32)
        DD = [A([128, 512], F32) for _ in range(2)]
        P.dma('gpsimd', K('ldwd'), lambda e: e.dma_start(out=WDm[:], in_=win_d[l, :, :, 2176:2688]), writes=[K('WDm')])
        for c in range(2):
            op('vector', lambda e, c=c: e.memset(ZG[:, c, 0:30], 0.0), writes=[K('ZG0', c)])
            for j in range(31):
                op('vector', lambda e, c=c, j=j: e.tensor_scalar(
                    out=DIAG[:, c * 31 + j, :], in0=IDF, scalar1=vcol(l, V_CM + c * 31 + j), scalar2=None, op0=ALU.mult),
                   reads=['CF', 'VEC'], writes=[K('DIAG', c)])
        it = 0
        for tb in range(4):
            t0 = tb * 512
            for c in range(2):
                b = it % 2; it += 1
                p1, p2 = 0 + 2 * b, 1 + 2 * b
                proj_fm(p1, WDm, 0 + c * 128, tb, K('WDm'))
                proj_fm(p2, WDm, 256 + c * 128, tb, K('WDm'))
                op('scalar', lambda e, b=b, p2=p2: e.activation(out=SGT[b][:], in_=ps[p2][:], func=AF.Sigmoid),
                   reads=pk(p2), writes=[K('SGT', b)])
                op('vector', lambda e, b=b, p1=p1, c=c, t0=t0: e.tensor_tensor(
                    out=ZG[:, c, 30 + t0:30 + t0 + 512], in0=SGT[b][:], in1=ps[p1][:], op=ALU.mult),
                   reads=[K('SGT', b)] + pk(p1), writes=[K('ZG', c, tb)])
            for c in range(2):
                zr = [K('ZG', c, tb), K('ZG0', c)] + ([K('ZG', c, tb - 1)] if tb > 0 else [])
                pcv = 4 + c
                for j in range(31):
                    op('tensor', lambda e, c=c, j=j, t0=t0, pcv=pcv: e.matmul(
                        ps[pcv][:], lhsT=DIAG[:, c * 31 + j, :], rhs=ZG[:, c, t0 + j:t0 + j + 512],
                        start=(j == 0), stop=(j == 30)),
                       reads=zr + [K('DIAG', c)], writes=pk(pcv))
                op('scalar', lambda e, c=c, pcv=pcv: e.activation(out=ZD[c][:], in_=ps[pcv][:], func=AF.Identity,
                                                                  bias=vcol(l, V_CMB + c), scale=1.0),
                   reads=pk(pcv) + ['VEC'], writes=[K('ZD', c)])
                op('vector', lambda e, c=c: e.tensor_tensor(out=ZD2[c][:], in0=ZD[c][:], in1=ZD[c][:], op=ALU.mult),
                   reads=[K('ZD', c)], writes=[K('ZD2', c)])
            for c in range(2):
                op('tensor', lambda e, c=c: e.matmul(ps[6][:], lhsT=ONESF[:], rhs=ZD[c][:], start=(c == 0), stop=(c == 1)),
                   reads=[K('ZD', c), 'ONES'], writes=pk(6))
            for c in range(2):
                op('tensor', lambda e, c=c: e.matmul(ps[0][:], lhsT=ONESF[:], rhs=ZD2[c][:], start=(c == 0), stop=(c == 1)),
                   reads=[K('ZD2', c), 'ONES'], writes=pk(0))
            op('scalar', lambda e: e.activation(out=MEAN[:], in_=ps[6][:], func=AF.Copy, scale=1.0 / G),
               reads=pk(6), writes=[K('MEAN')])
            op('vector', lambda e: e.tensor_tensor(out=MSQ[:], in0=MEAN[:], in1=MEAN[:], op=ALU.mult),
               reads=[K('MEAN')], writes=[K('MSQ')])
            op('vector', lambda e: e.scalar_tensor_tensor(out=VAR[:], in0=ps[0][:], scalar=1.0 / G, in1=MSQ[:],
                                                          op0=ALU.mult, op1=ALU.subtract),
               reads=pk(0) + [K('MSQ')], writes=[K('VAR')])
            op('scalar', lambda e: e.activation(out=LNV[:], in_=VAR[:], func=AF.Ln, bias=EPS[:, 1:2], scale=1.0),
               reads=[K('VAR'), 'EPS'], writes=[K('LNV')])
            op('scalar', lambda e: e.activation(out=RS[:], in_=LNV[:], func=AF.Exp, scale=-0.5),
               reads=[K('LNV')], writes=[K('RS')])
            for c in range(2):
                op('vector', lambda e, c=c: e.tensor_tensor(out=DD[c][:], in0=ZD[c][:], in1=MEAN[:], op=ALU.subtract),
                   reads=[K('ZD', c), K('MEAN')], writes=[K('DD', c)])
                op('vector', lambda e, c=c: e.tensor_tensor(out=DD[c][:], in0=DD[c][:], in1=RS[:], op=ALU.mult),
                   reads=[K('DD', c), K('RS')], writes=[K('DD', c)])
                op('scalar', lambda e, c=c, t0=t0: e.activation(out=YT[:, 6 + c, t0:t0 + 512], in_=DD[c][:], func=AF.Silu,
                                                                bias=vcol(l, V_CMLB + c), scale=vcol(l, V_CMLW + c)),
                   reads=[K('DD', c), 'VEC'], writes=[K('YT', 6 + c, tb)])

        A.release(grp_mark); P.barrier()
        WB = A([128, 8, 512], BF16)
        BCB = A([128, 512], F32)
        WMTF = A([128, 4, 128], F32)
        WMT = A([128, 4, 128], BF16)
        SGB = A([128, 2, 128], F32)
        ST6 = A([128, 6], F32)
        MV = A([128, 2], F32)
        RV = A([128, 2], F32)
        VN = [A([128, 256], F32) for _ in range(2)]
        VNB = [A([128, 256], BF16) for _ in range(2)]
        SB_ = [A([128, 128], F32) for _ in range(2)]
        P.dma('gpsimd', K('ldwb'), lambda e: e.dma_start(out=WB[:], in_=win_d[l, :, :, 768:1280]), writes=[K('WB')])
        P.dma('sync', K('ldb'), lambda e: e.dma_start(out=BCB[:], in_=bc_d[l, :, 256:768]), writes=[K('BCB')])
        P.dma('sync', K('ldb'), lambda e: e.dma_start(out=WMTF[:], in_=wmt_d[l]), writes=[K('WMTF')])
        P.dma('sync', K('ldb'), lambda e: e.dma_start(out=SGB[:], in_=sgb_d[l]), writes=[K('SGB')])
        for h in range(4):
            op('vector', lambda e, h=h: e.tensor_tensor(out=WMT[:, h, :], in0=WMTF[:, h, :], in1=TRI_IF, op=ALU.mult),
               reads=[K('WMTF'), 'CF'], writes=[K('WMT')])
        it = 0
        for tb in range(4):
            for c in range(2):
                proj_fm(4 + c, WB, c * 128, tb, K('WB'))
            for ti in range(4):
                tile_i = tb * 4 + ti; tt0 = tile_i * 128
                b = it % 2; it += 1
                pv = 0 + b; pss = 2 + b
                for k in range(8):
                    op('tensor', lambda e, k=k, pv=pv, tt0=tt0: e.matmul(
                        ps[pv][:, 0:256], lhsT=HM[:, k, 1 + tt0:1 + tt0 + 128], rhs=WB[:, k, 256:512],
                        start=(k == 0), stop=(k == 7)),
                       reads=[K('WB')] + HMr(k, tb), writes=pk(pv, 0, 256))
                op('vector', lambda e, pv=pv: e.bn_stats(out=ST6[:], in_=ps[pv][:, 0:256]),
                   reads=pk(pv, 0, 256), writes=[K('ST6')])
                op('vector', lambda e: e.bn_aggr(out=MV[:], in_=ST6[:]), reads=[K('ST6')], writes=[K('MV')])
                op('scalar', lambda e: e.activation(out=RV[:, 0:1], in_=MV[:, 1:2], func=AF.Ln, bias=EPS[:, 1:2], scale=1.0),
                   reads=[K('MV'), 'EPS'], writes=[K('RV0')])
                op('scalar', lambda e: e.activation(out=RV[:, 1:2], in_=RV[:, 0:1], func=AF.Exp, scale=-0.5),
                   reads=[K('RV0')], writes=[K('RV')])
                op('vector', lambda e, pv=pv, b=b: e.tensor_scalar(
                    out=VN[b][:], in0=ps[pv][:, 0:256], scalar1=MV[:, 0:1], scalar2=RV[:, 1:2],
                    op0=ALU.subtract, op1=ALU.mult),
                   reads=pk(pv, 0, 256) + [K('MV'), K('RV')], writes=[K('VN', b)])
                op('vector', lambda e, b=b: e.tensor_tensor(out=VN[b][:], in0=VN[b][:], in1=BCB[:, 0:256], op=ALU.mult),
                   reads=[K('VN', b), K('BCB')], writes=[K('VN', b)])
                op('vector', lambda e, b=b: e.tensor_tensor(out=VNB[b][:], in0=VN[b][:], in1=BCB[:, 256:512], op=ALU.add),
                   reads=[K('VN', b), K('BCB')], writes=[K('VNB', b)])
                for c in range(2):
                    for hl in range(2):
                        h = 2 * c + hl
                        op('tensor', lambda e, c=c, hl=hl, h=h, b=b, pss=pss: e.matmul(
                            ps[pss][hl * 64:(hl + 1) * 64, c * 128:(c + 1) * 128], lhsT=VNB[b][:, h * 64:(h + 1) * 64],
                            rhs=WMT[:, h, :], start=True, stop=True),
                           reads=[K('VNB', b), K('WMT')], writes=pk(pss, c * 128, c * 128 + 128))
                    op('vector', lambda e, c=c, b=b, pss=pss: e.tensor_tensor(
                        out=SB_[b][:], in0=ps[pss][:, c * 128:(c + 1) * 128], in1=SGB[:, c, :], op=ALU.add),
                       reads=pk(pss, c * 128, c * 128 + 128) + [K('SGB')], writes=[K('SB', b)])
                    op('vector', lambda e, c=c, b=b, ti=ti, tt0=tt0: e.tensor_tensor(
                        out=YT[:, 2 + c, tt0:tt0 + 128], in0=SB_[b][:], in1=ps[4 + c][:, ti * 128:(ti + 1) * 128], op=ALU.mult),
                       reads=[K('SB', b)] + pk(4 + c), writes=[K('YT', 2 + c, tb)])

        A.release(grp_mark); P.barrier()
        if 'C' not in os.environ.get('KSKIP', ''):
            rwkv(l, K, HM, YT, HMr, HMrs, A)

        A.release(grp_mark); P.barrier()
        WO = A([128, 8, D], BF16)
        MY = A([128, 8, 512], BF16)
        TMP = [A([128, 512], F32) for _ in range(2)]
        LNV = A([128, 512], F32)
        RS = A([128, 512], F32)
        SQ = [A([128, 512], BF16) for _ in range(2)]
        P.dma('gpsimd', K('ldwo'), lambda e: e.dma_start(out=WO[:], in_=wout_d[l]), writes=[K('WO')])
        it = 0
        for tb in range(4):
            t0 = tb * 512
            for dc in range(8):
                po = it % 2; sb = it % 2; it += 1
                for k in range(8):
                    op('tensor', lambda e, po=po, k=k, dc=dc, t0=t0: e.matmul(
                        ps[po][:], lhsT=WO[:, k, dc * 128:(dc + 1) * 128], rhs=YT[:, k, t0:t0 + 512],
                        start=(k == 0), stop=(k == 7)),
                       reads=[K('WO'), K('YT', k, tb)], writes=pk(po))
                op('scalar', lambda e, po=po, sb=sb: e.activation(out=SQ[sb][:], in_=ps[po][:], func=AF.Square),
                   reads=pk(po), writes=[K('SQ', sb)])
                op('vector', lambda e, po=po, dc=dc: e.tensor_copy(out=MY[:, dc, :], in_=ps[po][:]),
                   reads=pk(po), writes=[K('MY', dc)])
                op('tensor', lambda e, sb=sb, dc=dc: e.matmul(ps[6][:], lhsT=ONESB[:], rhs=SQ[sb][:], start=(dc == 0), stop=(dc == 7)),
                   reads=[K('SQ', sb), 'ONES'], writes=pk(6))
            rstd_from(ps[6], 512, 1.0 / D, 0, LNV, RS, pk(6), K('RS'), K('LNV'))
            for c in range(8):
                b = c % 2
                op('vector', lambda e, c=c, b=b: e.tensor_tensor(out=TMP[b][:], in0=MY[:, c, :], in1=RS[:], op=ALU.mult),
                   reads=[K('MY', c), K('RS')], writes=[K('TMP', b)])
                op('vector', lambda e, c=c, b=b, t0=t0: e.scalar_tensor_tensor(
                    out=XT[:, c, t0:t0 + 512], in0=TMP[b][:], scalar=vcol(l, V_G + 24 + c),
                    in1=XT[:, c, t0:t0 + 512], op0=ALU.mult, op1=ALU.add),
                   reads=[K('TMP', b), ('XT', c, tb), 'VEC'], writes=[('XT', c, tb)])

    def rwkv(l, K, HM, YT, HMr, HMrs, A):
        NB = 256
        WC = A([128, 8, 896], BF16)
        LORAB = A([128, 768], BF16)
        W0B = A([128, 256], F32)
        P.dma('gpsimd', K('ldwc'), lambda e: e.dma_start(out=WC[:], in_=win_d[l, :, :, 1280:2176]), writes=[K('WC')])
        P.dma('gpsimd', K('ldl'), lambda e: e.dma_start(out=LORAB[:], in_=lora_d[l]), writes=[K('LORA')])
        P.dma('sync', K('ldc'), lambda e: e.dma_start(out=W0B[:], in_=bc_d[l, :, 0:256]), writes=[K('W0B')])
        PC = A([128, 6 * NB], F32)
        MISC = A([128, NB], BF16)
        KKb = A([128, 2 * NB], F32); KPb = A([128, 2 * NB], F32); Bb = A([128, 2 * NB], F32)
        Gb = A([128, 2 * NB], BF16); BON = A([128, 2 * NB], BF16); VTB = A([128, 2 * NB], BF16)
        TQ = [A([128, NB], F32) for _ in range(2)]
        TS = [A([128, NB], F32) for _ in range(2)]
        SQB = [A([128, NB], BF16) for _ in range(2)]
        XW = A([128, 256], F32); SIGTM = A([128, 256], F32)
        EP = A([128, 256], F32); EM = A([128, 256], F32); EE = A([128, 256], F32)
        FT_ = A([128, 8 * 128], BF16)
        FH_ = A([128, 4 * 128], BF16)
        TM_ = A([128, 8 * 128], BF16)
        NM_ = A([128, 4 * 512], BF16)
        LP = [A([128, 512], BF16) for _ in range(2)]
        NP_ = [A([128, 512], BF16) for _ in range(2)]
        XF = [A([128, 512], BF16) for _ in range(2)]
        AHT = A([128, 256], BF16)
        UB = A([128, 256], BF16)
        S32 = A([128, 256], F32); SBF = A([128, 256], BF16)
        OT = A([128, 256], F32); OSQ = A([128, 256], F32)
        OM = A([128, 256], F32); OV = A([128, 256], F32); OL = A([128, 256], F32); ORS = A([128, 256], F32)
        BONF = CF[:, 512:640]
        vc = lambda off: vcol(l, off)
        cs = lambda c, a=0, b=NB: slice(c * NB + a, c * NB + b)
        hs = lambda h, a=0, b=128: slice(h * 128 + a, h * 128 + b)
        FT = lambda c, i, p0=0, p1=128: FT_[p0:p1, (c * 4 + i) * 128:(c * 4 + i + 1) * 128]
        FH = lambda c, i: FH_[:, (c * 2 + i) * 128:(c * 2 + i + 1) * 128]
        TM = lambda c, i, a=0, b=128: TM_[:, (c * 4 + i) * 128 + a:(c * 4 + i) * 128 + b]
        NM = lambda h, i, j: NM_[:, h * 512 + i * 256 + j * 128: h * 512 + i * 256 + (j + 1) * 128]
        op('vector', lambda e: e.memset(S32[:], 0.0), writes=[K('S32')])
        op('vector', lambda e: e.memset(SBF[:], 0.0), writes=[K('SBF')])
        allh = lambda nm, i: [K(nm, i, h) for h in range(4)]
        for sb in range(T // NB):
            t0 = sb * NB; tb = t0 // 512
            for cc in range(7):
                pb = cc % 2
                for k in range(8):
                    op('tensor', lambda e: e.matmul(ps[pb][:, 0:NB + 1], lhsT=WC[:, k, cc * 128:(cc + 1) * 128],
                                                    rhs=HM[:, k, t0:t0 + NB + 1], start=(k == 0), stop=(k == 7)),
                       reads=[K('WC')] + HMrs(k, tb), writes=pk(pb, 0, NB + 1))
                op('vector', lambda e: e.tensor_scalar(out=TS[pb][:], in0=ps[pb][:, 0:NB], scalar1=vc(V_MU + cc), scalar2=None,
                                                       op0=ALU.mult),
                   reads=pk(pb, 0, NB + 1) + ['VEC'], writes=[K('TS', pb)])
                dst = PC[:, cs(cc)] if cc < 6 else TQ[0][:]
                dk = K('PC', cc) if cc < 6 else K('TQ', 0)
                op('vector', lambda e: e.scalar_tensor_tensor(out=dst, in0=ps[pb][:, 1:NB + 1], scalar=OMMV[:, l * 7 + cc:l * 7 + cc + 1],
                                                              in1=TS[pb][:], op0=ALU.mult, op1=ALU.add),
                   reads=pk(pb, 0, NB + 1) + ['OMMV', K('TS', pb)], writes=[dk])
            op('scalar', lambda e: e.activation(out=MISC[0:32, :], in_=TQ[0][0:32, :], func=AF.Tanh), reads=[K('TQ', 0)], writes=[K('MISC', 0)])
            op('vector', lambda e: e.tensor_copy(out=MISC[32:64, :], in_=TQ[0][32:64, :]), reads=[K('TQ', 0)], writes=[K('MISC', 1)])
            op('scalar', lambda e: e.activation(out=MISC[64:128, :], in_=TQ[0][64:128, :], func=AF.Sigmoid),
               reads=[K('TQ', 0)], writes=[K('MISC', 2)])
            for c in range(2):
                op('tensor', lambda e: e.matmul(ps[2][:, 0:NB], lhsT=LORAB[32:64, 256 + c * 128:256 + (c + 1) * 128],
                                                rhs=MISC[32:64, :], start=True, stop=True),
                   reads=[K('LORA'), K('MISC', 1)], writes=pk(2, 0, NB))
                op('scalar', lambda e: e.activation(out=Bb[:, cs(c)], in_=ps[2][:, 0:NB], func=AF.Sigmoid, bias=vc(V_A0 + c), scale=1.0),
                   reads=pk(2, 0, NB) + ['VEC'], writes=[K('B', c)])
                op('tensor', lambda e: e.matmul(ps[3][:, 0:NB], lhsT=LORAB[64:128, 512 + c * 128:512 + (c + 1) * 128],
                                                rhs=MISC[64:128, :], start=True, stop=True),
                   reads=[K('LORA'), K('MISC', 2)], writes=pk(3, 0, NB))
                op('vector', lambda e: e.tensor_copy(out=Gb[:, cs(c)], in_=ps[3][:, 0:NB]), reads=pk(3, 0, NB), writes=[K('G', c)])
                op('vector', lambda e: e.tensor_scalar(out=KKb[:, cs(c)], in0=PC[:, cs(2 + c)], scalar1=vc(V_KK + c), scalar2=None, op0=ALU.mult),
                   reads=[K('PC', 2 + c), 'VEC'], writes=[K('KK', c)])
                op('scalar', lambda e: e.activation(out=SQB[c][:], in_=KKb[:, cs(c)], func=AF.Square), reads=[K('KK', c)], writes=[K('SQB', c)])
                op('tensor', lambda e: e.matmul(ps[4][:, 0:NB], lhsT=BONES, rhs=SQB[c][:], start=True, stop=True),
                   reads=[K('SQB', c), 'CB'], writes=pk(4, 0, NB))
                op('scalar', lambda e: e.activation(out=TQ[0][:], in_=ps[4][:, 0:NB], func=AF.Ln, bias=EPS[:, 3:4], scale=1.0),
                   reads=pk(4, 0, NB) + ['EPS'], writes=[K('TQ', 0)])
                op('scalar', lambda e: e.activation(out=TQ[1][:], in_=TQ[0][:], func=AF.Exp, scale=-0.5), reads=[K('TQ', 0)], writes=[K('TQ', 1)])
                op('vector', lambda e: e.tensor_tensor(out=KKb[:, cs(c)], in0=KKb[:, cs(c)], in1=TQ[1][:], op=ALU.mult),
                   reads=[K('KK', c), K('TQ', 1)], writes=[K('KK', c)])
                op('vector', lambda e: e.tensor_scalar(out=TQ[0][:], in0=Bb[:, cs(c)], scalar1=-1.0, scalar2=vc(V_KA + c), op0=ALU.add, op1=ALU.mult),
                   reads=[K('B', c), 'VEC'], writes=[K('TQ', 0)])
                op('vector', lambda e: e.scalar_tensor_tensor(out=KPb[:, cs(c)], in0=TQ[0][:], scalar=1.0, in1=PC[:, cs(2 + c)],
                                                              op0=ALU.add, op1=ALU.mult),
                   reads=[K('TQ', 0), K('PC', 2 + c)], writes=[K('KP', c)])
                op('vector', lambda e: e.tensor_tensor(out=Bb[:, cs(c)], in0=Bb[:, cs(c)], in1=KKb[:, cs(c)], op=ALU.mult),
                   reads=[K('B', c), K('KK', c)], writes=[K('B', c)])
                op('vector', lambda e: e.scalar_tensor_tensor(out=TQ[1][:], in0=PC[:, cs(c)], scalar=vc(V_RK + c), in1=KPb[:, cs(c)],
                                                              op0=ALU.mult, op1=ALU.mult),
                   reads=[K('PC', c), K('KP', c), 'VEC'], writes=[K('TQ', 1)])
                op('vector', lambda e: e.tensor_copy(out=SQB[c][:], in_=TQ[1][:]), reads=[K('TQ', 1)], writes=[K('SQB', c)])
                op('tensor', lambda e: e.matmul(ps[5][:, 0:NB], lhsT=BONES, rhs=SQB[c][:], start=True, stop=True),
                   reads=[K('SQB', c), 'CB'], writes=pk(5, 0, NB))
                op('scalar', lambda e: e.activation(out=BON[:, cs(c)], in_=ps[5][:, 0:NB], func=AF.Copy), reads=pk(5, 0, NB), writes=[K('BON', c)])
                op('vector', lambda e: e.tensor_copy(out=VTB[:, cs(c)], in_=PC[:, cs(4 + c)]), reads=[K('PC', 4 + c)], writes=[K('VTB', c)])
            for ti in range(NB // 128):
                q0 = ti * 128; q1 = q0 + 128; tt0 = t0 + q0
                op('tensor', lambda e: e.matmul(ps[2][:, 0:256], lhsT=MISC[0:32, q0:q1], rhs=LORAB[0:32, 0:256], start=True, stop=True),
                   reads=[K('MISC', 0), K('LORA')], writes=pk(2, 0, 256))
                op('vector', lambda e: e.tensor_tensor(out=XW[:], in0=ps[2][:, 0:256], in1=W0B[:], op=ALU.add),
                   reads=pk(2, 0, 256) + [K('W0B')], writes=[K('XW')])
                op('scalar', lambda e: e.activation(out=SIGTM[:], in_=XW[:], func=AF.Sigmoid), reads=[K('XW')], writes=[K('SIGTM')])
                for c in range(2):
                    op('tensor', lambda e: e.matmul(ps[3][:, c * 128:(c + 1) * 128], lhsT=SIGTM[:, c * 128:(c + 1) * 128], rhs=TRI_IF,
                                                    start=True, stop=True),
                       reads=[K('SIGTM'), 'CF'], writes=pk(3, c * 128, c * 128 + 128))
                    op('tensor', lambda e: e.matmul(ps[3][:, 256 + c * 128:256 + (c + 1) * 128], lhsT=SIGTM[:, c * 128:(c + 1) * 128],
                                                    rhs=TRI_SF, start=True, stop=True),
                       reads=[K('SIGTM'), 'CF'], writes=pk(3, 256 + c * 128, 256 + c * 128 + 128))
                op('scalar', lambda e: e.activation(out=EP[:], in_=ps[3][:, 0:256], func=AF.Exp, scale=-C_DECAY), reads=pk(3, 0, 256), writes=[K('EP')])
                op('scalar', lambda e: e.activation(out=EM[:], in_=ps[3][:, 0:256], func=AF.Exp, scale=C_DECAY), reads=pk(3, 0, 256), writes=[K('EM')])
                op('scalar', lambda e: e.activation(out=EE[:], in_=ps[3][:, 256:512], func=AF.Exp, scale=-C_DECAY), reads=pk(3, 256, 512), writes=[K('EE')])
                for c in range(2):
                    E_ = lambda X_: X_[:, c * 128:(c + 1) * 128]
                    gC = EP[:, c * 128 + 127:c * 128 + 128]
                    tq = slice(c * NB + q0, c * NB + q1)
                    op('vector', lambda e: e.scalar_tensor_tensor(out=FT(c, 0), in0=KKb[:, tq], scalar=-1.0, in1=E_(EE), op0=ALU.mult, op1=ALU.mult),
                       reads=[K('KK', c), K('EE')], writes=[K('FT', c)])
                    op('vector', lambda e: e.tensor_tensor(out=FT(c, 1), in0=Bb[:, tq], in1=E_(EM), op=ALU.mult),
                       reads=[K('B', c), K('EM')], writes=[K('FT', c)])
                    op('vector', lambda e: e.tensor_tensor(out=FT(c, 2), in0=KPb[:, tq], in1=E_(EM), op=ALU.mult),
                       reads=[K('KP', c), K('EM')], writes=[K('FT', c)])
                    op('vector', lambda e: e.tensor_tensor(out=FT(c, 3), in0=PC[:, tq], in1=E_(EP), op=ALU.mult),
                       reads=[K('PC', c), K('EP')], writes=[K('FT', c)])
                    op('vector', lambda e: e.scalar_tensor_tensor(out=FH(c, 0), in0=Bb[:, tq], scalar=gC, in1=E_(EM), op0=ALU.mult, op1=ALU.mult),
                       reads=[K('B', c), K('EM'), K('EP')], writes=[K('FH', c)])
                    op('vector', lambda e: e.scalar_tensor_tensor(out=FH(c, 1), in0=KPb[:, tq], scalar=gC, in1=E_(EM), op0=ALU.mult, op1=ALU.mult),
                       reads=[K('KP', c), K('EM'), K('EP')], writes=[K('FH', c)])
                    srcs = [(FT(c, 0), K('FT', c)), (VTB[:, tq], K('VTB', c)), (FH(c, 0), K('FH', c)), (FH(c, 1), K('FH', c))]
                    for i, (src, sk) in enumerate(srcs):
                        op('tensor', lambda e: e.transpose(out=psT[:, (c * 4 + i) * 128:(c * 4 + i + 1) * 128], in_=src, identity=IDB),
                           reads=[sk, 'CB'], writes=[('ps', 7)])
                    if c == 0:
                        op('scalar', lambda e: e.activation(out=TM_[:, 0:512], in_=psT[:, 0:512], func=AF.Copy), reads=[('ps', 7)], writes=[K('TM', 0)])
                    else:
                        op('vector', lambda e: e.tensor_copy(out=TM_[:, 512:1024], in_=psT[:, 512:1024]), reads=[('ps', 7)], writes=[K('TM', 1)])
                for h in range(4):
                    c = h // 2; hl = h % 2; r0 = hl * 64; r1 = r0 + 64
                    aT = FT(c, 0, r0, r1); bT = FT(c, 1, r0, r1); kT = FT(c, 2, r0, r1); rT = FT(c, 3, r0, r1)
                    pn = h % 2
                    for i, lt in enumerate((bT, kT)):
                        for j2, rt in enumerate((aT, rT)):
                            qq = i * 2 + j2
                            op('tensor', lambda e: e.matmul(ps[pn][:, qq * 128:(qq + 1) * 128], lhsT=lt, rhs=rt, start=True, stop=True),
                               reads=[K('FT', c)], writes=pk(pn, qq * 128, qq * 128 + 128))
                    for i in range(2):
                        op('vector', lambda e: e.tensor_tensor(out=(NM(h, i, 0) if i == 1 else NP_[0][:, hs(h)]),
                                                               in0=ps[pn][:, (i * 2) * 128:(i * 2 + 1) * 128],
                                                               in1=CB[:, 256:384], op=ALU.mult),
                           reads=pk(pn, i * 256, i * 256 + 128) + ['CB'], writes=[K('NM', h) if i == 1 else K('NP', 0, h)])
                        op('vector', lambda e: e.tensor_tensor(out=NM(h, i, 1), in0=ps[pn][:, (i * 2 + 1) * 128:(i * 2 + 2) * 128],
                                                               in1=CB[:, 128:256], op=ALU.mult),
                           reads=pk(pn, i * 256 + 128, i * 256 + 256) + ['CB'], writes=[K('NM', h)])
                    op('tensor', lambda e: e.matmul(ps[2][:, hs(h)], lhsT=aT, rhs=bT, start=True, stop=True),
                       reads=[K('FT', c)], writes=pk(2, h * 128, h * 128 + 128))
                    op('vector', lambda e: e.tensor_tensor(out=LP[0][:, hs(h)], in0=ps[2][:, hs(h)], in1=TRILB, op=ALU.mult),
                       reads=pk(2, h * 128, h * 128 + 128) + ['CB'], writes=[K('LP', 0, h)])
                    op('tensor', lambda e: e.matmul(ps[3][:, hs(h, 64, 128)], lhsT=NM(h, 1, 0), rhs=TM(c, 1, r0, r1), start=True, stop=True),
                       reads=[K('NM', h), K('TM', c)], writes=pk(3, h * 128, h * 128 + 128))
                    op('vector', lambda e: e.tensor_copy(out=XF[0][:, hs(h, 64, 128)], in_=ps[3][:, hs(h, 64, 128)]),
                       reads=pk(3, h * 128, h * 128 + 128), writes=[K('XF', 0, h)])
                    op('scalar', lambda e: e.activation(out=XF[0][:, hs(h, 0, 64)], in_=TM(c, 0, r0, r1), func=AF.Copy),
                       reads=[K('TM', c), K('XF', 0, h)], writes=[K('XF', 0, h)])
                cur = 0
                for lvl in range(7 if 'L' not in os.environ.get('KSKIP', '') else 0):
                    nxt = 1 - cur
                    for h in range(4):
                        op('tensor', lambda e: e.matmul(ps[4][:, hs(h)], lhsT=NP_[cur][:, hs(h)], rhs=XF[cur][:, hs(h)], start=True, stop=True),
                           reads=[K('NP', cur, h), K('XF', cur, h)], writes=pk(4, h * 128, h * 128 + 128))
                    op('vector', lambda e: e.tensor_tensor(out=XF[nxt][:], in0=XF[cur][:], in1=ps[4][:], op=ALU.add),
                       reads=allh('XF', cur) + pk(4), writes=allh('XF', nxt))
                    if lvl < 6:
                        for h in range(4):
                            op('tensor', lambda e: e.matmul(ps[5][:, hs(h)], lhsT=LP[cur][:, hs(h)], rhs=NP_[cur][:, hs(h)], start=True, stop=True),
                               reads=[K('LP', cur, h), K('NP', cur, h)], writes=pk(5, h * 128, h * 128 + 128))
                        op('scalar', lambda e: e.activation(out=NP_[nxt][:], in_=ps[5][:], func=AF.Copy), reads=pk(5), writes=allh('NP', nxt))
                        if lvl < 5:
                            for h in range(4):
                                op('tensor', lambda e: e.matmul(ps[6][:, hs(h)], lhsT=NP_[cur][:, hs(h)], rhs=LP[cur][:, hs(h)],
                                                                start=True, stop=True),
                                   reads=[K('LP', cur, h), K('NP', cur, h)], writes=pk(6, h * 128, h * 128 + 128))
                            op('scalar', lambda e: e.activation(out=LP[nxt][:], in_=ps[6][:], func=AF.Copy), reads=pk(6), writes=allh('LP', nxt))
                    cur = nxt
                XFf = XF[cur]
                for c in range(2):
                    for hl in range(2):
                        h = 2 * c + hl
                        op('tensor', lambda e: e.transpose(out=psT[hl * 64:(hl + 1) * 64, c * 128:(c + 1) * 128], in_=XFf[:, hs(h, 0, 64)], identity=IDB),
                           reads=[K('XF', cur, h), 'CB'], writes=[('ps', 7)])
                op('vector', lambda e: e.tensor_copy(out=AHT[:], in_=psT[:, 0:256]), reads=[('ps', 7)], writes=[K('AHT')])
                for c in range(2):
                    cb = slice(c * 128, (c + 1) * 128)
                    op('tensor', lambda e: e.matmul(ps[0][:, cb], lhsT=AHT[:, cb], rhs=SBF[:, cb], start=True, stop=True),
                       reads=[K('AHT'), K('SBF')], writes=pk(0, c * 128, c * 128 + 128))
                    for hl in range(2):
                        h = 2 * c + hl
                        op('vector', lambda e: e.tensor_tensor(out=UB[:, h * 64:(h + 1) * 64], in0=ps[0][:, c * 128 + hl * 64:c * 128 + hl * 64 + 64],
                                                               in1=XFf[:, hs(h, 64, 128)], op=ALU.add),
                           reads=pk(0, c * 128, c * 128 + 128) + [K('XF', cur, h)], writes=[K('UB', h)])
                    op('tensor', lambda e: e.matmul(ps[1][:, cb], lhsT=SBF[:, cb], rhs=FT(c, 3), start=True, stop=False, skip_group_check=True),
                       reads=[K('SBF'), K('FT', c)], writes=pk(1, c * 128, c * 128 + 128))
                    for hl in range(2):
                        h = 2 * c + hl; r0 = hl * 64; r1 = r0 + 64
                        op('tensor', lambda e: e.matmul(ps[1][r0:r1, cb], lhsT=UB[:, h * 64:(h + 1) * 64], rhs=NM(h, 0, 1),
                                                        start=False, stop=False, skip_group_check=True),
                           reads=[K('UB', h), K('NM', h)], writes=pk(1, c * 128, c * 128 + 128))
                        op('tensor', lambda e: e.matmul(ps[1][r0:r1, cb], lhsT=TM(c, 1, r0, r1), rhs=NM(h, 1, 1),
                                                        start=False, stop=True, skip_group_check=True),
                           reads=[K('TM', c), K('NM', h)], writes=pk(1, c * 128, c * 128 + 128))
                    for hl in range(2):
                        h = 2 * c + hl; r0 = hl * 64; r1 = r0 + 64
                        so = slice(256 + c * 128 + r0, 256 + c * 128 + r1)
                        sd = slice(c * 128 + r0, c * 128 + r1)
                        op('tensor', lambda e: e.matmul(ps[0][r0:r1, so], lhsT=TM(c, 2, r0, r1), rhs=UB[:, h * 64:(h + 1) * 64],
                                                        start=True, stop=False, skip_group_check=True),
                           reads=[K('TM', c), K('UB', h)], writes=pk(0, 256 + c * 128, 256 + c * 128 + 128))
                        op('tensor', lambda e: e.matmul(ps[0][r0:r1, so], lhsT=TM(c, 3, r0, r1), rhs=TM(c, 1, r0, r1),
                                                        start=False, stop=True, skip_group_check=True),
                           reads=[K('TM', c)], writes=pk(0, 256 + c * 128, 256 + c * 128 + 128))
                        op('vector', lambda e: e.scalar_tensor_tensor(out=S32[r0:r1, sd], in0=S32[r0:r1, sd], scalar=EP[r0:r1, c * 128 + 127:c * 128 + 128],
                                                                      in1=ps[0][r0:r1, so], op0=ALU.mult, op1=ALU.add),
                           reads=[K('S32'), K('EP')] + pk(0, 256 + c * 128, 256 + c * 128 + 128), writes=[K('S32')])
                        op('vector', lambda e: e.tensor_copy(out=SBF[r0:r1, sd], in_=S32[r0:r1, sd]), reads=[K('S32')], writes=[K('SBF')])
                if 'E' in os.environ.get('KSKIP', ''):
                    continue
                op('scalar', lambda e: e.activation(out=OT[:], in_=ps[1][:, 0:256], func=AF.Copy), reads=pk(1, 0, 256), writes=[K('OT')])
                op('vector', lambda e: e.tensor_tensor(out=OSQ[:], in0=OT[:], in1=OT[:], op=ALU.mult), reads=[K('OT')], writes=[K('OSQ')])
                for c in range(2):
                    cb = slice(c * 128, (c + 1) * 128)
                    op('tensor', lambda e: e.matmul(ps[2][:, cb], lhsT=BONF, rhs=OT[:, cb], start=True, stop=True),
                       reads=[K('OT'), 'CF'], writes=pk(2, c * 128, c * 128 + 128))
                    op('tensor', lambda e: e.matmul(ps[2][:, 256 + c * 128:256 + (c + 1) * 128], lhsT=BONF, rhs=OSQ[:, cb], start=True, stop=True),
                       reads=[K('OSQ'), 'CF'], writes=pk(2, 256 + c * 128, 256 + c * 128 + 128))
                v2 = lambda X_: X_.rearrange("p (c x) -> p c x", c=2)
                BV = lambda X_: v2(X_[:])[:, :, q0:q1]
                op('scalar', lambda e: e.activation(out=OM[:], in_=ps[2][:, 0:256], func=AF.Copy, scale=1.0 / 64), reads=pk(2), writes=[K('OM')])
                op('vector', lambda e: e.tensor_tensor(out=OV[:], in0=OM[:], in1=OM[:], op=ALU.mult), reads=[K('OM')], writes=[K('OV')])
                op('vector', lambda e: e.scalar_tensor_tensor(out=OV[:], in0=ps[2][:, 256:512], scalar=1.0 / 64, in1=OV[:],
                                                              op0=ALU.mult, op1=ALU.subtract),
                   reads=pk(2) + [K('OV')], writes=[K('OV')])
                op('scalar', lambda e: e.activation(out=OL[:], in_=OV[:], func=AF.Ln, bias=EPS[:, 2:3], scale=1.0), reads=[K('OV'), 'EPS'], writes=[K('OL')])
                op('scalar', lambda e: e.activation(out=ORS[:], in_=OL[:], func=AF.Exp, scale=-0.5), reads=[K('OL')], writes=[K('ORS')])
                op('vector', lambda e: e.tensor_tensor(out=OM[:], in0=OT[:], in1=OM[:], op=ALU.subtract), reads=[K('OT'), K('OM')], writes=[K('OM')])
                op('vector', lambda e: e.tensor_tensor(out=OM[:], in0=OM[:], in1=ORS[:], op=ALU.mult), reads=[K('OM'), K('ORS')], writes=[K('OM')])
                for c in range(2):
                    cb = slice(c * 128, (c + 1) * 128)
                    op('vector', lambda e: e.tensor_scalar(out=OM[:, cb], in0=OM[:, cb], scalar1=vc(V_RLW + c), scalar2=vc(V_RLB + c),
                                                           op0=ALU.mult, op1=ALU.add),
                       reads=[K('OM'), 'VEC'], writes=[K('OM')])
                op('vector', lambda e: e.tensor_tensor(out=v2(OV[:]), in0=BV(BON), in1=BV(VTB), op=ALU.mult),
                   reads=[K('BON', 0), K('BON', 1), K('VTB', 0), K('VTB', 1)], writes=[K('OV')])
                op('vector', lambda e: e.tensor_tensor(out=OM[:], in0=OM[:], in1=OV[:], op=ALU.add), reads=[K('OM'), K('OV')], writes=[K('OM')])
                op('vector', lambda e: e.tensor_tensor(out=YT[:, 4:6, tt0:tt0 + 128], in0=v2(OM[:]), in1=BV(Gb), op=ALU.mult),
                   reads=[K('OM'), K('G', 0), K('G', 1)], writes=[K('YT', 4, tb), K('YT', 5, tb)])

    seq = []
    for l in range(n_layers):
        seq += [('ffn', l, 0), ('mix', l), ('ffn', l, 1)]
    for s_ in seq:
        if s_[0] == 'ffn':
            ffn(s_[1], s_[2])
        else:
            mixer(s_[1])
        if stop_after is not None and tuple(stop_after) == tuple(s_):
            break
    for c in range(8):
        P.dma('sync', 'st_o', lambda e, c=c: e.dma_start(out=out_d[:, c, :], in_=XT[:, c, :]),
              reads=[('XT', c, tb) for tb in range(4)], writes=[('OUT', c)])
    P.wait_all('sync', [('OUT', c) for c in range(8)])
    with nc.Block() as block:
        P.emit(block)
    stack.close()
    return nc


def host_prep(inp):
    f = lambda a: np.ascontiguousarray(a, dtype=np.float32)
    tri_incl = np.triu(np.ones((128, 128), np.float32))
    tri_strict = np.triu(np.ones((128, 128), np.float32), 1)
    bo = np.zeros((128, 128), np.float32); bo[:64, :64] = 1; bo[64:, 64:] = 1
    consts = np.concatenate([np.eye(128, dtype=np.float32), tri_incl, tri_strict, tri_strict.T.copy(), bo], axis=1)
    fm = lambda v: np.asarray(v).reshape(-1, 128).T
    vecs = np.zeros((128, NL * NV), np.float32)
    for l in range(NL):
        o = l * NV
        for i, nm in enumerate(['ffn1_pre_g', 'ffn1_post_g', 'mix_pre_g', 'mix_post_g', 'ffn2_pre_g', 'ffn2_post_g']):
            vecs[:, o + V_G + i * 8: o + V_G + i * 8 + 8] = fm(inp[nm][l])
        for c in range(2):
            for j in range(3):
                vecs[:, o + V_SC + c * 3 + j] = inp['sc_conv_w'][l, j, c * 128:(c + 1) * 128]
            for j in range(31):
                vecs[:, o + V_CM + c * 31 + j] = inp['cm_conv_w'][l, j, c * 128:(c + 1) * 128]
        for off, nm in [(V_CMB, 'cm_conv_b'), (V_CMLW, 'cm_ln_w'), (V_CMLB, 'cm_ln_b'), (V_A0, 'rk_a0'), (V_KK, 'rk_k_k'),
                        (V_KA, 'rk_k_a'), (V_RLW, 'rk_ln_w'), (V_RLB, 'rk_ln_b')]:
            vecs[:, o + off: o + off + 2] = fm(inp[nm][l])
        vecs[:, o + V_RK: o + V_RK + 2] = fm(inp['rk_r_k'][l].reshape(-1))
        vecs[:, o + V_MU: o + V_MU + 7] = fm(inp['rk_mu'][l])
    wgu = np.empty((NL, 2, NFC, 128, 2, 8, 128), np.float32)
    wd = np.empty((NL, 2, 8, 128, NFC, 128), np.float32)
    wsrc = {('ffn1', 'w_gate'): inp['ffn1_w_gate'], ('ffn1', 'w_up'): inp['ffn1_w_up'], ('ffn1', 'w_down'): inp['ffn1_w_down'],
            ('ffn2', 'w_gate'): inp['ffn2_w_gate'], ('ffn2', 'w_up'): inp['ffn2_w_up'], ('ffn2', 'w_down'): inp['ffn2_w_down']}
    for wi, pre in enumerate(['ffn1', 'ffn2']):
        for gi, nm in enumerate(['w_gate', 'w_up']):
            w = np.asarray(wsrc[(pre, nm)])
            wgu[:, wi, :, :, gi, :, :] = w.reshape(NL, 8, 128, NFC, 128).transpose(0, 3, 2, 1, 4)
        w = np.asarray(wsrc[(pre, 'w_down')])
        wd[:, wi] = w.reshape(NL, NFC, 128, 8, 128).transpose(0, 3, 2, 1, 4)
    win = f(np.asarray(inp['w_in']).reshape(NL, 8, 128, INC).transpose(0, 2, 1, 3))
    wout = f(np.asarray(inp['w_out']).reshape(NL, 8, 128, D).transpose(0, 2, 1, 3))
    bc = np.empty((NL, 128, 768), np.float32)
    lora = np.zeros((NL, 128, 768), np.float32)
    for l in range(NL):
        row = np.concatenate([inp['rk_w0'][l], inp['sg_ln_w'][l], inp['sg_ln_b'][l]])
        bc[l] = np.broadcast_to(row[None, :], (128, row.shape[0]))
        lora[l, 0:32, 0:256] = inp['rk_w_up'][l]
        lora[l, 32:64, 256:512] = inp['rk_a_up'][l]
        lora[l, 64:128, 512:768] = inp['rk_g_up'][l]
    wmt = f(np.asarray(inp['sg_w']).transpose(0, 3, 1, 2))
    sgb = np.asarray(inp['sg_b'])
    sgbT = f(np.repeat(sgb.reshape(NL, 2, 2, 1, 128), 64, axis=3).reshape(NL, 2, 128, 128).transpose(0, 2, 1, 3))
    shared = dict(consts=f(consts), vecs=f(vecs), wgu=wgu, wd=wd, win=win, wout=wout, bc=bc, lora=lora, wmt=wmt, sgbT=sgbT)
    x = np.asarray(inp['x'])
    maps = []
    for b in range(8):
        xt = f(x[b].T.reshape(8, 128, T).transpose(1, 0, 2))
        m = dict(shared); m['xT'] = xt
        maps.append(m)
    return maps


_NC = None


def kernel(**inputs):
    global _NC
    inp = {k: np.asarray(v) for k, v in inputs.items()}
    maps = host_prep(inp)
    if _NC is None:
        _NC = build()
    res = run_bass_kernel_spmd(_NC, maps, core_ids=list(range(8)))
    out = np.empty((8, T, D), np.float32)
    for b in range(8):
        o = np.asarray(res.results[b]["outT"])
        out[b] = o.transpose(1, 0, 2).reshape(D, T).T
    return out
```

```python
import contextlib
import numpy as np
import concourse.bass as bass
import concourse.mybir as mybir
from concourse.bass_utils import run_bass_kernel_spmd

F32 = mybir.dt.float32
BF16 = mybir.dt.bfloat16
AF = mybir.ActivationFunctionType
ALU = mybir.AluOpType

D = 1024; T = 2048; DFF = 2816; NFC = 22; G = 256; INC = 2688
NL = 2
SAME_ENGINE_SYNC = True
RELAX_SAME_ENGINE = True
import os
DBG = int(os.environ.get('KDBG', '99'))
DBG2 = int(os.environ.get('KDBG2', '0'))
C_DECAY = float(np.exp(-0.5))

V_G = 0
V_SC = 48
V_CM = 54
V_CMB = 116; V_CMLW = 118; V_CMLB = 120
V_A0 = 122; V_KK = 124; V_KA = 126; V_RK = 128; V_RLW = 130; V_RLB = 132
V_MU = 134
NV = 141


class _Rec:
    def __getattr__(self, name):
        def f(*a, **kw):
            self.call = (name, a, kw)
            return self
        return f


def _call(fn):
    r = _Rec()
    fn(r)
    return r.call


class Prog:
    ENG = ('sync', 'scalar', 'vector', 'gpsimd', 'tensor')

    def __init__(s, nc, stack):
        s.nc = nc; s.stack = stack
        s.streams = {e: [] for e in s.ENG}
        s.sem = {}; s.cnt = {}
        s.lastw = {}; s.readers = {}
        s.known = {e: {} for e in s.ENG}
        for e in s.ENG:
            s._mksem(e)

    def _mksem(s, key):
        s.sem[key] = s.stack.enter_context(s.nc.semaphore("s_" + str(key)))
        s.cnt[key] = 0

    def _deps(s, eng, reads, writes):
        need = {}

        def add(tok):
            if tok is None:
                return
            k, v = tok
            if k == 'tensor' and eng == 'tensor':
                return
            if k == eng and not SAME_ENGINE_SYNC:
                return
            if k not in s.ENG:
                v = s.cnt[k]
            if need.get(k, 0) < v:
                need[k] = v
        for r in reads:
            add(s.lastw.get(r))
            if isinstance(r, tuple) and r[0] == 'ps':
                for k, v in s.readers.get(r, {}).items():
                    if k != eng:
                        add((k, v))
        for w in writes:
            tok = s.lastw.get(w)
            if tok is not None and not (RELAX_SAME_ENGINE and tok[0] == eng):
                add(tok)
            for k, v in s.readers.get(w, {}).items():
                if not (RELAX_SAME_ENGINE and k == eng):
                    add((k, v))
        waits = []
        for k, v in need.items():
            if s.known[eng].get(k, 0) < v:
                s.known[eng][k] = v
                waits.append((k, v))
        return waits

    def _commit(s, tok, reads, writes):
        for r in reads:
            d = s.readers.setdefault(r, {})
            if d.get(tok[0], 0) < tok[1]:
                d[tok[0]] = tok[1]
        for w in writes:
            s.lastw[w] = tok
            s.readers[w] = {}

    def op(s, eng, fn, reads=(), writes=()):
        reads = list(reads); writes = list(writes)
        waits = s._deps(eng, reads, writes)
        s.cnt[eng] += 1
        tok = (eng, s.cnt[eng])
        s.streams[eng].append((waits, _call(fn), eng, 1))
        s._commit(tok, reads, writes)

    def dma(s, eng, semkey, fn, reads=(), writes=()):
        reads = list(reads); writes = list(writes)
        if semkey not in s.sem:
            s._mksem(semkey)
        waits = s._deps(eng, reads, writes)
        s.cnt[semkey] += 16
        tok = (semkey, s.cnt[semkey])
        s.streams[eng].append((waits, _call(fn), semkey, 16))
        s._commit(tok, reads, writes)

    def barrier(s):
        for eng in s.ENG:
            waits = []
            for k, v in s.cnt.items():
                if v > 0 and s.known[eng].get(k, 0) < v and not (k == eng and k == 'tensor'):
                    s.known[eng][k] = v
                    waits.append((k, v))
            if waits:
                s.streams[eng].append((waits, None, None, 0))

    def wait_all(s, eng, keys):
        need = {}
        for k in keys:
            tok = s.lastw.get(k)
            if tok is not None and need.get(tok[0], 0) < tok[1]:
                need[tok[0]] = tok[1]
        s.streams[eng].append((list(need.items()), None, None, 0))

    def emit(s, block):
        waited = {e: set() for e in s.ENG}
        for eng in s.ENG:
            for waits, fn, semkey, amt in s.streams[eng]:
                for k, v in waits:
                    if k in waited:
                        waited[k].add(v)
        rank = {}
        for e in s.ENG:
            rank[e] = {v: i + 1 for i, v in enumerate(sorted(waited[e]))}
        for eng in s.ENG:
            items = s.streams[eng]

            def body(e, items=items, eng=eng):
                idx = 0
                for waits, fn, semkey, amt in items:
                    for k, v in waits:
                        e.wait_ge(s.sem[k], rank[k][v] if k in rank else v)
                    if fn is not None:
                        name, a, kw = fn
                        ins = getattr(e, name)(*a, **kw)
                        if semkey == eng:
                            idx += 1
                            if idx in rank[eng]:
                                ins.then_inc(s.sem[semkey], 1)
                        else:
                            ins.then_inc(s.sem[semkey], amt)
            getattr(block, eng)(body)


class Alloc:
    def __init__(s, nc):
        s.nc = nc
        s.base = (nc.sbuf_base + 63) // 64 * 64
        s.top = nc.sbuf_top
        s.off = s.base
        s.n = 0

    def __call__(s, shape, dtype):
        sz = int(np.prod(shape[1:])) * (4 if dtype == F32 else 2)
        sz = (sz + 63) // 64 * 64
        assert s.off + sz <= s.top, ("SBUF overflow", s.off, sz, s.top)
        s.n += 1
        t = s.nc.alloc_sbuf_tensor_at("sb%d" % s.n, list(shape), dtype, offset=s.off)
        s.off += sz
        return t

    def mark(s):
        return s.off

    def release(s, m):
        s.off = m


def pk(b, lo=0, hi=512):
    return [('ps', b)]


def build(n_layers=NL, stop_after=None):
    nc = bass.Bass("TRN2", target_bir_lowering=False)
    dt = lambda name, shape, kind="ExternalInput": nc.dram_tensor(name, list(shape), F32, kind=kind).ap()
    xin = dt("xT", [128, 8, T])
    consts_d = dt("consts", [128, 5 * 128])
    vecs_d = dt("vecs", [128, NL * NV])
    wgu_d = dt("wgu", [NL, 2, NFC, 128, 2, 8, 128])
    wd_d = dt("wd", [NL, 2, 8, 128, NFC, 128])
    win_d = dt("win", [NL, 128, 8, INC])
    wout_d = dt("wout", [NL, 128, 8, D])
    bc_d = dt("bc", [NL, 128, 768])
    lora_d = dt("lora", [NL, 128, 768])
    wmt_d = dt("wmt", [NL, 128, 4, 128])
    sgb_d = dt("sgbT", [NL, 128, 2, 128])
    out_d = dt("outT", [128, 8, T], kind="ExternalOutput")

    stack = contextlib.ExitStack()
    P = Prog(nc, stack)
    A = Alloc(nc)
    op = P.op

    XT = A([128, 8, T], F32)
    CF = A([128, 5 * 128], F32)
    IDF = CF[:, 0:128]; TRI_IF = CF[:, 128:256]; TRI_SF = CF[:, 256:384]
    CB = A([128, 5 * 128], BF16)
    IDB = CB[:, 0:128]; MASK_SI = CB[:, 128:384]
    TRILB = CB[:, 384:512]; BONES = CB[:, 512:640]
    ONESB = A([128, 128], BF16)
    CBX = A([128, 1024], BF16)
    ONESF = A([128, 128], F32)
    VEC = A([128, NL * NV], F32)
    HALFG = A([128, NL * 16], F32)
    EPS = A([128, 4], F32)
    OMMV = A([128, NL * 7], F32)
    ps = [nc.alloc_psum_tensor("psb%d" % i, [128, 512], F32) for i in range(7)]
    psT = nc.alloc_psum_tensor("psT", [128, 1024], BF16)

    for c in range(8):
        P.dma('sync', 'ld_x', lambda e, c=c: e.dma_start(out=XT[:, c, :], in_=xin[:, c, :]),
              writes=[('XT', c, tb) for tb in range(4)])
    P.dma('sync', 'ld_c', lambda e: e.dma_start(out=CF[:], in_=consts_d[:, :]), writes=['CF'])
    P.dma('sync', 'ld_c', lambda e: e.dma_start(out=VEC[:], in_=vecs_d[:, :]), writes=['VEC'])
    op('vector', lambda e: e.tensor_copy(out=CB[:], in_=CF[:]), reads=['CF'], writes=['CB'])
    op('vector', lambda e: e.memset(ONESB[:], 1.0), writes=['ONES'])
    for q in range(4):
        src = CB[:, 256:384] if q % 2 == 0 else CB[:, 128:256]
        op('vector', lambda e: e.tensor_copy(out=CBX[:, q * 128:(q + 1) * 128], in_=src), reads=['CB'], writes=['CBX'])
        op('vector', lambda e: e.tensor_copy(out=CBX[:, 512 + q * 128:512 + (q + 1) * 128], in_=CB[:, 384:512]), reads=['CB'], writes=['CBX'])
    op('vector', lambda e: e.memset(ONESF[:], 1.0), writes=['ONES'])
    for i, v in enumerate([1e-6, 1e-5, 64e-5, 1e-24]):
        op('vector', lambda e, i=i, v=v: e.memset(EPS[:, i:i + 1], v), writes=['EPS'])
    for l in range(NL):
        for j, gi in enumerate([1, 5]):
            op('vector', lambda e, l=l, j=j, gi=gi: e.tensor_scalar(
                out=HALFG[:, l * 16 + j * 8: l * 16 + j * 8 + 8],
                in0=VEC[:, l * NV + V_G + gi * 8: l * NV + V_G + gi * 8 + 8],
                scalar1=0.5, scalar2=None, op0=ALU.mult), reads=['VEC'], writes=['HALFG'])
    for l in range(NL):
        op('vector', lambda e, l=l: e.tensor_scalar(out=OMMV[:, l * 7:l * 7 + 7], in0=VEC[:, l * NV + V_MU:l * NV + V_MU + 7],
                                                    scalar1=-1.0, scalar2=1.0, op0=ALU.mult, op1=ALU.add),
           reads=['VEC'], writes=['OMMV'])

    def vcol(l, off):
        return VEC[:, l * NV + off: l * NV + off + 1]

    def rstd_from(psb, n, scale, epsi, LNV, RS, rk, wk, lk):
        op('scalar', lambda e: e.activation(out=LNV[:, 0:n], in_=psb[:, 0:n], func=AF.Ln,
                                            bias=EPS[:, epsi:epsi + 1], scale=scale),
           reads=rk + ['EPS'], writes=[lk])
        op('scalar', lambda e: e.activation(out=RS[:, 0:n], in_=LNV[:, 0:n], func=AF.Exp, scale=-0.5),
           reads=[lk], writes=[wk])

    work_mark = A.mark()

    def ffn(l, which):
        A.release(work_mark); P.barrier()
        gpre = V_G + (0 if which == 0 else 4) * 8
        HY = A([128, 8, T], BF16)
        ACTB = A([128, NFC, 1024], BF16)
        WGU = [A([128, 2, 8, 128], BF16) for _ in range(2)]
        WD = [A([128, NFC, 128], BF16) for _ in range(2)]
        SQ = [A([128, 512], BF16) for _ in range(2)]
        LNV = A([128, 512], F32)
        RS = [A([128, 512], F32) for _ in range(2)]
        SG = [A([128, 512], F32) for _ in range(2)]
        TMP = [A([128, 512], F32) for _ in range(2)]
        tag = 'f%d%d' % (l, which)
        K = lambda name, *idx: (tag, name) + idx
        it = 0
        if DBG <= 0:
            return
        for gtb in range(4):
            t0 = gtb * 512; rb = gtb % 2
            for c in range(8):
                b = c % 2
                op('scalar', lambda e, c=c, b=b, t0=t0: e.activation(out=SQ[b][:], in_=XT[:, c, t0:t0 + 512], func=AF.Square),
                   reads=[('XT', c, gtb)], writes=[K('SQ', b)])
                op('tensor', lambda e, c=c, b=b: e.matmul(ps[6][:], lhsT=ONESB[:], rhs=SQ[b][:], start=(c == 0), stop=(c == 7)),
                   reads=[K('SQ', b), 'ONES'], writes=pk(6))
            rstd_from(ps[6], 512, 1.0 / D, 0, LNV, RS[rb], pk(6), K('RS', rb), K('LNV'))
            for c in range(8):
                op('vector', lambda e, c=c, t0=t0, rb=rb: e.scalar_tensor_tensor(
                    out=HY[:, c, t0:t0 + 512], in0=XT[:, c, t0:t0 + 512], scalar=vcol(l, gpre + c),
                    in1=RS[rb][:], op0=ALU.mult, op1=ALU.mult),
                   reads=[('XT', c, gtb), K('RS', rb), 'VEC'], writes=[K('HY', c, gtb)])
        for half in range(2):
            T0 = half * 1024
            if DBG <= 1:
                return
            for fc in range(NFC):
                wb = fc % 2
                P.dma('gpsimd', K('ldgu', wb), lambda e, fc=fc, wb=wb: (e.dma_start(out=WGU[wb][:], in_=wgu_d[l, which, fc]) if not os.environ.get('KHALFDMA') else e.dma_start(out=WGU[wb][:, 0:1], in_=wgu_d[l, which, fc, :, 0:1])),
                      writes=[K('WGU', wb)])
                for tb in range(2):
                    pg = (it % 2) * 2; pu = pg + 1; sb = it % 2; it += 1
                    for gi, pb in ((0, pg), (1, pu)):
                        for k in range(8):
                            op('tensor', lambda e, gi=gi, pb=pb, k=k, wb=wb, tb=tb: e.matmul(
                                ps[pb][:], lhsT=WGU[wb][:, gi, k, :], rhs=HY[:, k, T0 + tb * 512:T0 + (tb + 1) * 512],
                                start=(k == 0), stop=(k == 7)),
                               reads=[K('WGU', wb), K('HY', k, half * 2 + tb)], writes=pk(pb))
                    op('scalar', lambda e, pg=pg, sb=sb: e.activation(out=SG[sb][:], in_=ps[pg][:], func=AF.Silu),
                       reads=pk(pg), writes=[K('SG', sb)])
                    op('vector', lambda e, pu=pu, sb=sb, fc=fc, tb=tb: e.tensor_tensor(
                        out=ACTB[:, fc, tb * 512:(tb + 1) * 512], in0=SG[sb][:], in1=ps[pu][:], op=ALU.mult),
                       reads=[K('SG', sb)] + pk(pu), writes=[K('ACT', fc, tb)])
            if DBG <= 2:
                return
            if DBG2 == 10:
                continue
            for dc in range(8):
                wb = dc % 2
                P.dma('gpsimd' if DBG2 != 12 else 'scalar', K('ldd', wb), lambda e, dc=dc, wb=wb: e.dma_start(out=WD[wb][:] if DBG2 != 12 else WD[wb][:, 0:11, :].bitcast(F32), in_=wd_d[l, which, dc] if DBG2 != 12 else wd_d[l, which, dc, :, 0:11, 0:64]),
                      writes=[K('WD', wb)])
                for tb in range(2):
                    if DBG2 == 2 or (DBG2 == 3 and dc >= 1):
                        continue
                    po = (it % 2); sb = it % 2; it += 1
                    if DBG2 == 9:
                        po += 2
                    for fc in range(NFC if DBG2 != 8 else 8):
                        lhs_ = WD[wb][:, fc, :] if DBG2 not in (6, 15) else WGU[wb][:, 0, fc % 8, :]
                        rhs_ = ACTB[:, fc, tb * 512:(tb + 1) * 512] if DBG2 not in (5, 15) else HY[:, fc % 8, T0 + tb * 512:T0 + (tb + 1) * 512]
                        op('tensor', lambda e, po=po, fc=fc, wb=wb, tb=tb: e.matmul(
                            ps[po][:], lhsT=lhs_, rhs=rhs_,
                            start=(fc == 0), stop=(fc == (NFC if DBG2 != 8 else 8) - 1)),
                           reads=([K('WD', wb)] if DBG2 != 13 else []) + ([K('ACT', fc, tb)] if DBG2 != 14 else []), writes=pk(po))
                    if DBG2 == 4:
                        continue
                    op('scalar', lambda e, po=po, sb=sb: e.activation(out=SQ[sb][:], in_=ps[po][:], func=AF.Square),
                       reads=pk(po), writes=[K('SQ', sb)])
                    op('vector', lambda e, po=po, dc=dc, tb=tb: e.tensor_copy(out=HY[:, dc, T0 + tb * 512:T0 + (tb + 1) * 512], in_=ps[po][:]),
                       reads=pk(po) + [K('SQ', sb)], writes=[K('HY', dc, half * 2 + tb)])
                    if DBG2 != 1:
                        op('tensor', lambda e, sb=sb, tb=tb, dc=dc: e.matmul(ps[4 + tb][:], lhsT=ONESB[:], rhs=SQ[sb][:],
                                                                            start=(dc == 0), stop=(dc == 7)),
                           reads=[K('SQ', sb), 'ONES'], writes=pk(4 + tb))
            if DBG <= 3:
                return
            hg = l * 16 + which * 8
            for tb in range(2):
                t0 = T0 + tb * 512; gtb = half * 2 + tb
                rstd_from(ps[4 + tb], 512, 1.0 / D, 0, LNV, RS[tb], pk(4 + tb), K('RS', tb), K('LNV'))
                for c in range(8):
                    b = c % 2
                    op('vector', lambda e, c=c, b=b, tb=tb: e.tensor_tensor(
                        out=TMP[b][:], in0=HY[:, c, T0 + tb * 512:T0 + (tb + 1) * 512], in1=RS[tb][:], op=ALU.mult),
                       reads=[K('HY', c, half * 2 + tb), K('RS', tb)], writes=[K('TMP', b)])
                    op('vector', lambda e, c=c, b=b, t0=t0: e.scalar_tensor_tensor(
                        out=XT[:, c, t0:t0 + 512], in0=TMP[b][:], scalar=HALFG[:, hg + c: hg + c + 1],
                        in1=XT[:, c, t0:t0 + 512], op0=ALU.mult, op1=ALU.add),
                       reads=[K('TMP', b), ('XT', c, gtb), 'HALFG'], writes=[('XT', c, gtb)])

    def mixer(l):
        A.release(work_mark); P.barrier()
        tag = 'm%d' % l
        K = lambda name, *idx: (tag, name) + idx
        HM = A([128, 8, T + 1], BF16)
        YT = A([128, 8, T], BF16)
        grp_mark = A.mark()
        LNV = A([128, 512], F32)
        RS = A([128, 512], F32)
        SQ = [A([128, 512], BF16) for _ in range(2)]
        XTall = lambda c: [('XT', c, tb) for tb in range(4)]
        for c in range(8):
            op('vector', lambda e, c=c: e.memset(HM[:, c, 0:1], 0.0), writes=[K('HM0', c)])
        for tb in range(4):
            t0 = tb * 512
            for c in range(8):
                b = c % 2
                op('scalar', lambda e, c=c, b=b, t0=t0: e.activation(out=SQ[b][:], in_=XT[:, c, t0:t0 + 512], func=AF.Square),
                   reads=[('XT', c, tb)], writes=[K('SQ', b)])
                op('tensor', lambda e, c=c, b=b: e.matmul(ps[6][:], lhsT=ONESB[:], rhs=SQ[b][:], start=(c == 0), stop=(c == 7)),
                   reads=[K('SQ', b), 'ONES'], writes=pk(6))
            rstd_from(ps[6], 512, 1.0 / D, 0, LNV, RS, pk(6), K('RS'), K('LNV'))
            for c in range(8):
                op('vector', lambda e, c=c, t0=t0: e.scalar_tensor_tensor(
                    out=HM[:, c, 1 + t0:1 + t0 + 512], in0=XT[:, c, t0:t0 + 512], scalar=vcol(l, V_G + 16 + c),
                    in1=RS[:], op0=ALU.mult, op1=ALU.mult),
                   reads=[('XT', c, tb), K('RS'), 'VEC'], writes=[K('HM', c, tb)])
        HMr = lambda k, tb: [K('HM', k, tb)]
        HMrs = lambda k, tb: [K('HM', k, tb), K('HM0', k)] + ([K('HM', k, tb - 1)] if tb > 0 else [])

        def proj_fm(pb, W, col0, tb, wkey, ncol=128, W2=None, prow=None):
            t0 = tb * 512
            outap = ps[pb][:] if prow is None else ps[pb][prow[0]:prow[1], :]
            n = 8 if W2 is None else 16
            for k in range(8):
                op('tensor', lambda e, k=k: e.matmul(outap, lhsT=W[:, k, col0:col0 + ncol], rhs=HM[:, k, 1 + t0:1 + t0 + 512],
                                                    start=(k == 0), stop=(k == n - 1)),
                   reads=[wkey] + HMr(k, tb), writes=pk(pb))
            if W2 is not None:
                for k in range(8):
                    op('tensor', lambda e, k=k: e.matmul(outap, lhsT=W2[:, k, col0:col0 + ncol], rhs=HM[:, k, t0:t0 + 512],
                                                        start=False, stop=(k == 7)),
                       reads=[wkey] + HMrs(k, tb), writes=pk(pb))

        A.release(grp_mark); P.barrier()
        WA = A([128, 8, 768], BF16)
        Z = A([128, 2, T + 2], F32)
        TA = [A([128, 512], F32) for _ in range(2)]
        ACC = [A([128, 512], F32) for _ in range(2)]
        P.dma('gpsimd', K('ldwa'), lambda e: e.dma_start(out=WA[:], in_=win_d[l, :, :, 0:768]), writes=[K('WA')])
        for c in range(2):
            op('vector', lambda e, c=c: e.memset(Z[:, c, 0:2], 0.0), writes=[K('Z0', c)])
        it = 0
        for tb in range(4):
            t0 = tb * 512
            for c in range(2):
                b = it % 2; it += 1
                pc_, px_, pb_ = 0 + 3 * b, 1 + 3 * b, 2 + 3 * b
                proj_fm(pc_, WA, 256 + c * 128, tb, K('WA'))
                proj_fm(px_, WA, 512 + c * 128, tb, K('WA'))
                proj_fm(pb_, WA, 0 + c * 128, tb, K('WA'))
                op('scalar', lambda e, b=b, pc_=pc_: e.activation(out=TA[b][:], in_=ps[pc_][:], func=AF.Copy),
                   reads=pk(pc_), writes=[K('TA', b)])
                op('vector', lambda e, b=b, px_=px_, c=c, t0=t0: e.tensor_tensor(
                    out=Z[:, c, 2 + t0:2 + t0 + 512], in0=TA[b][:], in1=ps[px_][:], op=ALU.mult),
                   reads=[K('TA', b)] + pk(px_), writes=[K('Z', c, tb)])
                zr = [K('Z', c, tb), K('Z0', c)] + ([K('Z', c, tb - 1)] if tb > 0 else [])
                op('vector', lambda e, b=b, c=c, t0=t0: e.tensor_scalar(
                    out=ACC[b][:], in0=Z[:, c, t0:t0 + 512], scalar1=vcol(l, V_SC + c * 3 + 0), scalar2=None, op0=ALU.mult),
                   reads=zr + ['VEC'], writes=[K('ACC', b)])
                for j in (1, 2):
                    op('vector', lambda e, b=b, c=c, t0=t0, j=j: e.scalar_tensor_tensor(
                        out=ACC[b][:], in0=Z[:, c, t0 + j:t0 + j + 512], scalar=vcol(l, V_SC + c * 3 + j),
                        in1=ACC[b][:], op0=ALU.mult, op1=ALU.add),
                       reads=zr + ['VEC', K('ACC', b)], writes=[K('ACC', b)])
                op('vector', lambda e, b=b, c=c, t0=t0, pb_=pb_: e.tensor_tensor(
                    out=YT[:, 0 + c, t0:t0 + 512], in0=ACC[b][:], in1=ps[pb_][:], op=ALU.mult),
                   reads=[K('ACC', b)] + pk(pb_), writes=[K('YT', 0 + c, tb)])

        A.release(grp_mark); P.barrier()
        WDm = A([128, 8, 512], BF16)
        ZG = A([128, 2, T + 30], BF16)
        DIAG = A([128, 62, 128], BF16)
        SGT = [A([128, 512], F32) for _ in range(2)]
        LNV = A([128, 512], F32)
        RS = A([128, 512], F32)
        ZD = [A([128, 512], F32) for _ in range(2)]
        ZD2 = [A([128, 512], F32) for _ in range(2)]
        MEAN = A([128, 512], F32)
        MSQ = A([128, 512], F32)
        VAR = A([128, 512], F32)
        DD = [A([128, 512], F32) for _ in range(2)]
        P.dma('gpsimd', K('ldwd'), lambda e: e.dma_start(out=WDm[:], in_=win_d[l, :, :, 2176:2688]), writes=[K('WDm')])
        for c in range(2):
            op('vector', lambda e, c=c: e.memset(ZG[:, c, 0:30], 0.0), writes=[K('ZG0', c)])
            for j in range(31):
                op('vector', lambda e, c=c, j=j: e.tensor_scalar(
                    out=DIAG[:, c * 31 + j, :], in0=IDF, scalar1=vcol(l, V_CM + c * 31 + j), scalar2=None, op0=ALU.mult),
                   reads=['CF', 'VEC'], writes=[K('DIAG', c)])
        it = 0
        for tb in range(4):
            t0 = tb * 512
            for c in range(2):
                b = it % 2; it += 1
                p1, p2 = 0 + 2 * b, 1 + 2 * b
                proj_fm(p1, WDm, 0 + c * 128, tb, K('WDm'))
                proj_fm(p2, WDm, 256 + c * 128, tb, K('WDm'))
                op('scalar', lambda e, b=b, p2=p2: e.activation(out=SGT[b][:], in_=ps[p2][:], func=AF.Sigmoid),
                   reads=pk(p2), writes=[K('SGT', b)])
                op('vector', lambda e, b=b, p1=p1, c=c, t0=t0: e.tensor_tensor(
                    out=ZG[:, c, 30 + t0:30 + t0 + 512], in0=SGT[b][:], in1=ps[p1][:], op=ALU.mult),
                   reads=[K('SGT', b)] + pk(p1), writes=[K('ZG', c, tb)])
            for c in range(2):
                zr = [K('ZG', c, tb), K('ZG0', c)] + ([K('ZG', c, tb - 1)] if tb > 0 else [])
                pcv = 4 + c
                for j in range(31):
                    op('tensor', lambda e, c=c, j=j, t0=t0, pcv=pcv: e.matmul(
                        ps[pcv][:], lhsT=DIAG[:, c * 31 + j, :], rhs=ZG[:, c, t0 + j:t0 + j + 512],
                        start=(j == 0), stop=(j == 30)),
                       reads=zr + [K('DIAG', c)], writes=pk(pcv))
                op('scalar', lambda e, c=c, pcv=pcv: e.activation(out=ZD[c][:], in_=ps[pcv][:], func=AF.Identity,
                                                                  bias=vcol(l, V_CMB + c), scale=1.0),
                   reads=pk(pcv) + ['VEC'], writes=[K('ZD', c)])
                op('vector', lambda e, c=c: e.tensor_tensor(out=ZD2[c][:], in0=ZD[c][:], in1=ZD[c][:], op=ALU.mult),
                   reads=[K('ZD', c)], writes=[K('ZD2', c)])
            for c in range(2):
                op('tensor', lambda e, c=c: e.matmul(ps[6][:], lhsT=ONESF[:], rhs=ZD[c][:], start=(c == 0), stop=(c == 1)),
                   reads=[K('ZD', c), 'ONES'], writes=pk(6))
            for c in range(2):
                op('tensor', lambda e, c=c: e.matmul(ps[0][:], lhsT=ONESF[:], rhs=ZD2[c][:], start=(c == 0), stop=(c == 1)),
                   reads=[K('ZD2', c), 'ONES'], writes=pk(0))
            op('scalar', lambda e: e.activation(out=MEAN[:], in_=ps[6][:], func=AF.Copy, scale=1.0 / G),
               reads=pk(6), writes=[K('MEAN')])
            op('vector', lambda e: e.tensor_tensor(out=MSQ[:], in0=MEAN[:], in1=MEAN[:], op=ALU.mult),
               reads=[K('MEAN')], writes=[K('MSQ')])
            op('vector', lambda e: e.scalar_tensor_tensor(out=VAR[:], in0=ps[0][:], scalar=1.0 / G, in1=MSQ[:],
                                                          op0=ALU.mult, op1=ALU.subtract),
               reads=pk(0) + [K('MSQ')], writes=[K('VAR')])
            op('scalar', lambda e: e.activation(out=LNV[:], in_=VAR[:], func=AF.Ln, bias=EPS[:, 1:2], scale=1.0),
               reads=[K('VAR'), 'EPS'], writes=[K('LNV')])
            op('scalar', lambda e: e.activation(out=RS[:], in_=LNV[:], func=AF.Exp, scale=-0.5),
               reads=[K('LNV')], writes=[K('RS')])
            for c in range(2):
                op('vector', lambda e, c=c: e.tensor_tensor(out=DD[c][:], in0=ZD[c][:], in1=MEAN[:], op=ALU.subtract),
                   reads=[K('ZD', c), K('MEAN')], writes=[K('DD', c)])
                op('vector', lambda e, c=c: e.tensor_tensor(out=DD[c][:], in0=DD[c][:], in1=RS[:], op=ALU.mult),
                   reads=[K('DD', c), K('RS')], writes=[K('DD', c)])
                op('scalar', lambda e, c=c, t0=t0: e.activation(out=YT[:, 6 + c, t0:t0 + 512], in_=DD[c][:], func=AF.Silu,
                                                                bias=vcol(l, V_CMLB + c), scale=vcol(l, V_CMLW + c)),
                   reads=[K('DD', c), 'VEC'], writes=[K('YT', 6 + c, tb)])

        A.release(grp_mark); P.barrier()
        WB = A([128, 8, 512], BF16)
        BCB = A([128, 512], F32)
        WMTF = A([128, 4, 128], F32)
        WMT = A([128, 4, 128], BF16)
        SGB = A([128, 2, 128], F32)
        ST6 = A([128, 6], F32)
        MV = A([128, 2], F32)
        RV = A([128, 2], F32)
        VN = [A([128, 256], F32) for _ in range(2)]
        VNB = [A([128, 256], BF16) for _ in range(2)]
        SB_ = [A([128, 128], F32) for _ in range(2)]
        P.dma('gpsimd', K('ldwb'), lambda e: e.dma_start(out=WB[:], in_=win_d[l, :, :, 768:1280]), writes=[K('WB')])
        P.dma('sync', K('ldb'), lambda e: e.dma_start(out=BCB[:], in_=bc_d[l, :, 256:768]), writes=[K('BCB')])
        P.dma('sync', K('ldb'), lambda e: e.dma_start(out=WMTF[:], in_=wmt_d[l]), writes=[K('WMTF')])
        P.dma('sync', K('ldb'), lambda e: e.dma_start(out=SGB[:], in_=sgb_d[l]), writes=[K('SGB')])
        for h in range(4):
            op('vector', lambda e, h=h: e.tensor_tensor(out=WMT[:, h, :], in0=WMTF[:, h, :], in1=TRI_IF, op=ALU.mult),
               reads=[K('WMTF'), 'CF'], writes=[K('WMT')])
        it = 0
        for tb in range(4):
            for c in range(2):
                proj_fm(4 + c, WB, c * 128, tb, K('WB'))
            for ti in range(4):
                tile_i = tb * 4 + ti; tt0 = tile_i * 128
                b = it % 2; it += 1
                pv = 0 + b; pss = 2 + b
                for k in range(8):
                    op('tensor', lambda e, k=k, pv=pv, tt0=tt0: e.matmul(
                        ps[pv][:, 0:256], lhsT=HM[:, k, 1 + tt0:1 + tt0 + 128], rhs=WB[:, k, 256:512],
                        start=(k == 0), stop=(k == 7)),
                       reads=[K('WB')] + HMr(k, tb), writes=pk(pv, 0, 256))
                op('vector', lambda e, pv=pv: e.bn_stats(out=ST6[:], in_=ps[pv][:, 0:256]),
                   reads=pk(pv, 0, 256), writes=[K('ST6')])
                op('vector', lambda e: e.bn_aggr(out=MV[:], in_=ST6[:]), reads=[K('ST6')], writes=[K('MV')])
                op('scalar', lambda e: e.activation(out=RV[:, 0:1], in_=MV[:, 1:2], func=AF.Ln, bias=EPS[:, 1:2], scale=1.0),
                   reads=[K('MV'), 'EPS'], writes=[K('RV0')])
                op('scalar', lambda e: e.activation(out=RV[:, 1:2], in_=RV[:, 0:1], func=AF.Exp, scale=-0.5),
                   reads=[K('RV0')], writes=[K('RV')])
                op('vector', lambda e, pv=pv, b=b: e.tensor_scalar(
                    out=VN[b][:], in0=ps[pv][:, 0:256], scalar1=MV[:, 0:1], scalar2=RV[:, 1:2],
                    op0=ALU.subtract, op1=ALU.mult),
                   reads=pk(pv, 0, 256) + [K('MV'), K('RV')], writes=[K('VN', b)])
                op('vector', lambda e, b=b: e.tensor_tensor(out=VN[b][:], in0=VN[b][:], in1=BCB[:, 0:256], op=ALU.mult),
                   reads=[K('VN', b), K('BCB')], writes=[K('VN', b)])
                op('vector', lambda e, b=b: e.tensor_tensor(out=VNB[b][:], in0=VN[b][:], in1=BCB[:, 256:512], op=ALU.add),
                   reads=[K('VN', b), K('BCB')], writes=[K('VNB', b)])
                for c in range(2):
                    for hl in range(2):
                        h = 2 * c + hl
                        op('tensor', lambda e, c=c, hl=hl, h=h, b=b, pss=pss: e.matmul(
                            ps[pss][hl * 64:(hl + 1) * 64, c * 128:(c + 1) * 128], lhsT=VNB[b][:, h * 64:(h + 1) * 64],
                            rhs=WMT[:, h, :], start=True, stop=True),
                           reads=[K('VNB', b), K('WMT')], writes=pk(pss, c * 128, c * 128 + 128))
                    op('vector', lambda e, c=c, b=b, pss=pss: e.tensor_tensor(
                        out=SB_[b][:], in0=ps[pss][:, c * 128:(c + 1) * 128], in1=SGB[:, c, :], op=ALU.add),
                       reads=pk(pss, c * 128, c * 128 + 128) + [K('SGB')], writes=[K('SB', b)])
                    op('vector', lambda e, c=c, b=b, ti=ti, tt0=tt0: e.tensor_tensor(
                        out=YT[:, 2 + c, tt0:tt0 + 128], in0=SB_[b][:], in1=ps[4 + c][:, ti * 128:(ti + 1) * 128], op=ALU.mult),
                       reads=[K('SB', b)] + pk(4 + c), writes=[K('YT', 2 + c, tb)])

        A.release(grp_mark); P.barrier()
        if 'C' not in os.environ.get('KSKIP', ''):
            rwkv(l, K, HM, YT, HMr, HMrs, A)

        A.release(grp_mark); P.barrier()
        WO = A([128, 8, D], BF16)
        MY = A([128, 8, 512], BF16)
        TMP = [A([128, 512], F32) for _ in range(2)]
        LNV = A([128, 512], F32)
        RS = A([128, 512], F32)
        SQ = [A([128, 512], BF16) for _ in range(2)]
        P.dma('gpsimd', K('ldwo'), lambda e: e.dma_start(out=WO[:], in_=wout_d[l]), writes=[K('WO')])
        it = 0
        for tb in range(4):
            t0 = tb * 512
            for dc in range(8):
                po = it % 2; sb = it % 2; it += 1
                for k in range(8):
                    op('tensor', lambda e, po=po, k=k, dc=dc, t0=t0: e.matmul(
                        ps[po][:], lhsT=WO[:, k, dc * 128:(dc + 1) * 128], rhs=YT[:, k, t0:t0 + 512],
                        start=(k == 0), stop=(k == 7)),
                       reads=[K('WO'), K('YT', k, tb)], writes=pk(po))
                op('scalar', lambda e, po=po, sb=sb: e.activation(out=SQ[sb][:], in_=ps[po][:], func=AF.Square),
                   reads=pk(po), writes=[K('SQ', sb)])
                op('vector', lambda e, po=po, dc=dc: e.tensor_copy(out=MY[:, dc, :], in_=ps[po][:]),
                   reads=pk(po), writes=[K('MY', dc)])
                op('tensor', lambda e, sb=sb, dc=dc: e.matmul(ps[6][:], lhsT=ONESB[:], rhs=SQ[sb][:], start=(dc == 0), stop=(dc == 7)),
                   reads=[K('SQ', sb), 'ONES'], writes=pk(6))
            rstd_from(ps[6], 512, 1.0 / D, 0, LNV, RS, pk(6), K('RS'), K('LNV'))
            for c in range(8):
                b = c % 2
                op('vector', lambda e, c=c, b=b: e.tensor_tensor(out=TMP[b][:], in0=MY[:, c, :], in1=RS[:], op=ALU.mult),
                   reads=[K('MY', c), K('RS')], writes=[K('TMP', b)])
                op('vector', lambda e, c=c, b=b, t0=t0: e.scalar_tensor_tensor(
                    out=XT[:, c, t0:t0 + 512], in0=TMP[b][:], scalar=vcol(l, V_G + 24 + c),
                    in1=XT[:, c, t0:t0 + 512], op0=ALU.mult, op1=ALU.add),
                   reads=[K('TMP', b), ('XT', c, tb), 'VEC'], writes=[('XT', c, tb)])

    def rwkv(l, K, HM, YT, HMr, HMrs, A):
        NB = 256
        WC = A([128, 8, 896], BF16)
        LORAB = A([128, 768], BF16)
        W0B = A([128, 256], F32)
        P.dma('gpsimd', K('ldwc'), lambda e: e.dma_start(out=WC[:], in_=win_d[l, :, :, 1280:2176]), writes=[K('WC')])
        P.dma('gpsimd', K('ldl'), lambda e: e.dma_start(out=LORAB[:], in_=lora_d[l]), writes=[K('LORA')])
        P.dma('sync', K('ldc'), lambda e: e.dma_start(out=W0B[:], in_=bc_d[l, :, 0:256]), writes=[K('W0B')])
        PC = A([128, 6 * NB], F32)
        MISC = A([128, NB], BF16)
        KKb = A([128, 2 * NB], F32); KPb = A([128, 2 * NB], F32); Bb = A([128, 2 * NB], F32)
        Gb = A([128, 2 * NB], BF16); BON = A([128, 2 * NB], BF16); VTB = A([128, 2 * NB], BF16)
        TQ = [A([128, NB], F32) for _ in range(2)]
        TS = [A([128, NB], F32) for _ in range(2)]
        SQB = [A([128, NB], BF16) for _ in range(2)]
        XW = A([128, 256], F32); SIGTM = A([128, 256], F32)
        EP = A([128, 256], F32); EM = A([128, 256], F32); EE = A([128, 256], F32)
        FT_ = A([128, 8 * 128], BF16)
        FH_ = A([128, 4 * 128], BF16)
        TM_ = A([128, 8 * 128], BF16)
        NM_ = A([128, 4 * 512], BF16)
        LP = [A([128, 512], BF16) for _ in range(2)]
        NP_ = [A([128, 512], BF16) for _ in range(2)]
        XF = [A([128, 512], BF16) for _ in range(2)]
        AHT = A([128, 256], BF16)
        UB = A([128, 256], BF16)
        S32 = A([128, 256], F32); SBF = A([128, 256], BF16)
        OT = A([128, 256], F32); OSQ = A([128, 256], F32)
        OM = A([128, 256], F32); OV = A([128, 256], F32); OL = A([128, 256], F32); ORS = A([128, 256], F32)
        BONF = CF[:, 512:640]
        vc = lambda off: vcol(l, off)
        cs = lambda c, a=0, b=NB: slice(c * NB + a, c * NB + b)
        hs = lambda h, a=0, b=128: slice(h * 128 + a, h * 128 + b)
        FT = lambda c, i, p0=0, p1=128: FT_[p0:p1, (c * 4 + i) * 128:(c * 4 + i + 1) * 128]
        FH = lambda c, i: FH_[:, (c * 2 + i) * 128:(c * 2 + i + 1) * 128]
        TM = lambda c, i, a=0, b=128: TM_[:, (c * 4 + i) * 128 + a:(c * 4 + i) * 128 + b]
        NM = lambda h, i, j: NM_[:, h * 512 + i * 256 + j * 128: h * 512 + i * 256 + (j + 1) * 128]
        op('vector', lambda e: e.memset(S32[:], 0.0), writes=[K('S32')])
        op('vector', lambda e: e.memset(SBF[:], 0.0), writes=[K('SBF')])
        allh = lambda nm, i: [K(nm, i, h) for h in range(4)]
        for sb in range(T // NB):
            t0 = sb * NB; tb = t0 // 512
            for cc in range(7):
                pb = cc % 2
                for k in range(8):
                    op('tensor', lambda e: e.matmul(ps[pb][:, 0:NB + 1], lhsT=WC[:, k, cc * 128:(cc + 1) * 128],
                                                    rhs=HM[:, k, t0:t0 + NB + 1], start=(k == 0), stop=(k == 7)),
                       reads=[K('WC')] + HMrs(k, tb), writes=pk(pb, 0, NB + 1))
                op('vector', lambda e: e.tensor_scalar(out=TS[pb][:], in0=ps[pb][:, 0:NB], scalar1=vc(V_MU + cc), scalar2=None,
                                                       op0=ALU.mult),
                   reads=pk(pb, 0, NB + 1) + ['VEC'], writes=[K('TS', pb)])
                dst = PC[:, cs(cc)] if cc < 6 else TQ[0][:]
                dk = K('PC', cc) if cc < 6 else K('TQ', 0)
                op('vector', lambda e: e.scalar_tensor_tensor(out=dst, in0=ps[pb][:, 1:NB + 1], scalar=OMMV[:, l * 7 + cc:l * 7 + cc + 1],
                                                              in1=TS[pb][:], op0=ALU.mult, op1=ALU.add),
                   reads=pk(pb, 0, NB + 1) + ['OMMV', K('TS', pb)], writes=[dk])
            op('scalar', lambda e: e.activation(out=MISC[0:32, :], in_=TQ[0][0:32, :], func=AF.Tanh), reads=[K('TQ', 0)], writes=[K('MISC', 0)])
            op('vector', lambda e: e.tensor_copy(out=MISC[32:64, :], in_=TQ[0][32:64, :]), reads=[K('TQ', 0)], writes=[K('MISC', 1)])
            op('scalar', lambda e: e.activation(out=MISC[64:128, :], in_=TQ[0][64:128, :], func=AF.Sigmoid),
               reads=[K('TQ', 0)], writes=[K('MISC', 2)])
            for c in range(2):
                op('tensor', lambda e: e.matmul(ps[2][:, 0:NB], lhsT=LORAB[32:64, 256 + c * 128:256 + (c + 1) * 128],
                                                rhs=MISC[32:64, :], start=True, stop=True),
                   reads=[K('LORA'), K('MISC', 1)], writes=pk(2, 0, NB))
                op('scalar', lambda e: e.activation(out=Bb[:, cs(c)], in_=ps[2][:, 0:NB], func=AF.Sigmoid, bias=vc(V_A0 + c), scale=1.0),
                   reads=pk(2, 0, NB) + ['VEC'], writes=[K('B', c)])
                op('tensor', lambda e: e.matmul(ps[3][:, 0:NB], lhsT=LORAB[64:128, 512 + c * 128:512 + (c + 1) * 128],
                                                rhs=MISC[64:128, :], start=True, stop=True),
                   reads=[K('LORA'), K('MISC', 2)], writes=pk(3, 0, NB))
                op('vector', lambda e: e.tensor_copy(out=Gb[:, cs(c)], in_=ps[3][:, 0:NB]), reads=pk(3, 0, NB), writes=[K('G', c)])
                op('vector', lambda e: e.tensor_scalar(out=KKb[:, cs(c)], in0=PC[:, cs(2 + c)], scalar1=vc(V_KK + c), scalar2=None, op0=ALU.mult),
                   reads=[K('PC', 2 + c), 'VEC'], writes=[K('KK', c)])
                op('scalar', lambda e: e.activation(out=SQB[c][:], in_=KKb[:, cs(c)], func=AF.Square), reads=[K('KK', c)], writes=[K('SQB', c)])
                op('tensor', lambda e: e.matmul(ps[4][:, 0:NB], lhsT=BONES, rhs=SQB[c][:], start=True, stop=True),
                   reads=[K('SQB', c), 'CB'], writes=pk(4, 0, NB))
                op('scalar', lambda e: e.activation(out=TQ[0][:], in_=ps[4][:, 0:NB], func=AF.Ln, bias=EPS[:, 3:4], scale=1.0),
                   reads=pk(4, 0, NB) + ['EPS'], writes=[K('TQ', 0)])
                op('scalar', lambda e: e.activation(out=TQ[1][:], in_=TQ[0][:], func=AF.Exp, scale=-0.5), reads=[K('TQ', 0)], writes=[K('TQ', 1)])
                op('vector', lambda e: e.tensor_tensor(out=KKb[:, cs(c)], in0=KKb[:, cs(c)], in1=TQ[1][:], op=ALU.mult),
                   reads=[K('KK', c), K('TQ', 1)], writes=[K('KK', c)])
                op('vector', lambda e: e.tensor_scalar(out=TQ[0][:], in0=Bb[:, cs(c)], scalar1=-1.0, scalar2=vc(V_KA + c), op0=ALU.add, op1=ALU.mult),
                   reads=[K('B', c), 'VEC'], writes=[K('TQ', 0)])
                op('vector', lambda e: e.scalar_tensor_tensor(out=KPb[:, cs(c)], in0=TQ[0][:], scalar=1.0, in1=PC[:, cs(2 + c)],
                                                              op0=ALU.add, op1=ALU.mult),
                   reads=[K('TQ', 0), K('PC', 2 + c)], writes=[K('KP', c)])
                op('vector', lambda e: e.tensor_tensor(out=Bb[:, cs(c)], in0=Bb[:, cs(c)], in1=KKb[:, cs(c)], op=ALU.mult),
                   reads=[K('B', c), K('KK', c)], writes=[K('B', c)])
                op('vector', lambda e: e.scalar_tensor_tensor(out=TQ[1][:], in0=PC[:, cs(c)], scalar=vc(V_RK + c), in1=KPb[:, cs(c)],
                                                              op0=ALU.mult, op1=ALU.mult),
                   reads=[K('PC', c), K('KP', c), 'VEC'], writes=[K('TQ', 1)])
                op('vector', lambda e: e.tensor_copy(out=SQB[c][:], in_=TQ[1][:]), reads=[K('TQ', 1)], writes=[K('SQB', c)])
                op('tensor', lambda e: e.matmul(ps[5][:, 0:NB], lhsT=BONES, rhs=SQB[c][:], start=True, stop=True),
                   reads=[K('SQB', c), 'CB'], writes=pk(5, 0, NB))
                op('scalar', lambda e: e.activation(out=BON[:, cs(c)], in_=ps[5][:, 0:NB], func=AF.Copy), reads=pk(5, 0, NB), writes=[K('BON', c)])
                op('vector', lambda e: e.tensor_copy(out=VTB[:, cs(c)], in_=PC[:, cs(4 + c)]), reads=[K('PC', 4 + c)], writes=[K('VTB', c)])
            for ti in range(NB // 128):
                q0 = ti * 128; q1 = q0 + 128; tt0 = t0 + q0
                op('tensor', lambda e: e.matmul(ps[2][:, 0:256], lhsT=MISC[0:32, q0:q1], rhs=LORAB[0:32, 0:256], start=True, stop=True),
                   reads=[K('MISC', 0), K('LORA')], writes=pk(2, 0, 256))
                op('vector', lambda e: e.tensor_tensor(out=XW[:], in0=ps[2][:, 0:256], in1=W0B[:], op=ALU.add),
                   reads=pk(2, 0, 256) + [K('W0B')], writes=[K('XW')])
                op('scalar', lambda e: e.activation(out=SIGTM[:], in_=XW[:], func=AF.Sigmoid), reads=[K('XW')], writes=[K('SIGTM')])
                for c in range(2):
                    op('tensor', lambda e: e.matmul(ps[3][:, c * 128:(c + 1) * 128], lhsT=SIGTM[:, c * 128:(c + 1) * 128], rhs=TRI_IF,
                                                    start=True, stop=True),
                       reads=[K('SIGTM'), 'CF'], writes=pk(3, c * 128, c * 128 + 128))
                    op('tensor', lambda e: e.matmul(ps[3][:, 256 + c * 128:256 + (c + 1) * 128], lhsT=SIGTM[:, c * 128:(c + 1) * 128],
                                                    rhs=TRI_SF, start=True, stop=True),
                       reads=[K('SIGTM'), 'CF'], writes=pk(3, 256 + c * 128, 256 + c * 128 + 128))
                op('scalar', lambda e: e.activation(out=EP[:], in_=ps[3][:, 0:256], func=AF.Exp, scale=-C_DECAY), reads=pk(3, 0, 256), writes=[K('EP')])
                op('scalar', lambda e: e.activation(out=EM[:], in_=ps[3][:, 0:256], func=AF.Exp, scale=C_DECAY), reads=pk(3, 0, 256), writes=[K('EM')])
                op('scalar', lambda e: e.activation(out=EE[:], in_=ps[3][:, 256:512], func=AF.Exp, scale=-C_DECAY), reads=pk(3, 256, 512), writes=[K('EE')])
                for c in range(2):
                    E_ = lambda X_: X_[:, c * 128:(c + 1) * 128]
                    gC = EP[:, c * 128 + 127:c * 128 + 128]
                    tq = slice(c * NB + q0, c * NB + q1)
                    op('vector', lambda e: e.scalar_tensor_tensor(out=FT(c, 0), in0=KKb[:, tq], scalar=-1.0, in1=E_(EE), op0=ALU.mult, op1=ALU.mult),
                       reads=[K('KK', c), K('EE')], writes=[K('FT', c)])
                    op('vector', lambda e: e.tensor_tensor(out=FT(c, 1), in0=Bb[:, tq], in1=E_(EM), op=ALU.mult),
                       reads=[K('B', c), K('EM')], writes=[K('FT', c)])
                    op('vector', lambda e: e.tensor_tensor(out=FT(c, 2), in0=KPb[:, tq], in1=E_(EM), op=ALU.mult),
                       reads=[K('KP', c), K('EM')], writes=[K('FT', c)])
                    op('vector', lambda e: e.tensor_tensor(out=FT(c, 3), in0=PC[:, tq], in1=E_(EP), op=ALU.mult),
                       reads=[K('PC', c), K('EP')], writes=[K('FT', c)])
                    op('vector', lambda e: e.scalar_tensor_tensor(out=FH(c, 0), in0=Bb[:, tq], scalar=gC, in1=E_(EM), op0=ALU.mult, op1=ALU.mult),
                       reads=[K('B', c), K('EM'), K('EP')], writes=[K('FH', c)])
                    op('vector', lambda e: e.scalar_tensor_tensor(out=FH(c, 1), in0=KPb[:, tq], scalar=gC, in1=E_(EM), op0=ALU.mult, op1=ALU.mult),
                       reads=[K('KP', c), K('EM'), K('EP')], writes=[K('FH', c)])
                    srcs = [(FT(c, 0), K('FT', c)), (VTB[:, tq], K('VTB', c)), (FH(c, 0), K('FH', c)), (FH(c, 1), K('FH', c))]
                    for i, (src, sk) in enumerate(srcs):
                        op('tensor', lambda e: e.transpose(out=psT[:, (c * 4 + i) * 128:(c * 4 + i + 1) * 128], in_=src, identity=IDB),
                           reads=[sk, 'CB'], writes=[('ps', 7)])
                    if c == 0:
                        op('scalar', lambda e: e.activation(out=TM_[:, 0:512], in_=psT[:, 0:512], func=AF.Copy), reads=[('ps', 7)], writes=[K('TM', 0)])
                    else:
                        op('vector', lambda e: e.tensor_copy(out=TM_[:, 512:1024], in_=psT[:, 512:1024]), reads=[('ps', 7)], writes=[K('TM', 1)])
                v4 = lambda X_: X_.rearrange("p (h x) -> p h x", h=4)
                for h in range(4):
                    c = h // 2; hl = h % 2; r0 = hl * 64; r1 = r0 + 64
                    aT = FT(c, 0, r0, r1); bT = FT(c, 1, r0, r1); kT = FT(c, 2, r0, r1); rT = FT(c, 3, r0, r1)
                    for i, lt in enumerate((bT, kT)):
                        for j2, rt in enumerate((aT, rT)):
                            qq = i * 2 + j2
                            op('tensor', lambda e: e.matmul(ps[h][:, qq * 128:(qq + 1) * 128], lhsT=lt, rhs=rt, start=True, stop=True),
                               reads=[K('FT', c)], writes=pk(h))
                    op('tensor', lambda e: e.matmul(ps[4 + hl][:, c * 128:(c + 1) * 128], lhsT=aT, rhs=bT, start=True, stop=True),
                       reads=[K('FT', c)], writes=pk(4 + hl))
                for h in range(4):
                    op('vector', lambda e: e.tensor_tensor(out=NM_[:, h * 512:(h + 1) * 512], in0=ps[h][:], in1=CBX[:, 0:512], op=ALU.mult),
                       reads=pk(h) + ['CBX'], writes=[K('NM', h)])
                for hl in range(2):
                    for c in range(2):
                        h = 2 * c + hl
                        op('vector', lambda e: e.tensor_tensor(out=LP[0][:, hs(h)], in0=ps[4 + hl][:, c * 128:(c + 1) * 128],
                                                               in1=TRILB, op=ALU.mult),
                           reads=pk(4 + hl) + ['CB'], writes=[K('LP', 0, h)])
                for h in range(4):
                    op('scalar', lambda e: e.activation(out=NP_[0][:, hs(h)], in_=NM(h, 0, 0), func=AF.Copy),
                       reads=[K('NM', h)], writes=[K('NP', 0, h)])
                for h in range(4):
                    c = h // 2; hl = h % 2; r0 = hl * 64; r1 = r0 + 64
                    op('tensor', lambda e: e.matmul(ps[6][:, hs(h, 64, 128)], lhsT=NM(h, 1, 0), rhs=TM(c, 1, r0, r1), start=True, stop=True),
                       reads=[K('NM', h), K('TM', c)], writes=pk(6))
                for h in range(4):
                    c = h // 2; hl = h % 2; r0 = hl * 64; r1 = r0 + 64
                    op('vector', lambda e: e.tensor_copy(out=XF[0][:, hs(h, 64, 128)], in_=ps[6][:, hs(h, 64, 128)]),
                       reads=pk(6), writes=[K('XF', 0, h)])
                    op('scalar', lambda e: e.activation(out=XF[0][:, hs(h, 0, 64)], in_=TM(c, 0, r0, r1), func=AF.Copy),
                       reads=[K('TM', c), K('XF', 0, h)], writes=[K('XF', 0, h)])
                cur = 0
                for lvl in range(7 if 'L' not in os.environ.get('KSKIP', '') else 0):
                    nxt = 1 - cur
                    for h in range(4):
                        op('tensor', lambda e: e.matmul(ps[4][:, hs(h)], lhsT=NP_[cur][:, hs(h)], rhs=XF[cur][:, hs(h)], start=True, stop=True),
                           reads=[K('NP', cur, h), K('XF', cur, h)], writes=pk(4, h * 128, h * 128 + 128))
                    op('vector', lambda e: e.tensor_tensor(out=XF[nxt][:], in0=XF[cur][:], in1=ps[4][:], op=ALU.add),
                       reads=allh('XF', cur) + pk(4), writes=allh('XF', nxt))
                    if lvl < 6:
                        for h in range(4):
                            op('tensor', lambda e: e.matmul(ps[5][:, hs(h)], lhsT=LP[cur][:, hs(h)], rhs=NP_[cur][:, hs(h)], start=True, stop=True),
                               reads=[K('LP', cur, h), K('NP', cur, h)], writes=pk(5, h * 128, h * 128 + 128))
                        op('scalar', lambda e: e.activation(out=NP_[nxt][:], in_=ps[5][:], func=AF.Copy), reads=pk(5), writes=allh('NP', nxt))
                        if lvl < 5:
                            for h in range(4):
                                op('tensor', lambda e: e.matmul(ps[6][:, hs(h)], lhsT=NP_[cur][:, hs(h)], rhs=LP[cur][:, hs(h)],
                                                                start=True, stop=True),
                                   reads=[K('LP', cur, h), K('NP', cur, h)], writes=pk(6, h * 128, h * 128 + 128))
                            op('scalar', lambda e: e.activation(out=LP[nxt][:], in_=ps[6][:], func=AF.Copy), reads=pk(6), writes=allh('LP', nxt))
                    cur = nxt
                XFf = XF[cur]
                for c in range(2):
                    for hl in range(2):
                        h = 2 * c + hl
                        op('tensor', lambda e: e.transpose(out=psT[hl * 64:(hl + 1) * 64, c * 128:(c + 1) * 128], in_=XFf[:, hs(h, 0, 64)], identity=IDB),
                           reads=[K('XF', cur, h), 'CB'], writes=[('ps', 7)])
                op('vector', lambda e: e.tensor_copy(out=AHT[:], in_=psT[:, 0:256]), reads=[('ps', 7)], writes=[K('AHT')])
                for c in range(2):
                    cb = slice(c * 128, (c + 1) * 128)
                    op('tensor', lambda e: e.matmul(ps[0][:, cb], lhsT=AHT[:, cb], rhs=SBF[:, cb], start=True, stop=True),
                       reads=[K('AHT'), K('SBF')], writes=pk(0, c * 128, c * 128 + 128))
                    for hl in range(2):
                        h = 2 * c + hl
                        op('vector', lambda e: e.tensor_tensor(out=UB[:, h * 64:(h + 1) * 64], in0=ps[0][:, c * 128 + hl * 64:c * 128 + hl * 64 + 64],
                                                               in1=XFf[:, hs(h, 64, 128)], op=ALU.add),
                           reads=pk(0, c * 128, c * 128 + 128) + [K('XF', cur, h)], writes=[K('UB', h)])
                    op('tensor', lambda e: e.matmul(ps[1][:, cb], lhsT=SBF[:, cb], rhs=FT(c, 3), start=True, stop=False, skip_group_check=True),
                       reads=[K('SBF'), K('FT', c)], writes=pk(1, c * 128, c * 128 + 128))
                    for hl in range(2):
                        h = 2 * c + hl; r0 = hl * 64; r1 = r0 + 64
                        op('tensor', lambda e: e.matmul(ps[1][r0:r1, cb], lhsT=UB[:, h * 64:(h + 1) * 64], rhs=NM(h, 0, 1),
                                                        start=False, stop=False, skip_group_check=True),
                           reads=[K('UB', h), K('NM', h)], writes=pk(1, c * 128, c * 128 + 128))
                        op('tensor', lambda e: e.matmul(ps[1][r0:r1, cb], lhsT=TM(c, 1, r0, r1), rhs=NM(h, 1, 1),
                                                        start=False, stop=True, skip_group_check=True),
                           reads=[K('TM', c), K('NM', h)], writes=pk(1, c * 128, c * 128 + 128))
                    for hl in range(2):
                        h = 2 * c + hl; r0 = hl * 64; r1 = r0 + 64
                        so = slice(256 + c * 128 + r0, 256 + c * 128 + r1)
                        sd = slice(c * 128 + r0, c * 128 + r1)
                        op('tensor', lambda e: e.matmul(ps[0][r0:r1, so], lhsT=TM(c, 2, r0, r1), rhs=UB[:, h * 64:(h + 1) * 64],
                                                        start=True, stop=False, skip_group_check=True),
                           reads=[K('TM', c), K('UB', h)], writes=pk(0, 256 + c * 128, 256 + c * 128 + 128))
                        op('tensor', lambda e: e.matmul(ps[0][r0:r1, so], lhsT=TM(c, 3, r0, r1), rhs=TM(c, 1, r0, r1),
                                                        start=False, stop=True, skip_group_check=True),
                           reads=[K('TM', c)], writes=pk(0, 256 + c * 128, 256 + c * 128 + 128))
                        op('vector', lambda e: e.scalar_tensor_tensor(out=S32[r0:r1, sd], in0=S32[r0:r1, sd], scalar=EP[r0:r1, c * 128 + 127:c * 128 + 128],
                                                                      in1=ps[0][r0:r1, so], op0=ALU.mult, op1=ALU.add),
                           reads=[K('S32'), K('EP')] + pk(0, 256 + c * 128, 256 + c * 128 + 128), writes=[K('S32')])
                        op('vector', lambda e: e.tensor_copy(out=SBF[r0:r1, sd], in_=S32[r0:r1, sd]), reads=[K('S32')], writes=[K('SBF')])
                if 'E' in os.environ.get('KSKIP', ''):
                    continue
                op('scalar', lambda e: e.activation(out=OT[:], in_=ps[1][:, 0:256], func=AF.Copy), reads=pk(1, 0, 256), writes=[K('OT')])
                op('vector', lambda e: e.tensor_tensor(out=OSQ[:], in0=OT[:], in1=OT[:], op=ALU.mult), reads=[K('OT')], writes=[K('OSQ')])
                for c in range(2):
                    cb = slice(c * 128, (c + 1) * 128)
                    op('tensor', lambda e: e.matmul(ps[2][:, cb], lhsT=BONF, rhs=OT[:, cb], start=True, stop=True),
                       reads=[K('OT'), 'CF'], writes=pk(2, c * 128, c * 128 + 128))
                    op('tensor', lambda e: e.matmul(ps[2][:, 256 + c * 128:256 + (c + 1) * 128], lhsT=BONF, rhs=OSQ[:, cb], start=True, stop=True),
                       reads=[K('OSQ'), 'CF'], writes=pk(2, 256 + c * 128, 256 + c * 128 + 128))
                v2 = lambda X_: X_.rearrange("p (c x) -> p c x", c=2)
                BV = lambda X_: v2(X_[:])[:, :, q0:q1]
                op('scalar', lambda e: e.activation(out=OM[:], in_=ps[2][:, 0:256], func=AF.Copy, scale=1.0 / 64), reads=pk(2), writes=[K('OM')])
                op('vector', lambda e: e.tensor_tensor(out=OV[:], in0=OM[:], in1=OM[:], op=ALU.mult), reads=[K('OM')], writes=[K('OV')])
                op('vector', lambda e: e.scalar_tensor_tensor(out=OV[:], in0=ps[2][:, 256:512], scalar=1.0 / 64, in1=OV[:],
                                                              op0=ALU.mult, op1=ALU.subtract),
                   reads=pk(2) + [K('OV')], writes=[K('OV')])
                op('scalar', lambda e: e.activation(out=OL[:], in_=OV[:], func=AF.Ln, bias=EPS[:, 2:3], scale=1.0), reads=[K('OV'), 'EPS'], writes=[K('OL')])
                op('scalar', lambda e: e.activation(out=ORS[:], in_=OL[:], func=AF.Exp, scale=-0.5), reads=[K('OL')], writes=[K('ORS')])
                op('vector', lambda e: e.tensor_tensor(out=OM[:], in0=OT[:], in1=OM[:], op=ALU.subtract), reads=[K('OT'), K('OM')], writes=[K('OM')])
                op('vector', lambda e: e.tensor_tensor(out=OM[:], in0=OM[:], in1=ORS[:], op=ALU.mult), reads=[K('OM'), K('ORS')], writes=[K('OM')])
                for c in range(2):
                    cb = slice(c * 128, (c + 1) * 128)
                    op('vector', lambda e: e.tensor_scalar(out=OM[:, cb], in0=OM[:, cb], scalar1=vc(V_RLW + c), scalar2=vc(V_RLB + c),
                                                           op0=ALU.mult, op1=ALU.add),
                       reads=[K('OM'), 'VEC'], writes=[K('OM')])
                op('vector', lambda e: e.tensor_tensor(out=v2(OV[:]), in0=BV(BON), in1=BV(VTB), op=ALU.mult),
                   reads=[K('BON', 0), K('BON', 1), K('VTB', 0), K('VTB', 1)], writes=[K('OV')])
                op('vector', lambda e: e.tensor_tensor(out=OM[:], in0=OM[:], in1=OV[:], op=ALU.add), reads=[K('OM'), K('OV')], writes=[K('OM')])
                op('vector', lambda e: e.tensor_tensor(out=YT[:, 4:6, tt0:tt0 + 128], in0=v2(OM[:]), in1=BV(Gb), op=ALU.mult),
                   reads=[K('OM'), K('G', 0), K('G', 1)], writes=[K('YT', 4, tb), K('YT', 5, tb)])

    seq = []
    for l in range(n_layers):
        seq += [('ffn', l, 0), ('mix', l), ('ffn', l, 1)]
    for s_ in seq:
        if s_[0] == 'ffn':
            ffn(s_[1], s_[2])
        else:
            mixer(s_[1])
        if stop_after is not None and tuple(stop_after) == tuple(s_):
            break
    for c in range(8):
        P.dma('sync', 'st_o', lambda e, c=c: e.dma_start(out=out_d[:, c, :], in_=XT[:, c, :]),
              reads=[('XT', c, tb) for tb in range(4)], writes=[('OUT', c)])
    P.wait_all('sync', [('OUT', c) for c in range(8)])
    with nc.Block() as block:
        P.emit(block)
    stack.close()
    return nc


def host_prep(inp):
    f = lambda a: np.ascontiguousarray(a, dtype=np.float32)
    tri_incl = np.triu(np.ones((128, 128), np.float32))
    tri_strict = np.triu(np.ones((128, 128), np.float32), 1)
    bo = np.zeros((128, 128), np.float32); bo[:64, :64] = 1; bo[64:, 64:] = 1
    consts = np.concatenate([np.eye(128, dtype=np.float32), tri_incl, tri_strict, tri_strict.T.copy(), bo], axis=1)
    fm = lambda v: np.asarray(v).reshape(-1, 128).T
    vecs = np.zeros((128, NL * NV), np.float32)
    for l in range(NL):
        o = l * NV
        for i, nm in enumerate(['ffn1_pre_g', 'ffn1_post_g', 'mix_pre_g', 'mix_post_g', 'ffn2_pre_g', 'ffn2_post_g']):
            vecs[:, o + V_G + i * 8: o + V_G + i * 8 + 8] = fm(inp[nm][l])
        for c in range(2):
            for j in range(3):
                vecs[:, o + V_SC + c * 3 + j] = inp['sc_conv_w'][l, j, c * 128:(c + 1) * 128]
            for j in range(31):
                vecs[:, o + V_CM + c * 31 + j] = inp['cm_conv_w'][l, j, c * 128:(c + 1) * 128]
        for off, nm in [(V_CMB, 'cm_conv_b'), (V_CMLW, 'cm_ln_w'), (V_CMLB, 'cm_ln_b'), (V_A0, 'rk_a0'), (V_KK, 'rk_k_k'),
                        (V_KA, 'rk_k_a'), (V_RLW, 'rk_ln_w'), (V_RLB, 'rk_ln_b')]:
            vecs[:, o + off: o + off + 2] = fm(inp[nm][l])
        vecs[:, o + V_RK: o + V_RK + 2] = fm(inp['rk_r_k'][l].reshape(-1))
        vecs[:, o + V_MU: o + V_MU + 7] = fm(inp['rk_mu'][l])
    wgu = np.empty((NL, 2, NFC, 128, 2, 8, 128), np.float32)
    wd = np.empty((NL, 2, 8, 128, NFC, 128), np.float32)
    wsrc = {('ffn1', 'w_gate'): inp['ffn1_w_gate'], ('ffn1', 'w_up'): inp['ffn1_w_up'], ('ffn1', 'w_down'): inp['ffn1_w_down'],
            ('ffn2', 'w_gate'): inp['ffn2_w_gate'], ('ffn2', 'w_up'): inp['ffn2_w_up'], ('ffn2', 'w_down'): inp['ffn2_w_down']}
    for wi, pre in enumerate(['ffn1', 'ffn2']):
        for gi, nm in enumerate(['w_gate', 'w_up']):
            w = np.asarray(wsrc[(pre, nm)])
            wgu[:, wi, :, :, gi, :, :] = w.reshape(NL, 8, 128, NFC, 128).transpose(0, 3, 2, 1, 4)
        w = np.asarray(wsrc[(pre, 'w_down')])
        wd[:, wi] = w.reshape(NL, NFC, 128, 8, 128).transpose(0, 3, 2, 1, 4)
    win = f(np.asarray(inp['w_in']).reshape(NL, 8, 128, INC).transpose(0, 2, 1, 3))
    wout = f(np.asarray(inp['w_out']).reshape(NL, 8, 128, D).transpose(0, 2, 1, 3))
    bc = np.empty((NL, 128, 768), np.float32)
    lora = np.zeros((NL, 128, 768), np.float32)
    for l in range(NL):
        row = np.concatenate([inp['rk_w0'][l], inp['sg_ln_w'][l], inp['sg_ln_b'][l]])
        bc[l] = np.broadcast_to(row[None, :], (128, row.shape[0]))
        lora[l, 0:32, 0:256] = inp['rk_w_up'][l]
        lora[l, 32:64, 256:512] = inp['rk_a_up'][l]
        lora[l, 64:128, 512:768] = inp['rk_g_up'][l]
    wmt = f(np.asarray(inp['sg_w']).transpose(0, 3, 1, 2))
    sgb = np.asarray(inp['sg_b'])
    sgbT = f(np.repeat(sgb.reshape(NL, 2, 2, 1, 128), 64, axis=3).reshape(NL, 2, 128, 128).transpose(0, 2, 1, 3))
    shared = dict(consts=f(consts), vecs=f(vecs), wgu=wgu, wd=wd, win=win, wout=wout, bc=bc, lora=lora, wmt=wmt, sgbT=sgbT)
    x = np.asarray(inp['x'])
    maps = []
    for b in range(8):
        xt = f(x[b].T.reshape(8, 128, T).transpose(1, 0, 2))
        m = dict(shared); m['xT'] = xt
        maps.append(m)
    return maps


_NC = None


def kernel(**inputs):
    global _NC
    inp = {k: np.asarray(v) for k, v in inputs.items()}
    maps = host_prep(inp)
    if _NC is None:
        _NC = build()
    res = run_bass_kernel_spmd(_NC, maps, core_ids=list(range(8)))
    out = np.empty((8, T, D), np.float32)
    for b in range(8):
        o = np.asarray(res.results[b]["outT"])
        out[b] = o.transpose(1, 0, 2).reshape(D, T).T
    return out
```

```python
import contextlib
import numpy as np
import concourse.bass as bass
import concourse.mybir as mybir
from concourse.bass_utils import run_bass_kernel_spmd

F32 = mybir.dt.float32
BF16 = mybir.dt.bfloat16
AF = mybir.ActivationFunctionType
ALU = mybir.AluOpType

D = 1024; T = 2048; DFF = 2816; NFC = 22; G = 256; INC = 2688
NL = 2
SAME_ENGINE_SYNC = True
RELAX_SAME_ENGINE = True
import os
DBG = int(os.environ.get('KDBG', '99'))
DBG2 = int(os.environ.get('KDBG2', '0'))
C_DECAY = float(np.exp(-0.5))

V_G = 0
V_SC = 48
V_CM = 54
V_CMB = 116; V_CMLW = 118; V_CMLB = 120
V_A0 = 122; V_KK = 124; V_KA = 126; V_RK = 128; V_RLW = 130; V_RLB = 132
V_MU = 134
NV = 141


class _Rec:
    def __getattr__(self, name):
        def f(*a, **kw):
            self.call = (name, a, kw)
            return self
        return f


def _call(fn):
    r = _Rec()
    fn(r)
    return r.call


class Prog:
    ENG = ('sync', 'scalar', 'vector', 'gpsimd', 'tensor')

    def __init__(s, nc, stack):
        s.nc = nc; s.stack = stack
        s.streams = {e: [] for e in s.ENG}
        s.sem = {}; s.cnt = {}
        s.lastw = {}; s.readers = {}
        s.known = {e: {} for e in s.ENG}
        for e in s.ENG:
            s._mksem(e)

    def _mksem(s, key):
        s.sem[key] = s.stack.enter_context(s.nc.semaphore("s_" + str(key)))
        s.cnt[key] = 0

    def _deps(s, eng, reads, writes):
        need = {}

        def add(tok):
            if tok is None:
                return
            k, v = tok
            if k == 'tensor' and eng == 'tensor':
                return
            if k == eng and not SAME_ENGINE_SYNC:
                return
            if k not in s.ENG:
                v = s.cnt[k]
            if need.get(k, 0) < v:
                need[k] = v
        for r in reads:
            add(s.lastw.get(r))
            if isinstance(r, tuple) and r[0] == 'ps':
                for k, v in s.readers.get(r, {}).items():
                    if k != eng:
                        add((k, v))
        for w in writes:
            tok = s.lastw.get(w)
            if tok is not None and not (RELAX_SAME_ENGINE and tok[0] == eng):
                add(tok)
            for k, v in s.readers.get(w, {}).items():
                if not (RELAX_SAME_ENGINE and k == eng):
                    add((k, v))
        waits = []
        for k, v in need.items():
            if s.known[eng].get(k, 0) < v:
                s.known[eng][k] = v
                waits.append((k, v))
        return waits

    def _commit(s, tok, reads, writes):
        for r in reads:
            d = s.readers.setdefault(r, {})
            if d.get(tok[0], 0) < tok[1]:
                d[tok[0]] = tok[1]
        for w in writes:
            s.lastw[w] = tok
            s.readers[w] = {}

    def op(s, eng, fn, reads=(), writes=()):
        reads = list(reads); writes = list(writes)
        waits = s._deps(eng, reads, writes)
        s.cnt[eng] += 1
        tok = (eng, s.cnt[eng])
        s.streams[eng].append((waits, _call(fn), eng, 1))
        s._commit(tok, reads, writes)

    def dma(s, eng, semkey, fn, reads=(), writes=()):
        reads = list(reads); writes = list(writes)
        if semkey not in s.sem:
            s._mksem(semkey)
        waits = s._deps(eng, reads, writes)
        s.cnt[semkey] += 16
        tok = (semkey, s.cnt[semkey])
        s.streams[eng].append((waits, _call(fn), semkey, 16))
        s._commit(tok, reads, writes)

    def barrier(s):
        for eng in s.ENG:
            waits = []
            for k, v in s.cnt.items():
                if v > 0 and s.known[eng].get(k, 0) < v and not (k == eng and k == 'tensor'):
                    s.known[eng][k] = v
                    waits.append((k, v))
            if waits:
                s.streams[eng].append((waits, None, None, 0))

    def wait_all(s, eng, keys):
        need = {}
        for k in keys:
            tok = s.lastw.get(k)
            if tok is not None and need.get(tok[0], 0) < tok[1]:
                need[tok[0]] = tok[1]
        s.streams[eng].append((list(need.items()), None, None, 0))

    def emit(s, block):
        waited = {e: set() for e in s.ENG}
        for eng in s.ENG:
            for waits, fn, semkey, amt in s.streams[eng]:
                for k, v in waits:
                    if k in waited:
                        waited[k].add(v)
        rank = {}
        for e in s.ENG:
            rank[e] = {v: i + 1 for i, v in enumerate(sorted(waited[e]))}
        for eng in s.ENG:
            items = s.streams[eng]

            def body(e, items=items, eng=eng):
                idx = 0
                for waits, fn, semkey, amt in items:
                    for k, v in waits:
                        e.wait_ge(s.sem[k], rank[k][v] if k in rank else v)
                    if fn is not None:
                        name, a, kw = fn
                        ins = getattr(e, name)(*a, **kw)
                        if semkey == eng:
                            idx += 1
                            if idx in rank[eng]:
                                ins.then_inc(s.sem[semkey], 1)
                        else:
                            ins.then_inc(s.sem[semkey], amt)
            getattr(block, eng)(body)


class Alloc:
    def __init__(s, nc):
        s.nc = nc
        s.base = (nc.sbuf_base + 63) // 64 * 64
        s.top = nc.sbuf_top
        s.off = s.base
        s.n = 0

    def __call__(s, shape, dtype):
        sz = int(np.prod(shape[1:])) * (4 if dtype == F32 else 2)
        sz = (sz + 63) // 64 * 64
        assert s.off + sz <= s.top, ("SBUF overflow", s.off, sz, s.top)
        s.n += 1
        t = s.nc.alloc_sbuf_tensor_at("sb%d" % s.n, list(shape), dtype, offset=s.off)
        s.off += sz
        return t

    def mark(s):
        return s.off

    def release(s, m):
        s.off = m


def pk(b, lo=0, hi=512):
    return [('ps', b)]


def build(n_layers=NL, stop_after=None):
    nc = bass.Bass("TRN2", target_bir_lowering=False)
    dt = lambda name, shape, kind="ExternalInput": nc.dram_tensor(name, list(shape), F32, kind=kind).ap()
    xin = dt("xT", [128, 8, T])
    consts_d = dt("consts", [128, 5 * 128])
    vecs_d = dt("vecs", [128, NL * NV])
    wgu_d = dt("wgu", [NL, 2, NFC, 128, 2, 8, 128])
    wd_d = dt("wd", [NL, 2, 8, 128, NFC, 128])
    win_d = dt("win", [NL, 128, 8, INC])
    wout_d = dt("wout", [NL, 128, 8, D])
    bc_d = dt("bc", [NL, 128, 768])
    lora_d = dt("lora", [NL, 128, 768])
    wmt_d = dt("wmt", [NL, 128, 4, 128])
    sgb_d = dt("sgbT", [NL, 128, 2, 128])
    out_d = dt("outT", [128, 8, T], kind="ExternalOutput")

    stack = contextlib.ExitStack()
    P = Prog(nc, stack)
    A = Alloc(nc)
    op = P.op

    XT = A([128, 8, T], F32)
    CF = A([128, 5 * 128], F32)
    IDF = CF[:, 0:128]; TRI_IF = CF[:, 128:256]; TRI_SF = CF[:, 256:384]
    CB = A([128, 5 * 128], BF16)
    IDB = CB[:, 0:128]; MASK_SI = CB[:, 128:384]
    TRILB = CB[:, 384:512]; BONES = CB[:, 512:640]
    ONESB = A([128, 128], BF16)
    CBX = A([128, 1024], BF16)
    ONESF = A([128, 128], F32)
    VEC = A([128, NL * NV], F32)
    HALFG = A([128, NL * 16], F32)
    EPS = A([128, 4], F32)
    OMMV = A([128, NL * 7], F32)
    ps = [nc.alloc_psum_tensor("psb%d" % i, [128, 512], F32) for i in range(7)]
    psT = nc.alloc_psum_tensor("psT", [128, 1024], BF16)

    for c in range(8):
        P.dma('sync', 'ld_x', lambda e, c=c: e.dma_start(out=XT[:, c, :], in_=xin[:, c, :]),
              writes=[('XT', c, tb) for tb in range(4)])
    P.dma('sync', 'ld_c', lambda e: e.dma_start(out=CF[:], in_=consts_d[:, :]), writes=['CF'])
    P.dma('sync', 'ld_c', lambda e: e.dma_start(out=VEC[:], in_=vecs_d[:, :]), writes=['VEC'])
    op('vector', lambda e: e.tensor_copy(out=CB[:], in_=CF[:]), reads=['CF'], writes=['CB'])
    op('vector', lambda e: e.memset(ONESB[:], 1.0), writes=['ONES'])
    for q in range(4):
        src = CB[:, 256:384] if q % 2 == 0 else CB[:, 128:256]
        op('vector', lambda e: e.tensor_copy(out=CBX[:, q * 128:(q + 1) * 128], in_=src), reads=['CB'], writes=['CBX'])
        op('vector', lambda e: e.tensor_copy(out=CBX[:, 512 + q * 128:512 + (q + 1) * 128], in_=CB[:, 384:512]), reads=['CB'], writes=['CBX'])
    op('vector', lambda e: e.memset(ONESF[:], 1.0), writes=['ONES'])
    for i, v in enumerate([1e-6, 1e-5, 64e-5, 1e-24]):
        op('vector', lambda e, i=i, v=v: e.memset(EPS[:, i:i + 1], v), writes=['EPS'])
    for l in range(NL):
        for j, gi in enumerate([1, 5]):
            op('vector', lambda e, l=l, j=j, gi=gi: e.tensor_scalar(
                out=HALFG[:, l * 16 + j * 8: l * 16 + j * 8 + 8],
                in0=VEC[:, l * NV + V_G + gi * 8: l * NV + V_G + gi * 8 + 8],
                scalar1=0.5, scalar2=None, op0=ALU.mult), reads=['VEC'], writes=['HALFG'])
    for l in range(NL):
        op('vector', lambda e, l=l: e.tensor_scalar(out=OMMV[:, l * 7:l * 7 + 7], in0=VEC[:, l * NV + V_MU:l * NV + V_MU + 7],
                                                    scalar1=-1.0, scalar2=1.0, op0=ALU.mult, op1=ALU.add),
           reads=['VEC'], writes=['OMMV'])

    def vcol(l, off):
        return VEC[:, l * NV + off: l * NV + off + 1]

    def rstd_from(psb, n, scale, epsi, LNV, RS, rk, wk, lk):
        op('scalar', lambda e: e.activation(out=LNV[:, 0:n], in_=psb[:, 0:n], func=AF.Ln,
                                            bias=EPS[:, epsi:epsi + 1], scale=scale),
           reads=rk + ['EPS'], writes=[lk])
        op('scalar', lambda e: e.activation(out=RS[:, 0:n], in_=LNV[:, 0:n], func=AF.Exp, scale=-0.5),
           reads=[lk], writes=[wk])

    work_mark = A.mark()

    def ffn(l, which):
        A.release(work_mark); P.barrier()
        gpre = V_G + (0 if which == 0 else 4) * 8
        HY = A([128, 8, T], BF16)
        ACTB = A([128, NFC, 1024], BF16)
        WGU = [A([128, 2, 8, 128], BF16) for _ in range(3)]
        WD = [A([128, NFC, 128], BF16) for _ in range(3)]
        SQ = [A([128, 512], BF16) for _ in range(2)]
        LNV = A([128, 512], F32)
        RS = [A([128, 512], F32) for _ in range(2)]
        SG = [A([128, 512], F32) for _ in range(2)]
        TMP = [A([128, 512], F32) for _ in range(2)]
        tag = 'f%d%d' % (l, which)
        K = lambda name, *idx: (tag, name) + idx
        it = 0
        if DBG <= 0:
            return
        def prenorm(gtb):
            t0 = gtb * 512; rb = gtb % 2
            for c in range(8):
                b = c % 2
                op('scalar', lambda e: e.activation(out=SQ[b][:], in_=XT[:, c, t0:t0 + 512], func=AF.Square),
                   reads=[('XT', c, gtb)], writes=[K('SQ', b)])
                op('tensor', lambda e: e.matmul(ps[6][:], lhsT=ONESB[:], rhs=SQ[b][:], start=(c == 0), stop=(c == 7)),
                   reads=[K('SQ', b), 'ONES'], writes=pk(6))
            rstd_from(ps[6], 512, 1.0 / D, 0, LNV, RS[rb], pk(6), K('RS', rb), K('LNV'))
            for c in range(8):
                op('vector', lambda e: e.scalar_tensor_tensor(
                    out=HY[:, c, t0:t0 + 512], in0=XT[:, c, t0:t0 + 512], scalar=vcol(l, gpre + c),
                    in1=RS[rb][:], op0=ALU.mult, op1=ALU.mult),
                   reads=[('XT', c, gtb), K('RS', rb), 'VEC'], writes=[K('HY', c, gtb)])
        prenorm(0)
        prenorm(1)
        for half in range(2):
            T0 = half * 1024
            if DBG <= 1:
                return
            for fc in range(NFC):
                wb = fc % 3
                if half == 0 and fc in (4, 10):
                    prenorm(2 if fc == 4 else 3)
                P.dma('gpsimd', K('ldgu', wb), lambda e, fc=fc, wb=wb: (e.dma_start(out=WGU[wb][:], in_=wgu_d[l, which, fc]) if not os.environ.get('KHALFDMA') else e.dma_start(out=WGU[wb][:, 0:1], in_=wgu_d[l, which, fc, :, 0:1])),
                      writes=[K('WGU', wb)])
                for tb in range(2):
                    pg = (it % 2) * 2; pu = pg + 1; sb = it % 2; it += 1
                    for gi, pb in ((0, pg), (1, pu)):
                        for k in range(8):
                            op('tensor', lambda e, gi=gi, pb=pb, k=k, wb=wb, tb=tb: e.matmul(
                                ps[pb][:], lhsT=WGU[wb][:, gi, k, :], rhs=HY[:, k, T0 + tb * 512:T0 + (tb + 1) * 512],
                                start=(k == 0), stop=(k == 7)),
                               reads=[K('WGU', wb), K('HY', k, half * 2 + tb)], writes=pk(pb))
                    op('scalar', lambda e, pg=pg, sb=sb: e.activation(out=SG[sb][:], in_=ps[pg][:], func=AF.Silu),
                       reads=pk(pg), writes=[K('SG', sb)])
                    op('vector', lambda e, pu=pu, sb=sb, fc=fc, tb=tb: e.tensor_tensor(
                        out=ACTB[:, fc, tb * 512:(tb + 1) * 512], in0=SG[sb][:], in1=ps[pu][:], op=ALU.mult),
                       reads=[K('SG', sb)] + pk(pu), writes=[K('ACT', fc, tb)])
            if DBG <= 2:
                return
            if DBG2 == 10:
                continue
            for dc in range(8):
                wb = dc % 3
                P.dma('gpsimd' if DBG2 != 12 else 'scalar', K('ldd', wb), lambda e, dc=dc, wb=wb: e.dma_start(out=WD[wb][:] if DBG2 != 12 else WD[wb][:, 0:11, :].bitcast(F32), in_=wd_d[l, which, dc] if DBG2 != 12 else wd_d[l, which, dc, :, 0:11, 0:64]),
                      writes=[K('WD', wb)])
                for tb in range(2):
                    if DBG2 == 2 or (DBG2 == 3 and dc >= 1):
                        continue
                    po = (it % 2); sb = it % 2; it += 1
                    if DBG2 == 9:
                        po += 2
                    for fc in range(NFC if DBG2 != 8 else 8):
                        lhs_ = WD[wb][:, fc, :] if DBG2 not in (6, 15) else WGU[wb][:, 0, fc % 8, :]
                        rhs_ = ACTB[:, fc, tb * 512:(tb + 1) * 512] if DBG2 not in (5, 15) else HY[:, fc % 8, T0 + tb * 512:T0 + (tb + 1) * 512]
                        op('tensor', lambda e, po=po, fc=fc, wb=wb, tb=tb: e.matmul(
                            ps[po][:], lhsT=lhs_, rhs=rhs_,
                            start=(fc == 0), stop=(fc == (NFC if DBG2 != 8 else 8) - 1)),
                           reads=([K('WD', wb)] if DBG2 != 13 else []) + ([K('ACT', fc, tb)] if DBG2 != 14 else []), writes=pk(po))
                    if DBG2 == 4:
                        continue
                    op('scalar', lambda e, po=po, sb=sb: e.activation(out=SQ[sb][:], in_=ps[po][:], func=AF.Square),
                       reads=pk(po), writes=[K('SQ', sb)])
                    op('vector', lambda e, po=po, dc=dc, tb=tb: e.tensor_copy(out=HY[:, dc, T0 + tb * 512:T0 + (tb + 1) * 512], in_=ps[po][:]),
                       reads=pk(po) + [K('SQ', sb)], writes=[K('HY', dc, half * 2 + tb)])
                    if DBG2 != 1:
                        op('tensor', lambda e, sb=sb, tb=tb, dc=dc: e.matmul(ps[4 + tb][:], lhsT=ONESB[:], rhs=SQ[sb][:],
                                                                            start=(dc == 0), stop=(dc == 7)),
                           reads=[K('SQ', sb), 'ONES'], writes=pk(4 + tb))
            if DBG <= 3:
                return
            hg = l * 16 + which * 8
            for tb in range(2):
                t0 = T0 + tb * 512; gtb = half * 2 + tb
                rstd_from(ps[4 + tb], 512, 1.0 / D, 0, LNV, RS[tb], pk(4 + tb), K('RS', tb), K('LNV'))
                for c in range(8):
                    b = c % 2
                    op('vector', lambda e, c=c, b=b, tb=tb: e.tensor_tensor(
                        out=TMP[b][:], in0=HY[:, c, T0 + tb * 512:T0 + (tb + 1) * 512], in1=RS[tb][:], op=ALU.mult),
                       reads=[K('HY', c, half * 2 + tb), K('RS', tb)], writes=[K('TMP', b)])
                    op('vector', lambda e, c=c, b=b, t0=t0: e.scalar_tensor_tensor(
                        out=XT[:, c, t0:t0 + 512], in0=TMP[b][:], scalar=HALFG[:, hg + c: hg + c + 1],
                        in1=XT[:, c, t0:t0 + 512], op0=ALU.mult, op1=ALU.add),
                       reads=[K('TMP', b), ('XT', c, gtb), 'HALFG'], writes=[('XT', c, gtb)])

    def mixer(l):
        A.release(work_mark); P.barrier()
        tag = 'm%d' % l
        K = lambda name, *idx: (tag, name) + idx
        HM = A([128, 8, T + 1], BF16)
        YT = A([128, 8, T], BF16)
        grp_mark = A.mark()
        LNV = A([128, 512], F32)
        RS = A([128, 512], F32)
        SQ = [A([128, 512], BF16) for _ in range(2)]
        XTall = lambda c: [('XT', c, tb) for tb in range(4)]
        for c in range(8):
            op('vector', lambda e, c=c: e.memset(HM[:, c, 0:1], 0.0), writes=[K('HM0', c)])
        for tb in range(4):
            t0 = tb * 512
            for c in range(8):
                b = c % 2
                op('scalar', lambda e, c=c, b=b, t0=t0: e.activation(out=SQ[b][:], in_=XT[:, c, t0:t0 + 512], func=AF.Square),
                   reads=[('XT', c, tb)], writes=[K('SQ', b)])
                op('tensor', lambda e, c=c, b=b: e.matmul(ps[6][:], lhsT=ONESB[:], rhs=SQ[b][:], start=(c == 0), stop=(c == 7)),
                   reads=[K('SQ', b), 'ONES'], writes=pk(6))
            rstd_from(ps[6], 512, 1.0 / D, 0, LNV, RS, pk(6), K('RS'), K('LNV'))
            for c in range(8):
                op('vector', lambda e, c=c, t0=t0: e.scalar_tensor_tensor(
                    out=HM[:, c, 1 + t0:1 + t0 + 512], in0=XT[:, c, t0:t0 + 512], scalar=vcol(l, V_G + 16 + c),
                    in1=RS[:], op0=ALU.mult, op1=ALU.mult),
                   reads=[('XT', c, tb), K('RS'), 'VEC'], writes=[K('HM', c, tb)])
        HMr = lambda k, tb: [K('HM', k, tb)]
        HMrs = lambda k, tb: [K('HM', k, tb), K('HM0', k)] + ([K('HM', k, tb - 1)] if tb > 0 else [])

        def proj_fm(pb, W, col0, tb, wkey, ncol=128, W2=None, prow=None):
            t0 = tb * 512
            outap = ps[pb][:] if prow is None else ps[pb][prow[0]:prow[1], :]
            n = 8 if W2 is None else 16
            for k in range(8):
                op('tensor', lambda e, k=k: e.matmul(outap, lhsT=W[:, k, col0:col0 + ncol], rhs=HM[:, k, 1 + t0:1 + t0 + 512],
                                                    start=(k == 0), stop=(k == n - 1)),
                   reads=[wkey] + HMr(k, tb), writes=pk(pb))
            if W2 is not None:
                for k in range(8):
                    op('tensor', lambda e, k=k: e.matmul(outap, lhsT=W2[:, k, col0:col0 + ncol], rhs=HM[:, k, t0:t0 + 512],
                                                        start=False, stop=(k == 7)),
                       reads=[wkey] + HMrs(k, tb), writes=pk(pb))

        A.release(grp_mark); P.barrier()
        WA = A([128, 8, 768], BF16)
        Z = A([128, 2, T + 2], F32)
        TA = [A([128, 512], F32) for _ in range(2)]
        ACC = [A([128, 512], F32) for _ in range(2)]
        P.dma('gpsimd', K('ldwa'), lambda e: e.dma_start(out=WA[:], in_=win_d[l, :, :, 0:768]), writes=[K('WA')])
        for c in range(2):
            op('vector', lambda e, c=c: e.memset(Z[:, c, 0:2], 0.0), writes=[K('Z0', c)])
        it = 0
        for tb in range(4):
            t0 = tb * 512
            for c in range(2):
                b = it % 2; it += 1
                pc_, px_, pb_ = 0 + 3 * b, 1 + 3 * b, 2 + 3 * b
                proj_fm(pc_, WA, 256 + c * 128, tb, K('WA'))
                proj_fm(px_, WA, 512 + c * 128, tb, K('WA'))
                proj_fm(pb_, WA, 0 + c * 128, tb, K('WA'))
                op('scalar', lambda e, b=b, pc_=pc_: e.activation(out=TA[b][:], in_=ps[pc_][:], func=AF.Copy),
                   reads=pk(pc_), writes=[K('TA', b)])
                op('vector', lambda e, b=b, px_=px_, c=c, t0=t0: e.tensor_tensor(
                    out=Z[:, c, 2 + t0:2 + t0 + 512], in0=TA[b][:], in1=ps[px_][:], op=ALU.mult),
                   reads=[K('TA', b)] + pk(px_), writes=[K('Z', c, tb)])
                zr = [K('Z', c, tb), K('Z0', c)] + ([K('Z', c, tb - 1)] if tb > 0 else [])
                op('vector', lambda e, b=b, c=c, t0=t0: e.tensor_scalar(
                    out=ACC[b][:], in0=Z[:, c, t0:t0 + 512], scalar1=vcol(l, V_SC + c * 3 + 0), scalar2=None, op0=ALU.mult),
                   reads=zr + ['VEC'], writes=[K('ACC', b)])
                for j in (1, 2):
                    op('vector', lambda e, b=b, c=c, t0=t0, j=j: e.scalar_tensor_tensor(
                        out=ACC[b][:], in0=Z[:, c, t0 + j:t0 + j + 512], scalar=vcol(l, V_SC + c * 3 + j),
                        in1=ACC[b][:], op0=ALU.mult, op1=ALU.add),
                       reads=zr + ['VEC', K('ACC', b)], writes=[K('ACC', b)])
                op('vector', lambda e, b=b, c=c, t0=t0, pb_=pb_: e.tensor_tensor(
                    out=YT[:, 0 + c, t0:t0 + 512], in0=ACC[b][:], in1=ps[pb_][:], op=ALU.mult),
                   reads=[K('ACC', b)] + pk(pb_), writes=[K('YT', 0 + c, tb)])

        A.release(grp_mark); P.barrier()
        WDm = A([128, 8, 512], BF16)
        ZG = A([128, 2, T + 30], BF16)
        DIAG = A([128, 62, 128], BF16)
        SGT = [A([128, 512], F32) for _ in range(2)]
        LNV = A([128, 512], F32)
        RS = A([128, 512], F32)
        ZD = [A([128, 512], F32) for _ in range(2)]
        ZD2 = [A([128, 512], F32) for _ in range(2)]
        MEAN = A([128, 512], F32)
        MSQ = A([128, 512], F32)
        VAR = A([128, 512], F32)
        DD = [A([128, 512], F32) for _ in range(2)]
        P.dma('gpsimd', K('ldwd'), lambda e: e.dma_start(out=WDm[:], in_=win_d[l, :, :, 2176:2688]), writes=[K('WDm')])
        for c in range(2):
            op('vector', lambda e, c=c: e.memset(ZG[:, c, 0:30], 0.0), writes=[K('ZG0', c)])
            for j in range(31):
                op('vector', lambda e, c=c, j=j: e.tensor_scalar(
                    out=DIAG[:, c * 31 + j, :], in0=IDF, scalar1=vcol(l, V_CM + c * 31 + j), scalar2=None, op0=ALU.mult),
                   reads=['CF', 'VEC'], writes=[K('DIAG', c)])
        it = 0
        for tb in range(4):
            t0 = tb * 512
            for c in range(2):
                b = it % 2; it += 1
                p1, p2 = 0 + 2 * b, 1 + 2 * b
                proj_fm(p1, WDm, 0 + c * 128, tb, K('WDm'))
                proj_fm(p2, WDm, 256 + c * 128, tb, K('WDm'))
                op('scalar', lambda e, b=b, p2=p2: e.activation(out=SGT[b][:], in_=ps[p2][:], func=AF.Sigmoid),
                   reads=pk(p2), writes=[K('SGT', b)])
                op('vector', lambda e, b=b, p1=p1, c=c, t0=t0: e.tensor_tensor(
                    out=ZG[:, c, 30 + t0:30 + t0 + 512], in0=SGT[b][:], in1=ps[p1][:], op=ALU.mult),
                   reads=[K('SGT', b)] + pk(p1), writes=[K('ZG', c, tb)])
            for c in range(2):
                zr = [K('ZG', c, tb), K('ZG0', c)] + ([K('ZG', c, tb - 1)] if tb > 0 else [])
                pcv = 4 + c
                for j in range(31):
                    op('tensor', lambda e, c=c, j=j, t0=t0, pcv=pcv: e.matmul(
                        ps[pcv][:], lhsT=DIAG[:, c * 31 + j, :], rhs=ZG[:, c, t0 + j:t0 + j + 512],
                        start=(j == 0), stop=(j == 30)),
                       reads=zr + [K('DIAG', c)], writes=pk(pcv))
                op('scalar', lambda e, c=c, pcv=pcv: e.activation(out=ZD[c][:], in_=ps[pcv][:], func=AF.Identity,
                                                                  bias=vcol(l, V_CMB + c), scale=1.0),
                   reads=pk(pcv) + ['VEC'], writes=[K('ZD', c)])
                op('vector', lambda e, c=c: e.tensor_tensor(out=ZD2[c][:], in0=ZD[c][:], in1=ZD[c][:], op=ALU.mult),
                   reads=[K('ZD', c)], writes=[K('ZD2', c)])
            for c in range(2):
                op('tensor', lambda e, c=c: e.matmul(ps[6][:], lhsT=ONESF[:], rhs=ZD[c][:], start=(c == 0), stop=(c == 1)),
                   reads=[K('ZD', c), 'ONES'], writes=pk(6))
            for c in range(2):
                op('tensor', lambda e, c=c: e.matmul(ps[0][:], lhsT=ONESF[:], rhs=ZD2[c][:], start=(c == 0), stop=(c == 1)),
                   reads=[K('ZD2', c), 'ONES'], writes=pk(0))
            op('scalar', lambda e: e.activation(out=MEAN[:], in_=ps[6][:], func=AF.Copy, scale=1.0 / G),
               reads=pk(6), writes=[K('MEAN')])
            op('vector', lambda e: e.tensor_tensor(out=MSQ[:], in0=MEAN[:], in1=MEAN[:], op=ALU.mult),
               reads=[K('MEAN')], writes=[K('MSQ')])
            op('vector', lambda e: e.scalar_tensor_tensor(out=VAR[:], in0=ps[0][:], scalar=1.0 / G, in1=MSQ[:],
                                                          op0=ALU.mult, op1=ALU.subtract),
               reads=pk(0) + [K('MSQ')], writes=[K('VAR')])
            op('scalar', lambda e: e.activation(out=LNV[:], in_=VAR[:], func=AF.Ln, bias=EPS[:, 1:2], scale=1.0),
               reads=[K('VAR'), 'EPS'], writes=[K('LNV')])
            op('scalar', lambda e: e.activation(out=RS[:], in_=LNV[:], func=AF.Exp, scale=-0.5),
               reads=[K('LNV')], writes=[K('RS')])
            for c in range(2):
                op('vector', lambda e, c=c: e.tensor_tensor(out=DD[c][:], in0=ZD[c][:], in1=MEAN[:], op=ALU.subtract),
                   reads=[K('ZD', c), K('MEAN')], writes=[K('DD', c)])
                op('vector', lambda e, c=c: e.tensor_tensor(out=DD[c][:], in0=DD[c][:], in1=RS[:], op=ALU.mult),
                   reads=[K('DD', c), K('RS')], writes=[K('DD', c)])
                op('scalar', lambda e, c=c, t0=t0: e.activation(out=YT[:, 6 + c, t0:t0 + 512], in_=DD[c][:], func=AF.Silu,
                                                                bias=vcol(l, V_CMLB + c), scale=vcol(l, V_CMLW + c)),
                   reads=[K('DD', c), 'VEC'], writes=[K('YT', 6 + c, tb)])

        A.release(grp_mark); P.barrier()
        WB = A([128, 8, 512], BF16)
        BCB = A([128, 512], F32)
        WMTF = A([128, 4, 128], F32)
        WMT = A([128, 4, 128], BF16)
        SGB = A([128, 2, 128], F32)
        ST6 = A([128, 6], F32)
        MV = A([128, 2], F32)
        RV = A([128, 2], F32)
        VN = [A([128, 256], F32) for _ in range(2)]
        VNB = [A([128, 256], BF16) for _ in range(2)]
        SB_ = [A([128, 128], F32) for _ in range(2)]
        P.dma('gpsimd', K('ldwb'), lambda e: e.dma_start(out=WB[:], in_=win_d[l, :, :, 768:1280]), writes=[K('WB')])
        P.dma('sync', K('ldb'), lambda e: e.dma_start(out=BCB[:], in_=bc_d[l, :, 256:768]), writes=[K('BCB')])
        P.dma('sync', K('ldb'), lambda e: e.dma_start(out=WMTF[:], in_=wmt_d[l]), writes=[K('WMTF')])
        P.dma('sync', K('ldb'), lambda e: e.dma_start(out=SGB[:], in_=sgb_d[l]), writes=[K('SGB')])
        for h in range(4):
            op('vector', lambda e, h=h: e.tensor_tensor(out=WMT[:, h, :], in0=WMTF[:, h, :], in1=TRI_IF, op=ALU.mult),
               reads=[K('WMTF'), 'CF'], writes=[K('WMT')])
        it = 0
        for tb in range(4):
            for c in range(2):
                proj_fm(4 + c, WB, c * 128, tb, K('WB'))
            for ti in range(4):
                tile_i = tb * 4 + ti; tt0 = tile_i * 128
                b = it % 2; it += 1
                pv = 0 + b; pss = 2 + b
                for k in range(8):
                    op('tensor', lambda e, k=k, pv=pv, tt0=tt0: e.matmul(
                        ps[pv][:, 0:256], lhsT=HM[:, k, 1 + tt0:1 + tt0 + 128], rhs=WB[:, k, 256:512],
                        start=(k == 0), stop=(k == 7)),
                       reads=[K('WB')] + HMr(k, tb), writes=pk(pv, 0, 256))
                op('vector', lambda e, pv=pv: e.bn_stats(out=ST6[:], in_=ps[pv][:, 0:256]),
                   reads=pk(pv, 0, 256), writes=[K('ST6')])
                op('vector', lambda e: e.bn_aggr(out=MV[:], in_=ST6[:]), reads=[K('ST6')], writes=[K('MV')])
                op('scalar', lambda e: e.activation(out=RV[:, 0:1], in_=MV[:, 1:2], func=AF.Ln, bias=EPS[:, 1:2], scale=1.0),
                   reads=[K('MV'), 'EPS'], writes=[K('RV0')])
                op('scalar', lambda e: e.activation(out=RV[:, 1:2], in_=RV[:, 0:1], func=AF.Exp, scale=-0.5),
                   reads=[K('RV0')], writes=[K('RV')])
                op('vector', lambda e, pv=pv, b=b: e.tensor_scalar(
                    out=VN[b][:], in0=ps[pv][:, 0:256], scalar1=MV[:, 0:1], scalar2=RV[:, 1:2],
                    op0=ALU.subtract, op1=ALU.mult),
                   reads=pk(pv, 0, 256) + [K('MV'), K('RV')], writes=[K('VN', b)])
                op('vector', lambda e, b=b: e.tensor_tensor(out=VN[b][:], in0=VN[b][:], in1=BCB[:, 0:256], op=ALU.mult),
                   reads=[K('VN', b), K('BCB')], writes=[K('VN', b)])
                op('vector', lambda e, b=b: e.tensor_tensor(out=VNB[b][:], in0=VN[b][:], in1=BCB[:, 256:512], op=ALU.add),
                   reads=[K('VN', b), K('BCB')], writes=[K('VNB', b)])
                for c in range(2):
                    for hl in range(2):
                        h = 2 * c + hl
                        op('tensor', lambda e, c=c, hl=hl, h=h, b=b, pss=pss: e.matmul(
                            ps[pss][hl * 64:(hl + 1) * 64, c * 128:(c + 1) * 128], lhsT=VNB[b][:, h * 64:(h + 1) * 64],
                            rhs=WMT[:, h, :], start=True, stop=True),
                           reads=[K('VNB', b), K('WMT')], writes=pk(pss, c * 128, c * 128 + 128))
                    op('vector', lambda e, c=c, b=b, pss=pss: e.tensor_tensor(
                        out=SB_[b][:], in0=ps[pss][:, c * 128:(c + 1) * 128], in1=SGB[:, c, :], op=ALU.add),
                       reads=pk(pss, c * 128, c * 128 + 128) + [K('SGB')], writes=[K('SB', b)])
                    op('vector', lambda e, c=c, b=b, ti=ti, tt0=tt0: e.tensor_tensor(
                        out=YT[:, 2 + c, tt0:tt0 + 128], in0=SB_[b][:], in1=ps[4 + c][:, ti * 128:(ti + 1) * 128], op=ALU.mult),
                       reads=[K('SB', b)] + pk(4 + c), writes=[K('YT', 2 + c, tb)])

        A.release(grp_mark); P.barrier()
        if 'C' not in os.environ.get('KSKIP', ''):
            rwkv(l, K, HM, YT, HMr, HMrs, A)

        A.release(grp_mark); P.barrier()
        WO = A([128, 8, D], BF16)
        MY = A([128, 8, 512], BF16)
        TMP = [A([128, 512], F32) for _ in range(2)]
        LNV = A([128, 512], F32)
        RS = A([128, 512], F32)
        SQ = [A([128, 512], BF16) for _ in range(2)]
        P.dma('gpsimd', K('ldwo'), lambda e: e.dma_start(out=WO[:], in_=wout_d[l]), writes=[K('WO')])
        it = 0
        for tb in range(4):
            t0 = tb * 512
            for dc in range(8):
                po = it % 2; sb = it % 2; it += 1
                for k in range(8):
                    op('tensor', lambda e, po=po, k=k, dc=dc, t0=t0: e.matmul(
                        ps[po][:], lhsT=WO[:, k, dc * 128:(dc + 1) * 128], rhs=YT[:, k, t0:t0 + 512],
                        start=(k == 0), stop=(k == 7)),
                       reads=[K('WO'), K('YT', k, tb)], writes=pk(po))
                op('scalar', lambda e, po=po, sb=sb: e.activation(out=SQ[sb][:], in_=ps[po][:], func=AF.Square),
                   reads=pk(po), writes=[K('SQ', sb)])
                op('vector', lambda e, po=po, dc=dc: e.tensor_copy(out=MY[:, dc, :], in_=ps[po][:]),
                   reads=pk(po), writes=[K('MY', dc)])
                op('tensor', lambda e, sb=sb, dc=dc: e.matmul(ps[6][:], lhsT=ONESB[:], rhs=SQ[sb][:], start=(dc == 0), stop=(dc == 7)),
                   reads=[K('SQ', sb), 'ONES'], writes=pk(6))
            rstd_from(ps[6], 512, 1.0 / D, 0, LNV, RS, pk(6), K('RS'), K('LNV'))
            for c in range(8):
                b = c % 2
                op('vector', lambda e, c=c, b=b: e.tensor_tensor(out=TMP[b][:], in0=MY[:, c, :], in1=RS[:], op=ALU.mult),
                   reads=[K('MY', c), K('RS')], writes=[K('TMP', b)])
                op('vector', lambda e, c=c, b=b, t0=t0: e.scalar_tensor_tensor(
                    out=XT[:, c, t0:t0 + 512], in0=TMP[b][:], scalar=vcol(l, V_G + 24 + c),
                    in1=XT[:, c, t0:t0 + 512], op0=ALU.mult, op1=ALU.add),
                   reads=[K('TMP', b), ('XT', c, tb), 'VEC'], writes=[('XT', c, tb)])

    def rwkv(l, K, HM, YT, HMr, HMrs, A):
        NB = 256
        WC = A([128, 8, 896], BF16)
        LORAB = A([128, 768], BF16)
        W0B = A([128, 256], F32)
        P.dma('gpsimd', K('ldwc'), lambda e: e.dma_start(out=WC[:], in_=win_d[l, :, :, 1280:2176]), writes=[K('WC')])
        P.dma('gpsimd', K('ldl'), lambda e: e.dma_start(out=LORAB[:], in_=lora_d[l]), writes=[K('LORA')])
        P.dma('sync', K('ldc'), lambda e: e.dma_start(out=W0B[:], in_=bc_d[l, :, 0:256]), writes=[K('W0B')])
        PC = A([128, 6 * NB], F32)
        MISC = A([128, NB], BF16)
        KKb = A([128, 2 * NB], F32); KPb = A([128, 2 * NB], F32); Bb = A([128, 2 * NB], F32)
        Gb = A([128, 2 * NB], BF16); BON = A([128, 2 * NB], BF16); VTB = A([128, 2 * NB], BF16)
        TQ = [A([128, NB], F32) for _ in range(2)]
        TS = [A([128, NB], F32) for _ in range(2)]
        SQB = [A([128, NB], BF16) for _ in range(2)]
        XW = A([128, 256], F32); SIGTM = A([128, 256], F32)
        EP = A([128, 256], F32); EM = A([128, 256], F32); EE = A([128, 256], F32)
        FT_ = A([128, 8 * 128], BF16)
        FH_ = A([128, 4 * 128], BF16)
        TM_ = A([128, 8 * 128], BF16)
        NM_ = A([128, 4 * 512], BF16)
        LP = [A([128, 512], BF16) for _ in range(2)]
        NP_ = [A([128, 512], BF16) for _ in range(2)]
        XF = [A([128, 512], BF16) for _ in range(2)]
        AHT = A([128, 256], BF16)
        UB = A([128, 256], BF16)
        S32 = A([128, 256], F32); SBF = A([128, 256], BF16)
        OT = A([128, 256], F32); OSQ = A([128, 256], F32)
        OM = A([128, 256], F32); OV = A([128, 256], F32); OL = A([128, 256], F32); ORS = A([128, 256], F32)
        BONF = CF[:, 512:640]
        vc = lambda off: vcol(l, off)
        cs = lambda c, a=0, b=NB: slice(c * NB + a, c * NB + b)
        hs = lambda h, a=0, b=128: slice(h * 128 + a, h * 128 + b)
        FT = lambda c, i, p0=0, p1=128: FT_[p0:p1, (c * 4 + i) * 128:(c * 4 + i + 1) * 128]
        FH = lambda c, i: FH_[:, (c * 2 + i) * 128:(c * 2 + i + 1) * 128]
        TM = lambda c, i, a=0, b=128: TM_[:, (c * 4 + i) * 128 + a:(c * 4 + i) * 128 + b]
        NM = lambda h, i, j: NM_[:, h * 512 + i * 256 + j * 128: h * 512 + i * 256 + (j + 1) * 128]
        op('vector', lambda e: e.memset(S32[:], 0.0), writes=[K('S32')])
        op('vector', lambda e: e.memset(SBF[:], 0.0), writes=[K('SBF')])
        allh = lambda nm, i: [K(nm, i, h) for h in range(4)]
        for sb in range(T // NB):
            t0 = sb * NB; tb = t0 // 512
            for cc in range(7):
                pb = cc % 2
                for k in range(8):
                    op('tensor', lambda e: e.matmul(ps[pb][:, 0:NB + 1], lhsT=WC[:, k, cc * 128:(cc + 1) * 128],
                                                    rhs=HM[:, k, t0:t0 + NB + 1], start=(k == 0), stop=(k == 7)),
                       reads=[K('WC')] + HMrs(k, tb), writes=pk(pb, 0, NB + 1))
                op('vector', lambda e: e.tensor_scalar(out=TS[pb][:], in0=ps[pb][:, 0:NB], scalar1=vc(V_MU + cc), scalar2=None,
                                                       op0=ALU.mult),
                   reads=pk(pb, 0, NB + 1) + ['VEC'], writes=[K('TS', pb)])
                dst = PC[:, cs(cc)] if cc < 6 else TQ[0][:]
                dk = K('PC', cc) if cc < 6 else K('TQ', 0)
                op('vector', lambda e: e.scalar_tensor_tensor(out=dst, in0=ps[pb][:, 1:NB + 1], scalar=OMMV[:, l * 7 + cc:l * 7 + cc + 1],
                                                              in1=TS[pb][:], op0=ALU.mult, op1=ALU.add),
                   reads=pk(pb, 0, NB + 1) + ['OMMV', K('TS', pb)], writes=[dk])
            op('scalar', lambda e: e.activation(out=MISC[0:32, :], in_=TQ[0][0:32, :], func=AF.Tanh), reads=[K('TQ', 0)], writes=[K('MISC', 0)])
            op('vector', lambda e: e.tensor_copy(out=MISC[32:64, :], in_=TQ[0][32:64, :]), reads=[K('TQ', 0)], writes=[K('MISC', 1)])
            op('scalar', lambda e: e.activation(out=MISC[64:128, :], in_=TQ[0][64:128, :], func=AF.Sigmoid),
               reads=[K('TQ', 0)], writes=[K('MISC', 2)])
            for c in range(2):
                op('tensor', lambda e: e.matmul(ps[2][:, 0:NB], lhsT=LORAB[32:64, 256 + c * 128:256 + (c + 1) * 128],
                                                rhs=MISC[32:64, :], start=True, stop=True),
                   reads=[K('LORA'), K('MISC', 1)], writes=pk(2, 0, NB))
                op('scalar', lambda e: e.activation(out=Bb[:, cs(c)], in_=ps[2][:, 0:NB], func=AF.Sigmoid, bias=vc(V_A0 + c), scale=1.0),
                   reads=pk(2, 0, NB) + ['VEC'], writes=[K('B', c)])
                op('tensor', lambda e: e.matmul(ps[3][:, 0:NB], lhsT=LORAB[64:128, 512 + c * 128:512 + (c + 1) * 128],
                                                rhs=MISC[64:128, :], start=True, stop=True),
                   reads=[K('LORA'), K('MISC', 2)], writes=pk(3, 0, NB))
                op('vector', lambda e: e.tensor_copy(out=Gb[:, cs(c)], in_=ps[3][:, 0:NB]), reads=pk(3, 0, NB), writes=[K('G', c)])
                op('vector', lambda e: e.tensor_scalar(out=KKb[:, cs(c)], in0=PC[:, cs(2 + c)], scalar1=vc(V_KK + c), scalar2=None, op0=ALU.mult),
                   reads=[K('PC', 2 + c), 'VEC'], writes=[K('KK', c)])
                op('scalar', lambda e: e.activation(out=SQB[c][:], in_=KKb[:, cs(c)], func=AF.Square), reads=[K('KK', c)], writes=[K('SQB', c)])
                op('tensor', lambda e: e.matmul(ps[4][:, 0:NB], lhsT=BONES, rhs=SQB[c][:], start=True, stop=True),
                   reads=[K('SQB', c), 'CB'], writes=pk(4, 0, NB))
                op('scalar', lambda e: e.activation(out=TQ[0][:], in_=ps[4][:, 0:NB], func=AF.Ln, bias=EPS[:, 3:4], scale=1.0),
                   reads=pk(4, 0, NB) + ['EPS'], writes=[K('TQ', 0)])
                op('scalar', lambda e: e.activation(out=TQ[1][:], in_=TQ[0][:], func=AF.Exp, scale=-0.5), reads=[K('TQ', 0)], writes=[K('TQ', 1)])
                op('vector', lambda e: e.tensor_tensor(out=KKb[:, cs(c)], in0=KKb[:, cs(c)], in1=TQ[1][:], op=ALU.mult),
                   reads=[K('KK', c), K('TQ', 1)], writes=[K('KK', c)])
                op('vector', lambda e: e.tensor_scalar(out=TQ[0][:], in0=Bb[:, cs(c)], scalar1=-1.0, scalar2=vc(V_KA + c), op0=ALU.add, op1=ALU.mult),
                   reads=[K('B', c), 'VEC'], writes=[K('TQ', 0)])
                op('vector', lambda e: e.scalar_tensor_tensor(out=KPb[:, cs(c)], in0=TQ[0][:], scalar=1.0, in1=PC[:, cs(2 + c)],
                                                              op0=ALU.add, op1=ALU.mult),
                   reads=[K('TQ', 0), K('PC', 2 + c)], writes=[K('KP', c)])
                op('vector', lambda e: e.tensor_tensor(out=Bb[:, cs(c)], in0=Bb[:, cs(c)], in1=KKb[:, cs(c)], op=ALU.mult),
                   reads=[K('B', c), K('KK', c)], writes=[K('B', c)])
                op('vector', lambda e: e.scalar_tensor_tensor(out=TQ[1][:], in0=PC[:, cs(c)], scalar=vc(V_RK + c), in1=KPb[:, cs(c)],
                                                              op0=ALU.mult, op1=ALU.mult),
                   reads=[K('PC', c), K('KP', c), 'VEC'], writes=[K('TQ', 1)])
                op('vector', lambda e: e.tensor_copy(out=SQB[c][:], in_=TQ[1][:]), reads=[K('TQ', 1)], writes=[K('SQB', c)])
                op('tensor', lambda e: e.matmul(ps[5][:, 0:NB], lhsT=BONES, rhs=SQB[c][:], start=True, stop=True),
                   reads=[K('SQB', c), 'CB'], writes=pk(5, 0, NB))
                op('scalar', lambda e: e.activation(out=BON[:, cs(c)], in_=ps[5][:, 0:NB], func=AF.Copy), reads=pk(5, 0, NB), writes=[K('BON', c)])
                op('vector', lambda e: e.tensor_copy(out=VTB[:, cs(c)], in_=PC[:, cs(4 + c)]), reads=[K('PC', 4 + c)], writes=[K('VTB', c)])
            for ti in range(NB // 128):
                q0 = ti * 128; q1 = q0 + 128; tt0 = t0 + q0
                op('tensor', lambda e: e.matmul(ps[2][:, 0:256], lhsT=MISC[0:32, q0:q1], rhs=LORAB[0:32, 0:256], start=True, stop=True),
                   reads=[K('MISC', 0), K('LORA')], writes=pk(2, 0, 256))
                op('vector', lambda e: e.tensor_tensor(out=XW[:], in0=ps[2][:, 0:256], in1=W0B[:], op=ALU.add),
                   reads=pk(2, 0, 256) + [K('W0B')], writes=[K('XW')])
                op('scalar', lambda e: e.activation(out=SIGTM[:], in_=XW[:], func=AF.Sigmoid), reads=[K('XW')], writes=[K('SIGTM')])
                for c in range(2):
                    op('tensor', lambda e: e.matmul(ps[3][:, c * 128:(c + 1) * 128], lhsT=SIGTM[:, c * 128:(c + 1) * 128], rhs=TRI_IF,
                                                    start=True, stop=True),
                       reads=[K('SIGTM'), 'CF'], writes=pk(3, c * 128, c * 128 + 128))
                    op('tensor', lambda e: e.matmul(ps[3][:, 256 + c * 128:256 + (c + 1) * 128], lhsT=SIGTM[:, c * 128:(c + 1) * 128],
                                                    rhs=TRI_SF, start=True, stop=True),
                       reads=[K('SIGTM'), 'CF'], writes=pk(3, 256 + c * 128, 256 + c * 128 + 128))
                op('scalar', lambda e: e.activation(out=EP[:], in_=ps[3][:, 0:256], func=AF.Exp, scale=-C_DECAY), reads=pk(3, 0, 256), writes=[K('EP')])
                op('scalar', lambda e: e.activation(out=EM[:], in_=ps[3][:, 0:256], func=AF.Exp, scale=C_DECAY), reads=pk(3, 0, 256), writes=[K('EM')])
                op('scalar', lambda e: e.activation(out=EE[:], in_=ps[3][:, 256:512], func=AF.Exp, scale=-C_DECAY), reads=pk(3, 256, 512), writes=[K('EE')])
                for c in range(2):
                    E_ = lambda X_: X_[:, c * 128:(c + 1) * 128]
                    gC = EP[:, c * 128 + 127:c * 128 + 128]
                    tq = slice(c * NB + q0, c * NB + q1)
                    op('vector', lambda e: e.scalar_tensor_tensor(out=FT(c, 0), in0=KKb[:, tq], scalar=-1.0, in1=E_(EE), op0=ALU.mult, op1=ALU.mult),
                       reads=[K('KK', c), K('EE')], writes=[K('FT', c)])
                    op('vector', lambda e: e.tensor_tensor(out=FT(c, 1), in0=Bb[:, tq], in1=E_(EM), op=ALU.mult),
                       reads=[K('B', c), K('EM')], writes=[K('FT', c)])
                    op('vector', lambda e: e.tensor_tensor(out=FT(c, 2), in0=KPb[:, tq], in1=E_(EM), op=ALU.mult),
                       reads=[K('KP', c), K('EM')], writes=[K('FT', c)])
                    op('vector', lambda e: e.tensor_tensor(out=FT(c, 3), in0=PC[:, tq], in1=E_(EP), op=ALU.mult),
                       reads=[K('PC', c), K('EP')], writes=[K('FT', c)])
                    op('vector', lambda e: e.scalar_tensor_tensor(out=FH(c, 0), in0=Bb[:, tq], scalar=gC, in1=E_(EM), op0=ALU.mult, op1=ALU.mult),
                       reads=[K('B', c), K('EM'), K('EP')], writes=[K('FH', c)])
                    op('vector', lambda e: e.scalar_tensor_tensor(out=FH(c, 1), in0=KPb[:, tq], scalar=gC, in1=E_(EM), op0=ALU.mult, op1=ALU.mult),
                       reads=[K('KP', c), K('EM'), K('EP')], writes=[K('FH', c)])
                    srcs = [(FT(c, 0), K('FT', c)), (VTB[:, tq], K('VTB', c)), (FH(c, 0), K('FH', c)), (FH(c, 1), K('FH', c))]
                    for i, (src, sk) in enumerate(srcs):
                        op('tensor', lambda e: e.transpose(out=psT[:, (c * 4 + i) * 128:(c * 4 + i + 1) * 128], in_=src, identity=IDB),
                           reads=[sk, 'CB'], writes=[('ps', 7)])
                    if c == 0:
                        op('scalar', lambda e: e.activation(out=TM_[:, 0:512], in_=psT[:, 0:512], func=AF.Copy), reads=[('ps', 7)], writes=[K('TM', 0)])
                    else:
                        op('vector', lambda e: e.tensor_copy(out=TM_[:, 512:1024], in_=psT[:, 512:1024]), reads=[('ps', 7)], writes=[K('TM', 1)])
                v4 = lambda X_: X_.rearrange("p (h x) -> p h x", h=4)
                for h in range(4):
                    c = h // 2; hl = h % 2; r0 = hl * 64; r1 = r0 + 64
                    aT = FT(c, 0, r0, r1); bT = FT(c, 1, r0, r1); kT = FT(c, 2, r0, r1); rT = FT(c, 3, r0, r1)
                    for i, lt in enumerate((bT, kT)):
                        for j2, rt in enumerate((aT, rT)):
                            qq = i * 2 + j2
                            op('tensor', lambda e: e.matmul(ps[h][:, qq * 128:(qq + 1) * 128], lhsT=lt, rhs=rt, start=True, stop=True),
                               reads=[K('FT', c)], writes=pk(h))
                    op('tensor', lambda e: e.matmul(ps[4 + hl][:, c * 128:(c + 1) * 128], lhsT=aT, rhs=bT, start=True, stop=True),
                       reads=[K('FT', c)], writes=pk(4 + hl))
                for h in range(4):
                    op('vector', lambda e: e.tensor_tensor(out=NM_[:, h * 512:(h + 1) * 512], in0=ps[h][:], in1=CBX[:, 0:512], op=ALU.mult),
                       reads=pk(h) + ['CBX'], writes=[K('NM', h)])
                for hl in range(2):
                    for c in range(2):
                        h = 2 * c + hl
                        op('vector', lambda e: e.tensor_tensor(out=LP[0][:, hs(h)], in0=ps[4 + hl][:, c * 128:(c + 1) * 128],
                                                               in1=TRILB, op=ALU.mult),
                           reads=pk(4 + hl) + ['CB'], writes=[K('LP', 0, h)])
                for h in range(4):
                    op('scalar', lambda e: e.activation(out=NP_[0][:, hs(h)], in_=NM(h, 0, 0), func=AF.Copy),
                       reads=[K('NM', h)], writes=[K('NP', 0, h)])
                for h in range(4):
                    c = h // 2; hl = h % 2; r0 = hl * 64; r1 = r0 + 64
                    op('tensor', lambda e: e.matmul(ps[6][:, hs(h, 64, 128)], lhsT=NM(h, 1, 0), rhs=TM(c, 1, r0, r1), start=True, stop=True),
                       reads=[K('NM', h), K('TM', c)], writes=pk(6))
                for h in range(4):
                    c = h // 2; hl = h % 2; r0 = hl * 64; r1 = r0 + 64
                    op('vector', lambda e: e.tensor_copy(out=XF[0][:, hs(h, 64, 128)], in_=ps[6][:, hs(h, 64, 128)]),
                       reads=pk(6), writes=[K('XF', 0, h)])
                    op('scalar', lambda e: e.activation(out=XF[0][:, hs(h, 0, 64)], in_=TM(c, 0, r0, r1), func=AF.Copy),
                       reads=[K('TM', c), K('XF', 0, h)], writes=[K('XF', 0, h)])
                cur = 0
                for lvl in range(7 if 'L' not in os.environ.get('KSKIP', '') else 0):
                    nxt = 1 - cur
                    for h in range(4):
                        op('tensor', lambda e: e.matmul(ps[4][:, hs(h)], lhsT=NP_[cur][:, hs(h)], rhs=XF[cur][:, hs(h)], start=True, stop=True),
                           reads=[K('NP', cur, h), K('XF', cur, h)], writes=pk(4, h * 128, h * 128 + 128))
                    op('vector', lambda e: e.tensor_tensor(out=XF[nxt][:], in0=XF[cur][:], in1=ps[4][:], op=ALU.add),
                       reads=allh('XF', cur) + pk(4), writes=allh('XF', nxt))
                    if lvl < 6:
                        for h in range(4):
                            op('tensor', lambda e: e.matmul(ps[5][:, hs(h)], lhsT=LP[cur][:, hs(h)], rhs=NP_[cur][:, hs(h)], start=True, stop=True),
                               reads=[K('LP', cur, h), K('NP', cur, h)], writes=pk(5, h * 128, h * 128 + 128))
                        op('scalar', lambda e: e.activation(out=NP_[nxt][:], in_=ps[5][:], func=AF.Copy), reads=pk(5), writes=allh('NP', nxt))
                        if lvl < 5:
                            for h in range(4):
                                op('tensor', lambda e: e.matmul(ps[6][:, hs(h)], lhsT=NP_[cur][:, hs(h)], rhs=LP[cur][:, hs(h)],
                                                                start=True, stop=True),
                                   reads=[K('LP', cur, h), K('NP', cur, h)], writes=pk(6, h * 128, h * 128 + 128))
                            op('scalar', lambda e: e.activation(out=LP[nxt][:], in_=ps[6][:], func=AF.Copy), reads=pk(6), writes=allh('LP', nxt))
                    cur = nxt
                XFf = XF[cur]
                for c in range(2):
                    for hl in range(2):
                        h = 2 * c + hl
                        op('tensor', lambda e: e.transpose(out=psT[hl * 64:(hl + 1) * 64, c * 128:(c + 1) * 128], in_=XFf[:, hs(h, 0, 64)], identity=IDB),
                           reads=[K('XF', cur, h), 'CB'], writes=[('ps', 7)])
                op('vector', lambda e: e.tensor_copy(out=AHT[:], in_=psT[:, 0:256]), reads=[('ps', 7)], writes=[K('AHT')])
                for c in range(2):
                    cb = slice(c * 128, (c + 1) * 128)
                    op('tensor', lambda e: e.matmul(ps[0][:, cb], lhsT=AHT[:, cb], rhs=SBF[:, cb], start=True, stop=True),
                       reads=[K('AHT'), K('SBF')], writes=pk(0, c * 128, c * 128 + 128))
                    for hl in range(2):
                        h = 2 * c + hl
                        op('vector', lambda e: e.tensor_tensor(out=UB[:, h * 64:(h + 1) * 64], in0=ps[0][:, c * 128 + hl * 64:c * 128 + hl * 64 + 64],
                                                               in1=XFf[:, hs(h, 64, 128)], op=ALU.add),
                           reads=pk(0, c * 128, c * 128 + 128) + [K('XF', cur, h)], writes=[K('UB', h)])
                    op('tensor', lambda e: e.matmul(ps[1][:, cb], lhsT=SBF[:, cb], rhs=FT(c, 3), start=True, stop=False, skip_group_check=True),
                       reads=[K('SBF'), K('FT', c)], writes=pk(1, c * 128, c * 128 + 128))
                    for hl in range(2):
                        h = 2 * c + hl; r0 = hl * 64; r1 = r0 + 64
                        op('tensor', lambda e: e.matmul(ps[1][r0:r1, cb], lhsT=UB[:, h * 64:(h + 1) * 64], rhs=NM(h, 0, 1),
                                                        start=False, stop=False, skip_group_check=True),
                           reads=[K('UB', h), K('NM', h)], writes=pk(1, c * 128, c * 128 + 128))
                        op('tensor', lambda e: e.matmul(ps[1][r0:r1, cb], lhsT=TM(c, 1, r0, r1), rhs=NM(h, 1, 1),
                                                        start=False, stop=True, skip_group_check=True),
                           reads=[K('TM', c), K('NM', h)], writes=pk(1, c * 128, c * 128 + 128))
                    for hl in range(2):
                        h = 2 * c + hl; r0 = hl * 64; r1 = r0 + 64
                        so = slice(256 + c * 128 + r0, 256 + c * 128 + r1)
                        sd = slice(c * 128 + r0, c * 128 + r1)
                        op('tensor', lambda e: e.matmul(ps[0][r0:r1, so], lhsT=TM(c, 2, r0, r1), rhs=UB[:, h * 64:(h + 1) * 64],
                                                        start=True, stop=False, skip_group_check=True),
                           reads=[K('TM', c), K('UB', h)], writes=pk(0, 256 + c * 128, 256 + c * 128 + 128))
                        op('tensor', lambda e: e.matmul(ps[0][r0:r1, so], lhsT=TM(c, 3, r0, r1), rhs=TM(c, 1, r0, r1),
                                                        start=False, stop=True, skip_group_check=True),
                           reads=[K('TM', c)], writes=pk(0, 256 + c * 128, 256 + c * 128 + 128))
                        op('vector', lambda e: e.scalar_tensor_tensor(out=S32[r0:r1, sd], in0=S32[r0:r1, sd], scalar=EP[r0:r1, c * 128 + 127:c * 128 + 128],
                                                                      in1=ps[0][r0:r1, so], op0=ALU.mult, op1=ALU.add),
                           reads=[K('S32'), K('EP')] + pk(0, 256 + c * 128, 256 + c * 128 + 128), writes=[K('S32')])
                        op('vector', lambda e: e.tensor_copy(out=SBF[r0:r1, sd], in_=S32[r0:r1, sd]), reads=[K('S32')], writes=[K('SBF')])
                if 'E' in os.environ.get('KSKIP', ''):
                    continue
                op('scalar', lambda e: e.activation(out=OT[:], in_=ps[1][:, 0:256], func=AF.Copy), reads=pk(1, 0, 256), writes=[K('OT')])
                op('vector', lambda e: e.tensor_tensor(out=OSQ[:], in0=OT[:], in1=OT[:], op=ALU.mult), reads=[K('OT')], writes=[K('OSQ')])
                for c in range(2):
                    cb = slice(c * 128, (c + 1) * 128)
                    op('tensor', lambda e: e.matmul(ps[2][:, cb], lhsT=BONF, rhs=OT[:, cb], start=True, stop=True),
                       reads=[K('OT'), 'CF'], writes=pk(2, c * 128, c * 128 + 128))
                    op('tensor', lambda e: e.matmul(ps[2][:, 256 + c * 128:256 + (c + 1) * 128], lhsT=BONF, rhs=OSQ[:, cb], start=True, stop=True),
                       reads=[K('OSQ'), 'CF'], writes=pk(2, 256 + c * 128, 256 + c * 128 + 128))
                v2 = lambda X_: X_.rearrange("p (c x) -> p c x", c=2)
                BV = lambda X_: v2(X_[:])[:, :, q0:q1]
                op('scalar', lambda e: e.activation(out=OM[:], in_=ps[2][:, 0:256], func=AF.Copy, scale=1.0 / 64), reads=pk(2), writes=[K('OM')])
                op('vector', lambda e: e.tensor_tensor(out=OV[:], in0=OM[:], in1=OM[:], op=ALU.mult), reads=[K('OM')], writes=[K('OV')])
                op('vector', lambda e: e.scalar_tensor_tensor(out=OV[:], in0=ps[2][:, 256:512], scalar=1.0 / 64, in1=OV[:],
                                                              op0=ALU.mult, op1=ALU.subtract),
                   reads=pk(2) + [K('OV')], writes=[K('OV')])
                op('scalar', lambda e: e.activation(out=OL[:], in_=OV[:], func=AF.Ln, bias=EPS[:, 2:3], scale=1.0), reads=[K('OV'), 'EPS'], writes=[K('OL')])
                op('scalar', lambda e: e.activation(out=ORS[:], in_=OL[:], func=AF.Exp, scale=-0.5), reads=[K('OL')], writes=[K('ORS')])
                op('vector', lambda e: e.tensor_tensor(out=OM[:], in0=OT[:], in1=OM[:], op=ALU.subtract), reads=[K('OT'), K('OM')], writes=[K('OM')])
                op('vector', lambda e: e.tensor_tensor(out=OM[:], in0=OM[:], in1=ORS[:], op=ALU.mult), reads=[K('OM'), K('ORS')], writes=[K('OM')])
                for c in range(2):
                    cb = slice(c * 128, (c + 1) * 128)
                    op('vector', lambda e: e.tensor_scalar(out=OM[:, cb], in0=OM[:, cb], scalar1=vc(V_RLW + c), scalar2=vc(V_RLB + c),
                                                           op0=ALU.mult, op1=ALU.add),
                       reads=[K('OM'), 'VEC'], writes=[K('OM')])
                op('vector', lambda e: e.tensor_tensor(out=v2(OV[:]), in0=BV(BON), in1=BV(VTB), op=ALU.mult),
                   reads=[K('BON', 0), K('BON', 1), K('VTB', 0), K('VTB', 1)], writes=[K('OV')])
                op('vector', lambda e: e.tensor_tensor(out=OM[:], in0=OM[:], in1=OV[:], op=ALU.add), reads=[K('OM'), K('OV')], writes=[K('OM')])
                op('vector', lambda e: e.tensor_tensor(out=YT[:, 4:6, tt0:tt0 + 128], in0=v2(OM[:]), in1=BV(Gb), op=ALU.mult),
                   reads=[K('OM'), K('G', 0), K('G', 1)], writes=[K('YT', 4, tb), K('YT', 5, tb)])

    seq = []
    for l in range(n_layers):
        seq += [('ffn', l, 0), ('mix', l), ('ffn', l, 1)]
    for s_ in seq:
        if s_[0] == 'ffn':
            ffn(s_[1], s_[2])
        else:
            mixer(s_[1])
        if stop_after is not None and tuple(stop_after) == tuple(s_):
            break
    for c in range(8):
        P.dma('sync', 'st_o', lambda e, c=c: e.dma_start(out=out_d[:, c, :], in_=XT[:, c, :]),
              reads=[('XT', c, tb) for tb in range(4)], writes=[('OUT', c)])
    P.wait_all('sync', [('OUT', c) for c in range(8)])
    with nc.Block() as block:
        P.emit(block)
    stack.close()
    return nc


def host_prep(inp):
    f = lambda a: np.ascontiguousarray(a, dtype=np.float32)
    tri_incl = np.triu(np.ones((128, 128), np.float32))
    tri_strict = np.triu(np.ones((128, 128), np.float32), 1)
    bo = np.zeros((128, 128), np.float32); bo[:64, :64] = 1; bo[64:, 64:] = 1
    consts = np.concatenate([np.eye(128, dtype=np.float32), tri_incl, tri_strict, tri_strict.T.copy(), bo], axis=1)
    fm = lambda v: np.asarray(v).reshape(-1, 128).T
    vecs = np.zeros((128, NL * NV), np.float32)
    for l in range(NL):
        o = l * NV
        for i, nm in enumerate(['ffn1_pre_g', 'ffn1_post_g', 'mix_pre_g', 'mix_post_g', 'ffn2_pre_g', 'ffn2_post_g']):
            vecs[:, o + V_G + i * 8: o + V_G + i * 8 + 8] = fm(inp[nm][l])
        for c in range(2):
            for j in range(3):
                vecs[:, o + V_SC + c * 3 + j] = inp['sc_conv_w'][l, j, c * 128:(c + 1) * 128]
            for j in range(31):
                vecs[:, o + V_CM + c * 31 + j] = inp['cm_conv_w'][l, j, c * 128:(c + 1) * 128]
        for off, nm in [(V_CMB, 'cm_conv_b'), (V_CMLW, 'cm_ln_w'), (V_CMLB, 'cm_ln_b'), (V_A0, 'rk_a0'), (V_KK, 'rk_k_k'),
                        (V_KA, 'rk_k_a'), (V_RLW, 'rk_ln_w'), (V_RLB, 'rk_ln_b')]:
            vecs[:, o + off: o + off + 2] = fm(inp[nm][l])
        vecs[:, o + V_RK: o + V_RK + 2] = fm(inp['rk_r_k'][l].reshape(-1))
        vecs[:, o + V_MU: o + V_MU + 7] = fm(inp['rk_mu'][l])
    wgu = np.empty((NL, 2, NFC, 128, 2, 8, 128), np.float32)
    wd = np.empty((NL, 2, 8, 128, NFC, 128), np.float32)
    wsrc = {('ffn1', 'w_gate'): inp['ffn1_w_gate'], ('ffn1', 'w_up'): inp['ffn1_w_up'], ('ffn1', 'w_down'): inp['ffn1_w_down'],
            ('ffn2', 'w_gate'): inp['ffn2_w_gate'], ('ffn2', 'w_up'): inp['ffn2_w_up'], ('ffn2', 'w_down'): inp['ffn2_w_down']}
    for wi, pre in enumerate(['ffn1', 'ffn2']):
        for gi, nm in enumerate(['w_gate', 'w_up']):
            w = np.asarray(wsrc[(pre, nm)])
            wgu[:, wi, :, :, gi, :, :] = w.reshape(NL, 8, 128, NFC, 128).transpose(0, 3, 2, 1, 4)
        w = np.asarray(wsrc[(pre, 'w_down')])
        wd[:, wi] = w.reshape(NL, NFC, 128, 8, 128).transpose(0, 3, 2, 1, 4)
    win = f(np.asarray(inp['w_in']).reshape(NL, 8, 128, INC).transpose(0, 2, 1, 3))
    wout = f(np.asarray(inp['w_out']).reshape(NL, 8, 128, D).transpose(0, 2, 1, 3))
    bc = np.empty((NL, 128, 768), np.float32)
    lora = np.zeros((NL, 128, 768), np.float32)
    for l in range(NL):
        row = np.concatenate([inp['rk_w0'][l], inp['sg_ln_w'][l], inp['sg_ln_b'][l]])
        bc[l] = np.broadcast_to(row[None, :], (128, row.shape[0]))
        lora[l, 0:32, 0:256] = inp['rk_w_up'][l]
        lora[l, 32:64, 256:512] = inp['rk_a_up'][l]
        lora[l, 64:128, 512:768] = inp['rk_g_up'][l]
    wmt = f(np.asarray(inp['sg_w']).transpose(0, 3, 1, 2))
    sgb = np.asarray(inp['sg_b'])
    sgbT = f(np.repeat(sgb.reshape(NL, 2, 2, 1, 128), 64, axis=3).reshape(NL, 2, 128, 128).transpose(0, 2, 1, 3))
    shared = dict(consts=f(consts), vecs=f(vecs), wgu=wgu, wd=wd, win=win, wout=wout, bc=bc, lora=lora, wmt=wmt, sgbT=sgbT)
    x = np.asarray(inp['x'])
    maps = []
    for b in range(8):
        xt = f(x[b].T.reshape(8, 128, T).transpose(1, 0, 2))
        m = dict(shared); m['xT'] = xt
        maps.append(m)
    return maps


_NC = None


def kernel(**inputs):
    global _NC
    inp = {k: np.asarray(v) for k, v in inputs.items()}
    maps = host_prep(inp)
    if _NC is None:
        _NC = build()
    res = run_bass_kernel_spmd(_NC, maps, core_ids=list(range(8)))
    out = np.empty((8, T, D), np.float32)
    for b in range(8):
        o = np.asarray(res.results[b]["outT"])
        out[b] = o.transpose(1, 0, 2).reshape(D, T).T
    return out
```
